# Optimizing a Trainium2 kernel written in Bass

```python
import jax, jax.numpy as jnp
from jax import lax
import numpy as np

D_MODEL = 1024
BATCH = 16
SEQ = 2048
DEPTH = 1

FOX_HEADS = 8
FOX_HEAD_DIM = 128
FOX_WIDTH = FOX_HEADS * FOX_HEAD_DIM
Q_BLOCK = 128
ML_HEADS = 4
ML_QK_DIM = 128
ML_V_DIM = 256
ML_QK_WIDTH = ML_HEADS * ML_QK_DIM
ML_WIDTH = ML_HEADS * ML_V_DIM
ML_CHUNK = 64
CONV_WIDTH = 4
EPS = 1e-6

SPLIT_SIZES = (FOX_WIDTH, FOX_WIDTH, FOX_WIDTH, FOX_HEADS, FOX_WIDTH,
               ML_QK_WIDTH, ML_QK_WIDTH, ML_WIDTH, ML_HEADS, ML_HEADS, ML_WIDTH, ML_WIDTH,
               D_MODEL, D_MODEL)
IN_WIDTH = sum(SPLIT_SIZES)

kernel_name = "fox_mlstm_gated_hybrid"


def rmsnorm(x, g):
    xf = x.astype(jnp.float32)
    y = xf * lax.rsqrt(jnp.mean(xf * xf, axis=-1, keepdims=True) + EPS)
    return (y * g.astype(jnp.float32)).astype(x.dtype)


def split_columns(z):
    parts, off = [], 0
    for size in SPLIT_SIZES:
        parts.append(z[..., off:off + size])
        off += size
    return parts


def fox_attention(q, k, v, logf):
    S = q.shape[1]
    c = jnp.transpose(jnp.cumsum(logf, axis=1), (0, 2, 1))
    q = q * (FOX_HEAD_DIM ** -0.5)
    outs = []
    for blk in range(S // Q_BLOCK):
        lo, hi = blk * Q_BLOCK, (blk + 1) * Q_BLOCK
        s = jnp.einsum('bqhd,bkhd->bhqk', q[:, lo:hi], k[:, :hi]).astype(jnp.float32)
        s = s + c[:, :, lo:hi, None] - c[:, :, None, :hi]
        causal = (lo + jnp.arange(Q_BLOCK))[:, None] >= jnp.arange(hi)[None, :]
        p = jax.nn.softmax(jnp.where(causal, s, -jnp.inf), axis=-1)
        outs.append(jnp.einsum('bhqk,bkhd->bqhd', p.astype(v.dtype), v[:, :hi]))
    return jnp.concatenate(outs, axis=1)


def causal_depthwise_conv(x, w, b):
    C = x.shape[-1]
    y = lax.conv_general_dilated(x, w[:, None, :].astype(x.dtype), window_strides=(1,),
                                 padding=[(CONV_WIDTH - 1, 0)],
                                 dimension_numbers=('NWC', 'WIO', 'NWC'),
                                 feature_group_count=C)
    return y + b.astype(x.dtype)


def mlstm_chunkwise(q, k, v, ig, lf):
    B, S, H, Dk = q.shape
    Dv = v.shape[-1]
    L = ML_CHUNK
    NC = S // L

    def chunks(a):
        return jnp.transpose(a.reshape(B, NC, L, H, a.shape[-1]), (1, 0, 3, 2, 4))

    qc, kc, vc = chunks(q * (Dk ** -0.5)), chunks(k), chunks(v)
    igc, lfc = chunks(ig[..., None])[..., 0], chunks(lf[..., None])[..., 0]
    causal = jnp.tril(jnp.ones((L, L), dtype=bool))

    def step(carry, inp):
        C, n, m = carry
        qb, kb, vb, ib, fb = inp
        b = jnp.cumsum(fb, axis=-1)
        D = jnp.where(causal, b[..., :, None] - b[..., None, :] + ib[..., None, :], -jnp.inf)
        inter = b + m[..., None]
        m_t = jnp.maximum(inter, jnp.max(D, axis=-1))
        w_intra = jnp.exp(D - m_t[..., None])
        w_inter = jnp.exp(inter - m_t)
        s = jnp.einsum('bhtd,bhsd->bhts', qb, kb) * w_intra
        num = jnp.einsum('bhts,bhsv->bhtv', s, vb) + w_inter[..., None] * jnp.einsum('bhtd,bhdv->bhtv', qb, C)
        den = jnp.sum(s, axis=-1) + w_inter * jnp.einsum('bhtd,bhd->bht', qb, n)
        h = num / jnp.maximum(jnp.abs(den), jnp.exp(-m_t))[..., None]
        bL = b[..., -1]
        logw = bL[..., None] - b + ib
        m_new = jnp.maximum(bL + m, jnp.max(logw, axis=-1))
        w = jnp.exp(logw - m_new[..., None])
        decay = jnp.exp(bL + m - m_new)
        C_new = decay[..., None, None] * C + jnp.einsum('bhs,bhsd,bhsv->bhdv', w, kb, vb)
        n_new = decay[..., None] * n + jnp.einsum('bhs,bhsd->bhd', w, kb)
        return (C_new, n_new, m_new), h

    init = (jnp.zeros((B, H, Dk, Dv), jnp.float32), jnp.zeros((B, H, Dk), jnp.float32),
            jnp.zeros((B, H), jnp.float32))
    _, h = lax.scan(step, init, (qc, kc, vc, igc, lfc))
    return jnp.transpose(h, (1, 0, 3, 2, 4)).reshape(B, S, H, Dv)


def setup_inputs(seed: int = 0) -> dict:
    key = jax.random.key(seed)
    ks = jax.random.split(key, 16)
    nrm = lambda k, shape, scale: jax.random.normal(k, shape, jnp.float32) * scale
    return {
        "x": nrm(ks[0], (BATCH, SEQ, D_MODEL), 1.0),
        "norm_g": 1.0 + nrm(ks[1], (DEPTH, D_MODEL), 0.05),
        "w_in": nrm(ks[2], (DEPTH, D_MODEL, IN_WIDTH), D_MODEL ** -0.5),
        "b_fox_f": jax.random.uniform(ks[3], (DEPTH, FOX_HEADS), jnp.float32, 1.0, 4.0),
        "conv_w": nrm(ks[4], (DEPTH, CONV_WIDTH, 2 * ML_QK_WIDTH), CONV_WIDTH ** -0.5),
        "conv_b": nrm(ks[5], (DEPTH, 2 * ML_QK_WIDTH), 0.01),
        "b_ml_i": nrm(ks[6], (DEPTH, ML_HEADS), 0.1),
        "b_ml_f": jax.random.uniform(ks[7], (DEPTH, ML_HEADS), jnp.float32, 3.0, 6.0),
        "ml_norm_g": 1.0 + nrm(ks[8], (DEPTH, ML_WIDTH), 0.05),
        "b_gate": nrm(ks[9], (DEPTH, 2 * D_MODEL), 0.1),
        "w_fox_down": nrm(ks[10], (DEPTH, FOX_WIDTH, D_MODEL), FOX_WIDTH ** -0.5),
        "w_ml_down": nrm(ks[11], (DEPTH, ML_WIDTH, D_MODEL), ML_WIDTH ** -0.5),
        "w_out": nrm(ks[12], (DEPTH, D_MODEL, D_MODEL), D_MODEL ** -0.5),
        "final_g": 1.0 + nrm(ks[13], (D_MODEL,), 0.05),
    }


def reference(x, norm_g, w_in, b_fox_f, conv_w, conv_b, b_ml_i, b_ml_f, ml_norm_g, b_gate,
              w_fox_down, w_ml_down, w_out, final_g):
    B, S, _ = x.shape
    f32 = jnp.float32
    for l in range(DEPTH):
        xn = rmsnorm(x, norm_g[l])
        z = jnp.einsum('bsd,de->bse', xn, w_in[l])
        fq, fk, fv, ff, fz, mq, mk, mv, mi, mf, mo, mz, ga, gb = split_columns(z)

        logf_a = jax.nn.log_sigmoid(ff.astype(f32) + b_fox_f[l].astype(f32))
        o_a = fox_attention(fq.reshape(B, S, FOX_HEADS, FOX_HEAD_DIM),
                            fk.reshape(B, S, FOX_HEADS, FOX_HEAD_DIM),
                            fv.reshape(B, S, FOX_HEADS, FOX_HEAD_DIM), logf_a)
        o_a = o_a.reshape(B, S, FOX_WIDTH) * jax.nn.silu(fz)
        y_a = jnp.einsum('bse,ed->bsd', o_a, w_fox_down[l])

        qk = jax.nn.silu(causal_depthwise_conv(jnp.concatenate([mq, mk], axis=-1), conv_w[l], conv_b[l]))
        q_b = qk[..., :ML_QK_WIDTH].reshape(B, S, ML_HEADS, ML_QK_DIM).astype(f32)
        k_b = qk[..., ML_QK_WIDTH:].reshape(B, S, ML_HEADS, ML_QK_DIM).astype(f32)
        v_b = mv.reshape(B, S, ML_HEADS, ML_V_DIM).astype(f32)
        ig = mi.astype(f32) + b_ml_i[l].astype(f32)
        lf = jax.nn.log_sigmoid(mf.astype(f32) + b_ml_f[l].astype(f32))
        h_b = mlstm_chunkwise(q_b, k_b, v_b, ig, lf)
        h_b = rmsnorm(h_b, ml_norm_g[l].reshape(ML_HEADS, ML_V_DIM)).reshape(B, S, ML_WIDTH).astype(x.dtype)
        h_b = h_b * jax.nn.sigmoid(mo) * jax.nn.silu(mz)
        y_b = jnp.einsum('bse,ed->bsd', h_b, w_ml_down[l])

        gates = jax.nn.sigmoid(jnp.concatenate([ga, gb], axis=-1) + b_gate[l])
        y = gates[..., :D_MODEL] * y_a + gates[..., D_MODEL:] * y_b
        x = x + jnp.einsum('bsd,de->bse', y, w_out[l])
    return rmsnorm(x, final_g)
```

```python
import types
import numpy as np
from contextlib import ExitStack
import concourse.bass as bass
import concourse.mybir as mybir
from concourse.bass_utils import run_bass_kernel_spmd

F32 = mybir.dt.float32
BF16 = mybir.dt.bfloat16
AF = mybir.ActivationFunctionType
ALU = mybir.AluOpType

S = 2048
D = 1024
NT = 16
KC = 8
NCORES = 8
EPS = 1e-6
C_FQ, C_FK, C_FV, C_FF, C_FZ = 0, 1024, 2048, 3072, 3080
C_MQ, C_MK, C_MV, C_MI, C_MF, C_MO, C_MZ = 4104, 4616, 5128, 6152, 6156, 6160, 7184
C_GA, C_GB = 8208, 9232
IN_W = 10256
SEM_LIMIT = 30000
NSLOT = 6


class SemObj:
    __slots__ = ("h", "owner", "val")

    def __init__(self, h, owner):
        self.h = h
        self.owner = owner
        self.val = 0


class Tile:
    __slots__ = ("w", "r", "name", "psum")

    def __init__(self, name, init=None, psum=False):
        self.name = name
        self.w = list(init) if init else []
        self.r = []
        self.psum = psum


class Eng:
    def __init__(self, kb, name, h, is_pe=False):
        self.kb = kb
        self.name = name
        self.h = h
        self.is_pe = is_pe
        self.waited = {}
        self.cur = None

    def sem(self):
        if self.cur is None or self.cur.val >= SEM_LIMIT:
            self.cur = self.kb.new_sem(self)
        return self.cur


class Op:
    __slots__ = ("id", "eng", "fns", "deps", "cost", "is_dma", "tok", "start")

    def __init__(self, oid, eng, is_dma):
        self.id = oid
        self.eng = eng
        self.fns = []
        self.deps = []
        self.cost = 0.0
        self.is_dma = is_dma
        self.tok = None
        self.start = 0.0


def _snap(fn):
    if fn.__closure__ is None:
        return fn
    cells = []
    for c in fn.__closure__:
        try:
            cells.append(types.CellType(c.cell_contents))
        except ValueError:
            cells.append(c)
    return types.FunctionType(fn.__code__, fn.__globals__, fn.__name__, fn.__defaults__, tuple(cells))


def _free_elems(acc):
    try:
        ap = acc.ap
        n = 1
        for (_, c) in ap[1:]:
            n *= c
        return max(int(n), 1)
    except Exception:
        return 64


class KB:
    def __init__(self, nc, es, costs=None):
        self.nc = nc
        self.es = es
        self.mode = "measure" if costs is None else "defer"
        self.costs_in = costs
        self.nsem = 0
        self.PE = Eng(self, "pe", nc.tensor, is_pe=True)
        self.ACT = Eng(self, "act", nc.scalar)
        self.DVE = Eng(self, "dve", nc.vector)
        self.POOL = Eng(self, "pool", nc.gpsimd)
        self.SP = Eng(self, "sp", nc.sync)
        self.engs = [self.PE, self.ACT, self.DVE, self.POOL, self.SP]
        self.dma_pool = {}
        self.dma_rr = {}
        self.scr_tokens = []
        self.scr_tiles = []
        self.scr_id = 0
        self.n_inst = 0
        self.ops = []
        self.open_pe = None

    def new_sem(self, owner):
        self.nsem += 1
        h = self.es.enter_context(self.nc.semaphore(f"s{self.nsem}"))
        return SemObj(h, owner)

    def _deps(self, o, reads, writes):
        eng = o.eng
        ops = self.ops
        d = o.deps
        for t in reads:
            d.extend(t.w)
            if t.psum:
                d.extend(r for r in t.r if ops[r].eng is not eng)
        for t in writes:
            d.extend(t.w)
            d.extend(t.r)

    def _touch(self, oid, reads, writes):
        for t in reads:
            if not t.r or t.r[-1] != oid:
                t.r.append(oid)
        for t in writes:
            t.w = [oid]
            t.r = []

    def _cost_of(self, eng, ins):
        try:
            i = ins.ins
            n = _free_elems(i.outs[0])
            if eng.is_pe:
                f32 = False
                try:
                    f32 = (i.ins[0].dtype == F32) and not getattr(i, "is_transpose", False)
                except Exception:
                    pass
                return (max(n, 64) * (4 if f32 else 1) + 64) / 2000.0
            if eng is self.ACT:
                return 0.22 + n * 0.00083
            if eng is self.DVE:
                return 0.10 + n * 0.00104
            return 0.25 + n * 0.002
        except Exception:
            return 0.3

    def op(self, eng, fn, reads=(), writes=(), inc=True):
        if eng.is_pe and self.open_pe is not None:
            o = self.open_pe
        else:
            assert self.open_pe is None, "non-PE op recorded while a PE group is open"
            o = Op(len(self.ops), eng, False)
            self.ops.append(o)
            if eng.is_pe and not inc:
                self.open_pe = o
        if eng.is_pe and inc:
            self.open_pe = None
        self._deps(o, reads, writes)
        self._touch(o.id, reads, writes)
        if self.mode == "measure":
            ins = fn()
            o.cost += self._cost_of(eng, ins)
            return ins
        o.fns.append(_snap(fn))
        return None

    def dma(self, q, out, in_, reads=(), writes=()):
        assert self.open_pe is None
        o = Op(len(self.ops), q, True)
        self.ops.append(o)
        self._deps(o, reads, writes)
        self._touch(o.id, reads, writes)
        if self.mode == "measure":
            q.h.dma_start(out=out, in_=in_)
            try:
                n = 1
                for (_, c) in out.ap:
                    n *= c
            except Exception:
                n = 131072
            o.cost = 2.0 + n * 4 / 150e3
            return None
        o.fns.append(lambda: q.h.dma_start(out=out, in_=in_))
        return None

    def sb(self, name, shape, dt, es=None):
        h = (es or self.es).enter_context(self.nc.sbuf_tensor(name, shape, dt))
        return h, Tile(name)

    def scratch(self, es, name, shape, dt):
        self.scr_id += 1
        h = es.enter_context(self.nc.sbuf_tensor(f"{name}_{self.scr_id}", shape, dt))
        t = Tile(name, self.scr_tokens)
        self.scr_tiles.append(t)
        return h, t

    def end_scratch_phase(self):
        acc = set(self.scr_tokens)
        for t in self.scr_tiles:
            acc.update(t.w)
            acc.update(t.r)
        self.scr_tokens = sorted(acc)
        self.scr_tiles = []

    def schedule(self):
        ops = self.ops
        n = len(ops)
        costs = self.costs_in
        assert len(costs) == n, (len(costs), n)
        succ = [[] for _ in range(n)]
        for o in ops:
            o.deps = sorted(set(d for d in o.deps if d != o.id))
            for d in o.deps:
                succ[d].append(o.id)
        pend = {e: [] for e in self.engs}
        for o in ops:
            pend[o.eng].append(o.id)
        done = [False] * n
        fin = [0.0] * n
        efree = {e: 0.0 for e in self.engs}
        best = {e: None for e in self.engs}
        dirty = set(self.engs)
        HOP, SAME, W = 0.30, 0.06, 40
        order = []
        INF = float("inf")
        while len(order) < n:
            for e in dirty:
                lst = pend[e]
                be, bi, bpos = INF, -1, -1
                ef = efree[e]
                lim = min(W, len(lst))
                for pos in range(lim):
                    oid = lst[pos]
                    rt = ef
                    ok = True
                    for d in ops[oid].deps:
                        if not done[d]:
                            ok = False
                            break
                        t = fin[d] + (SAME if ops[d].eng is e else HOP)
                        if t > rt:
                            rt = t
                    if ok and rt < be:
                        be, bi, bpos = rt, oid, pos
                        if rt <= ef:
                            break
                best[e] = (be, bi, bpos) if bi >= 0 else None
            dirty.clear()
            ce, cb = None, None
            for e in self.engs:
                b = best[e]
                if b is not None and (cb is None or b[0] < cb[0] or (b[0] == cb[0] and b[1] < cb[1])):
                    ce, cb = e, b
            assert ce is not None, "scheduler deadlock"
            st, oid, pos = cb
            o = ops[oid]
            o.start = st
            if o.is_dma:
                efree[ce] = st + 0.15
                fin[oid] = st + costs[oid]
            else:
                efree[ce] = st + costs[oid]
                fin[oid] = efree[ce]
            done[oid] = True
            del pend[ce][pos]
            order.append(o)
            dirty.add(ce)
            for sidx in succ[oid]:
                dirty.add(ops[sidx].eng)
        self.est_makespan = max(fin) if fin else 0.0
        return order

    def flush(self):
        order = self.schedule()
        ops = self.ops
        for o in order:
            eng = o.eng
            for d in o.deps:
                p = ops[d]
                if eng.is_pe and p.eng is eng and not p.is_dma:
                    continue
                s, v = p.tok
                if eng.waited.get(s, 0) >= v:
                    continue
                eng.h.wait_ge(s.h, v)
                eng.waited[s] = v
                self.n_inst += 1
            if o.is_dma:
                pool = self.dma_pool.setdefault(eng, [])
                if len(pool) < 8:
                    dsem = self.new_sem(None)
                    pool.append(dsem)
                else:
                    i = self.dma_rr.get(eng, 0)
                    dsem = pool[i % len(pool)]
                    self.dma_rr[eng] = i + 1
                if dsem.val and eng.waited.get(dsem, 0) < dsem.val:
                    eng.h.wait_ge(dsem.h, dsem.val)
                    eng.waited[dsem] = dsem.val
                    self.n_inst += 1
                ins = o.fns[0]()
                dsem.val += 16
                ins.then_inc(dsem.h, 16)
                o.tok = (dsem, dsem.val)
            else:
                ins = None
                for fn in o.fns:
                    ins = fn()
                    self.n_inst += 1
                s = eng.sem()
                s.val += 1
                ins.then_inc(s.h, 1)
                o.tok = (s, s.val)
            self.n_inst += 1
        SP = self.SP
        for q, pool in self.dma_pool.items():
            for dsem in pool:
                if dsem.val and SP.waited.get(dsem, 0) < dsem.val:
                    self.nc.sync.wait_ge(dsem.h, dsem.val)
                    SP.waited[dsem] = dsem.val


class WStream:
    def __init__(self, kb, slots, plan):
        self.kb = kb
        self.slots = slots
        self.plan = plan
        self.issued = 0
        self.taken = 0
        self.free = [True] * len(slots)

    def _pump(self):
        while self.issued < len(self.plan):
            si = self.issued % len(self.slots)
            if not self.free[si]:
                break
            h, t = self.slots[si]
            for (dst, src, c0, n) in self.plan[self.issued]:
                self.kb.dma(self.kb.POOL, out=h[:, :, dst:dst + n], in_=src[:, :, c0:c0 + n], writes=[t])
            self.free[si] = False
            self.issued += 1

    def next(self):
        self._pump()
        assert self.taken < self.issued, "weight stream stalled (slot not released)"
        si = self.taken % len(self.slots)
        self.taken += 1
        h, t = self.slots[si]
        return si, h, t

    def release(self, si):
        self.free[si] = True
        self._pump()


def unit_plan(nseq, wv, wmd, wfd, wout):
    plan = []
    for _ in range(nseq):
        for h in range(4):
            plan.append([(0, wv, C_MQ + h * 128, 128), (128, wv, C_MK + h * 128, 128)])
            plan.append([(0, wv, C_MV + h * 256, 256)])
            plan.append([(0, wv, C_MO + h * 256, 256)])
            plan.append([(0, wv, C_MZ + h * 256, 256)])
        for oc in range(8):
            plan.append([(0, wmd, oc * 128, 128), (128, wv, C_GB + oc * 128, 128)])
        for h in range(8):
            plan.append([(0, wv, C_FQ + h * 128, 128), (128, wv, C_FK + h * 128, 128)])
            plan.append([(0, wv, C_FV + h * 128, 128), (128, wv, C_FZ + h * 128, 128)])
        for oc in range(8):
            plan.append([(0, wfd, oc * 128, 128), (128, wv, C_GA + oc * 128, 128)])
        for u in range(4):
            plan.append([(0, wout, u * 256, 256)])
    return plan


def build(nseq=2, dbg=None, upto=None):
    _, _, costs = _build_pass(nseq, dbg, upto, None)
    nc, dbg_out, _ = _build_pass(nseq, dbg, upto, costs)
    return nc, dbg_out


def _build_pass(nseq, dbg, upto, costs):
    dbg = dbg or set()
    nc = bass.Bass("TRN2", target_bir_lowering=False)
    x_d = nc.dram_tensor("x", [nseq, S, D], F32, kind="ExternalInput").ap()
    w_in = nc.dram_tensor("w_in", [D, IN_W], F32, kind="ExternalInput").ap()
    w_fd = nc.dram_tensor("w_fox_down", [D, D], F32, kind="ExternalInput").ap()
    w_md = nc.dram_tensor("w_ml_down", [D, D], F32, kind="ExternalInput").ap()
    w_o = nc.dram_tensor("w_out", [D, D], F32, kind="ExternalInput").ap()
    norm_g = nc.dram_tensor("norm_g", [D], F32, kind="ExternalInput").ap()
    final_g = nc.dram_tensor("final_g", [D], F32, kind="ExternalInput").ap()
    ml_g = nc.dram_tensor("ml_norm_g", [D], F32, kind="ExternalInput").ap()
    gbias_d = nc.dram_tensor("gbias", [128, NT, 16], F32, kind="ExternalInput").ap()
    convw_d = nc.dram_tensor("convw", [128, 8, 4], F32, kind="ExternalInput").ap()
    convb_d = nc.dram_tensor("convb", [128, 8], F32, kind="ExternalInput").ap()
    bgate_d = nc.dram_tensor("bgate", [128, 16], F32, kind="ExternalInput").ap()
    y_d = nc.dram_tensor("y", [nseq, S, D], F32, kind="ExternalOutput").ap()
    dbg_out = {}

    def dbg_tensor(name, shape, dt):
        dbg_out[name] = nc.dram_tensor("dbg_" + name, shape, dt, kind="ExternalOutput").ap()
        return dbg_out[name]

    wv = w_in.rearrange("(kc p) c -> p kc c", p=128)
    wfd = w_fd.rearrange("(kc p) c -> p kc c", p=128)
    wmd = w_md.rearrange("(kc p) c -> p kc c", p=128)
    wout = w_o.rearrange("(kc p) c -> p kc c", p=128)

    with ExitStack() as es:
        kb = KB(nc, es, costs)
        PE, ACT, DVE, POOL, SP = kb.PE, kb.ACT, kb.DVE, kb.POOL, kb.SP
        op, dma = kb.op, kb.dma

        xnT, t_xnT = kb.sb("xnT", [128, KC, S], BF16)
        bufA, t_bufA = kb.sb("bufA", [128, KC, S], BF16)
        bufY, t_bufY = kb.sb("bufY", [128, KC, S], BF16)
        slots = [kb.sb(f"wslot{i}", [128, KC, 256], BF16) for i in range(NSLOT)]
        wg, t_wg = kb.sb("wg", [128, KC, 16], BF16)
        gX, t_gX = kb.sb("gX", [128, D], F32)
        gN = gF = gM = gX
        t_gN = t_gF = t_gM = t_gX
        gbias, t_gbias = kb.sb("gbias_sb", [128, NT, 16], F32)
        cw, t_cw = kb.sb("cw", [128, 8, 4], F32)
        cb, t_cb = kb.sb("cb", [128, 8], F32)
        bg, t_bg = kb.sb("bg", [128, 16], F32)
        epsT, t_eps = kb.sb("epsT", [128, 1], F32)
        ident, t_ident = kb.sb("ident", [128, 128], BF16)
        onesb, t_onesb = kb.sb("onesb", [128, 128], BF16)
        maskb, t_maskb = kb.sb("maskb", [128, 128], BF16)
        maskf, t_maskf = kb.sb("maskf", [128, 128], F32)
        onesf, t_onesf = kb.sb("onesf", [128, 128], F32)
        hmid, t_hmid = kb.sb("hmid", [128, 128], F32)
        xts = [kb.sb(f"xt{i}", [128, D], F32) for i in range(2)]
        xns = [kb.sb(f"xn{i}", [128, D], BF16) for i in range(2)]
        stats = [kb.sb(f"stat{i}", [128, 4], F32) for i in range(2)]
        zs, t_zs = kb.sb("zs", [128, NT, 16], F32)
        lf, t_lf = kb.sb("lf", [128, NT, 16], F32)
        tot, t_tot = kb.sb("tot", [128, NT, 16], F32)
        bc, t_bc = kb.sb("bc", [128, NT, 16], F32)
        half, t_half = kb.sb("half", [128, NT, 16], F32)
        pre, t_pre = kb.sb("pre", [128, NT, 16], F32)
        cc, t_cc = kb.sb("cc", [128, NT, 16], F32)
        RR, t_RR = kb.sb("RR", [128, NT, 16], F32)
        ew, t_ew = kb.sb("ew", [128, NT, 4], F32)
        eb, t_eb = kb.sb("eb", [128, NT, 4], F32)
        aa, t_aa = kb.sb("aa", [128, NT, 4], F32)
        gtmp, t_gtmp = kb.sb("gtmp", [128, NT, 16], F32)

        PB = []
        for i in range(6):
            h = es.enter_context(nc.psum_tensor(f"pb{i}", [128, 512], F32))
            PB.append((h, Tile(f"pb{i}", psum=True)))
        PT = []
        for i in range(2):
            h = es.enter_context(nc.psum_tensor(f"pt{i}", [128, 8, 128], BF16))
            PT.append((h, Tile(f"pt{i}", psum=True)))

        dma(SP, out=gbias[:], in_=gbias_d[:, :, :], writes=[t_gbias])
        dma(SP, out=cw[:], in_=convw_d[:, :, :], writes=[t_cw])
        dma(SP, out=cb[:], in_=convb_d[:, :], writes=[t_cb])
        dma(SP, out=bg[:], in_=bgate_d[:, :], writes=[t_bg])
        dma(POOL, out=wg[:, :, 0:8], in_=wv[:, :, C_FF:C_FF + 8], writes=[t_wg])
        dma(POOL, out=wg[:, :, 8:16], in_=wv[:, :, C_MI:C_MI + 8], writes=[t_wg])
        op(DVE, lambda: nc.vector.memset(epsT[:], EPS), writes=[t_eps])
        op(POOL, lambda: nc.gpsimd.memset(onesb[:], 1.0), writes=[t_onesb])
        op(POOL, lambda: nc.gpsimd.memset(onesf[:], 1.0), writes=[t_onesf])
        op(POOL, lambda: nc.gpsimd.memset(hmid[:], 0.0), writes=[t_hmid])
        op(POOL, lambda: nc.gpsimd.memset(hmid[0:64, :], 1.0), writes=[t_hmid])
        op(POOL, lambda: nc.gpsimd.affine_select(out=ident[:], in_=onesb[:], pattern=[[1, 128]],
                                                 compare_op=ALU.is_equal, fill=0.0, base=0,
                                                 channel_multiplier=-1), reads=[t_onesb], writes=[t_ident])
        op(POOL, lambda: nc.gpsimd.affine_select(out=maskb[:], in_=onesb[:], pattern=[[1, 128]],
                                                 compare_op=ALU.is_ge, fill=0.0, base=0,
                                                 channel_multiplier=-1), reads=[t_onesb], writes=[t_maskb])
        op(POOL, lambda: nc.gpsimd.affine_select(out=maskf[:], in_=onesf[:], pattern=[[1, 128]],
                                                 compare_op=ALU.is_ge, fill=0.0, base=0,
                                                 channel_multiplier=-1), reads=[t_onesf], writes=[t_maskf])

        ws = WStream(kb, slots, unit_plan(nseq, wv, wmd, wfd, wout))
        proj_rr = [0]

        def proj_bank():
            proj_rr[0] += 1
            return PB[proj_rr[0] % 2]

        def tap(name, src_ap, shape, dt, tiles):
            if name in dbg:
                d = dbg_tensor(name, shape, dt)
                dma(SP, out=d, in_=src_ap, reads=tiles)

        def rmsnorm_tail(xt_h, xt_t, si, g_h, g_t, out_ap, out_tiles, sq_out_ap, sq_out_tiles):
            st_h, st_t = stats[si % 2]
            op(ACT, lambda: nc.scalar.activation(out=sq_out_ap, in_=xt_h[:], func=AF.Square,
                                                 accum_out=st_h[:, 0:1]),
               reads=[xt_t], writes=sq_out_tiles + [st_t])
            op(ACT, lambda: nc.scalar.activation(out=st_h[:, 1:2], in_=st_h[:, 0:1], func=AF.Sqrt,
                                                 scale=1.0 / D, bias=epsT[:]),
               reads=[st_t, t_eps], writes=[st_t])
            op(DVE, lambda: nc.vector.reciprocal(out=st_h[:, 2:3], in_=st_h[:, 1:2]), reads=[st_t], writes=[st_t])
            op(DVE, lambda: nc.vector.scalar_tensor_tensor(out=out_ap, in0=xt_h[:], scalar=st_h[:, 2:3],
                                                           in1=g_h[:], op0=ALU.mult, op1=ALU.mult),
               reads=[xt_t, st_t, g_t], writes=out_tiles)

        for seq in range(nseq):
            first = (seq == 0)
            dma(SP, out=gX[:], in_=norm_g.partition_broadcast(128), writes=[t_gX])
            for tb in range(NT):
                xt_h, xt_t = xts[tb % 2]
                xn_h, xn_t = xns[tb % 2]
                dma(SP, out=xt_h[:], in_=x_d[seq, tb * 128:(tb + 1) * 128, :], writes=[xt_t])
                rmsnorm_tail(xt_h, xt_t, tb, gN, t_gN, xn_h[:], [xn_t], xn_h[:], [xn_t])
                pt_h, pt_t = PT[tb % 2]
                for kc in range(KC):
                    op(PE, lambda kc=kc: nc.tensor.transpose(out=pt_h[:, kc, :], in_=xn_h[:, kc * 128:(kc + 1) * 128],
                                                             identity=ident[:]),
                       reads=[xn_t, t_ident], writes=[pt_t], inc=(kc == KC - 1))
                op(ACT, lambda: nc.scalar.copy(out=xnT[:, :, tb * 128:(tb + 1) * 128], in_=pt_h[:]),
                   reads=[pt_t], writes=[t_xnT])
            if first:
                tap("xnT", xnT[:], [128, KC, S], BF16, [t_xnT])

            if upto == "P0":
                break

            pz_h, pz_t = PB[2]
            for tb in range(NT):
                for kc in range(KC):
                    op(PE, lambda kc=kc, tb=tb: nc.tensor.matmul(pz_h[:, tb * 16:(tb + 1) * 16],
                                                                 lhsT=xnT[:, kc, tb * 128:(tb + 1) * 128],
                                                                 rhs=wg[:, kc, :], start=(kc == 0), stop=(kc == KC - 1)),
                       reads=[t_xnT, t_wg], writes=[pz_t], inc=(kc == KC - 1 and tb == NT - 1))
            op(DVE, lambda: nc.vector.tensor_tensor(out=zs[:].rearrange("p a b -> p (a b)"), in0=pz_h[:, 0:256],
                                                    in1=gbias[:].rearrange("p a b -> p (a b)"), op=ALU.add),
               reads=[pz_t, t_gbias], writes=[t_zs])
            if upto == "G1":
                tap("lf", zs[:], [128, NT, 16], F32, [t_zs])
                break
            op(ACT, lambda: nc.scalar.activation(out=gtmp[:], in_=zs[:], func=AF.Exp, scale=-1.0),
               reads=[t_zs], writes=[t_gtmp])
            op(ACT, lambda: nc.scalar.activation(out=gtmp[:], in_=gtmp[:], func=AF.Ln, bias=1.0, scale=1.0),
               reads=[t_gtmp], writes=[t_gtmp])
            op(DVE, lambda: nc.vector.tensor_scalar(out=lf[:], in0=gtmp[:], scalar1=-1.0, scalar2=None, op0=ALU.mult),
               reads=[t_gtmp], writes=[t_lf])
            if upto == "G2":
                tap("lf", lf[:], [128, NT, 16], F32, [t_lf])
                break
            lf2 = lf[:].rearrange("p a b -> p (a b)")
            for (lhs_h, lhs_t, dst_h, dst_t, pbi) in ((onesf, t_onesf, tot, t_tot, 3), (maskf, t_maskf, bc, t_bc, 4),
                                                       (hmid, t_hmid, half, t_half, 5)):
                pp_h, pp_t = PB[pbi]
                op(PE, lambda lhs_h=lhs_h, pp_h=pp_h: nc.tensor.matmul(pp_h[:, 0:256], lhsT=lhs_h[:], rhs=lf2,
                                                                       start=True, stop=True),
                   reads=[lhs_t, t_lf], writes=[pp_t])
                op(DVE, lambda dst_h=dst_h, pp_h=pp_h: nc.vector.tensor_copy(out=dst_h[:].rearrange("p a b -> p (a b)"),
                                                                             in_=pp_h[:, 0:256]),
                   reads=[pp_t], writes=[dst_t])
            if upto == "G3":
                tap("lf", lf[:], [128, NT, 16], F32, [t_lf])
                tap("cc", bc[:], [128, NT, 16], F32, [t_bc])
                break
            op(DVE, lambda: nc.vector.memset(pre[:, 0, :], 0.0), writes=[t_pre])
            for tb in range(1, NT):
                op(DVE, lambda tb=tb: nc.vector.tensor_tensor(out=pre[:, tb, :], in0=pre[:, tb - 1, :],
                                                              in1=tot[:, tb - 1, :], op=ALU.add),
                   reads=[t_pre, t_tot], writes=[t_pre])
            op(DVE, lambda: nc.vector.tensor_tensor(out=cc[:], in0=bc[:], in1=pre[:], op=ALU.add),
               reads=[t_bc, t_pre], writes=[t_cc])
            op(DVE, lambda: nc.vector.tensor_tensor(out=RR[:], in0=half[:], in1=pre[:], op=ALU.add),
               reads=[t_half, t_pre], writes=[t_RR])
            if upto == "G4":
                tap("cc", cc[:], [128, NT, 16], F32, [t_cc])
                break
            op(DVE, lambda: nc.vector.tensor_tensor(out=ew[:], in0=zs[:, :, 8:12], in1=bc[:, :, 12:16], op=ALU.subtract),
               reads=[t_zs, t_bc], writes=[t_ew])
            if upto == "G5":
                tap("cc", cc[:], [128, NT, 16], F32, [t_cc, t_ew])
                break
            op(ACT, lambda: nc.scalar.activation(out=ew[:], in_=ew[:], func=AF.Exp), reads=[t_ew], writes=[t_ew])
            if upto == "G6":
                tap("cc", cc[:], [128, NT, 16], F32, [t_cc, t_ew])
                break
            op(ACT, lambda: nc.scalar.activation(out=eb[:], in_=bc[:, :, 12:16], func=AF.Exp), reads=[t_bc], writes=[t_eb])
            op(DVE, lambda: nc.vector.tensor_scalar(out=eb[:], in0=eb[:], scalar1=float(128 ** -0.5), scalar2=None,
                                                    op0=ALU.mult), reads=[t_eb], writes=[t_eb])
            if upto == "G7":
                tap("cc", cc[:], [128, NT, 16], F32, [t_cc, t_ew, t_eb])
                break
            op(ACT, lambda: nc.scalar.activation(out=aa[:], in_=tot[:, :, 12:16], func=AF.Exp), reads=[t_tot], writes=[t_aa])
            if first:
                tap("lf", lf[:], [128, NT, 16], F32, [t_lf])
                tap("cc", cc[:], [128, NT, 16], F32, [t_cc])
                tap("RR", RR[:], [128, NT, 16], F32, [t_RR])
                tap("ew", ew[:], [128, NT, 4], F32, [t_ew])
                tap("eb", eb[:], [128, NT, 4], F32, [t_eb])
                tap("aa", aa[:], [128, NT, 4], F32, [t_aa])

            if upto == "G":
                break

            dma(SP, out=gX[:], in_=ml_g.partition_broadcast(128), writes=[t_gX])
            with ExitStack() as ms:
                qT, t_qT = kb.scratch(ms, "m_qT", [128, S], BF16)
                kT, t_kT = kb.scratch(ms, "m_kT", [128, S], BF16)
                cbuf, t_cbuf = kb.scratch(ms, "m_cbuf", [128, S + 4], F32)
                accs = [kb.scratch(ms, f"m_acc{i}", [128, 1024], F32) for i in range(2)]
                kTok, t_kTok = kb.scratch(ms, "m_kTok", [128, NT, 128], BF16)
                vaug, t_vaug = kb.scratch(ms, "m_vaug", [128, NT, 258], BF16)
                Crun = [kb.scratch(ms, f"m_C{i}", [128, 258], F32) for i in range(2)]
                Cbf, t_Cbf = kb.scratch(ms, "m_Cbf", [128, NT, 258], BF16)
                numS, t_numS = kb.scratch(ms, "m_numS", [128, 4, 258], F32)
                Gp, t_Gp = kb.scratch(ms, "m_Gp", [128, 4, 256], BF16)
                STm = [kb.scratch(ms, f"m_STm{i}", [128, 128], BF16) for i in range(2)]
                hn = [kb.scratch(ms, f"m_hn{i}", [128, 256], BF16) for i in range(2)]
                s12 = [kb.scratch(ms, f"m_s12{i}", [128, 512], BF16) for i in range(2)]
                sm, t_sm = kb.scratch(ms, "m_sm", [128, 12, 4], F32)
                junk, t_junk = kb.scratch(ms, "m_junk", [128, 256], BF16)
                op(DVE, lambda: nc.vector.memset(cbuf[:, 0:4], 0.0), writes=[t_cbuf])
                CB0 = 1
                for h in range(4):
                    s1i, u1, t_u1 = ws.next()
                    s2i, u2, t_u2 = ws.next()
                    s3i, u3, t_u3 = ws.next()
                    s4i, u4, t_u4 = ws.next()
                    for which, (dst, t_dst) in enumerate(((qT, t_qT), (kT, t_kT))):
                        chunk = which * 4 + h
                        for tc in range(4):
                            pp_h, pp_t = proj_bank()
                            for kc in range(KC):
                                op(PE, lambda kc=kc, tc=tc, which=which, pp_h=pp_h: nc.tensor.matmul(
                                    pp_h[:, 0:512], lhsT=u1[:, kc, which * 128:(which + 1) * 128],
                                    rhs=xnT[:, kc, tc * 512:(tc + 1) * 512], start=(kc == 0), stop=(kc == KC - 1)),
                                   reads=[t_u1, t_xnT], writes=[pp_t], inc=(kc == KC - 1))
                            op(ACT, lambda tc=tc, pp_h=pp_h: nc.scalar.copy(out=cbuf[:, 4 + tc * 512:4 + (tc + 1) * 512],
                                                                           in_=pp_h[:, 0:512]),
                               reads=[pp_t], writes=[t_cbuf])
                        for hf in range(2):
                            o = hf * 1024
                            ac_h, ac_t = accs[hf]
                            op(DVE, lambda o=o, ac_h=ac_h, chunk=chunk: nc.vector.tensor_scalar(
                                out=ac_h[:], in0=cbuf[:, 4 + o:4 + o + 1024], scalar1=cw[:, chunk, 3:4], scalar2=None,
                                op0=ALU.mult), reads=[t_cbuf, t_cw], writes=[ac_t])
                            for j in (2, 1, 0):
                                op(DVE, lambda o=o, ac_h=ac_h, chunk=chunk, j=j: nc.vector.scalar_tensor_tensor(
                                    out=ac_h[:], in0=cbuf[:, 1 + j + o:1 + j + o + 1024], scalar=cw[:, chunk, j:j + 1],
                                    in1=ac_h[:], op0=ALU.mult, op1=ALU.add), reads=[t_cbuf, t_cw, ac_t], writes=[ac_t])
                            op(ACT, lambda o=o, ac_h=ac_h, chunk=chunk, dst=dst: nc.scalar.activation(
                                out=dst[:, o:o + 1024], in_=ac_h[:], func=AF.Silu, bias=cb[:, chunk:chunk + 1]),
                               reads=[ac_t, t_cb], writes=[t_dst])
                    ws.release(s1i)
                    for g in range(2):
                        pt_h, pt_t = PT[g]
                        for c8 in range(8):
                            c = g * 8 + c8
                            op(PE, lambda c=c, c8=c8, pt_h=pt_h: nc.tensor.transpose(
                                out=pt_h[:, c8, :], in_=kT[:, c * 128:(c + 1) * 128], identity=ident[:]),
                               reads=[t_kT, t_ident], writes=[pt_t], inc=(c8 == 7))
                        for c8 in range(8):
                            c = g * 8 + c8
                            op(ACT, lambda c=c, c8=c8, pt_h=pt_h: nc.scalar.activation(
                                out=kTok[:, c, :], in_=pt_h[:, c8, :], func=AF.Copy, scale=aa[:, c, h:h + 1]),
                               reads=[pt_t, t_aa], writes=[t_kTok])
                    for tb in range(NT):
                        pp_h, pp_t = proj_bank()
                        for kc in range(KC):
                            op(PE, lambda kc=kc, tb=tb, pp_h=pp_h: nc.tensor.matmul(
                                pp_h[:, 0:256], lhsT=xnT[:, kc, tb * 128:(tb + 1) * 128], rhs=u2[:, kc, :],
                                start=(kc == 0), stop=(kc == KC - 1)),
                               reads=[t_u2, t_xnT], writes=[pp_t], inc=(kc == KC - 1))
                        op(ACT, lambda tb=tb, pp_h=pp_h: nc.scalar.activation(
                            out=vaug[:, tb, 0:256], in_=pp_h[:, 0:256], func=AF.Copy, scale=ew[:, tb, h:h + 1]),
                           reads=[pp_t, t_ew], writes=[t_vaug])
                    ws.release(s2i)
                    op(DVE, lambda: nc.vector.tensor_copy(out=vaug[:, :, 256:257], in_=ew[:, :, h:h + 1]),
                       reads=[t_ew], writes=[t_vaug])
                    dcb = (PB[5], PB[2])
                    for c in range(NT - 1):
                        pd_h, pd_t = dcb[c % 2]
                        op(PE, lambda c=c, pd_h=pd_h: nc.tensor.matmul(pd_h[:, 0:257], lhsT=kTok[:, c, :],
                                                                       rhs=vaug[:, c, 0:257], start=True, stop=True),
                           reads=[t_kTok, t_vaug], writes=[pd_t])
                        cn_h, cn_t = Crun[(c + 1) % 2]
                        cp_h, cp_t = Crun[c % 2]
                        if c == 0:
                            op(DVE, lambda pd_h=pd_h, cn_h=cn_h: nc.vector.tensor_copy(out=cn_h[:, 0:257], in_=pd_h[:, 0:257]),
                               reads=[pd_t], writes=[cn_t])
                        else:
                            op(DVE, lambda c=c, pd_h=pd_h, cn_h=cn_h, cp_h=cp_h: nc.vector.scalar_tensor_tensor(
                                out=cn_h[:, 0:257], in0=cp_h[:, 0:257], scalar=aa[:, c, h:h + 1], in1=pd_h[:, 0:257],
                                op0=ALU.mult, op1=ALU.add), reads=[cp_t, pd_t, t_aa], writes=[cn_t])
                        op(ACT, lambda c=c, cn_h=cn_h: nc.scalar.copy(out=Cbf[:, c + 1, 0:257], in_=cn_h[:, 0:257]),
                           reads=[cn_t], writes=[t_Cbf])
                    for c in range(NT):
                        ps_h, ps_t = PB[2]
                        op(PE, lambda c=c: nc.tensor.matmul(ps_h[:, 0:128], lhsT=kT[:, c * 128:(c + 1) * 128],
                                                            rhs=qT[:, c * 128:(c + 1) * 128], start=True, stop=True),
                           reads=[t_kT, t_qT], writes=[ps_t])
                        st_h, st_t = STm[c % 2]
                        op(DVE, lambda st_h=st_h: nc.vector.tensor_tensor(out=st_h[:], in0=ps_h[:, 0:128], in1=maskf[:],
                                                                          op=ALU.mult),
                           reads=[ps_t, t_maskf], writes=[st_t])
                        pn_h, pn_t = PB[3 + c % 2]
                        op(PE, lambda c=c, pn_h=pn_h, st_h=st_h: nc.tensor.matmul(
                            pn_h[:, 0:257], lhsT=st_h[:], rhs=vaug[:, c, 0:257], start=True, stop=(c == 0)),
                           reads=[st_t, t_vaug], writes=[pn_t], inc=(c == 0))
                        if c > 0:
                            op(PE, lambda c=c, pn_h=pn_h: nc.tensor.matmul(
                                pn_h[:, 0:257], lhsT=qT[:, c * 128:(c + 1) * 128], rhs=Cbf[:, c, 0:257],
                                start=False, stop=True), reads=[t_qT, t_Cbf], writes=[pn_t])
                        op(ACT, lambda c=c, pn_h=pn_h: nc.scalar.copy(out=numS[:, c % 4, 0:257], in_=pn_h[:, 0:257]),
                           reads=[pn_t], writes=[t_numS])
                        pp_h, pp_t = proj_bank()
                        for gi, (ug, t_ug) in enumerate(((u3, t_u3), (u4, t_u4))):
                            for kc in range(KC):
                                op(PE, lambda kc=kc, c=c, gi=gi, ug=ug, pp_h=pp_h: nc.tensor.matmul(
                                    pp_h[:, gi * 256:(gi + 1) * 256], lhsT=xnT[:, kc, c * 128:(c + 1) * 128],
                                    rhs=ug[:, kc, :], start=(kc == 0), stop=(kc == KC - 1)),
                                   reads=[t_ug, t_xnT], writes=[pp_t], inc=(kc == KC - 1 and gi == 1))
                        sg_h, sg_t = s12[c % 2]
                        op(ACT, lambda pp_h=pp_h, sg_h=sg_h: nc.scalar.activation(out=sg_h[:, 0:256], in_=pp_h[:, 0:256],
                                                                                  func=AF.Sigmoid),
                           reads=[pp_t], writes=[sg_t])
                        op(ACT, lambda pp_h=pp_h, sg_h=sg_h: nc.scalar.activation(out=sg_h[:, 256:512], in_=pp_h[:, 256:512],
                                                                                  func=AF.Silu),
                           reads=[pp_t], writes=[sg_t])
                        op(DVE, lambda sg_h=sg_h: nc.vector.tensor_tensor(out=sg_h[:, 0:256], in0=sg_h[:, 0:256],
                                                                          in1=sg_h[:, 256:512], op=ALU.mult),
                           reads=[sg_t], writes=[sg_t])
                        op(DVE, lambda c=c, sg_h=sg_h: nc.vector.tensor_tensor(out=Gp[:, c % 4, :], in0=sg_h[:, 0:256],
                                                                               in1=gM[:, h * 256:(h + 1) * 256], op=ALU.mult),
                           reads=[sg_t, t_gM], writes=[t_Gp])
                        if c % 4 == 3:
                            cs = c - 3
                            den = numS[:, :, 256]
                            ebs = eb[:, cs:cs + 4, h]
                            op(DVE, lambda: nc.vector.tensor_tensor(out=sm[:, 0, :], in0=den, in1=ebs, op=ALU.mult),
                               reads=[t_numS, t_eb], writes=[t_sm])
                            op(DVE, lambda: nc.vector.scalar_tensor_tensor(out=sm[:, 1, :], in0=sm[:, 0, :], scalar=-1.0,
                                                                           in1=sm[:, 0, :], op0=ALU.mult, op1=ALU.max),
                               reads=[t_sm], writes=[t_sm])
                            op(DVE, lambda: nc.vector.tensor_scalar(out=sm[:, 2, :], in0=sm[:, 1, :], scalar1=1.0,
                                                                    scalar2=None, op0=ALU.max), reads=[t_sm], writes=[t_sm])
                            op(DVE, lambda: nc.vector.reciprocal(out=sm[:, 3, :], in_=sm[:, 2, :]), reads=[t_sm], writes=[t_sm])
                            op(DVE, lambda: nc.vector.tensor_tensor(out=sm[:, 4, :], in0=sm[:, 3, :], in1=ebs, op=ALU.mult),
                               reads=[t_sm, t_eb], writes=[t_sm])
                            for c8 in range(4):
                                op(ACT, lambda c8=c8: nc.scalar.activation(out=junk[:], in_=numS[:, c8, 0:256],
                                                                           func=AF.Square, accum_out=sm[:, 5, c8:c8 + 1]),
                                   reads=[t_numS], writes=[t_junk, t_sm])
                            op(DVE, lambda: nc.vector.tensor_tensor(out=sm[:, 6, :], in0=sm[:, 4, :], in1=sm[:, 4, :],
                                                                    op=ALU.mult), reads=[t_sm], writes=[t_sm])
                            op(DVE, lambda: nc.vector.tensor_tensor(out=sm[:, 7, :], in0=sm[:, 6, :], in1=sm[:, 5, :],
                                                                    op=ALU.mult), reads=[t_sm], writes=[t_sm])
                            op(ACT, lambda: nc.scalar.activation(out=sm[:, 8, :], in_=sm[:, 7, :], func=AF.Sqrt,
                                                                 scale=1.0 / 256, bias=epsT[:]),
                               reads=[t_sm, t_eps], writes=[t_sm])
                            op(DVE, lambda: nc.vector.reciprocal(out=sm[:, 9, :], in_=sm[:, 8, :]), reads=[t_sm], writes=[t_sm])
                            op(DVE, lambda: nc.vector.tensor_tensor(out=sm[:, 10, :], in0=sm[:, 9, :], in1=sm[:, 4, :],
                                                                    op=ALU.mult), reads=[t_sm], writes=[t_sm])
                            for c8 in range(4):
                                hn_h, hn_t = hn[c8 % 2]
                                op(DVE, lambda c8=c8, hn_h=hn_h: nc.vector.scalar_tensor_tensor(
                                    out=hn_h[:], in0=numS[:, c8, 0:256], scalar=sm[:, 10, c8:c8 + 1], in1=Gp[:, c8, :],
                                    op0=ALU.mult, op1=ALU.mult), reads=[t_numS, t_sm, t_Gp], writes=[hn_t])
                                for j in range(2):
                                    op(PE, lambda c8=c8, j=j, hn_h=hn_h: nc.tensor.transpose(
                                        out=PT[j][0][:, c8, :], in_=hn_h[:, j * 128:(j + 1) * 128], identity=ident[:]),
                                       reads=[hn_t, t_ident], writes=[PT[j][1]])
                            for j in range(2):
                                op(ACT, lambda j=j, cs=cs: nc.scalar.copy(
                                    out=bufA[:, 2 * h + j, cs * 128:(cs + 4) * 128],
                                    in_=PT[j][0][:, 0:4, :].rearrange("p a b -> p (a b)")),
                                   reads=[PT[j][1]], writes=[t_bufA])
                    ws.release(s3i)
                    ws.release(s4i)
                kb.end_scratch_phase()
            if first:
                tap("hbT", bufA[:], [128, KC, S], BF16, [t_bufA])

            if upto == "M":
                break

            with ExitStack() as ms:
                sgs = [kb.scratch(ms, f"d_sg{i}", [128, 512], F32) for i in range(2)]
                for oc in range(8):
                    si, u, t_u = ws.next()
                    for tc in range(4):
                        py_h, py_t = proj_bank()
                        for kc in range(KC):
                            op(PE, lambda kc=kc, tc=tc, py_h=py_h: nc.tensor.matmul(
                                py_h[:, 0:512], lhsT=u[:, kc, 0:128], rhs=bufA[:, kc, tc * 512:(tc + 1) * 512],
                                start=(kc == 0), stop=(kc == KC - 1)), reads=[t_u, t_bufA], writes=[py_t], inc=(kc == KC - 1))
                        pg_h, pg_t = proj_bank()
                        for kc in range(KC):
                            op(PE, lambda kc=kc, tc=tc, pg_h=pg_h: nc.tensor.matmul(
                                pg_h[:, 0:512], lhsT=u[:, kc, 128:256], rhs=xnT[:, kc, tc * 512:(tc + 1) * 512],
                                start=(kc == 0), stop=(kc == KC - 1)), reads=[t_u, t_xnT], writes=[pg_t], inc=(kc == KC - 1))
                        sg_h, sg_t = sgs[tc % 2]
                        op(ACT, lambda pg_h=pg_h, sg_h=sg_h, oc=oc: nc.scalar.activation(
                            out=sg_h[:], in_=pg_h[:, 0:512], func=AF.Sigmoid, bias=bg[:, 8 + oc:9 + oc]),
                           reads=[pg_t, t_bg], writes=[sg_t])
                        op(DVE, lambda py_h=py_h, sg_h=sg_h, oc=oc, tc=tc: nc.vector.tensor_tensor(
                            out=bufY[:, oc, tc * 512:(tc + 1) * 512], in0=py_h[:, 0:512], in1=sg_h[:], op=ALU.mult),
                           reads=[py_t, sg_t], writes=[t_bufY])
                    ws.release(si)
                kb.end_scratch_phase()
            if first:
                tap("ybg", bufY[:], [128, KC, S], BF16, [t_bufY])

            if upto == "MD":
                break

            with ExitStack() as ms:
                fq = [kb.scratch(ms, f"f_q{i}", [128, S], BF16) for i in range(2)]
                fk = [kb.scratch(ms, f"f_k{i}", [128, S], BF16) for i in range(2)]
                fv = [kb.scratch(ms, f"f_v{i}", [128, NT, 130], BF16) for i in range(2)]
                fzs = [kb.scratch(ms, f"f_z{i}", [128, NT, 128], BF16) for i in range(2)]
                bms = [kb.scratch(ms, f"f_bm{i}", [128, NT, NT], F32) for i in range(2)]
                ptb = [kb.scratch(ms, f"f_pt{i}", [128, 512], BF16) for i in range(2)]
                obs = [kb.scratch(ms, f"f_ob{i}", [128, 4, 128], BF16) for i in range(2)]
                rin, t_rin = kb.scratch(ms, "f_rin", [128, 16], F32)
                for i in range(2):
                    op(DVE, lambda i=i: nc.vector.memset(fv[i][0][:, :, 128:129], 1.0), writes=[fv[i][1]])
                sc = float(128 ** -0.5)
                sb_i = 0
                for h in range(8):
                    s1i, u1, t_u1 = ws.next()
                    s2i, u2, t_u2 = ws.next()
                    qT, t_qT = fq[h % 2]
                    kT, t_kT = fk[h % 2]
                    V, t_V = fv[h % 2]
                    FZ, t_FZ = fzs[h % 2]
                    bm, t_bm = bms[h % 2]
                    for which, (dst, t_dst) in enumerate(((qT, t_qT), (kT, t_kT))):
                        for tc in range(4):
                            pp_h, pp_t = proj_bank()
                            for kc in range(KC):
                                op(PE, lambda kc=kc, tc=tc, which=which, pp_h=pp_h: nc.tensor.matmul(
                                    pp_h[:, 0:512], lhsT=u1[:, kc, which * 128:(which + 1) * 128],
                                    rhs=xnT[:, kc, tc * 512:(tc + 1) * 512], start=(kc == 0), stop=(kc == KC - 1)),
                                   reads=[t_u1, t_xnT], writes=[pp_t], inc=(kc == KC - 1))
                            op(ACT, lambda tc=tc, pp_h=pp_h, dst=dst: nc.scalar.copy(
                                out=dst[:, tc * 512:(tc + 1) * 512], in_=pp_h[:, 0:512]), reads=[pp_t], writes=[t_dst])
                    ws.release(s1i)
                    for tb in range(NT):
                        pp_h, pp_t = proj_bank()
                        for kc in range(KC):
                            op(PE, lambda kc=kc, tb=tb, pp_h=pp_h: nc.tensor.matmul(
                                pp_h[:, 0:256], lhsT=xnT[:, kc, tb * 128:(tb + 1) * 128], rhs=u2[:, kc, :],
                                start=(kc == 0), stop=(kc == KC - 1)), reads=[t_u2, t_xnT], writes=[pp_t], inc=(kc == KC - 1))
                        op(DVE, lambda tb=tb, pp_h=pp_h: nc.vector.tensor_copy(out=V[:, tb, 0:128], in_=pp_h[:, 0:128]),
                           reads=[pp_t], writes=[t_V])
                        op(ACT, lambda tb=tb, pp_h=pp_h: nc.scalar.activation(out=FZ[:, tb, :], in_=pp_h[:, 128:256],
                                                                              func=AF.Silu), reads=[pp_t], writes=[t_FZ])
                    ws.release(s2i)
                    for j in range(NT):
                        op(DVE, lambda j=j: nc.vector.tensor_scalar(out=bm[:, j, :], in0=RR[:, :, h],
                                                                    scalar1=cc[:, j, h:h + 1], scalar2=None,
                                                                    op0=ALU.subtract), reads=[t_RR, t_cc], writes=[t_bm])
                    steps = [(I, j) for I in range(4) for j in range(4 * I + 4)]

                    def emit_ST(n):
                        I, j = steps[n]
                        i0 = max(j, 4 * I)
                        ps_h, ps_t = PB[2 + n % 2]
                        N = (4 * I + 4 - i0) * 128
                        op(PE, lambda: nc.tensor.matmul(ps_h[:, 0:N], lhsT=kT[:, j * 128:(j + 1) * 128],
                                                        rhs=qT[:, i0 * 128:(4 * I + 4) * 128], start=True, stop=True),
                           reads=[t_kT, t_qT], writes=[ps_t])

                    def emit_exp(n):
                        I, j = steps[n]
                        i0 = max(j, 4 * I)
                        ps_h, ps_t = PB[2 + n % 2]
                        p_h, p_t = ptb[n % 2]
                        for i in range(i0, 4 * I + 4):
                            lo = (i - i0) * 128
                            op(ACT, lambda i=i, lo=lo: nc.scalar.activation(
                                out=p_h[:, lo:lo + 128], in_=ps_h[:, lo:lo + 128], func=AF.Exp, scale=sc,
                                bias=bm[:, j, i:i + 1]), reads=[ps_t, t_bm], writes=[p_t])
                        if j >= 4 * I:
                            op(DVE, lambda: nc.vector.tensor_tensor(out=p_h[:, 0:128], in0=p_h[:, 0:128], in1=maskb[:],
                                                                    op=ALU.mult), reads=[p_t, t_maskb], writes=[p_t])

                    def emit_PV(n):
                        I, j = steps[n]
                        i0 = max(j, 4 * I)
                        p_h, p_t = ptb[n % 2]
                        for i in range(i0, 4 * I + 4):
                            il = i - 4 * I
                            lo = (i - i0) * 128
                            po_h, po_t = PB[4 + il // 2]
                            off = (il % 2) * 129
                            op(PE, lambda i=i, lo=lo, po_h=po_h, off=off, il=il: nc.tensor.matmul(
                                po_h[:, off:off + 129], lhsT=p_h[:, lo:lo + 128], rhs=V[:, j, 0:129],
                                start=(j == 0 and il % 2 == 0), stop=(j == i), skip_group_check=True),
                               reads=[p_t, t_V], writes=[po_t])
                            if j == i:
                                ob_h, ob_t = obs[I % 2]
                                op(DVE, lambda po_h=po_h, off=off, i=i: nc.vector.reciprocal(
                                    out=rin[:, i:i + 1], in_=po_h[:, off + 128:off + 129]), reads=[po_t], writes=[t_rin])
                                op(DVE, lambda po_h=po_h, off=off, i=i, il=il, ob_h=ob_h: nc.vector.scalar_tensor_tensor(
                                    out=ob_h[:, il, :], in0=po_h[:, off:off + 128], scalar=rin[:, i:i + 1], in1=FZ[:, i, :],
                                    op0=ALU.mult, op1=ALU.mult), reads=[po_t, t_rin, t_FZ], writes=[ob_t])
                                pt_h, pt_t = PT[I % 2]
                                op(PE, lambda il=il, ob_h=ob_h, pt_h=pt_h: nc.tensor.transpose(
                                    out=pt_h[:, il, :], in_=ob_h[:, il, :], identity=ident[:]),
                                   reads=[ob_t, t_ident], writes=[pt_t])
                                if il == 3:
                                    op(ACT, lambda pt_h=pt_h, I=I: nc.scalar.copy(
                                        out=bufA[:, h, I * 512:(I + 1) * 512],
                                        in_=pt_h[:, 0:4, :].rearrange("p a b -> p (a b)")),
                                       reads=[pt_t], writes=[t_bufA])

                    emit_ST(0)
                    for n in range(len(steps)):
                        emit_exp(n)
                        if n + 1 < len(steps):
                            emit_ST(n + 1)
                        emit_PV(n)
                kb.end_scratch_phase()
            if first:
                tap("oaT", bufA[:], [128, KC, S], BF16, [t_bufA])

            if upto == "F":
                break

            with ExitStack() as ms:
                sgs = [kb.scratch(ms, f"e_sg{i}", [128, 512], F32) for i in range(2)]
                for oc in range(8):
                    si, u, t_u = ws.next()
                    for tc in range(4):
                        py_h, py_t = proj_bank()
                        for kc in range(KC):
                            op(PE, lambda kc=kc, tc=tc, py_h=py_h: nc.tensor.matmul(
                                py_h[:, 0:512], lhsT=u[:, kc, 0:128], rhs=bufA[:, kc, tc * 512:(tc + 1) * 512],
                                start=(kc == 0), stop=(kc == KC - 1)), reads=[t_u, t_bufA], writes=[py_t], inc=(kc == KC - 1))
                        pg_h, pg_t = proj_bank()
                        for kc in range(KC):
                            op(PE, lambda kc=kc, tc=tc, pg_h=pg_h: nc.tensor.matmul(
                                pg_h[:, 0:512], lhsT=u[:, kc, 128:256], rhs=xnT[:, kc, tc * 512:(tc + 1) * 512],
                                start=(kc == 0), stop=(kc == KC - 1)), reads=[t_u, t_xnT], writes=[pg_t], inc=(kc == KC - 1))
                        sg_h, sg_t = sgs[tc % 2]
                        op(ACT, lambda pg_h=pg_h, sg_h=sg_h, oc=oc: nc.scalar.activation(
                            out=sg_h[:], in_=pg_h[:, 0:512], func=AF.Sigmoid, bias=bg[:, oc:oc + 1]),
                           reads=[pg_t, t_bg], writes=[sg_t])
                        op(DVE, lambda py_h=py_h, sg_h=sg_h: nc.vector.tensor_tensor(
                            out=sg_h[:], in0=py_h[:, 0:512], in1=sg_h[:], op=ALU.mult), reads=[py_t, sg_t], writes=[sg_t])
                        op(DVE, lambda sg_h=sg_h, oc=oc, tc=tc: nc.vector.tensor_tensor(
                            out=bufY[:, oc, tc * 512:(tc + 1) * 512], in0=sg_h[:], in1=bufY[:, oc, tc * 512:(tc + 1) * 512],
                            op=ALU.add), reads=[sg_t, t_bufY], writes=[t_bufY])
                    ws.release(si)
                kb.end_scratch_phase()
            if first:
                tap("yT", bufY[:], [128, KC, S], BF16, [t_bufY])

            if upto == "FD":
                break

            dma(SP, out=gX[:], in_=final_g.partition_broadcast(128), writes=[t_gX])
            us = [ws.next() for _ in range(4)]
            for tb in range(NT):
                xt_h, xt_t = xts[tb % 2]
                dma(SP, out=xt_h[:], in_=x_d[seq, tb * 128:(tb + 1) * 128, :], writes=[xt_t])
                banks = [proj_bank(), proj_bank()]
                for hf in range(2):
                    po_h, po_t = banks[hf]
                    for n in range(2):
                        si, u, t_u = us[2 * hf + n]
                        for kc in range(KC):
                            op(PE, lambda kc=kc, n=n, u=u, po_h=po_h: nc.tensor.matmul(
                                po_h[:, n * 256:(n + 1) * 256], lhsT=bufY[:, kc, tb * 128:(tb + 1) * 128], rhs=u[:, kc, :],
                                start=(kc == 0), stop=(kc == KC - 1)), reads=[t_u, t_bufY], writes=[po_t],
                               inc=(kc == KC - 1 and n == 1))
                    op(DVE, lambda hf=hf, po_h=po_h: nc.vector.tensor_tensor(
                        out=xt_h[:, hf * 512:(hf + 1) * 512], in0=po_h[:, 0:512], in1=xt_h[:, hf * 512:(hf + 1) * 512],
                        op=ALU.add), reads=[po_t, xt_t], writes=[xt_t])
                xn_h, xn_t = xns[tb % 2]
                rmsnorm_tail(xt_h, xt_t, tb, gF, t_gF, xt_h[:], [xt_t], xn_h[:], [xn_t])
                dma(SP, out=y_d[seq, tb * 128:(tb + 1) * 128, :], in_=xt_h[:], reads=[xt_t])
            for (si, u, t_u) in us:
                ws.release(si)

        if kb.mode == "defer":
            kb.flush()
            build.n_inst = kb.n_inst
            build.nsem = kb.nsem
            build.est_us = kb.est_makespan
        out_costs = [o.cost for o in kb.ops]
    return nc, dbg_out, out_costs


def host_inputs(inputs):
    f = lambda a: np.ascontiguousarray(np.asarray(a, dtype=np.float32))
    gb = np.concatenate([f(inputs["b_fox_f"])[0], f(inputs["b_ml_i"])[0], f(inputs["b_ml_f"])[0]])
    gbias = np.ascontiguousarray(np.broadcast_to(gb[None, None, :], (128, NT, 16)))
    convw = np.ascontiguousarray(f(inputs["conv_w"])[0].reshape(4, 8, 128).transpose(2, 1, 0))
    convb = np.ascontiguousarray(f(inputs["conv_b"])[0].reshape(8, 128).T)
    bgate = np.ascontiguousarray(f(inputs["b_gate"])[0].reshape(16, 128).T)
    return {
        "w_in": f(inputs["w_in"])[0],
        "w_fox_down": f(inputs["w_fox_down"])[0],
        "w_ml_down": f(inputs["w_ml_down"])[0],
        "w_out": f(inputs["w_out"])[0],
        "norm_g": f(inputs["norm_g"])[0],
        "final_g": f(inputs["final_g"]),
        "ml_norm_g": f(inputs["ml_norm_g"])[0],
        "gbias": gbias,
        "convw": convw,
        "convb": convb,
        "bgate": bgate,
    }


def kernel(**inputs):
    x = np.ascontiguousarray(np.asarray(inputs["x"], dtype=np.float32))
    B = x.shape[0]
    nseq = B // NCORES
    shared = host_inputs(inputs)
    nc, _ = build(nseq)
    in_maps = []
    for c in range(NCORES):
        m = dict(shared)
        m["x"] = x[c * nseq:(c + 1) * nseq]
        in_maps.append(m)
    res = run_bass_kernel_spmd(nc, in_maps, core_ids=list(range(NCORES)))
    out = np.concatenate([np.asarray(r["y"]) for r in res.results], axis=0)
    return out.astype(np.float32, copy=False)
```

```python
import types
import numpy as np
from contextlib import ExitStack
import concourse.bass as bass
import concourse.mybir as mybir
from concourse.bass_utils import run_bass_kernel_spmd

F32 = mybir.dt.float32
BF16 = mybir.dt.bfloat16
AF = mybir.ActivationFunctionType
ALU = mybir.AluOpType

S = 2048
D = 1024
NT = 16
KC = 8
NCORES = 8
EPS = 1e-6
C_FQ, C_FK, C_FV, C_FF, C_FZ = 0, 1024, 2048, 3072, 3080
C_MQ, C_MK, C_MV, C_MI, C_MF, C_MO, C_MZ = 4104, 4616, 5128, 6152, 6156, 6160, 7184
C_GA, C_GB = 8208, 9232
IN_W = 10256
SEM_LIMIT = 30000
NSLOT = 6


class SemObj:
    __slots__ = ("h", "owner", "val")

    def __init__(self, h, owner):
        self.h = h
        self.owner = owner
        self.val = 0


class Tile:
    __slots__ = ("w", "r", "name", "psum")

    def __init__(self, name, init=None, psum=False):
        self.name = name
        self.w = list(init) if init else []
        self.r = []
        self.psum = psum


class Eng:
    def __init__(self, kb, name, h, is_pe=False):
        self.kb = kb
        self.name = name
        self.h = h
        self.is_pe = is_pe
        self.waited = {}
        self.cur = None

    def sem(self):
        if self.cur is None or self.cur.val >= SEM_LIMIT:
            self.cur = self.kb.new_sem(self)
        return self.cur


class Op:
    __slots__ = ("id", "eng", "fns", "deps", "cost", "is_dma", "tok", "start")

    def __init__(self, oid, eng, is_dma):
        self.id = oid
        self.eng = eng
        self.fns = []
        self.deps = []
        self.cost = 0.0
        self.is_dma = is_dma
        self.tok = None
        self.start = 0.0


def _snap(fn):
    if fn.__closure__ is None:
        return fn
    cells = []
    for c in fn.__closure__:
        try:
            cells.append(types.CellType(c.cell_contents))
        except ValueError:
            cells.append(c)
    return types.FunctionType(fn.__code__, fn.__globals__, fn.__name__, fn.__defaults__, tuple(cells))


def _free_elems(acc):
    try:
        ap = acc.ap
        n = 1
        for (_, c) in ap[1:]:
            n *= c
        return max(int(n), 1)
    except Exception:
        return 64


class KB:
    def __init__(self, nc, es, costs=None):
        self.nc = nc
        self.es = es
        self.mode = "measure" if costs is None else "defer"
        self.costs_in = costs
        self.nsem = 0
        self.PE = Eng(self, "pe", nc.tensor, is_pe=True)
        self.ACT = Eng(self, "act", nc.scalar)
        self.DVE = Eng(self, "dve", nc.vector)
        self.POOL = Eng(self, "pool", nc.gpsimd)
        self.SP = Eng(self, "sp", nc.sync)
        self.engs = [self.PE, self.ACT, self.DVE, self.POOL, self.SP]
        self.dma_pool = {}
        self.dma_rr = {}
        self.scr_tokens = []
        self.scr_tiles = []
        self.scr_id = 0
        self.n_inst = 0
        self.ops = []
        self.open_pe = None

    def new_sem(self, owner):
        self.nsem += 1
        h = self.es.enter_context(self.nc.semaphore(f"s{self.nsem}"))
        return SemObj(h, owner)

    def _deps(self, o, reads, writes):
        eng = o.eng
        ops = self.ops
        d = o.deps
        for t in reads:
            d.extend(t.w)
            if t.psum:
                d.extend(r for r in t.r if ops[r].eng is not eng)
        for t in writes:
            d.extend(t.w)
            d.extend(t.r)

    def _touch(self, oid, reads, writes):
        for t in reads:
            if not t.r or t.r[-1] != oid:
                t.r.append(oid)
        for t in writes:
            t.w = [oid]
            t.r = []

    def _cost_of(self, eng, ins):
        try:
            i = ins.ins
            n = _free_elems(i.outs[0])
            if eng.is_pe:
                f32 = False
                try:
                    f32 = (i.ins[0].dtype == F32) and not getattr(i, "is_transpose", False)
                except Exception:
                    pass
                return (max(n, 64) * (4 if f32 else 1) + 64) / 2000.0
            if eng is self.ACT:
                return 0.22 + n * 0.00083
            if eng is self.DVE:
                return 0.10 + n * 0.00104
            return 0.25 + n * 0.002
        except Exception:
            return 0.3

    def op(self, eng, fn, reads=(), writes=(), inc=True):
        if eng.is_pe and self.open_pe is not None:
            o = self.open_pe
        else:
            assert self.open_pe is None, "non-PE op recorded while a PE group is open"
            o = Op(len(self.ops), eng, False)
            self.ops.append(o)
            if eng.is_pe and not inc:
                self.open_pe = o
        if eng.is_pe and inc:
            self.open_pe = None
        self._deps(o, reads, writes)
        self._touch(o.id, reads, writes)
        if self.mode == "measure":
            ins = fn()
            o.cost += self._cost_of(eng, ins)
            return ins
        o.fns.append(_snap(fn))
        return None

    def dma(self, q, out, in_, reads=(), writes=()):
        assert self.open_pe is None
        o = Op(len(self.ops), q, True)
        self.ops.append(o)
        self._deps(o, reads, writes)
        self._touch(o.id, reads, writes)
        if self.mode == "measure":
            q.h.dma_start(out=out, in_=in_)
            try:
                n = 1
                for (_, c) in out.ap:
                    n *= c
            except Exception:
                n = 131072
            o.cost = 2.0 + n * 4 / 150e3
            return None
        o.fns.append(lambda: q.h.dma_start(out=out, in_=in_))
        return None

    def sb(self, name, shape, dt, es=None):
        h = (es or self.es).enter_context(self.nc.sbuf_tensor(name, shape, dt))
        return h, Tile(name)

    def scratch(self, es, name, shape, dt):
        self.scr_id += 1
        h = es.enter_context(self.nc.sbuf_tensor(f"{name}_{self.scr_id}", shape, dt))
        t = Tile(name, self.scr_tokens)
        self.scr_tiles.append(t)
        return h, t

    def end_scratch_phase(self):
        acc = set(self.scr_tokens)
        for t in self.scr_tiles:
            acc.update(t.w)
            acc.update(t.r)
        self.scr_tokens = sorted(acc)
        self.scr_tiles = []

    def schedule(self):
        ops = self.ops
        n = len(ops)
        costs = self.costs_in
        assert len(costs) == n, (len(costs), n)
        succ = [[] for _ in range(n)]
        for o in ops:
            o.deps = sorted(set(d for d in o.deps if d != o.id))
            for d in o.deps:
                succ[d].append(o.id)
        pend = {e: [] for e in self.engs}
        for o in ops:
            pend[o.eng].append(o.id)
        done = [False] * n
        fin = [0.0] * n
        efree = {e: 0.0 for e in self.engs}
        best = {e: None for e in self.engs}
        dirty = set(self.engs)
        HOP, SAME, W = 0.30, 0.06, 40
        order = []
        INF = float("inf")
        while len(order) < n:
            for e in dirty:
                lst = pend[e]
                be, bi, bpos = INF, -1, -1
                ef = efree[e]
                lim = min(W, len(lst))
                for pos in range(lim):
                    oid = lst[pos]
                    rt = ef
                    ok = True
                    for d in ops[oid].deps:
                        if not done[d]:
                            ok = False
                            break
                        t = fin[d] + (SAME if ops[d].eng is e else HOP)
                        if t > rt:
                            rt = t
                    if ok and rt < be:
                        be, bi, bpos = rt, oid, pos
                        if rt <= ef:
                            break
                best[e] = (be, bi, bpos) if bi >= 0 else None
            dirty.clear()
            ce, cb = None, None
            for e in self.engs:
                b = best[e]
                if b is not None and (cb is None or b[0] < cb[0] or (b[0] == cb[0] and b[1] < cb[1])):
                    ce, cb = e, b
            assert ce is not None, "scheduler deadlock"
            st, oid, pos = cb
            o = ops[oid]
            o.start = st
            if o.is_dma:
                efree[ce] = st + 0.15
                fin[oid] = st + costs[oid]
            else:
                efree[ce] = st + costs[oid]
                fin[oid] = efree[ce]
            done[oid] = True
            del pend[ce][pos]
            order.append(o)
            dirty.add(ce)
            for sidx in succ[oid]:
                dirty.add(ops[sidx].eng)
        self.est_makespan = max(fin) if fin else 0.0
        return order

    def flush(self):
        order = self.schedule()
        ops = self.ops
        for o in order:
            eng = o.eng
            for d in o.deps:
                p = ops[d]
                if eng.is_pe and p.eng is eng and not p.is_dma:
                    continue
                s, v = p.tok
                if eng.waited.get(s, 0) >= v:
                    continue
                eng.h.wait_ge(s.h, v)
                eng.waited[s] = v
                self.n_inst += 1
            if o.is_dma:
                pool = self.dma_pool.setdefault(eng, [])
                if len(pool) < 8:
                    dsem = self.new_sem(None)
                    pool.append(dsem)
                else:
                    i = self.dma_rr.get(eng, 0)
                    dsem = pool[i % len(pool)]
                    self.dma_rr[eng] = i + 1
                if dsem.val and eng.waited.get(dsem, 0) < dsem.val:
                    eng.h.wait_ge(dsem.h, dsem.val)
                    eng.waited[dsem] = dsem.val
                    self.n_inst += 1
                ins = o.fns[0]()
                dsem.val += 16
                ins.then_inc(dsem.h, 16)
                o.tok = (dsem, dsem.val)
            else:
                ins = None
                for fn in o.fns:
                    ins = fn()
                    self.n_inst += 1
                s = eng.sem()
                s.val += 1
                ins.then_inc(s.h, 1)
                o.tok = (s, s.val)
            self.n_inst += 1
        SP = self.SP
        for q, pool in self.dma_pool.items():
            for dsem in pool:
                if dsem.val and SP.waited.get(dsem, 0) < dsem.val:
                    self.nc.sync.wait_ge(dsem.h, dsem.val)
                    SP.waited[dsem] = dsem.val


class WStream:
    def __init__(self, kb, slots, plan):
        self.kb = kb
        self.slots = slots
        self.plan = plan
        self.issued = 0
        self.taken = 0
        self.free = [True] * len(slots)

    def _pump(self):
        while self.issued < len(self.plan):
            si = self.issued % len(self.slots)
            if not self.free[si]:
                break
            h, t = self.slots[si]
            for (dst, src, c0, n) in self.plan[self.issued]:
                self.kb.dma(self.kb.POOL, out=h[:, :, dst:dst + n], in_=src[:, :, c0:c0 + n], writes=[t])
            self.free[si] = False
            self.issued += 1

    def next(self):
        self._pump()
        assert self.taken < self.issued, "weight stream stalled (slot not released)"
        si = self.taken % len(self.slots)
        self.taken += 1
        h, t = self.slots[si]
        return si, h, t

    def release(self, si):
        self.free[si] = True
        self._pump()


def unit_plan(nseq, wv, wmd, wfd, wout):
    plan = []
    for _ in range(nseq):
        for h in range(4):
            plan.append([(0, wv, C_MQ + h * 128, 128), (128, wv, C_MK + h * 128, 128)])
            plan.append([(0, wv, C_MV + h * 256, 256)])
            plan.append([(0, wv, C_MO + h * 256, 256)])
            plan.append([(0, wv, C_MZ + h * 256, 256)])
        for oc in range(8):
            plan.append([(0, wmd, oc * 128, 128), (128, wv, C_GB + oc * 128, 128)])
        for h in range(8):
            plan.append([(0, wv, C_FQ + h * 128, 128), (128, wv, C_FK + h * 128, 128)])
            plan.append([(0, wv, C_FV + h * 128, 128), (128, wv, C_FZ + h * 128, 128)])
        for oc in range(8):
            plan.append([(0, wfd, oc * 128, 128), (128, wv, C_GA + oc * 128, 128)])
        for u in range(4):
            plan.append([(0, wout, u * 256, 256)])
    return plan


def build(nseq=2, dbg=None, upto=None):
    _, _, costs = _build_pass(nseq, dbg, upto, None)
    nc, dbg_out, _ = _build_pass(nseq, dbg, upto, costs)
    return nc, dbg_out


def _build_pass(nseq, dbg, upto, costs):
    dbg = dbg or set()
    nc = bass.Bass("TRN2", target_bir_lowering=False)
    x_d = nc.dram_tensor("x", [nseq, S, D], F32, kind="ExternalInput").ap()
    w_in = nc.dram_tensor("w_in", [D, IN_W], F32, kind="ExternalInput").ap()
    w_fd = nc.dram_tensor("w_fox_down", [D, D], F32, kind="ExternalInput").ap()
    w_md = nc.dram_tensor("w_ml_down", [D, D], F32, kind="ExternalInput").ap()
    w_o = nc.dram_tensor("w_out", [D, D], F32, kind="ExternalInput").ap()
    norm_g = nc.dram_tensor("norm_g", [D], F32, kind="ExternalInput").ap()
    final_g = nc.dram_tensor("final_g", [D], F32, kind="ExternalInput").ap()
    ml_g = nc.dram_tensor("ml_norm_g", [D], F32, kind="ExternalInput").ap()
    gbias_d = nc.dram_tensor("gbias", [128, NT, 16], F32, kind="ExternalInput").ap()
    convw_d = nc.dram_tensor("convw", [128, 8, 4], F32, kind="ExternalInput").ap()
    convb_d = nc.dram_tensor("convb", [128, 8], F32, kind="ExternalInput").ap()
    bgate_d = nc.dram_tensor("bgate", [128, 16], F32, kind="ExternalInput").ap()
    y_d = nc.dram_tensor("y", [nseq, S, D], F32, kind="ExternalOutput").ap()
    dbg_out = {}

    def dbg_tensor(name, shape, dt):
        dbg_out[name] = nc.dram_tensor("dbg_" + name, shape, dt, kind="ExternalOutput").ap()
        return dbg_out[name]

    wv = w_in.rearrange("(kc p) c -> p kc c", p=128)
    wfd = w_fd.rearrange("(kc p) c -> p kc c", p=128)
    wmd = w_md.rearrange("(kc p) c -> p kc c", p=128)
    wout = w_o.rearrange("(kc p) c -> p kc c", p=128)

    with ExitStack() as es:
        kb = KB(nc, es, costs)
        PE, ACT, DVE, POOL, SP = kb.PE, kb.ACT, kb.DVE, kb.POOL, kb.SP
        op, dma = kb.op, kb.dma

        xnT, t_xnT = kb.sb("xnT", [128, KC, S], BF16)
        bufA, t_bufA = kb.sb("bufA", [128, KC, S], BF16)
        slots = [kb.sb(f"wslot{i}", [128, KC, 256], BF16) for i in range(NSLOT)]
        wg, t_wg = kb.sb("wg", [128, KC, 16], BF16)
        gX, t_gX = kb.sb("gX", [128, D], F32)
        gN = gF = gM = gX
        t_gN = t_gF = t_gM = t_gX
        gbias, t_gbias = kb.sb("gbias_sb", [128, NT, 16], F32)
        cw, t_cw = kb.sb("cw", [128, 8, 4], F32)
        cb, t_cb = kb.sb("cb", [128, 8], F32)
        bg, t_bg = kb.sb("bg", [128, 16], F32)
        cbh, t_cbh = kb.sb("cbh", [128, 8], F32)
        bgh, t_bgh = kb.sb("bgh", [128, 16], F32)
        epsT, t_eps = kb.sb("epsT", [128, 1], F32)
        ident, t_ident = kb.sb("ident", [128, 128], BF16)
        onesb, t_onesb = kb.sb("onesb", [128, 128], BF16)
        maskb, t_maskb = kb.sb("maskb", [128, 128], BF16)
        maskf, t_maskf = kb.sb("maskf", [128, 128], F32)
        onesf, t_onesf = kb.sb("onesf", [128, 128], F32)
        hmid, t_hmid = kb.sb("hmid", [128, 128], F32)
        xts = [kb.sb(f"xt{i}", [128, D], F32) for i in range(2)]
        xns = [kb.sb(f"xn{i}", [128, D], BF16) for i in range(2)]
        stats = [kb.sb(f"stat{i}", [128, 4], F32) for i in range(2)]
        zs, t_zs = kb.sb("zs", [128, NT, 16], F32)
        lf, t_lf = kb.sb("lf", [128, NT, 16], F32)
        tot, t_tot = kb.sb("tot", [128, NT, 16], F32)
        bc, t_bc = kb.sb("bc", [128, NT, 16], F32)
        half, t_half = kb.sb("half", [128, NT, 16], F32)
        pre, t_pre = kb.sb("pre", [128, NT, 16], F32)
        cc, t_cc = kb.sb("cc", [128, NT, 16], F32)
        RR, t_RR = kb.sb("RR", [128, NT, 16], F32)
        ew, t_ew = kb.sb("ew", [128, NT, 4], F32)
        eb, t_eb = kb.sb("eb", [128, NT, 4], F32)
        aa, t_aa = kb.sb("aa", [128, NT, 4], F32)
        gtmp, t_gtmp = kb.sb("gtmp", [128, NT, 16], F32)

        PB = []
        for i in range(6):
            h = es.enter_context(nc.psum_tensor(f"pb{i}", [128, 512], F32))
            PB.append((h, Tile(f"pb{i}", psum=True)))
        PT = []
        for i in range(2):
            h = es.enter_context(nc.psum_tensor(f"pt{i}", [128, 8, 128], BF16))
            PT.append((h, Tile(f"pt{i}", psum=True)))

        dma(SP, out=gbias[:], in_=gbias_d[:, :, :], writes=[t_gbias])
        dma(SP, out=cw[:], in_=convw_d[:, :, :], writes=[t_cw])
        dma(SP, out=cb[:], in_=convb_d[:, :], writes=[t_cb])
        dma(SP, out=bg[:], in_=bgate_d[:, :], writes=[t_bg])
        dma(POOL, out=wg[:, :, 0:8], in_=wv[:, :, C_FF:C_FF + 8], writes=[t_wg])
        dma(POOL, out=wg[:, :, 8:16], in_=wv[:, :, C_MI:C_MI + 8], writes=[t_wg])
        op(DVE, lambda: nc.vector.memset(epsT[:], EPS), writes=[t_eps])
        op(DVE, lambda: nc.vector.tensor_scalar(out=cbh[:], in0=cb[:], scalar1=0.5, scalar2=None, op0=ALU.mult),
           reads=[t_cb], writes=[t_cbh])
        op(DVE, lambda: nc.vector.tensor_scalar(out=bgh[:], in0=bg[:], scalar1=0.5, scalar2=None, op0=ALU.mult),
           reads=[t_bg], writes=[t_bgh])
        op(POOL, lambda: nc.gpsimd.memset(onesb[:], 1.0), writes=[t_onesb])
        op(POOL, lambda: nc.gpsimd.memset(onesf[:], 1.0), writes=[t_onesf])
        op(POOL, lambda: nc.gpsimd.memset(hmid[:], 0.0), writes=[t_hmid])
        op(POOL, lambda: nc.gpsimd.memset(hmid[0:64, :], 1.0), writes=[t_hmid])
        op(POOL, lambda: nc.gpsimd.affine_select(out=ident[:], in_=onesb[:], pattern=[[1, 128]],
                                                 compare_op=ALU.is_equal, fill=0.0, base=0,
                                                 channel_multiplier=-1), reads=[t_onesb], writes=[t_ident])
        op(POOL, lambda: nc.gpsimd.affine_select(out=maskb[:], in_=onesb[:], pattern=[[1, 128]],
                                                 compare_op=ALU.is_ge, fill=0.0, base=0,
                                                 channel_multiplier=-1), reads=[t_onesb], writes=[t_maskb])
        op(POOL, lambda: nc.gpsimd.affine_select(out=maskf[:], in_=onesf[:], pattern=[[1, 128]],
                                                 compare_op=ALU.is_ge, fill=0.0, base=0,
                                                 channel_multiplier=-1), reads=[t_onesf], writes=[t_maskf])

        ws = WStream(kb, slots, unit_plan(nseq, wv, wmd, wfd, wout))
        proj_rr = [0]

        def proj_bank():
            proj_rr[0] += 1
            return PB[proj_rr[0] % 2]

        def tap(name, src_ap, shape, dt, tiles):
            if name in dbg:
                d = dbg_tensor(name, shape, dt)
                dma(SP, out=d, in_=src_ap, reads=tiles)

        def rmsnorm_tail(xt_h, xt_t, si, g_h, g_t, out_ap, out_tiles, sq_out_ap, sq_out_tiles):
            st_h, st_t = stats[si % 2]
            op(ACT, lambda: nc.scalar.activation(out=sq_out_ap, in_=xt_h[:], func=AF.Square,
                                                 accum_out=st_h[:, 0:1]),
               reads=[xt_t], writes=sq_out_tiles + [st_t])
            op(ACT, lambda: nc.scalar.activation(out=st_h[:, 1:2], in_=st_h[:, 0:1], func=AF.Sqrt,
                                                 scale=1.0 / D, bias=epsT[:]),
               reads=[st_t, t_eps], writes=[st_t])
            op(DVE, lambda: nc.vector.reciprocal(out=st_h[:, 2:3], in_=st_h[:, 1:2]), reads=[st_t], writes=[st_t])
            op(DVE, lambda: nc.vector.scalar_tensor_tensor(out=out_ap, in0=xt_h[:], scalar=st_h[:, 2:3],
                                                           in1=g_h[:], op0=ALU.mult, op1=ALU.mult),
               reads=[xt_t, st_t, g_t], writes=out_tiles)

        for seq in range(nseq):
            first = (seq == 0)
            dma(SP, out=gX[:], in_=norm_g.partition_broadcast(128), writes=[t_gX])
            for tb in range(NT):
                xt_h, xt_t = xts[tb % 2]
                xn_h, xn_t = xns[tb % 2]
                dma(SP, out=xt_h[:], in_=x_d[seq, tb * 128:(tb + 1) * 128, :], writes=[xt_t])
                rmsnorm_tail(xt_h, xt_t, tb, gN, t_gN, xn_h[:], [xn_t], xn_h[:], [xn_t])
                pt_h, pt_t = PT[tb % 2]
                for kc in range(KC):
                    op(PE, lambda kc=kc: nc.tensor.transpose(out=pt_h[:, kc, :], in_=xn_h[:, kc * 128:(kc + 1) * 128],
                                                             identity=ident[:]),
                       reads=[xn_t, t_ident], writes=[pt_t], inc=(kc == KC - 1))
                op(ACT, lambda: nc.scalar.copy(out=xnT[:, :, tb * 128:(tb + 1) * 128], in_=pt_h[:]),
                   reads=[pt_t], writes=[t_xnT])
            if first:
                tap("xnT", xnT[:], [128, KC, S], BF16, [t_xnT])

            if upto == "P0":
                break

            pz_h, pz_t = PB[2]
            for tb in range(NT):
                for kc in range(KC):
                    op(PE, lambda kc=kc, tb=tb: nc.tensor.matmul(pz_h[:, tb * 16:(tb + 1) * 16],
                                                                 lhsT=xnT[:, kc, tb * 128:(tb + 1) * 128],
                                                                 rhs=wg[:, kc, :], start=(kc == 0), stop=(kc == KC - 1)),
                       reads=[t_xnT, t_wg], writes=[pz_t], inc=(kc == KC - 1 and tb == NT - 1))
            op(DVE, lambda: nc.vector.tensor_tensor(out=zs[:].rearrange("p a b -> p (a b)"), in0=pz_h[:, 0:256],
                                                    in1=gbias[:].rearrange("p a b -> p (a b)"), op=ALU.add),
               reads=[pz_t, t_gbias], writes=[t_zs])
            if upto == "G1":
                tap("lf", zs[:], [128, NT, 16], F32, [t_zs])
                break
            op(ACT, lambda: nc.scalar.activation(out=gtmp[:], in_=zs[:], func=AF.Exp, scale=-1.0),
               reads=[t_zs], writes=[t_gtmp])
            op(ACT, lambda: nc.scalar.activation(out=gtmp[:], in_=gtmp[:], func=AF.Ln, bias=1.0, scale=1.0),
               reads=[t_gtmp], writes=[t_gtmp])
            op(DVE, lambda: nc.vector.tensor_scalar(out=lf[:], in0=gtmp[:], scalar1=-1.0, scalar2=None, op0=ALU.mult),
               reads=[t_gtmp], writes=[t_lf])
            if upto == "G2":
                tap("lf", lf[:], [128, NT, 16], F32, [t_lf])
                break
            lf2 = lf[:].rearrange("p a b -> p (a b)")
            for (lhs_h, lhs_t, dst_h, dst_t, pbi) in ((onesf, t_onesf, tot, t_tot, 3), (maskf, t_maskf, bc, t_bc, 4),
                                                       (hmid, t_hmid, half, t_half, 5)):
                pp_h, pp_t = PB[pbi]
                op(PE, lambda lhs_h=lhs_h, pp_h=pp_h: nc.tensor.matmul(pp_h[:, 0:256], lhsT=lhs_h[:], rhs=lf2,
                                                                       start=True, stop=True),
                   reads=[lhs_t, t_lf], writes=[pp_t])
                op(DVE, lambda dst_h=dst_h, pp_h=pp_h: nc.vector.tensor_copy(out=dst_h[:].rearrange("p a b -> p (a b)"),
                                                                             in_=pp_h[:, 0:256]),
                   reads=[pp_t], writes=[dst_t])
            if upto == "G3":
                tap("lf", lf[:], [128, NT, 16], F32, [t_lf])
                tap("cc", bc[:], [128, NT, 16], F32, [t_bc])
                break
            op(DVE, lambda: nc.vector.memset(pre[:, 0, :], 0.0), writes=[t_pre])
            for tb in range(1, NT):
                op(DVE, lambda tb=tb: nc.vector.tensor_tensor(out=pre[:, tb, :], in0=pre[:, tb - 1, :],
                                                              in1=tot[:, tb - 1, :], op=ALU.add),
                   reads=[t_pre, t_tot], writes=[t_pre])
            op(DVE, lambda: nc.vector.tensor_tensor(out=cc[:], in0=bc[:], in1=pre[:], op=ALU.add),
               reads=[t_bc, t_pre], writes=[t_cc])
            op(DVE, lambda: nc.vector.tensor_tensor(out=RR[:], in0=half[:], in1=pre[:], op=ALU.add),
               reads=[t_half, t_pre], writes=[t_RR])
            if upto == "G4":
                tap("cc", cc[:], [128, NT, 16], F32, [t_cc])
                break
            op(DVE, lambda: nc.vector.tensor_tensor(out=ew[:], in0=zs[:, :, 8:12], in1=bc[:, :, 12:16], op=ALU.subtract),
               reads=[t_zs, t_bc], writes=[t_ew])
            if upto == "G5":
                tap("cc", cc[:], [128, NT, 16], F32, [t_cc, t_ew])
                break
            op(ACT, lambda: nc.scalar.activation(out=ew[:], in_=ew[:], func=AF.Exp), reads=[t_ew], writes=[t_ew])
            if upto == "G6":
                tap("cc", cc[:], [128, NT, 16], F32, [t_cc, t_ew])
                break
            op(ACT, lambda: nc.scalar.activation(out=eb[:], in_=bc[:, :, 12:16], func=AF.Exp), reads=[t_bc], writes=[t_eb])
            op(DVE, lambda: nc.vector.tensor_scalar(out=eb[:], in0=eb[:], scalar1=float(128 ** -0.5), scalar2=None,
                                                    op0=ALU.mult), reads=[t_eb], writes=[t_eb])
            if upto == "G7":
                tap("cc", cc[:], [128, NT, 16], F32, [t_cc, t_ew, t_eb])
                break
            op(ACT, lambda: nc.scalar.activation(out=aa[:], in_=tot[:, :, 12:16], func=AF.Exp), reads=[t_tot], writes=[t_aa])
            if first:
                tap("lf", lf[:], [128, NT, 16], F32, [t_lf])
                tap("cc", cc[:], [128, NT, 16], F32, [t_cc])
                tap("RR", RR[:], [128, NT, 16], F32, [t_RR])
                tap("ew", ew[:], [128, NT, 4], F32, [t_ew])
                tap("eb", eb[:], [128, NT, 4], F32, [t_eb])
                tap("aa", aa[:], [128, NT, 4], F32, [t_aa])

            if upto == "G":
                break

            dma(SP, out=gX[:], in_=ml_g.partition_broadcast(128), writes=[t_gX])
            with ExitStack() as ms:
                qT, t_qT = kb.scratch(ms, "m_qT", [128, S], BF16)
                kT, t_kT = kb.scratch(ms, "m_kT", [128, S], BF16)
                cbuf, t_cbuf = kb.scratch(ms, "m_cbuf", [128, S + 4], F32)
                accs = [kb.scratch(ms, f"m_acc{i}", [128, 1024], F32) for i in range(2)]
                ths = [kb.scratch(ms, f"m_th{i}", [128, 1024], F32) for i in range(2)]
                kTok, t_kTok = kb.scratch(ms, "m_kTok", [128, NT, 128], BF16)
                vaug, t_vaug = kb.scratch(ms, "m_vaug", [128, NT, 258], BF16)
                Crun = [kb.scratch(ms, f"m_C{i}", [128, 258], F32) for i in range(2)]
                Cbf, t_Cbf = kb.scratch(ms, "m_Cbf", [128, NT, 258], BF16)
                numS, t_numS = kb.scratch(ms, "m_numS", [128, 4, 258], F32)
                Gp, t_Gp = kb.scratch(ms, "m_Gp", [128, 4, 256], BF16)
                STm = [kb.scratch(ms, f"m_STm{i}", [128, 128], BF16) for i in range(2)]
                hn = [kb.scratch(ms, f"m_hn{i}", [128, 256], BF16) for i in range(2)]
                s12 = [kb.scratch(ms, f"m_s12{i}", [128, 512], F32) for i in range(2)]
                sm, t_sm = kb.scratch(ms, "m_sm", [128, 12, 4], F32)
                junk, t_junk = kb.scratch(ms, "m_junk", [128, 256], BF16)
                op(DVE, lambda: nc.vector.memset(cbuf[:, 0:4], 0.0), writes=[t_cbuf])
                CB0 = 1
                for h in range(4):
                    s1i, u1, t_u1 = ws.next()
                    s2i, u2, t_u2 = ws.next()
                    s3i, u3, t_u3 = ws.next()
                    s4i, u4, t_u4 = ws.next()
                    for which, (dst, t_dst) in enumerate(((qT, t_qT), (kT, t_kT))):
                        chunk = which * 4 + h
                        for tc in range(4):
                            pp_h, pp_t = proj_bank()
                            for kc in range(KC):
                                op(PE, lambda kc=kc, tc=tc, which=which, pp_h=pp_h: nc.tensor.matmul(
                                    pp_h[:, 0:512], lhsT=u1[:, kc, which * 128:(which + 1) * 128],
                                    rhs=xnT[:, kc, tc * 512:(tc + 1) * 512], start=(kc == 0), stop=(kc == KC - 1)),
                                   reads=[t_u1, t_xnT], writes=[pp_t], inc=(kc == KC - 1))
                            op(ACT, lambda tc=tc, pp_h=pp_h: nc.scalar.copy(out=cbuf[:, 4 + tc * 512:4 + (tc + 1) * 512],
                                                                           in_=pp_h[:, 0:512]),
                               reads=[pp_t], writes=[t_cbuf])
                        for hf in range(2):
                            o = hf * 1024
                            ac_h, ac_t = accs[hf]
                            op(DVE, lambda o=o, ac_h=ac_h, chunk=chunk: nc.vector.tensor_scalar(
                                out=ac_h[:], in0=cbuf[:, 4 + o:4 + o + 1024], scalar1=cw[:, chunk, 3:4], scalar2=None,
                                op0=ALU.mult), reads=[t_cbuf, t_cw], writes=[ac_t])
                            for j in (2, 1, 0):
                                op(DVE, lambda o=o, ac_h=ac_h, chunk=chunk, j=j: nc.vector.scalar_tensor_tensor(
                                    out=ac_h[:], in0=cbuf[:, 1 + j + o:1 + j + o + 1024], scalar=cw[:, chunk, j:j + 1],
                                    in1=ac_h[:], op0=ALU.mult, op1=ALU.add), reads=[t_cbuf, t_cw, ac_t], writes=[ac_t])
                            th_h, th_t = ths[hf]
                            op(ACT, lambda ac_h=ac_h, chunk=chunk, th_h=th_h: nc.scalar.activation(
                                out=th_h[:], in_=ac_h[:], func=AF.Tanh, bias=cbh[:, chunk:chunk + 1], scale=0.5),
                               reads=[ac_t, t_cbh], writes=[th_t])
                            op(DVE, lambda th_h=th_h: nc.vector.tensor_scalar(
                                out=th_h[:], in0=th_h[:], scalar1=0.5, scalar2=0.5, op0=ALU.mult, op1=ALU.add),
                               reads=[th_t], writes=[th_t])
                            op(DVE, lambda o=o, ac_h=ac_h, chunk=chunk, dst=dst, th_h=th_h: nc.vector.scalar_tensor_tensor(
                                out=dst[:, o:o + 1024], in0=ac_h[:], scalar=cb[:, chunk:chunk + 1], in1=th_h[:],
                                op0=ALU.add, op1=ALU.mult), reads=[ac_t, t_cb, th_t], writes=[t_dst])
                    ws.release(s1i)
                    for g in range(2):
                        pt_h, pt_t = PT[g]
                        for c8 in range(8):
                            c = g * 8 + c8
                            op(PE, lambda c=c, c8=c8, pt_h=pt_h: nc.tensor.transpose(
                                out=pt_h[:, c8, :], in_=kT[:, c * 128:(c + 1) * 128], identity=ident[:]),
                               reads=[t_kT, t_ident], writes=[pt_t], inc=(c8 == 7))
                        for c8 in range(8):
                            c = g * 8 + c8
                            op(ACT, lambda c=c, c8=c8, pt_h=pt_h: nc.scalar.activation(
                                out=kTok[:, c, :], in_=pt_h[:, c8, :], func=AF.Copy, scale=aa[:, c, h:h + 1]),
                               reads=[pt_t, t_aa], writes=[t_kTok])
                    for tb in range(NT):
                        pp_h, pp_t = proj_bank()
                        for kc in range(KC):
                            op(PE, lambda kc=kc, tb=tb, pp_h=pp_h: nc.tensor.matmul(
                                pp_h[:, 0:256], lhsT=xnT[:, kc, tb * 128:(tb + 1) * 128], rhs=u2[:, kc, :],
                                start=(kc == 0), stop=(kc == KC - 1)),
                               reads=[t_u2, t_xnT], writes=[pp_t], inc=(kc == KC - 1))
                        op(ACT, lambda tb=tb, pp_h=pp_h: nc.scalar.activation(
                            out=vaug[:, tb, 0:256], in_=pp_h[:, 0:256], func=AF.Copy, scale=ew[:, tb, h:h + 1]),
                           reads=[pp_t, t_ew], writes=[t_vaug])
                    ws.release(s2i)
                    op(DVE, lambda: nc.vector.tensor_copy(out=vaug[:, :, 256:257], in_=ew[:, :, h:h + 1]),
                       reads=[t_ew], writes=[t_vaug])
                    dcb = (PB[5], PB[2])
                    for c in range(NT - 1):
                        pd_h, pd_t = dcb[c % 2]
                        op(PE, lambda c=c, pd_h=pd_h: nc.tensor.matmul(pd_h[:, 0:257], lhsT=kTok[:, c, :],
                                                                       rhs=vaug[:, c, 0:257], start=True, stop=True),
                           reads=[t_kTok, t_vaug], writes=[pd_t])
                        cn_h, cn_t = Crun[(c + 1) % 2]
                        cp_h, cp_t = Crun[c % 2]
                        if c == 0:
                            op(DVE, lambda pd_h=pd_h, cn_h=cn_h: nc.vector.tensor_copy(out=cn_h[:, 0:257], in_=pd_h[:, 0:257]),
                               reads=[pd_t], writes=[cn_t])
                        else:
                            op(DVE, lambda c=c, pd_h=pd_h, cn_h=cn_h, cp_h=cp_h: nc.vector.scalar_tensor_tensor(
                                out=cn_h[:, 0:257], in0=cp_h[:, 0:257], scalar=aa[:, c, h:h + 1], in1=pd_h[:, 0:257],
                                op0=ALU.mult, op1=ALU.add), reads=[cp_t, pd_t, t_aa], writes=[cn_t])
                        op(ACT, lambda c=c, cn_h=cn_h: nc.scalar.copy(out=Cbf[:, c + 1, 0:257], in_=cn_h[:, 0:257]),
                           reads=[cn_t], writes=[t_Cbf])
                    for c in range(NT):
                        ps_h, ps_t = PB[2]
                        op(PE, lambda c=c: nc.tensor.matmul(ps_h[:, 0:128], lhsT=kT[:, c * 128:(c + 1) * 128],
                                                            rhs=qT[:, c * 128:(c + 1) * 128], start=True, stop=True),
                           reads=[t_kT, t_qT], writes=[ps_t])
                        st_h, st_t = STm[c % 2]
                        op(DVE, lambda st_h=st_h: nc.vector.tensor_tensor(out=st_h[:], in0=ps_h[:, 0:128], in1=maskf[:],
                                                                          op=ALU.mult),
                           reads=[ps_t, t_maskf], writes=[st_t])
                        pn_h, pn_t = PB[3 + c % 2]
                        op(PE, lambda c=c, pn_h=pn_h, st_h=st_h: nc.tensor.matmul(
                            pn_h[:, 0:257], lhsT=st_h[:], rhs=vaug[:, c, 0:257], start=True, stop=(c == 0)),
                           reads=[st_t, t_vaug], writes=[pn_t], inc=(c == 0))
                        if c > 0:
                            op(PE, lambda c=c, pn_h=pn_h: nc.tensor.matmul(
                                pn_h[:, 0:257], lhsT=qT[:, c * 128:(c + 1) * 128], rhs=Cbf[:, c, 0:257],
                                start=False, stop=True), reads=[t_qT, t_Cbf], writes=[pn_t])
                        op(ACT, lambda c=c, pn_h=pn_h: nc.scalar.copy(out=numS[:, c % 4, 0:257], in_=pn_h[:, 0:257]),
                           reads=[pn_t], writes=[t_numS])
                        pp_h, pp_t = proj_bank()
                        for gi, (ug, t_ug) in enumerate(((u3, t_u3), (u4, t_u4))):
                            for kc in range(KC):
                                op(PE, lambda kc=kc, c=c, gi=gi, ug=ug, pp_h=pp_h: nc.tensor.matmul(
                                    pp_h[:, gi * 256:(gi + 1) * 256], lhsT=xnT[:, kc, c * 128:(c + 1) * 128],
                                    rhs=ug[:, kc, :], start=(kc == 0), stop=(kc == KC - 1)),
                                   reads=[t_ug, t_xnT], writes=[pp_t], inc=(kc == KC - 1 and gi == 1))
                        sg_h, sg_t = s12[c % 2]
                        op(ACT, lambda pp_h=pp_h, sg_h=sg_h: nc.scalar.activation(out=sg_h[:, 0:512], in_=pp_h[:, 0:512],
                                                                                  func=AF.Tanh, scale=0.5),
                           reads=[pp_t], writes=[sg_t])
                        op(DVE, lambda pp_h=pp_h, sg_h=sg_h: nc.vector.scalar_tensor_tensor(
                            out=sg_h[:, 256:512], in0=sg_h[:, 256:512], scalar=1.0, in1=pp_h[:, 256:512],
                            op0=ALU.add, op1=ALU.mult), reads=[sg_t, pp_t], writes=[sg_t])
                        op(DVE, lambda sg_h=sg_h: nc.vector.scalar_tensor_tensor(
                            out=sg_h[:, 0:256], in0=sg_h[:, 0:256], scalar=1.0, in1=sg_h[:, 256:512],
                            op0=ALU.add, op1=ALU.mult), reads=[sg_t], writes=[sg_t])
                        op(DVE, lambda c=c, sg_h=sg_h: nc.vector.tensor_tensor(out=Gp[:, c % 4, :], in0=sg_h[:, 0:256],
                                                                               in1=gM[:, h * 256:(h + 1) * 256], op=ALU.mult),
                           reads=[sg_t, t_gM], writes=[t_Gp])
                        if c % 4 == 3:
                            cs = c - 3
                            den = numS[:, :, 256]
                            ebs = eb[:, cs:cs + 4, h]
                            op(DVE, lambda: nc.vector.tensor_tensor(out=sm[:, 0, :], in0=den, in1=ebs, op=ALU.mult),
                               reads=[t_numS, t_eb], writes=[t_sm])
                            op(DVE, lambda: nc.vector.scalar_tensor_tensor(out=sm[:, 1, :], in0=sm[:, 0, :], scalar=-1.0,
                                                                           in1=sm[:, 0, :], op0=ALU.mult, op1=ALU.max),
                               reads=[t_sm], writes=[t_sm])
                            op(DVE, lambda: nc.vector.tensor_scalar(out=sm[:, 2, :], in0=sm[:, 1, :], scalar1=1.0,
                                                                    scalar2=None, op0=ALU.max), reads=[t_sm], writes=[t_sm])
                            op(DVE, lambda: nc.vector.reciprocal(out=sm[:, 3, :], in_=sm[:, 2, :]), reads=[t_sm], writes=[t_sm])
                            op(DVE, lambda: nc.vector.tensor_tensor(out=sm[:, 4, :], in0=sm[:, 3, :], in1=ebs, op=ALU.mult),
                               reads=[t_sm, t_eb], writes=[t_sm])
                            for c8 in range(4):
                                op(ACT, lambda c8=c8: nc.scalar.activation(out=junk[:], in_=numS[:, c8, 0:256],
                                                                           func=AF.Square, accum_out=sm[:, 5, c8:c8 + 1]),
                                   reads=[t_numS], writes=[t_junk, t_sm])
                            op(DVE, lambda: nc.vector.tensor_tensor(out=sm[:, 6, :], in0=sm[:, 4, :], in1=sm[:, 4, :],
                                                                    op=ALU.mult), reads=[t_sm], writes=[t_sm])
                            op(DVE, lambda: nc.vector.tensor_tensor(out=sm[:, 7, :], in0=sm[:, 6, :], in1=sm[:, 5, :],
                                                                    op=ALU.mult), reads=[t_sm], writes=[t_sm])
                            op(ACT, lambda: nc.scalar.activation(out=sm[:, 8, :], in_=sm[:, 7, :], func=AF.Sqrt,
                                                                 scale=1.0 / 256, bias=epsT[:]),
                               reads=[t_sm, t_eps], writes=[t_sm])
                            op(DVE, lambda: nc.vector.reciprocal(out=sm[:, 9, :], in_=sm[:, 8, :]), reads=[t_sm], writes=[t_sm])
                            op(DVE, lambda: nc.vector.scalar_tensor_tensor(out=sm[:, 10, :], in0=sm[:, 9, :], scalar=0.25,
                                                                           in1=sm[:, 4, :], op0=ALU.mult, op1=ALU.mult),
                               reads=[t_sm], writes=[t_sm])
                            for c8 in range(4):
                                hn_h, hn_t = hn[c8 % 2]
                                op(DVE, lambda c8=c8, hn_h=hn_h: nc.vector.scalar_tensor_tensor(
                                    out=hn_h[:], in0=numS[:, c8, 0:256], scalar=sm[:, 10, c8:c8 + 1], in1=Gp[:, c8, :],
                                    op0=ALU.mult, op1=ALU.mult), reads=[t_numS, t_sm, t_Gp], writes=[hn_t])
                                for j in range(2):
                                    op(PE, lambda c8=c8, j=j, hn_h=hn_h: nc.tensor.transpose(
                                        out=PT[j][0][:, c8, :], in_=hn_h[:, j * 128:(j + 1) * 128], identity=ident[:]),
                                       reads=[hn_t, t_ident], writes=[PT[j][1]])
                            for j in range(2):
                                op(ACT, lambda j=j, cs=cs: nc.scalar.copy(
                                    out=bufA[:, 2 * h + j, cs * 128:(cs + 4) * 128],
                                    in_=PT[j][0][:, 0:4, :].rearrange("p a b -> p (a b)")),
                                   reads=[PT[j][1]], writes=[t_bufA])
                    ws.release(s3i)
                    ws.release(s4i)
                kb.end_scratch_phase()
            if first:
                tap("hbT", bufA[:], [128, KC, S], BF16, [t_bufA])

            if upto == "M":
                break

            ys = es.enter_context(ExitStack())
            bufY, t_bufY = kb.scratch(ys, "bufY", [128, KC, S], BF16)
            with ExitStack() as ms:
                sgs = [kb.scratch(ms, f"d_sg{i}", [128, 512], F32) for i in range(2)]
                for oc in range(8):
                    si, u, t_u = ws.next()
                    for tc in range(4):
                        py_h, py_t = proj_bank()
                        for kc in range(KC):
                            op(PE, lambda kc=kc, tc=tc, py_h=py_h: nc.tensor.matmul(
                                py_h[:, 0:512], lhsT=u[:, kc, 0:128], rhs=bufA[:, kc, tc * 512:(tc + 1) * 512],
                                start=(kc == 0), stop=(kc == KC - 1)), reads=[t_u, t_bufA], writes=[py_t], inc=(kc == KC - 1))
                        pg_h, pg_t = proj_bank()
                        for kc in range(KC):
                            op(PE, lambda kc=kc, tc=tc, pg_h=pg_h: nc.tensor.matmul(
                                pg_h[:, 0:512], lhsT=u[:, kc, 128:256], rhs=xnT[:, kc, tc * 512:(tc + 1) * 512],
                                start=(kc == 0), stop=(kc == KC - 1)), reads=[t_u, t_xnT], writes=[pg_t], inc=(kc == KC - 1))
                        sg_h, sg_t = sgs[tc % 2]
                        op(ACT, lambda pg_h=pg_h, sg_h=sg_h, oc=oc: nc.scalar.activation(
                            out=sg_h[:], in_=pg_h[:, 0:512], func=AF.Tanh, bias=bgh[:, 8 + oc:9 + oc], scale=0.5),
                           reads=[pg_t, t_bgh], writes=[sg_t])
                        op(DVE, lambda py_h=py_h, sg_h=sg_h, oc=oc, tc=tc: nc.vector.scalar_tensor_tensor(
                            out=bufY[:, oc, tc * 512:(tc + 1) * 512], in0=sg_h[:], scalar=1.0, in1=py_h[:, 0:512],
                            op0=ALU.add, op1=ALU.mult), reads=[py_t, sg_t], writes=[t_bufY])
                    ws.release(si)
                kb.end_scratch_phase()
            if first:
                tap("ybg", bufY[:], [128, KC, S], BF16, [t_bufY])

            if upto == "MD":
                break

            with ExitStack() as ms:
                fq = [kb.scratch(ms, f"f_q{i}", [128, S], BF16) for i in range(2)]
                fk = [kb.scratch(ms, f"f_k{i}", [128, S], BF16) for i in range(2)]
                fv = [kb.scratch(ms, f"f_v{i}", [128, NT, 130], BF16) for i in range(2)]
                fzs = [kb.scratch(ms, f"f_z{i}", [128, NT, 128], BF16) for i in range(2)]
                bms = [kb.scratch(ms, f"f_bm{i}", [128, NT, NT], F32) for i in range(2)]
                ptb = [kb.scratch(ms, f"f_pt{i}", [128, 512], BF16) for i in range(2)]
                obs = [kb.scratch(ms, f"f_ob{i}", [128, 4, 128], BF16) for i in range(2)]
                rin, t_rin = kb.scratch(ms, "f_rin", [128, 16], F32)
                tzs = [kb.scratch(ms, f"f_tz{i}", [128, 128], F32) for i in range(4)]
                for i in range(2):
                    op(DVE, lambda i=i: nc.vector.memset(fv[i][0][:, :, 128:129], 2.0), writes=[fv[i][1]])
                sc = float(128 ** -0.5)
                sb_i = 0
                for h in range(8):
                    s1i, u1, t_u1 = ws.next()
                    s2i, u2, t_u2 = ws.next()
                    qT, t_qT = fq[h % 2]
                    kT, t_kT = fk[h % 2]
                    V, t_V = fv[h % 2]
                    FZ, t_FZ = fzs[h % 2]
                    bm, t_bm = bms[h % 2]
                    for which, (dst, t_dst) in enumerate(((qT, t_qT), (kT, t_kT))):
                        for tc in range(4):
                            pp_h, pp_t = proj_bank()
                            for kc in range(KC):
                                op(PE, lambda kc=kc, tc=tc, which=which, pp_h=pp_h: nc.tensor.matmul(
                                    pp_h[:, 0:512], lhsT=u1[:, kc, which * 128:(which + 1) * 128],
                                    rhs=xnT[:, kc, tc * 512:(tc + 1) * 512], start=(kc == 0), stop=(kc == KC - 1)),
                                   reads=[t_u1, t_xnT], writes=[pp_t], inc=(kc == KC - 1))
                            op(ACT, lambda tc=tc, pp_h=pp_h, dst=dst: nc.scalar.copy(
                                out=dst[:, tc * 512:(tc + 1) * 512], in_=pp_h[:, 0:512]), reads=[pp_t], writes=[t_dst])
                    ws.release(s1i)
                    for tb in range(NT):
                        pp_h, pp_t = proj_bank()
                        for kc in range(KC):
                            op(PE, lambda kc=kc, tb=tb, pp_h=pp_h: nc.tensor.matmul(
                                pp_h[:, 0:256], lhsT=xnT[:, kc, tb * 128:(tb + 1) * 128], rhs=u2[:, kc, :],
                                start=(kc == 0), stop=(kc == KC - 1)), reads=[t_u2, t_xnT], writes=[pp_t], inc=(kc == KC - 1))
                        op(DVE, lambda tb=tb, pp_h=pp_h: nc.vector.tensor_copy(out=V[:, tb, 0:128], in_=pp_h[:, 0:128]),
                           reads=[pp_t], writes=[t_V])
                        tz_h, tz_t = tzs[tb % 4]
                        op(ACT, lambda pp_h=pp_h, tz_h=tz_h: nc.scalar.activation(out=tz_h[:], in_=pp_h[:, 128:256],
                                                                                  func=AF.Tanh, scale=0.5),
                           reads=[pp_t], writes=[tz_t])
                        op(DVE, lambda tb=tb, pp_h=pp_h, tz_h=tz_h: nc.vector.scalar_tensor_tensor(
                            out=FZ[:, tb, :], in0=tz_h[:], scalar=1.0, in1=pp_h[:, 128:256], op0=ALU.add, op1=ALU.mult),
                           reads=[pp_t, tz_t], writes=[t_FZ])
                    ws.release(s2i)
                    for j in range(NT):
                        op(DVE, lambda j=j: nc.vector.tensor_scalar(out=bm[:, j, :], in0=RR[:, :, h],
                                                                    scalar1=cc[:, j, h:h + 1], scalar2=None,
                                                                    op0=ALU.subtract), reads=[t_RR, t_cc], writes=[t_bm])
                    steps = [(I, j) for I in range(4) for j in range(4 * I + 4)]

                    def emit_ST(n):
                        I, j = steps[n]
                        i0 = max(j, 4 * I)
                        ps_h, ps_t = PB[2 + n % 2]
                        N = (4 * I + 4 - i0) * 128
                        op(PE, lambda: nc.tensor.matmul(ps_h[:, 0:N], lhsT=kT[:, j * 128:(j + 1) * 128],
                                                        rhs=qT[:, i0 * 128:(4 * I + 4) * 128], start=True, stop=True),
                           reads=[t_kT, t_qT], writes=[ps_t])

                    def emit_exp(n):
                        I, j = steps[n]
                        i0 = max(j, 4 * I)
                        ps_h, ps_t = PB[2 + n % 2]
                        p_h, p_t = ptb[n % 2]
                        for i in range(i0, 4 * I + 4):
                            lo = (i - i0) * 128
                            op(ACT, lambda i=i, lo=lo: nc.scalar.activation(
                                out=p_h[:, lo:lo + 128], in_=ps_h[:, lo:lo + 128], func=AF.Exp, scale=sc,
                                bias=bm[:, j, i:i + 1]), reads=[ps_t, t_bm], writes=[p_t])
                        if j >= 4 * I:
                            op(DVE, lambda: nc.vector.tensor_tensor(out=p_h[:, 0:128], in0=p_h[:, 0:128], in1=maskb[:],
                                                                    op=ALU.mult), reads=[p_t, t_maskb], writes=[p_t])

                    def emit_PV(n):
                        I, j = steps[n]
                        i0 = max(j, 4 * I)
                        p_h, p_t = ptb[n % 2]
                        for i in range(i0, 4 * I + 4):
                            il = i - 4 * I
                            lo = (i - i0) * 128
                            po_h, po_t = PB[4 + il // 2]
                            off = (il % 2) * 129
                            op(PE, lambda i=i, lo=lo, po_h=po_h, off=off, il=il: nc.tensor.matmul(
                                po_h[:, off:off + 129], lhsT=p_h[:, lo:lo + 128], rhs=V[:, j, 0:129],
                                start=(j == 0 and il % 2 == 0), stop=(j == i), skip_group_check=True),
                               reads=[p_t, t_V], writes=[po_t])
                            if j == i:
                                ob_h, ob_t = obs[I % 2]
                                op(DVE, lambda po_h=po_h, off=off, i=i: nc.vector.reciprocal(
                                    out=rin[:, i:i + 1], in_=po_h[:, off + 128:off + 129]), reads=[po_t], writes=[t_rin])
                                op(DVE, lambda po_h=po_h, off=off, i=i, il=il, ob_h=ob_h: nc.vector.scalar_tensor_tensor(
                                    out=ob_h[:, il, :], in0=po_h[:, off:off + 128], scalar=rin[:, i:i + 1], in1=FZ[:, i, :],
                                    op0=ALU.mult, op1=ALU.mult), reads=[po_t, t_rin, t_FZ], writes=[ob_t])
                                pt_h, pt_t = PT[I % 2]
                                op(PE, lambda il=il, ob_h=ob_h, pt_h=pt_h: nc.tensor.transpose(
                                    out=pt_h[:, il, :], in_=ob_h[:, il, :], identity=ident[:]),
                                   reads=[ob_t, t_ident], writes=[pt_t])
                                if il == 3:
                                    op(ACT, lambda pt_h=pt_h, I=I: nc.scalar.copy(
                                        out=bufA[:, h, I * 512:(I + 1) * 512],
                                        in_=pt_h[:, 0:4, :].rearrange("p a b -> p (a b)")),
                                       reads=[pt_t], writes=[t_bufA])

                    emit_ST(0)
                    for n in range(len(steps)):
                        emit_exp(n)
                        if n + 1 < len(steps):
                            emit_ST(n + 1)
                        emit_PV(n)
                kb.end_scratch_phase()
            if first:
                tap("oaT", bufA[:], [128, KC, S], BF16, [t_bufA])

            if upto == "F":
                break

            with ExitStack() as ms:
                sgs = [kb.scratch(ms, f"e_sg{i}", [128, 512], F32) for i in range(2)]
                for oc in range(8):
                    si, u, t_u = ws.next()
                    for tc in range(4):
                        py_h, py_t = proj_bank()
                        for kc in range(KC):
                            op(PE, lambda kc=kc, tc=tc, py_h=py_h: nc.tensor.matmul(
                                py_h[:, 0:512], lhsT=u[:, kc, 0:128], rhs=bufA[:, kc, tc * 512:(tc + 1) * 512],
                                start=(kc == 0), stop=(kc == KC - 1)), reads=[t_u, t_bufA], writes=[py_t], inc=(kc == KC - 1))
                        pg_h, pg_t = proj_bank()
                        for kc in range(KC):
                            op(PE, lambda kc=kc, tc=tc, pg_h=pg_h: nc.tensor.matmul(
                                pg_h[:, 0:512], lhsT=u[:, kc, 128:256], rhs=xnT[:, kc, tc * 512:(tc + 1) * 512],
                                start=(kc == 0), stop=(kc == KC - 1)), reads=[t_u, t_xnT], writes=[pg_t], inc=(kc == KC - 1))
                        sg_h, sg_t = sgs[tc % 2]
                        op(ACT, lambda pg_h=pg_h, sg_h=sg_h, oc=oc: nc.scalar.activation(
                            out=sg_h[:], in_=pg_h[:, 0:512], func=AF.Tanh, bias=bgh[:, oc:oc + 1], scale=0.5),
                           reads=[pg_t, t_bgh], writes=[sg_t])
                        op(DVE, lambda py_h=py_h, sg_h=sg_h: nc.vector.scalar_tensor_tensor(
                            out=sg_h[:], in0=sg_h[:], scalar=1.0, in1=py_h[:, 0:512], op0=ALU.add, op1=ALU.mult),
                           reads=[py_t, sg_t], writes=[sg_t])
                        op(DVE, lambda sg_h=sg_h, oc=oc, tc=tc: nc.vector.tensor_tensor(
                            out=bufY[:, oc, tc * 512:(tc + 1) * 512], in0=sg_h[:], in1=bufY[:, oc, tc * 512:(tc + 1) * 512],
                            op=ALU.add), reads=[sg_t, t_bufY], writes=[t_bufY])
                    ws.release(si)
                kb.end_scratch_phase()
            if first:
                tap("yT", bufY[:], [128, KC, S], BF16, [t_bufY])

            if upto == "FD":
                break

            dma(SP, out=gX[:], in_=final_g.partition_broadcast(128), writes=[t_gX])
            us = [ws.next() for _ in range(4)]
            for tb in range(NT):
                xt_h, xt_t = xts[tb % 2]
                dma(SP, out=xt_h[:], in_=x_d[seq, tb * 128:(tb + 1) * 128, :], writes=[xt_t])
                banks = [proj_bank(), proj_bank()]
                for hf in range(2):
                    po_h, po_t = banks[hf]
                    for n in range(2):
                        si, u, t_u = us[2 * hf + n]
                        for kc in range(KC):
                            op(PE, lambda kc=kc, n=n, u=u, po_h=po_h: nc.tensor.matmul(
                                po_h[:, n * 256:(n + 1) * 256], lhsT=bufY[:, kc, tb * 128:(tb + 1) * 128], rhs=u[:, kc, :],
                                start=(kc == 0), stop=(kc == KC - 1)), reads=[t_u, t_bufY], writes=[po_t],
                               inc=(kc == KC - 1 and n == 1))
                    op(DVE, lambda hf=hf, po_h=po_h: nc.vector.scalar_tensor_tensor(
                        out=xt_h[:, hf * 512:(hf + 1) * 512], in0=po_h[:, 0:512], scalar=0.5,
                        in1=xt_h[:, hf * 512:(hf + 1) * 512], op0=ALU.mult, op1=ALU.add),
                       reads=[po_t, xt_t], writes=[xt_t])
                xn_h, xn_t = xns[tb % 2]
                rmsnorm_tail(xt_h, xt_t, tb, gF, t_gF, xt_h[:], [xt_t], xn_h[:], [xn_t])
                dma(SP, out=y_d[seq, tb * 128:(tb + 1) * 128, :], in_=xt_h[:], reads=[xt_t])
            for (si, u, t_u) in us:
                ws.release(si)
            kb.scr_tiles.append(t_bufY)
            kb.end_scratch_phase()
            ys.close()

        if kb.mode == "defer":
            kb.flush()
            build.n_inst = kb.n_inst
            build.nsem = kb.nsem
            build.est_us = kb.est_makespan
        out_costs = [o.cost for o in kb.ops]
    return nc, dbg_out, out_costs


def host_inputs(inputs):
    f = lambda a: np.ascontiguousarray(np.asarray(a, dtype=np.float32))
    gb = np.concatenate([f(inputs["b_fox_f"])[0], f(inputs["b_ml_i"])[0], f(inputs["b_ml_f"])[0]])
    gbias = np.ascontiguousarray(np.broadcast_to(gb[None, None, :], (128, NT, 16)))
    convw = np.ascontiguousarray(f(inputs["conv_w"])[0].reshape(4, 8, 128).transpose(2, 1, 0))
    convb = np.ascontiguousarray(f(inputs["conv_b"])[0].reshape(8, 128).T)
    bgate = np.ascontiguousarray(f(inputs["b_gate"])[0].reshape(16, 128).T)
    return {
        "w_in": f(inputs["w_in"])[0],
        "w_fox_down": f(inputs["w_fox_down"])[0],
        "w_ml_down": f(inputs["w_ml_down"])[0],
        "w_out": f(inputs["w_out"])[0],
        "norm_g": f(inputs["norm_g"])[0],
        "final_g": f(inputs["final_g"]),
        "ml_norm_g": f(inputs["ml_norm_g"])[0],
        "gbias": gbias,
        "convw": convw,
        "convb": convb,
        "bgate": bgate,
    }


def kernel(**inputs):
    x = np.ascontiguousarray(np.asarray(inputs["x"], dtype=np.float32))
    B = x.shape[0]
    nseq = B // NCORES
    shared = host_inputs(inputs)
    nc, _ = build(nseq)
    in_maps = []
    for c in range(NCORES):
        m = dict(shared)
        m["x"] = x[c * nseq:(c + 1) * nseq]
        in_maps.append(m)
    res = run_bass_kernel_spmd(nc, in_maps, core_ids=list(range(NCORES)))
    out = np.concatenate([np.asarray(r["y"]) for r in res.results], axis=0)
    return out.astype(np.float32, copy=False)
```

```python
import types
import numpy as np
from contextlib import ExitStack
import concourse.bass as bass
import concourse.mybir as mybir
from concourse.bass_utils import run_bass_kernel_spmd

F32 = mybir.dt.float32
BF16 = mybir.dt.bfloat16
AF = mybir.ActivationFunctionType
ALU = mybir.AluOpType

S = 2048
D = 1024
NT = 16
KC = 8
NCORES = 8
EPS = 1e-6
C_FQ, C_FK, C_FV, C_FF, C_FZ = 0, 1024, 2048, 3072, 3080
C_MQ, C_MK, C_MV, C_MI, C_MF, C_MO, C_MZ = 4104, 4616, 5128, 6152, 6156, 6160, 7184
C_GA, C_GB = 8208, 9232
IN_W = 10256
SEM_LIMIT = 30000
SLACK_US = 0.0
PE_MAXG = 2
NPTB = 3
NXT = 4
NSLOT = 6


class SemObj:
    __slots__ = ("h", "owner", "val")

    def __init__(self, h, owner):
        self.h = h
        self.owner = owner
        self.val = 0


class Tile:
    __slots__ = ("w", "r", "name", "psum")

    def __init__(self, name, init=None, psum=False):
        self.name = name
        self.w = list(init) if init else []
        self.r = []
        self.psum = psum


class Eng:
    def __init__(self, kb, name, h, is_pe=False):
        self.kb = kb
        self.name = name
        self.h = h
        self.is_pe = is_pe
        self.waited = {}
        self.cur = None

    def sem(self):
        if self.cur is None or self.cur.val >= SEM_LIMIT:
            self.cur = self.kb.new_sem(self)
        return self.cur


class Op:
    __slots__ = ("id", "eng", "fns", "deps", "cost", "is_dma", "tok", "start", "ninstr")

    def __init__(self, oid, eng, is_dma):
        self.id = oid
        self.eng = eng
        self.fns = []
        self.deps = []
        self.cost = 0.0
        self.is_dma = is_dma
        self.tok = None
        self.start = 0.0


def _snap(fn):
    if fn.__closure__ is None:
        return fn
    cells = []
    for c in fn.__closure__:
        try:
            cells.append(types.CellType(c.cell_contents))
        except ValueError:
            cells.append(c)
    return types.FunctionType(fn.__code__, fn.__globals__, fn.__name__, fn.__defaults__, tuple(cells))


def _free_elems(acc):
    try:
        ap = acc.ap
        n = 1
        for (_, c) in ap[1:]:
            n *= c
        return max(int(n), 1)
    except Exception:
        return 64


class KB:
    def __init__(self, nc, es, costs=None):
        self.nc = nc
        self.es = es
        self.mode = "measure" if costs is None else "defer"
        self.costs_in = costs
        self.nsem = 0
        self.PE = Eng(self, "pe", nc.tensor, is_pe=True)
        self.ACT = Eng(self, "act", nc.scalar)
        self.DVE = Eng(self, "dve", nc.vector)
        self.POOL = Eng(self, "pool", nc.gpsimd)
        self.SP = Eng(self, "sp", nc.sync)
        self.engs = [self.PE, self.ACT, self.DVE, self.POOL, self.SP]
        self.dma_pool = {}
        self.dma_rr = {}
        self.scr_tokens = []
        self.scr_tiles = []
        self.scr_id = 0
        self.n_inst = 0
        self.ops = []
        self.open_pe = None

    def new_sem(self, owner):
        self.nsem += 1
        h = self.es.enter_context(self.nc.semaphore(f"s{self.nsem}"))
        return SemObj(h, owner)

    def _deps(self, o, reads, writes):
        eng = o.eng
        ops = self.ops
        d = o.deps
        for t in reads:
            d.extend(t.w)
            if t.psum:
                d.extend(r for r in t.r if ops[r].eng is not eng)
        for t in writes:
            d.extend(t.w)
            d.extend(t.r)

    def _touch(self, oid, reads, writes):
        for t in reads:
            if not t.r or t.r[-1] != oid:
                t.r.append(oid)
        for t in writes:
            t.w = [oid]
            t.r = []

    def _cost_of(self, eng, ins):
        try:
            i = ins.ins
            n = _free_elems(i.outs[0])
            if eng.is_pe:
                f32 = False
                try:
                    f32 = (i.ins[0].dtype == F32) and not getattr(i, "is_transpose", False)
                except Exception:
                    pass
                return (max(n, 64) * (4 if f32 else 1) + 64) / 2000.0
            if eng is self.ACT:
                return 0.22 + n * 0.00083
            if eng is self.DVE:
                return 0.10 + n * 0.00104
            return 0.25 + n * 0.002
        except Exception:
            return 0.3

    def op(self, eng, fn, reads=(), writes=(), inc=True):
        if eng.is_pe and self.open_pe is not None:
            o = self.open_pe
        else:
            assert self.open_pe is None, "non-PE op recorded while a PE group is open"
            o = Op(len(self.ops), eng, False)
            self.ops.append(o)
            if eng.is_pe and not inc:
                self.open_pe = o
        if eng.is_pe:
            o.ninstr = getattr(o, "ninstr", 0) + 1
            if inc or o.ninstr >= PE_MAXG:
                self.open_pe = None
        self._deps(o, reads, writes)
        self._touch(o.id, reads, writes)
        if self.mode == "measure":
            ins = fn()
            o.cost += self._cost_of(eng, ins)
            return ins
        o.fns.append(_snap(fn))
        return None

    def dma(self, q, out, in_, reads=(), writes=()):
        assert self.open_pe is None
        o = Op(len(self.ops), q, True)
        self.ops.append(o)
        self._deps(o, reads, writes)
        self._touch(o.id, reads, writes)
        if self.mode == "measure":
            q.h.dma_start(out=out, in_=in_)
            try:
                n = 1
                for (_, c) in out.ap:
                    n *= c
            except Exception:
                n = 131072
            o.cost = 2.0 + n * 4 / 150e3
            return None
        o.fns.append(lambda: q.h.dma_start(out=out, in_=in_))
        return None

    def sb(self, name, shape, dt, es=None):
        h = (es or self.es).enter_context(self.nc.sbuf_tensor(name, shape, dt))
        return h, Tile(name)

    def scratch(self, es, name, shape, dt):
        self.scr_id += 1
        h = es.enter_context(self.nc.sbuf_tensor(f"{name}_{self.scr_id}", shape, dt))
        t = Tile(name, self.scr_tokens)
        self.scr_tiles.append(t)
        return h, t

    def end_scratch_phase(self):
        acc = set(self.scr_tokens)
        for t in self.scr_tiles:
            acc.update(t.w)
            acc.update(t.r)
        self.scr_tokens = sorted(acc)
        self.scr_tiles = []

    def schedule(self):
        ops = self.ops
        n = len(ops)
        costs = self.costs_in
        assert len(costs) == n, (len(costs), n)
        succ = [[] for _ in range(n)]
        for o in ops:
            o.deps = sorted(set(d for d in o.deps if d != o.id))
            for d in o.deps:
                succ[d].append(o.id)
        pend = {e: [] for e in self.engs}
        for o in ops:
            pend[o.eng].append(o.id)
        done = [False] * n
        fin = [0.0] * n
        efree = {e: 0.0 for e in self.engs}
        best = {e: None for e in self.engs}
        dirty = set(self.engs)
        HOP, SAME, W = 0.30, 0.06, 400
        SLACK = SLACK_US
        order = []
        INF = float("inf")
        while len(order) < n:
            for e in dirty:
                lst = pend[e]
                be, bi, bpos = INF, -1, -1
                ef = efree[e]
                lim = min(W, len(lst))
                for pos in range(lim):
                    oid = lst[pos]
                    rt = ef
                    ok = True
                    for d in ops[oid].deps:
                        if not done[d]:
                            ok = False
                            break
                        t = fin[d] + (SAME if ops[d].eng is e else HOP)
                        if t > rt:
                            rt = t
                    if ok and rt < be - SLACK:
                        be, bi, bpos = rt, oid, pos
                        if rt <= ef:
                            break
                best[e] = (be, bi, bpos) if bi >= 0 else None
            dirty.clear()
            ce, cb = None, None
            for e in self.engs:
                b = best[e]
                if b is not None and (cb is None or b[0] < cb[0] or (b[0] == cb[0] and b[1] < cb[1])):
                    ce, cb = e, b
            assert ce is not None, "scheduler deadlock"
            st, oid, pos = cb
            o = ops[oid]
            o.start = st
            if o.is_dma:
                efree[ce] = st + 0.15
                fin[oid] = st + costs[oid]
            else:
                efree[ce] = st + costs[oid]
                fin[oid] = efree[ce]
            done[oid] = True
            del pend[ce][pos]
            order.append(o)
            dirty.add(ce)
            for sidx in succ[oid]:
                dirty.add(ops[sidx].eng)
        self.est_makespan = max(fin) if fin else 0.0
        return order

    def flush(self):
        order = self.schedule()
        ops = self.ops
        for o in order:
            eng = o.eng
            for d in o.deps:
                p = ops[d]
                if eng.is_pe and p.eng is eng and not p.is_dma:
                    continue
                s, v = p.tok
                if eng.waited.get(s, 0) >= v:
                    continue
                eng.h.wait_ge(s.h, v)
                eng.waited[s] = v
                self.n_inst += 1
            if o.is_dma:
                pool = self.dma_pool.setdefault(eng, [])
                if len(pool) < 8:
                    dsem = self.new_sem(None)
                    pool.append(dsem)
                else:
                    i = self.dma_rr.get(eng, 0)
                    dsem = pool[i % len(pool)]
                    self.dma_rr[eng] = i + 1
                if dsem.val and eng.waited.get(dsem, 0) < dsem.val:
                    eng.h.wait_ge(dsem.h, dsem.val)
                    eng.waited[dsem] = dsem.val
                    self.n_inst += 1
                ins = o.fns[0]()
                dsem.val += 16
                ins.then_inc(dsem.h, 16)
                o.tok = (dsem, dsem.val)
            else:
                ins = None
                for fn in o.fns:
                    ins = fn()
                    self.n_inst += 1
                s = eng.sem()
                s.val += 1
                ins.then_inc(s.h, 1)
                o.tok = (s, s.val)
            self.n_inst += 1
        SP = self.SP
        for q, pool in self.dma_pool.items():
            for dsem in pool:
                if dsem.val and SP.waited.get(dsem, 0) < dsem.val:
                    self.nc.sync.wait_ge(dsem.h, dsem.val)
                    SP.waited[dsem] = dsem.val


class WStream:
    def __init__(self, kb, slots, plan):
        self.kb = kb
        self.slots = slots
        self.plan = plan
        self.issued = 0
        self.taken = 0
        self.free = [True] * len(slots)

    def _pump(self):
        while self.issued < len(self.plan):
            si = self.issued % len(self.slots)
            if not self.free[si]:
                break
            h, t = self.slots[si]
            for (dst, src, c0, n) in self.plan[self.issued]:
                self.kb.dma(self.kb.POOL, out=h[:, :, dst:dst + n], in_=src[:, :, c0:c0 + n], writes=[t])
            self.free[si] = False
            self.issued += 1

    def next(self):
        self._pump()
        assert self.taken < self.issued, "weight stream stalled (slot not released)"
        si = self.taken % len(self.slots)
        self.taken += 1
        h, t = self.slots[si]
        return si, h, t

    def release(self, si):
        self.free[si] = True
        self._pump()


def unit_plan(nseq, wv, wmd, wfd, wout):
    plan = []
    for _ in range(nseq):
        for h in range(4):
            plan.append([(0, wv, C_MQ + h * 128, 128), (128, wv, C_MK + h * 128, 128)])
            plan.append([(0, wv, C_MV + h * 256, 256)])
            plan.append([(0, wv, C_MO + h * 256, 256)])
            plan.append([(0, wv, C_MZ + h * 256, 256)])
        for oc in range(8):
            plan.append([(0, wmd, oc * 128, 128), (128, wv, C_GB + oc * 128, 128)])
        for h in range(8):
            plan.append([(0, wv, C_FQ + h * 128, 128), (128, wv, C_FK + h * 128, 128)])
            plan.append([(0, wv, C_FV + h * 128, 128), (128, wv, C_FZ + h * 128, 128)])
        for oc in range(8):
            plan.append([(0, wfd, oc * 128, 128), (128, wv, C_GA + oc * 128, 128)])
        for u in range(4):
            plan.append([(0, wout, u * 256, 256)])
    return plan


def build(nseq=2, dbg=None, upto=None):
    _, _, costs = _build_pass(nseq, dbg, upto, None)
    nc, dbg_out, _ = _build_pass(nseq, dbg, upto, costs)
    return nc, dbg_out


def _build_pass(nseq, dbg, upto, costs):
    dbg = dbg or set()
    nc = bass.Bass("TRN2", target_bir_lowering=False)
    x_d = nc.dram_tensor("x", [nseq, S, D], F32, kind="ExternalInput").ap()
    w_in = nc.dram_tensor("w_in", [D, IN_W], F32, kind="ExternalInput").ap()
    w_fd = nc.dram_tensor("w_fox_down", [D, D], F32, kind="ExternalInput").ap()
    w_md = nc.dram_tensor("w_ml_down", [D, D], F32, kind="ExternalInput").ap()
    w_o = nc.dram_tensor("w_out", [D, D], F32, kind="ExternalInput").ap()
    norm_g = nc.dram_tensor("norm_g", [D], F32, kind="ExternalInput").ap()
    final_g = nc.dram_tensor("final_g", [D], F32, kind="ExternalInput").ap()
    ml_g = nc.dram_tensor("ml_norm_g", [D], F32, kind="ExternalInput").ap()
    gbias_d = nc.dram_tensor("gbias", [128, NT, 16], F32, kind="ExternalInput").ap()
    convw_d = nc.dram_tensor("convw", [128, 8, 4], F32, kind="ExternalInput").ap()
    convb_d = nc.dram_tensor("convb", [128, 8], F32, kind="ExternalInput").ap()
    bgate_d = nc.dram_tensor("bgate", [128, 16], F32, kind="ExternalInput").ap()
    y_d = nc.dram_tensor("y", [nseq, S, D], F32, kind="ExternalOutput").ap()
    dbg_out = {}

    def dbg_tensor(name, shape, dt):
        dbg_out[name] = nc.dram_tensor("dbg_" + name, shape, dt, kind="ExternalOutput").ap()
        return dbg_out[name]

    wv = w_in.rearrange("(kc p) c -> p kc c", p=128)
    wfd = w_fd.rearrange("(kc p) c -> p kc c", p=128)
    wmd = w_md.rearrange("(kc p) c -> p kc c", p=128)
    wout = w_o.rearrange("(kc p) c -> p kc c", p=128)

    with ExitStack() as es:
        kb = KB(nc, es, costs)
        PE, ACT, DVE, POOL, SP = kb.PE, kb.ACT, kb.DVE, kb.POOL, kb.SP
        op, dma = kb.op, kb.dma

        xnT, t_xnT = kb.sb("xnT", [128, KC, S], BF16)
        bufA, t_bufA = kb.sb("bufA", [128, KC, S], BF16)
        slots = [kb.sb(f"wslot{i}", [128, KC, 256], BF16) for i in range(NSLOT)]
        wg, t_wg = kb.sb("wg", [128, KC, 16], BF16)
        gX, t_gX = kb.sb("gX", [128, D], F32)
        gN = gF = gM = gX
        t_gN = t_gF = t_gM = t_gX
        gbias, t_gbias = kb.sb("gbias_sb", [128, NT, 16], F32)
        cw, t_cw = kb.sb("cw", [128, 8, 4], F32)
        cb, t_cb = kb.sb("cb", [128, 8], F32)
        bg, t_bg = kb.sb("bg", [128, 16], F32)
        cbh, t_cbh = kb.sb("cbh", [128, 8], F32)
        bgh, t_bgh = kb.sb("bgh", [128, 16], F32)
        epsT, t_eps = kb.sb("epsT", [128, 1], F32)
        ident, t_ident = kb.sb("ident", [128, 128], BF16)
        onesb, t_onesb = kb.sb("onesb", [128, 128], BF16)
        maskb, t_maskb = kb.sb("maskb", [128, 128], BF16)
        maskf, t_maskf = kb.sb("maskf", [128, 128], F32)
        onesf, t_onesf = kb.sb("onesf", [128, 128], F32)
        hmid, t_hmid = kb.sb("hmid", [128, 128], F32)
        xts = [kb.sb(f"xt{i}", [128, D], F32) for i in range(NXT)]
        xns = [kb.sb(f"xn{i}", [128, D], BF16) for i in range(NXT)]
        stats = [kb.sb(f"stat{i}", [128, 4], F32) for i in range(NXT)]
        zs, t_zs = kb.sb("zs", [128, NT, 16], F32)
        lf, t_lf = kb.sb("lf", [128, NT, 16], F32)
        tot, t_tot = kb.sb("tot", [128, NT, 16], F32)
        bc, t_bc = kb.sb("bc", [128, NT, 16], F32)
        half, t_half = kb.sb("half", [128, NT, 16], F32)
        pre, t_pre = kb.sb("pre", [128, NT, 16], F32)
        cc, t_cc = kb.sb("cc", [128, NT, 16], F32)
        RR, t_RR = kb.sb("RR", [128, NT, 16], F32)
        ew, t_ew = kb.sb("ew", [128, NT, 4], F32)
        eb, t_eb = kb.sb("eb", [128, NT, 4], F32)
        aa, t_aa = kb.sb("aa", [128, NT, 4], F32)
        gtmp, t_gtmp = kb.sb("gtmp", [128, NT, 16], F32)

        PB = []
        for i in range(6):
            h = es.enter_context(nc.psum_tensor(f"pb{i}", [128, 512], F32))
            PB.append((h, Tile(f"pb{i}", psum=True)))
        PT = []
        for i in range(2):
            h = es.enter_context(nc.psum_tensor(f"pt{i}", [128, 8, 128], BF16))
            PT.append((h, Tile(f"pt{i}", psum=True)))

        dma(SP, out=gbias[:], in_=gbias_d[:, :, :], writes=[t_gbias])
        dma(SP, out=cw[:], in_=convw_d[:, :, :], writes=[t_cw])
        dma(SP, out=cb[:], in_=convb_d[:, :], writes=[t_cb])
        dma(SP, out=bg[:], in_=bgate_d[:, :], writes=[t_bg])
        dma(POOL, out=wg[:, :, 0:8], in_=wv[:, :, C_FF:C_FF + 8], writes=[t_wg])
        dma(POOL, out=wg[:, :, 8:16], in_=wv[:, :, C_MI:C_MI + 8], writes=[t_wg])
        op(DVE, lambda: nc.vector.memset(epsT[:], EPS), writes=[t_eps])
        op(DVE, lambda: nc.vector.tensor_scalar(out=cbh[:], in0=cb[:], scalar1=0.5, scalar2=None, op0=ALU.mult),
           reads=[t_cb], writes=[t_cbh])
        op(DVE, lambda: nc.vector.tensor_scalar(out=bgh[:], in0=bg[:], scalar1=0.5, scalar2=None, op0=ALU.mult),
           reads=[t_bg], writes=[t_bgh])
        op(POOL, lambda: nc.gpsimd.memset(onesb[:], 1.0), writes=[t_onesb])
        op(POOL, lambda: nc.gpsimd.memset(onesf[:], 1.0), writes=[t_onesf])
        op(POOL, lambda: nc.gpsimd.memset(hmid[:], 0.0), writes=[t_hmid])
        op(POOL, lambda: nc.gpsimd.memset(hmid[0:64, :], 1.0), writes=[t_hmid])
        op(POOL, lambda: nc.gpsimd.affine_select(out=ident[:], in_=onesb[:], pattern=[[1, 128]],
                                                 compare_op=ALU.is_equal, fill=0.0, base=0,
                                                 channel_multiplier=-1), reads=[t_onesb], writes=[t_ident])
        op(POOL, lambda: nc.gpsimd.affine_select(out=maskb[:], in_=onesb[:], pattern=[[1, 128]],
                                                 compare_op=ALU.is_ge, fill=0.0, base=0,
                                                 channel_multiplier=-1), reads=[t_onesb], writes=[t_maskb])
        op(POOL, lambda: nc.gpsimd.affine_select(out=maskf[:], in_=onesf[:], pattern=[[1, 128]],
                                                 compare_op=ALU.is_ge, fill=0.0, base=0,
                                                 channel_multiplier=-1), reads=[t_onesf], writes=[t_maskf])

        ws = WStream(kb, slots, unit_plan(nseq, wv, wmd, wfd, wout))
        proj_rr = [0]
        proj_n = [2]

        def proj_bank():
            proj_rr[0] += 1
            return PB[proj_rr[0] % proj_n[0]]

        def tap(name, src_ap, shape, dt, tiles):
            if name in dbg:
                d = dbg_tensor(name, shape, dt)
                dma(SP, out=d, in_=src_ap, reads=tiles)

        def rmsnorm_tail(xt_h, xt_t, si, g_h, g_t, out_ap, out_tiles, sq_out_ap, sq_out_tiles):
            st_h, st_t = stats[si % NXT]
            op(ACT, lambda: nc.scalar.activation(out=sq_out_ap, in_=xt_h[:], func=AF.Square,
                                                 accum_out=st_h[:, 0:1]),
               reads=[xt_t], writes=sq_out_tiles + [st_t])
            op(ACT, lambda: nc.scalar.activation(out=st_h[:, 1:2], in_=st_h[:, 0:1], func=AF.Sqrt,
                                                 scale=1.0 / D, bias=epsT[:]),
               reads=[st_t, t_eps], writes=[st_t])
            op(DVE, lambda: nc.vector.reciprocal(out=st_h[:, 2:3], in_=st_h[:, 1:2]), reads=[st_t], writes=[st_t])
            op(DVE, lambda: nc.vector.scalar_tensor_tensor(out=out_ap, in0=xt_h[:], scalar=st_h[:, 2:3],
                                                           in1=g_h[:], op0=ALU.mult, op1=ALU.mult),
               reads=[xt_t, st_t, g_t], writes=out_tiles)

        for seq in range(nseq):
            first = (seq == 0)
            proj_n[0] = 2
            dma(SP, out=gX[:], in_=norm_g.partition_broadcast(128), writes=[t_gX])
            for tb in range(NT):
                xt_h, xt_t = xts[tb % NXT]
                xn_h, xn_t = xns[tb % NXT]
                dma(SP, out=xt_h[:], in_=x_d[seq, tb * 128:(tb + 1) * 128, :], writes=[xt_t])
                rmsnorm_tail(xt_h, xt_t, tb, gN, t_gN, xn_h[:], [xn_t], xn_h[:], [xn_t])
                pt_h, pt_t = PT[tb % 2]
                for kc in range(KC):
                    op(PE, lambda kc=kc: nc.tensor.transpose(out=pt_h[:, kc, :], in_=xn_h[:, kc * 128:(kc + 1) * 128],
                                                             identity=ident[:]),
                       reads=[xn_t, t_ident], writes=[pt_t], inc=(kc == KC - 1))
                op(ACT, lambda: nc.scalar.copy(out=xnT[:, :, tb * 128:(tb + 1) * 128], in_=pt_h[:]),
                   reads=[pt_t], writes=[t_xnT])
            if first:
                tap("xnT", xnT[:], [128, KC, S], BF16, [t_xnT])

            if upto == "P0":
                break

            pz_h, pz_t = PB[2]
            for tb in range(NT):
                for kc in range(KC):
                    op(PE, lambda kc=kc, tb=tb: nc.tensor.matmul(pz_h[:, tb * 16:(tb + 1) * 16],
                                                                 lhsT=xnT[:, kc, tb * 128:(tb + 1) * 128],
                                                                 rhs=wg[:, kc, :], start=(kc == 0), stop=(kc == KC - 1)),
                       reads=[t_xnT, t_wg], writes=[pz_t], inc=(kc == KC - 1 and tb == NT - 1))
            op(DVE, lambda: nc.vector.tensor_tensor(out=zs[:].rearrange("p a b -> p (a b)"), in0=pz_h[:, 0:256],
                                                    in1=gbias[:].rearrange("p a b -> p (a b)"), op=ALU.add),
               reads=[pz_t, t_gbias], writes=[t_zs])
            if upto == "G1":
                tap("lf", zs[:], [128, NT, 16], F32, [t_zs])
                break
            op(ACT, lambda: nc.scalar.activation(out=gtmp[:], in_=zs[:], func=AF.Exp, scale=-1.0),
               reads=[t_zs], writes=[t_gtmp])
            op(ACT, lambda: nc.scalar.activation(out=gtmp[:], in_=gtmp[:], func=AF.Ln, bias=1.0, scale=1.0),
               reads=[t_gtmp], writes=[t_gtmp])
            op(DVE, lambda: nc.vector.tensor_scalar(out=lf[:], in0=gtmp[:], scalar1=-1.0, scalar2=None, op0=ALU.mult),
               reads=[t_gtmp], writes=[t_lf])
            if upto == "G2":
                tap("lf", lf[:], [128, NT, 16], F32, [t_lf])
                break
            lf2 = lf[:].rearrange("p a b -> p (a b)")
            for (lhs_h, lhs_t, dst_h, dst_t, pbi) in ((onesf, t_onesf, tot, t_tot, 3), (maskf, t_maskf, bc, t_bc, 4),
                                                       (hmid, t_hmid, half, t_half, 5)):
                pp_h, pp_t = PB[pbi]
                op(PE, lambda lhs_h=lhs_h, pp_h=pp_h: nc.tensor.matmul(pp_h[:, 0:256], lhsT=lhs_h[:], rhs=lf2,
                                                                       start=True, stop=True),
                   reads=[lhs_t, t_lf], writes=[pp_t])
                op(DVE, lambda dst_h=dst_h, pp_h=pp_h: nc.vector.tensor_copy(out=dst_h[:].rearrange("p a b -> p (a b)"),
                                                                             in_=pp_h[:, 0:256]),
                   reads=[pp_t], writes=[dst_t])
            if upto == "G3":
                tap("lf", lf[:], [128, NT, 16], F32, [t_lf])
                tap("cc", bc[:], [128, NT, 16], F32, [t_bc])
                break
            op(DVE, lambda: nc.vector.memset(pre[:, 0, :], 0.0), writes=[t_pre])
            for tb in range(1, NT):
                op(DVE, lambda tb=tb: nc.vector.tensor_tensor(out=pre[:, tb, :], in0=pre[:, tb - 1, :],
                                                              in1=tot[:, tb - 1, :], op=ALU.add),
                   reads=[t_pre, t_tot], writes=[t_pre])
            op(DVE, lambda: nc.vector.tensor_tensor(out=cc[:], in0=bc[:], in1=pre[:], op=ALU.add),
               reads=[t_bc, t_pre], writes=[t_cc])
            op(DVE, lambda: nc.vector.tensor_tensor(out=RR[:], in0=half[:], in1=pre[:], op=ALU.add),
               reads=[t_half, t_pre], writes=[t_RR])
            if upto == "G4":
                tap("cc", cc[:], [128, NT, 16], F32, [t_cc])
                break
            op(DVE, lambda: nc.vector.tensor_tensor(out=ew[:], in0=zs[:, :, 8:12], in1=bc[:, :, 12:16], op=ALU.subtract),
               reads=[t_zs, t_bc], writes=[t_ew])
            if upto == "G5":
                tap("cc", cc[:], [128, NT, 16], F32, [t_cc, t_ew])
                break
            op(ACT, lambda: nc.scalar.activation(out=ew[:], in_=ew[:], func=AF.Exp), reads=[t_ew], writes=[t_ew])
            if upto == "G6":
                tap("cc", cc[:], [128, NT, 16], F32, [t_cc, t_ew])
                break
            op(ACT, lambda: nc.scalar.activation(out=eb[:], in_=bc[:, :, 12:16], func=AF.Exp), reads=[t_bc], writes=[t_eb])
            op(DVE, lambda: nc.vector.tensor_scalar(out=eb[:], in0=eb[:], scalar1=float(128 ** -0.5), scalar2=None,
                                                    op0=ALU.mult), reads=[t_eb], writes=[t_eb])
            if upto == "G7":
                tap("cc", cc[:], [128, NT, 16], F32, [t_cc, t_ew, t_eb])
                break
            op(ACT, lambda: nc.scalar.activation(out=aa[:], in_=tot[:, :, 12:16], func=AF.Exp), reads=[t_tot], writes=[t_aa])
            if first:
                tap("lf", lf[:], [128, NT, 16], F32, [t_lf])
                tap("cc", cc[:], [128, NT, 16], F32, [t_cc])
                tap("RR", RR[:], [128, NT, 16], F32, [t_RR])
                tap("ew", ew[:], [128, NT, 4], F32, [t_ew])
                tap("eb", eb[:], [128, NT, 4], F32, [t_eb])
                tap("aa", aa[:], [128, NT, 4], F32, [t_aa])

            if upto == "G":
                break

            dma(SP, out=gX[:], in_=ml_g.partition_broadcast(128), writes=[t_gX])
            with ExitStack() as ms:
                qT, t_qT = kb.scratch(ms, "m_qT", [128, S], BF16)
                kT, t_kT = kb.scratch(ms, "m_kT", [128, S], BF16)
                cbuf, t_cbuf = kb.scratch(ms, "m_cbuf", [128, S + 4], F32)
                accs = [kb.scratch(ms, f"m_acc{i}", [128, 1024], F32) for i in range(2)]
                ths = [kb.scratch(ms, f"m_th{i}", [128, 1024], F32) for i in range(2)]
                kTok, t_kTok = kb.scratch(ms, "m_kTok", [128, NT, 128], BF16)
                vaug, t_vaug = kb.scratch(ms, "m_vaug", [128, NT, 258], BF16)
                Crun = [kb.scratch(ms, f"m_C{i}", [128, 258], F32) for i in range(2)]
                Cbf, t_Cbf = kb.scratch(ms, "m_Cbf", [128, NT, 258], BF16)
                numS, t_numS = kb.scratch(ms, "m_numS", [128, 4, 258], F32)
                Gp, t_Gp = kb.scratch(ms, "m_Gp", [128, 4, 256], BF16)
                STm = [kb.scratch(ms, f"m_STm{i}", [128, 128], BF16) for i in range(2)]
                hn = [kb.scratch(ms, f"m_hn{i}", [128, 256], BF16) for i in range(2)]
                s12 = [kb.scratch(ms, f"m_s12{i}", [128, 512], F32) for i in range(2)]
                sm, t_sm = kb.scratch(ms, "m_sm", [128, 12, 4], F32)
                junk, t_junk = kb.scratch(ms, "m_junk", [128, 256], BF16)
                op(DVE, lambda: nc.vector.memset(cbuf[:, 0:4], 0.0), writes=[t_cbuf])
                CB0 = 1
                for h in range(4):
                    s1i, u1, t_u1 = ws.next()
                    s2i, u2, t_u2 = ws.next()
                    s3i, u3, t_u3 = ws.next()
                    s4i, u4, t_u4 = ws.next()
                    for which, (dst, t_dst) in enumerate(((qT, t_qT), (kT, t_kT))):
                        chunk = which * 4 + h
                        for tc in range(4):
                            pp_h, pp_t = proj_bank()
                            for kc in range(KC):
                                op(PE, lambda kc=kc, tc=tc, which=which, pp_h=pp_h: nc.tensor.matmul(
                                    pp_h[:, 0:512], lhsT=u1[:, kc, which * 128:(which + 1) * 128],
                                    rhs=xnT[:, kc, tc * 512:(tc + 1) * 512], start=(kc == 0), stop=(kc == KC - 1)),
                                   reads=[t_u1, t_xnT], writes=[pp_t], inc=(kc == KC - 1))
                            op(ACT, lambda tc=tc, pp_h=pp_h: nc.scalar.copy(out=cbuf[:, 4 + tc * 512:4 + (tc + 1) * 512],
                                                                           in_=pp_h[:, 0:512]),
                               reads=[pp_t], writes=[t_cbuf])
                        for hf in range(2):
                            o = hf * 1024
                            ac_h, ac_t = accs[hf]
                            op(DVE, lambda o=o, ac_h=ac_h, chunk=chunk: nc.vector.tensor_scalar(
                                out=ac_h[:], in0=cbuf[:, 4 + o:4 + o + 1024], scalar1=cw[:, chunk, 3:4], scalar2=None,
                                op0=ALU.mult), reads=[t_cbuf, t_cw], writes=[ac_t])
                            for j in (2, 1, 0):
                                op(DVE, lambda o=o, ac_h=ac_h, chunk=chunk, j=j: nc.vector.scalar_tensor_tensor(
                                    out=ac_h[:], in0=cbuf[:, 1 + j + o:1 + j + o + 1024], scalar=cw[:, chunk, j:j + 1],
                                    in1=ac_h[:], op0=ALU.mult, op1=ALU.add), reads=[t_cbuf, t_cw, ac_t], writes=[ac_t])
                            th_h, th_t = ths[hf]
                            op(ACT, lambda ac_h=ac_h, chunk=chunk, th_h=th_h: nc.scalar.activation(
                                out=th_h[:], in_=ac_h[:], func=AF.Tanh, bias=cbh[:, chunk:chunk + 1], scale=0.5),
                               reads=[ac_t, t_cbh], writes=[th_t])
                            op(DVE, lambda th_h=th_h: nc.vector.tensor_scalar(
                                out=th_h[:], in0=th_h[:], scalar1=0.5, scalar2=0.5, op0=ALU.mult, op1=ALU.add),
                               reads=[th_t], writes=[th_t])
                            op(DVE, lambda o=o, ac_h=ac_h, chunk=chunk, dst=dst, th_h=th_h: nc.vector.scalar_tensor_tensor(
                                out=dst[:, o:o + 1024], in0=ac_h[:], scalar=cb[:, chunk:chunk + 1], in1=th_h[:],
                                op0=ALU.add, op1=ALU.mult), reads=[ac_t, t_cb, th_t], writes=[t_dst])
                    ws.release(s1i)
                    for g in range(2):
                        pt_h, pt_t = PT[g]
                        for c8 in range(8):
                            c = g * 8 + c8
                            op(PE, lambda c=c, c8=c8, pt_h=pt_h: nc.tensor.transpose(
                                out=pt_h[:, c8, :], in_=kT[:, c * 128:(c + 1) * 128], identity=ident[:]),
                               reads=[t_kT, t_ident], writes=[pt_t], inc=(c8 == 7))
                        for c8 in range(8):
                            c = g * 8 + c8
                            op(ACT, lambda c=c, c8=c8, pt_h=pt_h: nc.scalar.activation(
                                out=kTok[:, c, :], in_=pt_h[:, c8, :], func=AF.Copy, scale=aa[:, c, h:h + 1]),
                               reads=[pt_t, t_aa], writes=[t_kTok])
                    for tb in range(NT):
                        pp_h, pp_t = proj_bank()
                        for kc in range(KC):
                            op(PE, lambda kc=kc, tb=tb, pp_h=pp_h: nc.tensor.matmul(
                                pp_h[:, 0:256], lhsT=xnT[:, kc, tb * 128:(tb + 1) * 128], rhs=u2[:, kc, :],
                                start=(kc == 0), stop=(kc == KC - 1)),
                               reads=[t_u2, t_xnT], writes=[pp_t], inc=(kc == KC - 1))
                        op(ACT, lambda tb=tb, pp_h=pp_h: nc.scalar.activation(
                            out=vaug[:, tb, 0:256], in_=pp_h[:, 0:256], func=AF.Copy, scale=ew[:, tb, h:h + 1]),
                           reads=[pp_t, t_ew], writes=[t_vaug])
                    ws.release(s2i)
                    op(DVE, lambda: nc.vector.tensor_copy(out=vaug[:, :, 256:257], in_=ew[:, :, h:h + 1]),
                       reads=[t_ew], writes=[t_vaug])
                    dcb = (PB[5], PB[2])
                    for c in range(NT - 1):
                        pd_h, pd_t = dcb[c % 2]
                        op(PE, lambda c=c, pd_h=pd_h: nc.tensor.matmul(pd_h[:, 0:257], lhsT=kTok[:, c, :],
                                                                       rhs=vaug[:, c, 0:257], start=True, stop=True),
                           reads=[t_kTok, t_vaug], writes=[pd_t])
                        cn_h, cn_t = Crun[(c + 1) % 2]
                        cp_h, cp_t = Crun[c % 2]
                        if c == 0:
                            op(DVE, lambda pd_h=pd_h, cn_h=cn_h: nc.vector.tensor_copy(out=cn_h[:, 0:257], in_=pd_h[:, 0:257]),
                               reads=[pd_t], writes=[cn_t])
                        else:
                            op(DVE, lambda c=c, pd_h=pd_h, cn_h=cn_h, cp_h=cp_h: nc.vector.scalar_tensor_tensor(
                                out=cn_h[:, 0:257], in0=cp_h[:, 0:257], scalar=aa[:, c, h:h + 1], in1=pd_h[:, 0:257],
                                op0=ALU.mult, op1=ALU.add), reads=[cp_t, pd_t, t_aa], writes=[cn_t])
                        op(ACT, lambda c=c, cn_h=cn_h: nc.scalar.copy(out=Cbf[:, c + 1, 0:257], in_=cn_h[:, 0:257]),
                           reads=[cn_t], writes=[t_Cbf])
                    for c in range(NT):
                        ps_h, ps_t = PB[2]
                        op(PE, lambda c=c: nc.tensor.matmul(ps_h[:, 0:128], lhsT=kT[:, c * 128:(c + 1) * 128],
                                                            rhs=qT[:, c * 128:(c + 1) * 128], start=True, stop=True),
                           reads=[t_kT, t_qT], writes=[ps_t])
                        st_h, st_t = STm[c % 2]
                        op(DVE, lambda st_h=st_h: nc.vector.tensor_tensor(out=st_h[:], in0=ps_h[:, 0:128], in1=maskf[:],
                                                                          op=ALU.mult),
                           reads=[ps_t, t_maskf], writes=[st_t])
                        pn_h, pn_t = PB[3 + c % 2]
                        op(PE, lambda c=c, pn_h=pn_h, st_h=st_h: nc.tensor.matmul(
                            pn_h[:, 0:257], lhsT=st_h[:], rhs=vaug[:, c, 0:257], start=True, stop=(c == 0)),
                           reads=[st_t, t_vaug], writes=[pn_t], inc=(c == 0))
                        if c > 0:
                            op(PE, lambda c=c, pn_h=pn_h: nc.tensor.matmul(
                                pn_h[:, 0:257], lhsT=qT[:, c * 128:(c + 1) * 128], rhs=Cbf[:, c, 0:257],
                                start=False, stop=True), reads=[t_qT, t_Cbf], writes=[pn_t])
                        op(ACT, lambda c=c, pn_h=pn_h: nc.scalar.copy(out=numS[:, c % 4, 0:257], in_=pn_h[:, 0:257]),
                           reads=[pn_t], writes=[t_numS])
                        pp_h, pp_t = proj_bank()
                        for gi, (ug, t_ug) in enumerate(((u3, t_u3), (u4, t_u4))):
                            for kc in range(KC):
                                op(PE, lambda kc=kc, c=c, gi=gi, ug=ug, pp_h=pp_h: nc.tensor.matmul(
                                    pp_h[:, gi * 256:(gi + 1) * 256], lhsT=xnT[:, kc, c * 128:(c + 1) * 128],
                                    rhs=ug[:, kc, :], start=(kc == 0), stop=(kc == KC - 1)),
                                   reads=[t_ug, t_xnT], writes=[pp_t], inc=(kc == KC - 1 and gi == 1))
                        sg_h, sg_t = s12[c % 2]
                        op(ACT, lambda pp_h=pp_h, sg_h=sg_h: nc.scalar.activation(out=sg_h[:, 0:512], in_=pp_h[:, 0:512],
                                                                                  func=AF.Tanh, scale=0.5),
                           reads=[pp_t], writes=[sg_t])
                        op(DVE, lambda pp_h=pp_h, sg_h=sg_h: nc.vector.scalar_tensor_tensor(
                            out=sg_h[:, 256:512], in0=sg_h[:, 256:512], scalar=1.0, in1=pp_h[:, 256:512],
                            op0=ALU.add, op1=ALU.mult), reads=[sg_t, pp_t], writes=[sg_t])
                        op(DVE, lambda sg_h=sg_h: nc.vector.scalar_tensor_tensor(
                            out=sg_h[:, 0:256], in0=sg_h[:, 0:256], scalar=1.0, in1=sg_h[:, 256:512],
                            op0=ALU.add, op1=ALU.mult), reads=[sg_t], writes=[sg_t])
                        op(DVE, lambda c=c, sg_h=sg_h: nc.vector.tensor_tensor(out=Gp[:, c % 4, :], in0=sg_h[:, 0:256],
                                                                               in1=gM[:, h * 256:(h + 1) * 256], op=ALU.mult),
                           reads=[sg_t, t_gM], writes=[t_Gp])
                        if c % 4 == 3:
                            cs = c - 3
                            den = numS[:, :, 256]
                            ebs = eb[:, cs:cs + 4, h]
                            op(DVE, lambda: nc.vector.tensor_tensor(out=sm[:, 0, :], in0=den, in1=ebs, op=ALU.mult),
                               reads=[t_numS, t_eb], writes=[t_sm])
                            op(DVE, lambda: nc.vector.scalar_tensor_tensor(out=sm[:, 1, :], in0=sm[:, 0, :], scalar=-1.0,
                                                                           in1=sm[:, 0, :], op0=ALU.mult, op1=ALU.max),
                               reads=[t_sm], writes=[t_sm])
                            op(DVE, lambda: nc.vector.tensor_scalar(out=sm[:, 2, :], in0=sm[:, 1, :], scalar1=1.0,
                                                                    scalar2=None, op0=ALU.max), reads=[t_sm], writes=[t_sm])
                            op(DVE, lambda: nc.vector.reciprocal(out=sm[:, 3, :], in_=sm[:, 2, :]), reads=[t_sm], writes=[t_sm])
                            op(DVE, lambda: nc.vector.tensor_tensor(out=sm[:, 4, :], in0=sm[:, 3, :], in1=ebs, op=ALU.mult),
                               reads=[t_sm, t_eb], writes=[t_sm])
                            for c8 in range(4):
                                op(ACT, lambda c8=c8: nc.scalar.activation(out=junk[:], in_=numS[:, c8, 0:256],
                                                                           func=AF.Square, accum_out=sm[:, 5, c8:c8 + 1]),
                                   reads=[t_numS], writes=[t_junk, t_sm])
                            op(DVE, lambda: nc.vector.tensor_tensor(out=sm[:, 6, :], in0=sm[:, 4, :], in1=sm[:, 4, :],
                                                                    op=ALU.mult), reads=[t_sm], writes=[t_sm])
                            op(DVE, lambda: nc.vector.tensor_tensor(out=sm[:, 7, :], in0=sm[:, 6, :], in1=sm[:, 5, :],
                                                                    op=ALU.mult), reads=[t_sm], writes=[t_sm])
                            op(ACT, lambda: nc.scalar.activation(out=sm[:, 8, :], in_=sm[:, 7, :], func=AF.Sqrt,
                                                                 scale=1.0 / 256, bias=epsT[:]),
                               reads=[t_sm, t_eps], writes=[t_sm])
                            op(DVE, lambda: nc.vector.reciprocal(out=sm[:, 9, :], in_=sm[:, 8, :]), reads=[t_sm], writes=[t_sm])
                            op(DVE, lambda: nc.vector.scalar_tensor_tensor(out=sm[:, 10, :], in0=sm[:, 9, :], scalar=0.25,
                                                                           in1=sm[:, 4, :], op0=ALU.mult, op1=ALU.mult),
                               reads=[t_sm], writes=[t_sm])
                            for c8 in range(4):
                                hn_h, hn_t = hn[c8 % 2]
                                op(DVE, lambda c8=c8, hn_h=hn_h: nc.vector.scalar_tensor_tensor(
                                    out=hn_h[:], in0=numS[:, c8, 0:256], scalar=sm[:, 10, c8:c8 + 1], in1=Gp[:, c8, :],
                                    op0=ALU.mult, op1=ALU.mult), reads=[t_numS, t_sm, t_Gp], writes=[hn_t])
                                for j in range(2):
                                    op(PE, lambda c8=c8, j=j, hn_h=hn_h: nc.tensor.transpose(
                                        out=PT[j][0][:, c8, :], in_=hn_h[:, j * 128:(j + 1) * 128], identity=ident[:]),
                                       reads=[hn_t, t_ident], writes=[PT[j][1]])
                            for j in range(2):
                                op(ACT, lambda j=j, cs=cs: nc.scalar.copy(
                                    out=bufA[:, 2 * h + j, cs * 128:(cs + 4) * 128],
                                    in_=PT[j][0][:, 0:4, :].rearrange("p a b -> p (a b)")),
                                   reads=[PT[j][1]], writes=[t_bufA])
                    ws.release(s3i)
                    ws.release(s4i)
                kb.end_scratch_phase()
            if first:
                tap("hbT", bufA[:], [128, KC, S], BF16, [t_bufA])

            if upto == "M":
                break

            ys = es.enter_context(ExitStack())
            bufY, t_bufY = kb.scratch(ys, "bufY", [128, KC, S], BF16)
            proj_n[0] = 6
            with ExitStack() as ms:
                sgs = [kb.scratch(ms, f"d_sg{i}", [128, 512], F32) for i in range(2)]
                for oc in range(8):
                    si, u, t_u = ws.next()
                    for tc in range(4):
                        py_h, py_t = proj_bank()
                        for kc in range(KC):
                            op(PE, lambda kc=kc, tc=tc, py_h=py_h: nc.tensor.matmul(
                                py_h[:, 0:512], lhsT=u[:, kc, 0:128], rhs=bufA[:, kc, tc * 512:(tc + 1) * 512],
                                start=(kc == 0), stop=(kc == KC - 1)), reads=[t_u, t_bufA], writes=[py_t], inc=(kc == KC - 1))
                        pg_h, pg_t = proj_bank()
                        for kc in range(KC):
                            op(PE, lambda kc=kc, tc=tc, pg_h=pg_h: nc.tensor.matmul(
                                pg_h[:, 0:512], lhsT=u[:, kc, 128:256], rhs=xnT[:, kc, tc * 512:(tc + 1) * 512],
                                start=(kc == 0), stop=(kc == KC - 1)), reads=[t_u, t_xnT], writes=[pg_t], inc=(kc == KC - 1))
                        sg_h, sg_t = sgs[tc % 2]
                        op(ACT, lambda pg_h=pg_h, sg_h=sg_h, oc=oc: nc.scalar.activation(
                            out=sg_h[:], in_=pg_h[:, 0:512], func=AF.Tanh, bias=bgh[:, 8 + oc:9 + oc], scale=0.5),
                           reads=[pg_t, t_bgh], writes=[sg_t])
                        op(DVE, lambda py_h=py_h, sg_h=sg_h, oc=oc, tc=tc: nc.vector.scalar_tensor_tensor(
                            out=bufY[:, oc, tc * 512:(tc + 1) * 512], in0=sg_h[:], scalar=1.0, in1=py_h[:, 0:512],
                            op0=ALU.add, op1=ALU.mult), reads=[py_t, sg_t], writes=[t_bufY])
                    ws.release(si)
                kb.end_scratch_phase()
            if first:
                tap("ybg", bufY[:], [128, KC, S], BF16, [t_bufY])

            if upto == "MD":
                break

            proj_n[0] = 2
            with ExitStack() as ms:
                fq = [kb.scratch(ms, f"f_q{i}", [128, S], BF16) for i in range(2)]
                fk = [kb.scratch(ms, f"f_k{i}", [128, S], BF16) for i in range(2)]
                fv = [kb.scratch(ms, f"f_v{i}", [128, NT, 130], BF16) for i in range(2)]
                fzs = [kb.scratch(ms, f"f_z{i}", [128, NT, 128], BF16) for i in range(2)]
                bms = [kb.scratch(ms, f"f_bm{i}", [128, NT, NT], F32) for i in range(2)]
                ptb = [kb.scratch(ms, f"f_pt{i}", [128, 512], BF16) for i in range(NPTB)]
                obs = [kb.scratch(ms, f"f_ob{i}", [128, 4, 128], BF16) for i in range(2)]
                rin, t_rin = kb.scratch(ms, "f_rin", [128, 16], F32)
                tzs = [kb.scratch(ms, f"f_tz{i}", [128, 128], F32) for i in range(4)]
                for i in range(2):
                    op(DVE, lambda i=i: nc.vector.memset(fv[i][0][:, :, 128:129], 2.0), writes=[fv[i][1]])
                sc = float(128 ** -0.5)
                sb_i = 0
                for h in range(8):
                    s1i, u1, t_u1 = ws.next()
                    s2i, u2, t_u2 = ws.next()
                    qT, t_qT = fq[h % 2]
                    kT, t_kT = fk[h % 2]
                    V, t_V = fv[h % 2]
                    FZ, t_FZ = fzs[h % 2]
                    bm, t_bm = bms[h % 2]
                    for which, (dst, t_dst) in enumerate(((qT, t_qT), (kT, t_kT))):
                        for tc in range(4):
                            pp_h, pp_t = proj_bank()
                            for kc in range(KC):
                                op(PE, lambda kc=kc, tc=tc, which=which, pp_h=pp_h: nc.tensor.matmul(
                                    pp_h[:, 0:512], lhsT=u1[:, kc, which * 128:(which + 1) * 128],
                                    rhs=xnT[:, kc, tc * 512:(tc + 1) * 512], start=(kc == 0), stop=(kc == KC - 1)),
                                   reads=[t_u1, t_xnT], writes=[pp_t], inc=(kc == KC - 1))
                            op(DVE, lambda tc=tc, pp_h=pp_h, dst=dst: nc.vector.tensor_copy(
                                out=dst[:, tc * 512:(tc + 1) * 512], in_=pp_h[:, 0:512]), reads=[pp_t], writes=[t_dst])
                    ws.release(s1i)
                    for tb in range(NT):
                        pp_h, pp_t = proj_bank()
                        for kc in range(KC):
                            op(PE, lambda kc=kc, tb=tb, pp_h=pp_h: nc.tensor.matmul(
                                pp_h[:, 0:256], lhsT=xnT[:, kc, tb * 128:(tb + 1) * 128], rhs=u2[:, kc, :],
                                start=(kc == 0), stop=(kc == KC - 1)), reads=[t_u2, t_xnT], writes=[pp_t], inc=(kc == KC - 1))
                        op(DVE, lambda tb=tb, pp_h=pp_h: nc.vector.tensor_copy(out=V[:, tb, 0:128], in_=pp_h[:, 0:128]),
                           reads=[pp_t], writes=[t_V])
                        tz_h, tz_t = tzs[tb % 4]
                        op(ACT, lambda pp_h=pp_h, tz_h=tz_h: nc.scalar.activation(out=tz_h[:], in_=pp_h[:, 128:256],
                                                                                  func=AF.Tanh, scale=0.5),
                           reads=[pp_t], writes=[tz_t])
                        op(DVE, lambda tb=tb, pp_h=pp_h, tz_h=tz_h: nc.vector.scalar_tensor_tensor(
                            out=FZ[:, tb, :], in0=tz_h[:], scalar=1.0, in1=pp_h[:, 128:256], op0=ALU.add, op1=ALU.mult),
                           reads=[pp_t, tz_t], writes=[t_FZ])
                    ws.release(s2i)
                    for j in range(NT):
                        op(DVE, lambda j=j: nc.vector.tensor_scalar(out=bm[:, j, :], in0=RR[:, :, h],
                                                                    scalar1=cc[:, j, h:h + 1], scalar2=None,
                                                                    op0=ALU.subtract), reads=[t_RR, t_cc], writes=[t_bm])
                    steps = [(I, j) for I in range(4) for j in range(4 * I + 4)]

                    def emit_ST(n):
                        I, j = steps[n]
                        i0 = max(j, 4 * I)
                        ps_h, ps_t = PB[2 + n % 2]
                        N = (4 * I + 4 - i0) * 128
                        op(PE, lambda: nc.tensor.matmul(ps_h[:, 0:N], lhsT=kT[:, j * 128:(j + 1) * 128],
                                                        rhs=qT[:, i0 * 128:(4 * I + 4) * 128], start=True, stop=True),
                           reads=[t_kT, t_qT], writes=[ps_t])

                    def emit_exp(n):
                        I, j = steps[n]
                        i0 = max(j, 4 * I)
                        ps_h, ps_t = PB[2 + n % 2]
                        p_h, p_t = ptb[n % NPTB]
                        for i in range(i0, 4 * I + 4):
                            lo = (i - i0) * 128
                            op(ACT, lambda i=i, lo=lo: nc.scalar.activation(
                                out=p_h[:, lo:lo + 128], in_=ps_h[:, lo:lo + 128], func=AF.Exp, scale=sc,
                                bias=bm[:, j, i:i + 1]), reads=[ps_t, t_bm], writes=[p_t])
                        if j >= 4 * I:
                            op(DVE, lambda: nc.vector.tensor_tensor(out=p_h[:, 0:128], in0=p_h[:, 0:128], in1=maskb[:],
                                                                    op=ALU.mult), reads=[p_t, t_maskb], writes=[p_t])

                    def emit_PV(n):
                        I, j = steps[n]
                        i0 = max(j, 4 * I)
                        p_h, p_t = ptb[n % NPTB]
                        for i in range(i0, 4 * I + 4):
                            il = i - 4 * I
                            lo = (i - i0) * 128
                            po_h, po_t = PB[4 + il // 2]
                            off = (il % 2) * 129
                            op(PE, lambda i=i, lo=lo, po_h=po_h, off=off, il=il: nc.tensor.matmul(
                                po_h[:, off:off + 129], lhsT=p_h[:, lo:lo + 128], rhs=V[:, j, 0:129],
                                start=(j == 0 and il % 2 == 0), stop=(j == i), skip_group_check=True),
                               reads=[p_t, t_V], writes=[po_t])
                            if j == i:
                                ob_h, ob_t = obs[I % 2]
                                op(DVE, lambda po_h=po_h, off=off, i=i: nc.vector.reciprocal(
                                    out=rin[:, i:i + 1], in_=po_h[:, off + 128:off + 129]), reads=[po_t], writes=[t_rin])
                                op(DVE, lambda po_h=po_h, off=off, i=i, il=il, ob_h=ob_h: nc.vector.scalar_tensor_tensor(
                                    out=ob_h[:, il, :], in0=po_h[:, off:off + 128], scalar=rin[:, i:i + 1], in1=FZ[:, i, :],
                                    op0=ALU.mult, op1=ALU.mult), reads=[po_t, t_rin, t_FZ], writes=[ob_t])
                                pt_h, pt_t = PT[I % 2]
                                pb0 = 0
                                op(PE, lambda il=il, ob_h=ob_h, pt_h=pt_h, pb0=pb0: nc.tensor.transpose(
                                    out=pt_h[:, pb0 + il, :], in_=ob_h[:, il, :], identity=ident[:]),
                                   reads=[ob_t, t_ident], writes=[pt_t])
                                if il == 3:
                                    op(DVE, lambda pt_h=pt_h, I=I: nc.vector.tensor_copy(
                                        out=bufA[:, h, I * 512:(I + 1) * 512],
                                        in_=pt_h[:, pb0:pb0 + 4, :].rearrange("p a b -> p (a b)")),
                                       reads=[pt_t], writes=[t_bufA])

                    emit_ST(0)
                    for n in range(len(steps)):
                        emit_exp(n)
                        if n + 1 < len(steps):
                            emit_ST(n + 1)
                        emit_PV(n)
                kb.end_scratch_phase()
            if first:
                tap("oaT", bufA[:], [128, KC, S], BF16, [t_bufA])

            if upto == "F":
                break

            proj_n[0] = 6
            with ExitStack() as ms:
                sgs = [kb.scratch(ms, f"e_sg{i}", [128, 512], F32) for i in range(2)]
                for oc in range(8):
                    si, u, t_u = ws.next()
                    for tc in range(4):
                        py_h, py_t = proj_bank()
                        for kc in range(KC):
                            op(PE, lambda kc=kc, tc=tc, py_h=py_h: nc.tensor.matmul(
                                py_h[:, 0:512], lhsT=u[:, kc, 0:128], rhs=bufA[:, kc, tc * 512:(tc + 1) * 512],
                                start=(kc == 0), stop=(kc == KC - 1)), reads=[t_u, t_bufA], writes=[py_t], inc=(kc == KC - 1))
                        pg_h, pg_t = proj_bank()
                        for kc in range(KC):
                            op(PE, lambda kc=kc, tc=tc, pg_h=pg_h: nc.tensor.matmul(
                                pg_h[:, 0:512], lhsT=u[:, kc, 128:256], rhs=xnT[:, kc, tc * 512:(tc + 1) * 512],
                                start=(kc == 0), stop=(kc == KC - 1)), reads=[t_u, t_xnT], writes=[pg_t], inc=(kc == KC - 1))
                        sg_h, sg_t = sgs[tc % 2]
                        op(ACT, lambda pg_h=pg_h, sg_h=sg_h, oc=oc: nc.scalar.activation(
                            out=sg_h[:], in_=pg_h[:, 0:512], func=AF.Tanh, bias=bgh[:, oc:oc + 1], scale=0.5),
                           reads=[pg_t, t_bgh], writes=[sg_t])
                        op(DVE, lambda py_h=py_h, sg_h=sg_h: nc.vector.scalar_tensor_tensor(
                            out=sg_h[:], in0=sg_h[:], scalar=1.0, in1=py_h[:, 0:512], op0=ALU.add, op1=ALU.mult),
                           reads=[py_t, sg_t], writes=[sg_t])
                        op(DVE, lambda sg_h=sg_h, oc=oc, tc=tc: nc.vector.tensor_tensor(
                            out=bufY[:, oc, tc * 512:(tc + 1) * 512], in0=sg_h[:], in1=bufY[:, oc, tc * 512:(tc + 1) * 512],
                            op=ALU.add), reads=[sg_t, t_bufY], writes=[t_bufY])
                    ws.release(si)
                kb.end_scratch_phase()
            if first:
                tap("yT", bufY[:], [128, KC, S], BF16, [t_bufY])

            if upto == "FD":
                break

            dma(SP, out=gX[:], in_=final_g.partition_broadcast(128), writes=[t_gX])
            us = [ws.next() for _ in range(4)]
            for tb in range(NT):
                xt_h, xt_t = xts[tb % NXT]
                dma(SP, out=xt_h[:], in_=x_d[seq, tb * 128:(tb + 1) * 128, :], writes=[xt_t])
                banks = [proj_bank(), proj_bank()]
                for hf in range(2):
                    po_h, po_t = banks[hf]
                    for n in range(2):
                        si, u, t_u = us[2 * hf + n]
                        for kc in range(KC):
                            op(PE, lambda kc=kc, n=n, u=u, po_h=po_h: nc.tensor.matmul(
                                po_h[:, n * 256:(n + 1) * 256], lhsT=bufY[:, kc, tb * 128:(tb + 1) * 128], rhs=u[:, kc, :],
                                start=(kc == 0), stop=(kc == KC - 1)), reads=[t_u, t_bufY], writes=[po_t],
                               inc=(kc == KC - 1 and n == 1))
                    op(DVE, lambda hf=hf, po_h=po_h: nc.vector.scalar_tensor_tensor(
                        out=xt_h[:, hf * 512:(hf + 1) * 512], in0=po_h[:, 0:512], scalar=0.5,
                        in1=xt_h[:, hf * 512:(hf + 1) * 512], op0=ALU.mult, op1=ALU.add),
                       reads=[po_t, xt_t], writes=[xt_t])
                xn_h, xn_t = xns[tb % NXT]
                rmsnorm_tail(xt_h, xt_t, tb, gF, t_gF, xt_h[:], [xt_t], xn_h[:], [xn_t])
                dma(SP, out=y_d[seq, tb * 128:(tb + 1) * 128, :], in_=xt_h[:], reads=[xt_t])
            for (si, u, t_u) in us:
                ws.release(si)
            kb.scr_tiles.append(t_bufY)
            kb.end_scratch_phase()
            ys.close()

        if kb.mode == "defer":
            kb.flush()
            build.n_inst = kb.n_inst
            build.nsem = kb.nsem
            build.est_us = kb.est_makespan
        out_costs = [o.cost for o in kb.ops]
    return nc, dbg_out, out_costs


def host_inputs(inputs):
    f = lambda a: np.ascontiguousarray(np.asarray(a, dtype=np.float32))
    gb = np.concatenate([f(inputs["b_fox_f"])[0], f(inputs["b_ml_i"])[0], f(inputs["b_ml_f"])[0]])
    gbias = np.ascontiguousarray(np.broadcast_to(gb[None, None, :], (128, NT, 16)))
    convw = np.ascontiguousarray(f(inputs["conv_w"])[0].reshape(4, 8, 128).transpose(2, 1, 0))
    convb = np.ascontiguousarray(f(inputs["conv_b"])[0].reshape(8, 128).T)
    bgate = np.ascontiguousarray(f(inputs["b_gate"])[0].reshape(16, 128).T)
    return {
        "w_in": f(inputs["w_in"])[0],
        "w_fox_down": f(inputs["w_fox_down"])[0],
        "w_ml_down": f(inputs["w_ml_down"])[0],
        "w_out": f(inputs["w_out"])[0],
        "norm_g": f(inputs["norm_g"])[0],
        "final_g": f(inputs["final_g"]),
        "ml_norm_g": f(inputs["ml_norm_g"])[0],
        "gbias": gbias,
        "convw": convw,
        "convb": convb,
        "bgate": bgate,
    }


def kernel(**inputs):
    x = np.ascontiguousarray(np.asarray(inputs["x"], dtype=np.float32))
    B = x.shape[0]
    nseq = B // NCORES
    shared = host_inputs(inputs)
    nc, _ = build(nseq)
    in_maps = []
    for c in range(NCORES):
        m = dict(shared)
        m["x"] = x[c * nseq:(c + 1) * nseq]
        in_maps.append(m)
    res = run_bass_kernel_spmd(nc, in_maps, core_ids=list(range(NCORES)))
    out = np.concatenate([np.asarray(r["y"]) for r in res.results], axis=0)
    return out.astype(np.float32, copy=False)
```

```python
import types
import numpy as np
from contextlib import ExitStack
import concourse.bass as bass
import concourse.mybir as mybir
from concourse.bass_utils import run_bass_kernel_spmd

F32 = mybir.dt.float32
BF16 = mybir.dt.bfloat16
AF = mybir.ActivationFunctionType
ALU = mybir.AluOpType

S = 2048
D = 1024
NT = 16
KC = 8
NCORES = 8
EPS = 1e-6
C_FQ, C_FK, C_FV, C_FF, C_FZ = 0, 1024, 2048, 3072, 3080
C_MQ, C_MK, C_MV, C_MI, C_MF, C_MO, C_MZ = 4104, 4616, 5128, 6152, 6156, 6160, 7184
C_GA, C_GB = 8208, 9232
IN_W = 10256
SEM_LIMIT = 30000
SLACK_US = 0.0
PE_MAXG = 2
NPTB = 3
NXT = 4
MDB = 1
NSLOT = 6


class SemObj:
    __slots__ = ("h", "owner", "val")

    def __init__(self, h, owner):
        self.h = h
        self.owner = owner
        self.val = 0


class Tile:
    __slots__ = ("w", "r", "name", "psum")

    def __init__(self, name, init=None, psum=False):
        self.name = name
        self.w = list(init) if init else []
        self.r = []
        self.psum = psum


class Eng:
    def __init__(self, kb, name, h, is_pe=False):
        self.kb = kb
        self.name = name
        self.h = h
        self.is_pe = is_pe
        self.waited = {}
        self.cur = None

    def sem(self):
        if self.cur is None or self.cur.val >= SEM_LIMIT:
            self.cur = self.kb.new_sem(self)
        return self.cur


class Op:
    __slots__ = ("id", "eng", "fns", "deps", "cost", "is_dma", "tok", "start", "ninstr")

    def __init__(self, oid, eng, is_dma):
        self.id = oid
        self.eng = eng
        self.fns = []
        self.deps = []
        self.cost = 0.0
        self.is_dma = is_dma
        self.tok = None
        self.start = 0.0


def _snap(fn):
    if fn.__closure__ is None:
        return fn
    cells = []
    for c in fn.__closure__:
        try:
            cells.append(types.CellType(c.cell_contents))
        except ValueError:
            cells.append(c)
    return types.FunctionType(fn.__code__, fn.__globals__, fn.__name__, fn.__defaults__, tuple(cells))


def _free_elems(acc):
    try:
        ap = acc.ap
        n = 1
        for (_, c) in ap[1:]:
            n *= c
        return max(int(n), 1)
    except Exception:
        return 64


class KB:
    def __init__(self, nc, es, costs=None):
        self.nc = nc
        self.es = es
        self.mode = "measure" if costs is None else "defer"
        self.costs_in = costs
        self.nsem = 0
        self.PE = Eng(self, "pe", nc.tensor, is_pe=True)
        self.ACT = Eng(self, "act", nc.scalar)
        self.DVE = Eng(self, "dve", nc.vector)
        self.POOL = Eng(self, "pool", nc.gpsimd)
        self.SP = Eng(self, "sp", nc.sync)
        self.engs = [self.PE, self.ACT, self.DVE, self.POOL, self.SP]
        self.dma_pool = {}
        self.dma_rr = {}
        self.scr_tokens = []
        self.scr_tiles = []
        self.scr_id = 0
        self.n_inst = 0
        self.ops = []
        self.open_pe = None

    def new_sem(self, owner):
        self.nsem += 1
        h = self.es.enter_context(self.nc.semaphore(f"s{self.nsem}"))
        return SemObj(h, owner)

    def _deps(self, o, reads, writes):
        eng = o.eng
        ops = self.ops
        d = o.deps
        for t in reads:
            d.extend(t.w)
            if t.psum:
                d.extend(r for r in t.r if ops[r].eng is not eng)
        for t in writes:
            d.extend(t.w)
            d.extend(t.r)

    def _touch(self, oid, reads, writes):
        for t in reads:
            if not t.r or t.r[-1] != oid:
                t.r.append(oid)
        for t in writes:
            t.w = [oid]
            t.r = []

    def _cost_of(self, eng, ins):
        try:
            i = ins.ins
            n = _free_elems(i.outs[0])
            if eng.is_pe:
                f32 = False
                try:
                    f32 = (i.ins[0].dtype == F32) and not getattr(i, "is_transpose", False)
                except Exception:
                    pass
                return (max(n, 200) * (4 if f32 else 1) * 0.5 + 8.0) / 1000.0
            if eng is self.ACT:
                return 0.22 + n * 0.00083
            if eng is self.DVE:
                return 0.10 + n * 0.00104
            return 0.25 + n * 0.002
        except Exception:
            return 0.3

    def op(self, eng, fn, reads=(), writes=(), inc=True):
        if eng.is_pe and self.open_pe is not None:
            o = self.open_pe
        else:
            assert self.open_pe is None, "non-PE op recorded while a PE group is open"
            o = Op(len(self.ops), eng, False)
            self.ops.append(o)
            if eng.is_pe and not inc:
                self.open_pe = o
        if eng.is_pe:
            o.ninstr = getattr(o, "ninstr", 0) + 1
            if inc or o.ninstr >= PE_MAXG:
                self.open_pe = None
        self._deps(o, reads, writes)
        self._touch(o.id, reads, writes)
        if self.mode == "measure":
            ins = fn()
            o.cost += self._cost_of(eng, ins)
            return ins
        o.fns.append(_snap(fn))
        return None

    def dma(self, q, out, in_, reads=(), writes=()):
        assert self.open_pe is None
        o = Op(len(self.ops), q, True)
        self.ops.append(o)
        self._deps(o, reads, writes)
        self._touch(o.id, reads, writes)
        if self.mode == "measure":
            q.h.dma_start(out=out, in_=in_)
            try:
                n = 1
                for (_, c) in out.ap:
                    n *= c
            except Exception:
                n = 131072
            o.cost = 2.0 + n * 4 / 150e3
            return None
        o.fns.append(lambda: q.h.dma_start(out=out, in_=in_))
        return None

    def sb(self, name, shape, dt, es=None):
        h = (es or self.es).enter_context(self.nc.sbuf_tensor(name, shape, dt))
        return h, Tile(name)

    def scratch(self, es, name, shape, dt):
        self.scr_id += 1
        h = es.enter_context(self.nc.sbuf_tensor(f"{name}_{self.scr_id}", shape, dt))
        t = Tile(name, self.scr_tokens)
        self.scr_tiles.append(t)
        return h, t

    def end_scratch_phase(self):
        acc = set(self.scr_tokens)
        for t in self.scr_tiles:
            acc.update(t.w)
            acc.update(t.r)
        self.scr_tokens = sorted(acc)
        self.scr_tiles = []

    def schedule(self):
        ops = self.ops
        n = len(ops)
        costs = self.costs_in
        assert len(costs) == n, (len(costs), n)
        succ = [[] for _ in range(n)]
        for o in ops:
            o.deps = sorted(set(d for d in o.deps if d != o.id))
            for d in o.deps:
                succ[d].append(o.id)
        pend = {e: [] for e in self.engs}
        for o in ops:
            pend[o.eng].append(o.id)
        done = [False] * n
        fin = [0.0] * n
        efree = {e: 0.0 for e in self.engs}
        best = {e: None for e in self.engs}
        dirty = set(self.engs)
        HOP, SAME, W = 0.30, 0.06, 400
        SLACK = SLACK_US
        order = []
        INF = float("inf")
        while len(order) < n:
            for e in dirty:
                lst = pend[e]
                be, bi, bpos = INF, -1, -1
                ef = efree[e]
                lim = min(W, len(lst))
                for pos in range(lim):
                    oid = lst[pos]
                    rt = ef
                    ok = True
                    for d in ops[oid].deps:
                        if not done[d]:
                            ok = False
                            break
                        t = fin[d] + (SAME if ops[d].eng is e else HOP)
                        if t > rt:
                            rt = t
                    if ok and rt < be - SLACK:
                        be, bi, bpos = rt, oid, pos
                        if rt <= ef:
                            break
                best[e] = (be, bi, bpos) if bi >= 0 else None
            dirty.clear()
            ce, cb = None, None
            for e in self.engs:
                b = best[e]
                if b is not None and (cb is None or b[0] < cb[0] or (b[0] == cb[0] and b[1] < cb[1])):
                    ce, cb = e, b
            assert ce is not None, "scheduler deadlock"
            st, oid, pos = cb
            o = ops[oid]
            o.start = st
            if o.is_dma:
                efree[ce] = st + 0.15
                fin[oid] = st + costs[oid]
            else:
                efree[ce] = st + costs[oid]
                fin[oid] = efree[ce]
            done[oid] = True
            del pend[ce][pos]
            order.append(o)
            dirty.add(ce)
            for sidx in succ[oid]:
                dirty.add(ops[sidx].eng)
        self.est_makespan = max(fin) if fin else 0.0
        return order

    def flush(self):
        order = self.schedule()
        ops = self.ops
        for o in order:
            eng = o.eng
            for d in o.deps:
                p = ops[d]
                if eng.is_pe and p.eng is eng and not p.is_dma:
                    continue
                s, v = p.tok
                if eng.waited.get(s, 0) >= v:
                    continue
                eng.h.wait_ge(s.h, v)
                eng.waited[s] = v
                self.n_inst += 1
            if o.is_dma:
                pool = self.dma_pool.setdefault(eng, [])
                if len(pool) < 8:
                    dsem = self.new_sem(None)
                    pool.append(dsem)
                else:
                    i = self.dma_rr.get(eng, 0)
                    dsem = pool[i % len(pool)]
                    self.dma_rr[eng] = i + 1
                if dsem.val and eng.waited.get(dsem, 0) < dsem.val:
                    eng.h.wait_ge(dsem.h, dsem.val)
                    eng.waited[dsem] = dsem.val
                    self.n_inst += 1
                ins = o.fns[0]()
                dsem.val += 16
                ins.then_inc(dsem.h, 16)
                o.tok = (dsem, dsem.val)
            else:
                ins = None
                for fn in o.fns:
                    ins = fn()
                    self.n_inst += 1
                s = eng.sem()
                s.val += 1
                ins.then_inc(s.h, 1)
                o.tok = (s, s.val)
            self.n_inst += 1
        SP = self.SP
        for q, pool in self.dma_pool.items():
            for dsem in pool:
                if dsem.val and SP.waited.get(dsem, 0) < dsem.val:
                    self.nc.sync.wait_ge(dsem.h, dsem.val)
                    SP.waited[dsem] = dsem.val


class WStream:
    def __init__(self, kb, slots, plan):
        self.kb = kb
        self.slots = slots
        self.plan = plan
        self.issued = 0
        self.taken = 0
        self.free = [True] * len(slots)

    def _pump(self):
        while self.issued < len(self.plan):
            si = self.issued % len(self.slots)
            if not self.free[si]:
                break
            h, t = self.slots[si]
            for (dst, src, c0, n) in self.plan[self.issued]:
                self.kb.dma(self.kb.POOL, out=h[:, :, dst:dst + n], in_=src[:, :, c0:c0 + n], writes=[t])
            self.free[si] = False
            self.issued += 1

    def next(self):
        self._pump()
        assert self.taken < self.issued, "weight stream stalled (slot not released)"
        si = self.taken % len(self.slots)
        self.taken += 1
        h, t = self.slots[si]
        return si, h, t

    def release(self, si):
        self.free[si] = True
        self._pump()


def unit_plan(nseq, wv, wmd, wfd, wout):
    plan = []
    for _ in range(nseq):
        for h in range(4):
            plan.append([(0, wv, C_MQ + h * 128, 128), (128, wv, C_MK + h * 128, 128)])
            plan.append([(0, wv, C_MV + h * 256, 256)])
            plan.append([(0, wv, C_MO + h * 256, 256)])
            plan.append([(0, wv, C_MZ + h * 256, 256)])
        for oc in range(8):
            plan.append([(0, wmd, oc * 128, 128), (128, wv, C_GB + oc * 128, 128)])
        for h in range(8):
            plan.append([(0, wv, C_FQ + h * 128, 128), (128, wv, C_FK + h * 128, 128)])
            plan.append([(0, wv, C_FV + h * 128, 128), (128, wv, C_FZ + h * 128, 128)])
        for oc in range(8):
            plan.append([(0, wfd, oc * 128, 128), (128, wv, C_GA + oc * 128, 128)])
        for u in range(4):
            plan.append([(0, wout, u * 256, 256)])
    return plan


def build(nseq=2, dbg=None, upto=None):
    _, _, costs = _build_pass(nseq, dbg, upto, None)
    nc, dbg_out, _ = _build_pass(nseq, dbg, upto, costs)
    return nc, dbg_out


def _build_pass(nseq, dbg, upto, costs):
    dbg = dbg or set()
    nc = bass.Bass("TRN2", target_bir_lowering=False)
    x_d = nc.dram_tensor("x", [nseq, S, D], F32, kind="ExternalInput").ap()
    w_in = nc.dram_tensor("w_in", [D, IN_W], F32, kind="ExternalInput").ap()
    w_fd = nc.dram_tensor("w_fox_down", [D, D], F32, kind="ExternalInput").ap()
    w_md = nc.dram_tensor("w_ml_down", [D, D], F32, kind="ExternalInput").ap()
    w_o = nc.dram_tensor("w_out", [D, D], F32, kind="ExternalInput").ap()
    norm_g = nc.dram_tensor("norm_g", [D], F32, kind="ExternalInput").ap()
    final_g = nc.dram_tensor("final_g", [D], F32, kind="ExternalInput").ap()
    ml_g = nc.dram_tensor("ml_norm_g", [D], F32, kind="ExternalInput").ap()
    gbias_d = nc.dram_tensor("gbias", [128, NT, 16], F32, kind="ExternalInput").ap()
    convw_d = nc.dram_tensor("convw", [128, 8, 4], F32, kind="ExternalInput").ap()
    convb_d = nc.dram_tensor("convb", [128, 8], F32, kind="ExternalInput").ap()
    bgate_d = nc.dram_tensor("bgate", [128, 16], F32, kind="ExternalInput").ap()
    y_d = nc.dram_tensor("y", [nseq, S, D], F32, kind="ExternalOutput").ap()
    dbg_out = {}

    def dbg_tensor(name, shape, dt):
        dbg_out[name] = nc.dram_tensor("dbg_" + name, shape, dt, kind="ExternalOutput").ap()
        return dbg_out[name]

    wv = w_in.rearrange("(kc p) c -> p kc c", p=128)
    wfd = w_fd.rearrange("(kc p) c -> p kc c", p=128)
    wmd = w_md.rearrange("(kc p) c -> p kc c", p=128)
    wout = w_o.rearrange("(kc p) c -> p kc c", p=128)

    with ExitStack() as es:
        kb = KB(nc, es, costs)
        PE, ACT, DVE, POOL, SP = kb.PE, kb.ACT, kb.DVE, kb.POOL, kb.SP
        op, dma = kb.op, kb.dma

        xnT, t_xnT = kb.sb("xnT", [128, KC, S], BF16)
        bufA, t_bufA = kb.sb("bufA", [128, KC, S], BF16)
        slots = [kb.sb(f"wslot{i}", [128, KC, 256], BF16) for i in range(NSLOT)]
        wg, t_wg = kb.sb("wg", [128, KC, 16], BF16)
        gX, t_gX = kb.sb("gX", [128, D], F32)
        gN, t_gN = kb.sb("gN", [128, D], F32)
        gF = gM = gX
        t_gF = t_gM = t_gX
        gbias, t_gbias = kb.sb("gbias_sb", [128, NT, 16], F32)
        cw, t_cw = kb.sb("cw", [128, 8, 4], F32)
        cb, t_cb = kb.sb("cb", [128, 8], F32)
        bg, t_bg = kb.sb("bg", [128, 16], F32)
        cbh, t_cbh = kb.sb("cbh", [128, 8], F32)
        bgh, t_bgh = kb.sb("bgh", [128, 16], F32)
        epsT, t_eps = kb.sb("epsT", [128, 1], F32)
        ident, t_ident = kb.sb("ident", [128, 128], BF16)
        onesb, t_onesb = kb.sb("onesb", [128, 128], BF16)
        maskb, t_maskb = kb.sb("maskb", [128, 128], BF16)
        maskf, t_maskf = kb.sb("maskf", [128, 128], F32)
        onesf, t_onesf = kb.sb("onesf", [128, 128], F32)
        hmid, t_hmid = kb.sb("hmid", [128, 128], F32)
        xts = [kb.sb(f"xt{i}", [128, D], F32) for i in range(NXT)]
        xns = [kb.sb(f"xn{i}", [128, D], BF16) for i in range(NXT)]
        stats = [kb.sb(f"stat{i}", [128, 4], F32) for i in range(NXT)]
        zs, t_zs = kb.sb("zs", [128, NT, 16], F32)
        lf, t_lf = kb.sb("lf", [128, NT, 16], F32)
        tot, t_tot = kb.sb("tot", [128, NT, 16], F32)
        bc, t_bc = kb.sb("bc", [128, NT, 16], F32)
        half, t_half = kb.sb("half", [128, NT, 16], F32)
        pre, t_pre = kb.sb("pre", [128, NT, 16], F32)
        cc, t_cc = kb.sb("cc", [128, NT, 16], F32)
        RR, t_RR = kb.sb("RR", [128, NT, 16], F32)
        ew, t_ew = kb.sb("ew", [128, NT, 4], F32)
        eb, t_eb = kb.sb("eb", [128, NT, 4], F32)
        aa, t_aa = kb.sb("aa", [128, NT, 4], F32)
        gtmp, t_gtmp = kb.sb("gtmp", [128, NT, 16], F32)

        PB = []
        for i in range(6):
            h = es.enter_context(nc.psum_tensor(f"pb{i}", [128, 512], F32))
            PB.append((h, Tile(f"pb{i}", psum=True)))
        PT = []
        for i in range(2):
            h = es.enter_context(nc.psum_tensor(f"pt{i}", [128, 8, 128], BF16))
            PT.append((h, Tile(f"pt{i}", psum=True)))

        dma(SP, out=gbias[:], in_=gbias_d[:, :, :], writes=[t_gbias])
        dma(SP, out=cw[:], in_=convw_d[:, :, :], writes=[t_cw])
        dma(SP, out=cb[:], in_=convb_d[:, :], writes=[t_cb])
        dma(SP, out=bg[:], in_=bgate_d[:, :], writes=[t_bg])
        dma(POOL, out=wg[:, :, 0:8], in_=wv[:, :, C_FF:C_FF + 8], writes=[t_wg])
        dma(POOL, out=wg[:, :, 8:16], in_=wv[:, :, C_MI:C_MI + 8], writes=[t_wg])
        op(DVE, lambda: nc.vector.memset(epsT[:], EPS), writes=[t_eps])
        op(DVE, lambda: nc.vector.tensor_scalar(out=cbh[:], in0=cb[:], scalar1=0.5, scalar2=None, op0=ALU.mult),
           reads=[t_cb], writes=[t_cbh])
        op(DVE, lambda: nc.vector.tensor_scalar(out=bgh[:], in0=bg[:], scalar1=0.5, scalar2=None, op0=ALU.mult),
           reads=[t_bg], writes=[t_bgh])
        op(POOL, lambda: nc.gpsimd.memset(onesb[:], 1.0), writes=[t_onesb])
        op(POOL, lambda: nc.gpsimd.memset(onesf[:], 1.0), writes=[t_onesf])
        op(POOL, lambda: nc.gpsimd.memset(hmid[:], 0.0), writes=[t_hmid])
        op(POOL, lambda: nc.gpsimd.memset(hmid[0:64, :], 1.0), writes=[t_hmid])
        op(POOL, lambda: nc.gpsimd.affine_select(out=ident[:], in_=onesb[:], pattern=[[1, 128]],
                                                 compare_op=ALU.is_equal, fill=0.0, base=0,
                                                 channel_multiplier=-1), reads=[t_onesb], writes=[t_ident])
        op(POOL, lambda: nc.gpsimd.affine_select(out=maskb[:], in_=onesb[:], pattern=[[1, 128]],
                                                 compare_op=ALU.is_ge, fill=0.0, base=0,
                                                 channel_multiplier=-1), reads=[t_onesb], writes=[t_maskb])
        op(POOL, lambda: nc.gpsimd.affine_select(out=maskf[:], in_=onesf[:], pattern=[[1, 128]],
                                                 compare_op=ALU.is_ge, fill=0.0, base=0,
                                                 channel_multiplier=-1), reads=[t_onesf], writes=[t_maskf])

        ws = WStream(kb, slots, unit_plan(nseq, wv, wmd, wfd, wout))
        proj_rr = [0]
        proj_n = [2]

        def proj_bank():
            proj_rr[0] += 1
            return PB[proj_rr[0] % proj_n[0]]

        def tap(name, src_ap, shape, dt, tiles):
            if name in dbg:
                d = dbg_tensor(name, shape, dt)
                dma(SP, out=d, in_=src_ap, reads=tiles)

        def rmsnorm_tail(xt_h, xt_t, si, g_h, g_t, out_ap, out_tiles, sq_out_ap, sq_out_tiles):
            st_h, st_t = stats[si % NXT]
            op(ACT, lambda: nc.scalar.activation(out=sq_out_ap, in_=xt_h[:], func=AF.Square,
                                                 accum_out=st_h[:, 0:1]),
               reads=[xt_t], writes=sq_out_tiles + [st_t])
            op(ACT, lambda: nc.scalar.activation(out=st_h[:, 1:2], in_=st_h[:, 0:1], func=AF.Sqrt,
                                                 scale=1.0 / D, bias=epsT[:]),
               reads=[st_t, t_eps], writes=[st_t])
            op(DVE, lambda: nc.vector.reciprocal(out=st_h[:, 2:3], in_=st_h[:, 1:2]), reads=[st_t], writes=[st_t])
            op(DVE, lambda: nc.vector.scalar_tensor_tensor(out=out_ap, in0=xt_h[:], scalar=st_h[:, 2:3],
                                                           in1=g_h[:], op0=ALU.mult, op1=ALU.mult),
               reads=[xt_t, st_t, g_t], writes=out_tiles)

        pending_O = None
        for seq in range(nseq):
            first = (seq == 0)
            proj_n[0] = 2
            if first:
                dma(SP, out=gN[:], in_=norm_g.partition_broadcast(128), writes=[t_gN])
            for tb in range(NT):
                xt_h, xt_t = xts[tb % NXT]
                xn_h, xn_t = xns[tb % NXT]
                dma(SP, out=xt_h[:], in_=x_d[seq, tb * 128:(tb + 1) * 128, :], writes=[xt_t])
                rmsnorm_tail(xt_h, xt_t, tb, gN, t_gN, xn_h[:], [xn_t], xn_h[:], [xn_t])
                pt_h, pt_t = PT[tb % 2]
                for kc in range(KC):
                    op(PE, lambda kc=kc: nc.tensor.transpose(out=pt_h[:, kc, :], in_=xn_h[:, kc * 128:(kc + 1) * 128],
                                                             identity=ident[:]),
                       reads=[xn_t, t_ident], writes=[pt_t], inc=(kc == KC - 1))
                op(ACT, lambda: nc.scalar.copy(out=xnT[:, :, tb * 128:(tb + 1) * 128], in_=pt_h[:]),
                   reads=[pt_t], writes=[t_xnT])
            if first:
                tap("xnT", xnT[:], [128, KC, S], BF16, [t_xnT])

            if upto == "P0":
                break

            pz_h, pz_t = PB[2]
            for tb in range(NT):
                for kc in range(KC):
                    op(PE, lambda kc=kc, tb=tb: nc.tensor.matmul(pz_h[:, tb * 16:(tb + 1) * 16],
                                                                 lhsT=xnT[:, kc, tb * 128:(tb + 1) * 128],
                                                                 rhs=wg[:, kc, :], start=(kc == 0), stop=(kc == KC - 1)),
                       reads=[t_xnT, t_wg], writes=[pz_t], inc=(kc == KC - 1 and tb == NT - 1))
            op(DVE, lambda: nc.vector.tensor_tensor(out=zs[:].rearrange("p a b -> p (a b)"), in0=pz_h[:, 0:256],
                                                    in1=gbias[:].rearrange("p a b -> p (a b)"), op=ALU.add),
               reads=[pz_t, t_gbias], writes=[t_zs])
            if upto == "G1":
                tap("lf", zs[:], [128, NT, 16], F32, [t_zs])
                break
            op(ACT, lambda: nc.scalar.activation(out=gtmp[:], in_=zs[:], func=AF.Exp, scale=-1.0),
               reads=[t_zs], writes=[t_gtmp])
            op(ACT, lambda: nc.scalar.activation(out=gtmp[:], in_=gtmp[:], func=AF.Ln, bias=1.0, scale=1.0),
               reads=[t_gtmp], writes=[t_gtmp])
            op(DVE, lambda: nc.vector.tensor_scalar(out=lf[:], in0=gtmp[:], scalar1=-1.0, scalar2=None, op0=ALU.mult),
               reads=[t_gtmp], writes=[t_lf])
            if upto == "G2":
                tap("lf", lf[:], [128, NT, 16], F32, [t_lf])
                break
            lf2 = lf[:].rearrange("p a b -> p (a b)")
            for (lhs_h, lhs_t, dst_h, dst_t, pbi) in ((onesf, t_onesf, tot, t_tot, 3), (maskf, t_maskf, bc, t_bc, 4),
                                                       (hmid, t_hmid, half, t_half, 5)):
                pp_h, pp_t = PB[pbi]
                op(PE, lambda lhs_h=lhs_h, pp_h=pp_h: nc.tensor.matmul(pp_h[:, 0:256], lhsT=lhs_h[:], rhs=lf2,
                                                                       start=True, stop=True),
                   reads=[lhs_t, t_lf], writes=[pp_t])
                op(DVE, lambda dst_h=dst_h, pp_h=pp_h: nc.vector.tensor_copy(out=dst_h[:].rearrange("p a b -> p (a b)"),
                                                                             in_=pp_h[:, 0:256]),
                   reads=[pp_t], writes=[dst_t])
            if upto == "G3":
                tap("lf", lf[:], [128, NT, 16], F32, [t_lf])
                tap("cc", bc[:], [128, NT, 16], F32, [t_bc])
                break
            op(DVE, lambda: nc.vector.memset(pre[:, 0, :], 0.0), writes=[t_pre])
            for tb in range(1, NT):
                op(DVE, lambda tb=tb: nc.vector.tensor_tensor(out=pre[:, tb, :], in0=pre[:, tb - 1, :],
                                                              in1=tot[:, tb - 1, :], op=ALU.add),
                   reads=[t_pre, t_tot], writes=[t_pre])
            op(DVE, lambda: nc.vector.tensor_tensor(out=cc[:], in0=bc[:], in1=pre[:], op=ALU.add),
               reads=[t_bc, t_pre], writes=[t_cc])
            op(DVE, lambda: nc.vector.tensor_tensor(out=RR[:], in0=half[:], in1=pre[:], op=ALU.add),
               reads=[t_half, t_pre], writes=[t_RR])
            if upto == "G4":
                tap("cc", cc[:], [128, NT, 16], F32, [t_cc])
                break
            op(DVE, lambda: nc.vector.tensor_tensor(out=ew[:], in0=zs[:, :, 8:12], in1=bc[:, :, 12:16], op=ALU.subtract),
               reads=[t_zs, t_bc], writes=[t_ew])
            if upto == "G5":
                tap("cc", cc[:], [128, NT, 16], F32, [t_cc, t_ew])
                break
            op(ACT, lambda: nc.scalar.activation(out=ew[:], in_=ew[:], func=AF.Exp), reads=[t_ew], writes=[t_ew])
            if upto == "G6":
                tap("cc", cc[:], [128, NT, 16], F32, [t_cc, t_ew])
                break
            op(ACT, lambda: nc.scalar.activation(out=eb[:], in_=bc[:, :, 12:16], func=AF.Exp), reads=[t_bc], writes=[t_eb])
            op(DVE, lambda: nc.vector.tensor_scalar(out=eb[:], in0=eb[:], scalar1=float(128 ** -0.5), scalar2=None,
                                                    op0=ALU.mult), reads=[t_eb], writes=[t_eb])
            if upto == "G7":
                tap("cc", cc[:], [128, NT, 16], F32, [t_cc, t_ew, t_eb])
                break
            op(ACT, lambda: nc.scalar.activation(out=aa[:], in_=tot[:, :, 12:16], func=AF.Exp), reads=[t_tot], writes=[t_aa])
            if first:
                tap("lf", lf[:], [128, NT, 16], F32, [t_lf])
                tap("cc", cc[:], [128, NT, 16], F32, [t_cc])
                tap("RR", RR[:], [128, NT, 16], F32, [t_RR])
                tap("ew", ew[:], [128, NT, 4], F32, [t_ew])
                tap("eb", eb[:], [128, NT, 4], F32, [t_eb])
                tap("aa", aa[:], [128, NT, 4], F32, [t_aa])

            if upto == "G":
                break

            if pending_O is not None:
                pending_O()
                pending_O = None
            dma(SP, out=gX[:], in_=ml_g.partition_broadcast(128), writes=[t_gX])
            with ExitStack() as ms:
                mq2 = [kb.scratch(ms, f"m_qT{i}", [128, S], BF16) for i in range(MDB)]
                mk2 = [kb.scratch(ms, f"m_kT{i}", [128, S], BF16) for i in range(MDB)]
                mc2 = [kb.scratch(ms, f"m_cbuf{i}", [128, S + 4], F32) for i in range(MDB)]
                accs = [kb.scratch(ms, f"m_acc{i}", [128, 1024], F32) for i in range(2)]
                ths = [kb.scratch(ms, f"m_th{i}", [128, 1024], F32) for i in range(2)]
                kTok, t_kTok = kb.scratch(ms, "m_kTok", [128, NT, 128], BF16)
                vaug, t_vaug = kb.scratch(ms, "m_vaug", [128, NT, 258], BF16)
                Crun = [kb.scratch(ms, f"m_C{i}", [128, 258], F32) for i in range(2)]
                Cbf, t_Cbf = kb.scratch(ms, "m_Cbf", [128, NT, 258], BF16)
                numS, t_numS = kb.scratch(ms, "m_numS", [128, 4, 258], F32)
                Gp, t_Gp = kb.scratch(ms, "m_Gp", [128, 4, 256], BF16)
                STm = [kb.scratch(ms, f"m_STm{i}", [128, 128], BF16) for i in range(2)]
                hn = [kb.scratch(ms, f"m_hn{i}", [128, 256], BF16) for i in range(2)]
                s12 = [kb.scratch(ms, f"m_s12{i}", [128, 512], F32) for i in range(2)]
                sm, t_sm = kb.scratch(ms, "m_sm", [128, 12, 4], F32)
                junk, t_junk = kb.scratch(ms, "m_junk", [128, 256], BF16)
                for (cb_h, cb_t) in mc2:
                    op(DVE, lambda cb_h=cb_h: nc.vector.memset(cb_h[:, 0:4], 0.0), writes=[cb_t])
                for h in range(4):
                    qT, t_qT = mq2[h % MDB]
                    kT, t_kT = mk2[h % MDB]
                    cbuf, t_cbuf = mc2[h % MDB]
                    s1i, u1, t_u1 = ws.next()
                    s2i, u2, t_u2 = ws.next()
                    s3i, u3, t_u3 = ws.next()
                    s4i, u4, t_u4 = ws.next()
                    for which, (dst, t_dst) in enumerate(((qT, t_qT), (kT, t_kT))):
                        chunk = which * 4 + h
                        for tc in range(4):
                            pp_h, pp_t = proj_bank()
                            for kc in range(KC):
                                op(PE, lambda kc=kc, tc=tc, which=which, pp_h=pp_h: nc.tensor.matmul(
                                    pp_h[:, 0:512], lhsT=u1[:, kc, which * 128:(which + 1) * 128],
                                    rhs=xnT[:, kc, tc * 512:(tc + 1) * 512], start=(kc == 0), stop=(kc == KC - 1)),
                                   reads=[t_u1, t_xnT], writes=[pp_t], inc=(kc == KC - 1))
                            op(ACT, lambda tc=tc, pp_h=pp_h: nc.scalar.copy(out=cbuf[:, 4 + tc * 512:4 + (tc + 1) * 512],
                                                                           in_=pp_h[:, 0:512]),
                               reads=[pp_t], writes=[t_cbuf])
                        for hf in range(2):
                            o = hf * 1024
                            ac_h, ac_t = accs[hf]
                            op(DVE, lambda o=o, ac_h=ac_h, chunk=chunk: nc.vector.tensor_scalar(
                                out=ac_h[:], in0=cbuf[:, 4 + o:4 + o + 1024], scalar1=cw[:, chunk, 3:4], scalar2=None,
                                op0=ALU.mult), reads=[t_cbuf, t_cw], writes=[ac_t])
                            for j in (2, 1, 0):
                                op(DVE, lambda o=o, ac_h=ac_h, chunk=chunk, j=j: nc.vector.scalar_tensor_tensor(
                                    out=ac_h[:], in0=cbuf[:, 1 + j + o:1 + j + o + 1024], scalar=cw[:, chunk, j:j + 1],
                                    in1=ac_h[:], op0=ALU.mult, op1=ALU.add), reads=[t_cbuf, t_cw, ac_t], writes=[ac_t])
                            th_h, th_t = ths[hf]
                            op(ACT, lambda ac_h=ac_h, chunk=chunk, th_h=th_h: nc.scalar.activation(
                                out=th_h[:], in_=ac_h[:], func=AF.Tanh, bias=cbh[:, chunk:chunk + 1], scale=0.5),
                               reads=[ac_t, t_cbh], writes=[th_t])
                            op(DVE, lambda th_h=th_h: nc.vector.tensor_scalar(
                                out=th_h[:], in0=th_h[:], scalar1=0.5, scalar2=0.5, op0=ALU.mult, op1=ALU.add),
                               reads=[th_t], writes=[th_t])
                            op(DVE, lambda o=o, ac_h=ac_h, chunk=chunk, dst=dst, th_h=th_h: nc.vector.scalar_tensor_tensor(
                                out=dst[:, o:o + 1024], in0=ac_h[:], scalar=cb[:, chunk:chunk + 1], in1=th_h[:],
                                op0=ALU.add, op1=ALU.mult), reads=[ac_t, t_cb, th_t], writes=[t_dst])
                    ws.release(s1i)
                    for g in range(2):
                        pt_h, pt_t = PT[g]
                        for c8 in range(8):
                            c = g * 8 + c8
                            op(PE, lambda c=c, c8=c8, pt_h=pt_h: nc.tensor.transpose(
                                out=pt_h[:, c8, :], in_=kT[:, c * 128:(c + 1) * 128], identity=ident[:]),
                               reads=[t_kT, t_ident], writes=[pt_t], inc=(c8 == 7))
                        for c8 in range(8):
                            c = g * 8 + c8
                            op(ACT, lambda c=c, c8=c8, pt_h=pt_h: nc.scalar.activation(
                                out=kTok[:, c, :], in_=pt_h[:, c8, :], func=AF.Copy, scale=aa[:, c, h:h + 1]),
                               reads=[pt_t, t_aa], writes=[t_kTok])
                    for tb in range(NT):
                        pp_h, pp_t = proj_bank()
                        for kc in range(KC):
                            op(PE, lambda kc=kc, tb=tb, pp_h=pp_h: nc.tensor.matmul(
                                pp_h[:, 0:256], lhsT=xnT[:, kc, tb * 128:(tb + 1) * 128], rhs=u2[:, kc, :],
                                start=(kc == 0), stop=(kc == KC - 1)),
                               reads=[t_u2, t_xnT], writes=[pp_t], inc=(kc == KC - 1))
                        op(ACT, lambda tb=tb, pp_h=pp_h: nc.scalar.activation(
                            out=vaug[:, tb, 0:256], in_=pp_h[:, 0:256], func=AF.Copy, scale=ew[:, tb, h:h + 1]),
                           reads=[pp_t, t_ew], writes=[t_vaug])
                    ws.release(s2i)
                    op(DVE, lambda: nc.vector.tensor_copy(out=vaug[:, :, 256:257], in_=ew[:, :, h:h + 1]),
                       reads=[t_ew], writes=[t_vaug])
                    dcb = (PB[5], PB[2])
                    for c in range(NT - 1):
                        pd_h, pd_t = dcb[c % 2]
                        op(PE, lambda c=c, pd_h=pd_h: nc.tensor.matmul(pd_h[:, 0:257], lhsT=kTok[:, c, :],
                                                                       rhs=vaug[:, c, 0:257], start=True, stop=True),
                           reads=[t_kTok, t_vaug], writes=[pd_t])
                        cn_h, cn_t = Crun[(c + 1) % 2]
                        cp_h, cp_t = Crun[c % 2]
                        if c == 0:
                            op(DVE, lambda pd_h=pd_h, cn_h=cn_h: nc.vector.tensor_copy(out=cn_h[:, 0:257], in_=pd_h[:, 0:257]),
                               reads=[pd_t], writes=[cn_t])
                        else:
                            op(DVE, lambda c=c, pd_h=pd_h, cn_h=cn_h, cp_h=cp_h: nc.vector.scalar_tensor_tensor(
                                out=cn_h[:, 0:257], in0=cp_h[:, 0:257], scalar=aa[:, c, h:h + 1], in1=pd_h[:, 0:257],
                                op0=ALU.mult, op1=ALU.add), reads=[cp_t, pd_t, t_aa], writes=[cn_t])
                        op(ACT, lambda c=c, cn_h=cn_h: nc.scalar.copy(out=Cbf[:, c + 1, 0:257], in_=cn_h[:, 0:257]),
                           reads=[cn_t], writes=[t_Cbf])
                    for c in range(NT):
                        ps_h, ps_t = PB[2]
                        op(PE, lambda c=c: nc.tensor.matmul(ps_h[:, 0:128], lhsT=kT[:, c * 128:(c + 1) * 128],
                                                            rhs=qT[:, c * 128:(c + 1) * 128], start=True, stop=True),
                           reads=[t_kT, t_qT], writes=[ps_t])
                        st_h, st_t = STm[c % 2]
                        op(DVE, lambda st_h=st_h: nc.vector.tensor_tensor(out=st_h[:], in0=ps_h[:, 0:128], in1=maskf[:],
                                                                          op=ALU.mult),
                           reads=[ps_t, t_maskf], writes=[st_t])
                        pn_h, pn_t = PB[3 + c % 2]
                        op(PE, lambda c=c, pn_h=pn_h, st_h=st_h: nc.tensor.matmul(
                            pn_h[:, 0:257], lhsT=st_h[:], rhs=vaug[:, c, 0:257], start=True, stop=(c == 0)),
                           reads=[st_t, t_vaug], writes=[pn_t], inc=(c == 0))
                        if c > 0:
                            op(PE, lambda c=c, pn_h=pn_h: nc.tensor.matmul(
                                pn_h[:, 0:257], lhsT=qT[:, c * 128:(c + 1) * 128], rhs=Cbf[:, c, 0:257],
                                start=False, stop=True), reads=[t_qT, t_Cbf], writes=[pn_t])
                        op(ACT, lambda c=c, pn_h=pn_h: nc.scalar.copy(out=numS[:, c % 4, 0:257], in_=pn_h[:, 0:257]),
                           reads=[pn_t], writes=[t_numS])
                        pp_h, pp_t = proj_bank()
                        for gi, (ug, t_ug) in enumerate(((u3, t_u3), (u4, t_u4))):
                            for kc in range(KC):
                                op(PE, lambda kc=kc, c=c, gi=gi, ug=ug, pp_h=pp_h: nc.tensor.matmul(
                                    pp_h[:, gi * 256:(gi + 1) * 256], lhsT=xnT[:, kc, c * 128:(c + 1) * 128],
                                    rhs=ug[:, kc, :], start=(kc == 0), stop=(kc == KC - 1)),
                                   reads=[t_ug, t_xnT], writes=[pp_t], inc=(kc == KC - 1 and gi == 1))
                        sg_h, sg_t = s12[c % 2]
                        op(ACT, lambda pp_h=pp_h, sg_h=sg_h: nc.scalar.activation(out=sg_h[:, 0:512], in_=pp_h[:, 0:512],
                                                                                  func=AF.Tanh, scale=0.5),
                           reads=[pp_t], writes=[sg_t])
                        op(DVE, lambda pp_h=pp_h, sg_h=sg_h: nc.vector.scalar_tensor_tensor(
                            out=sg_h[:, 256:512], in0=sg_h[:, 256:512], scalar=1.0, in1=pp_h[:, 256:512],
                            op0=ALU.add, op1=ALU.mult), reads=[sg_t, pp_t], writes=[sg_t])
                        op(DVE, lambda sg_h=sg_h: nc.vector.scalar_tensor_tensor(
                            out=sg_h[:, 0:256], in0=sg_h[:, 0:256], scalar=1.0, in1=sg_h[:, 256:512],
                            op0=ALU.add, op1=ALU.mult), reads=[sg_t], writes=[sg_t])
                        op(DVE, lambda c=c, sg_h=sg_h: nc.vector.tensor_tensor(out=Gp[:, c % 4, :], in0=sg_h[:, 0:256],
                                                                               in1=gM[:, h * 256:(h + 1) * 256], op=ALU.mult),
                           reads=[sg_t, t_gM], writes=[t_Gp])
                        if c % 4 == 3:
                            cs = c - 3
                            den = numS[:, :, 256]
                            ebs = eb[:, cs:cs + 4, h]
                            op(DVE, lambda: nc.vector.tensor_tensor(out=sm[:, 0, :], in0=den, in1=ebs, op=ALU.mult),
                               reads=[t_numS, t_eb], writes=[t_sm])
                            op(DVE, lambda: nc.vector.scalar_tensor_tensor(out=sm[:, 1, :], in0=sm[:, 0, :], scalar=-1.0,
                                                                           in1=sm[:, 0, :], op0=ALU.mult, op1=ALU.max),
                               reads=[t_sm], writes=[t_sm])
                            op(DVE, lambda: nc.vector.tensor_scalar(out=sm[:, 2, :], in0=sm[:, 1, :], scalar1=1.0,
                                                                    scalar2=None, op0=ALU.max), reads=[t_sm], writes=[t_sm])
                            op(DVE, lambda: nc.vector.reciprocal(out=sm[:, 3, :], in_=sm[:, 2, :]), reads=[t_sm], writes=[t_sm])
                            op(DVE, lambda: nc.vector.tensor_tensor(out=sm[:, 4, :], in0=sm[:, 3, :], in1=ebs, op=ALU.mult),
                               reads=[t_sm, t_eb], writes=[t_sm])
                            for c8 in range(4):
                                op(ACT, lambda c8=c8: nc.scalar.activation(out=junk[:], in_=numS[:, c8, 0:256],
                                                                           func=AF.Square, accum_out=sm[:, 5, c8:c8 + 1]),
                                   reads=[t_numS], writes=[t_junk, t_sm])
                            op(DVE, lambda: nc.vector.tensor_tensor(out=sm[:, 6, :], in0=sm[:, 4, :], in1=sm[:, 4, :],
                                                                    op=ALU.mult), reads=[t_sm], writes=[t_sm])
                            op(DVE, lambda: nc.vector.tensor_tensor(out=sm[:, 7, :], in0=sm[:, 6, :], in1=sm[:, 5, :],
                                                                    op=ALU.mult), reads=[t_sm], writes=[t_sm])
                            op(ACT, lambda: nc.scalar.activation(out=sm[:, 8, :], in_=sm[:, 7, :], func=AF.Sqrt,
                                                                 scale=1.0 / 256, bias=epsT[:]),
                               reads=[t_sm, t_eps], writes=[t_sm])
                            op(DVE, lambda: nc.vector.reciprocal(out=sm[:, 9, :], in_=sm[:, 8, :]), reads=[t_sm], writes=[t_sm])
                            op(DVE, lambda: nc.vector.scalar_tensor_tensor(out=sm[:, 10, :], in0=sm[:, 9, :], scalar=0.25,
                                                                           in1=sm[:, 4, :], op0=ALU.mult, op1=ALU.mult),
                               reads=[t_sm], writes=[t_sm])
                            for c8 in range(4):
                                hn_h, hn_t = hn[c8 % 2]
                                op(DVE, lambda c8=c8, hn_h=hn_h: nc.vector.scalar_tensor_tensor(
                                    out=hn_h[:], in0=numS[:, c8, 0:256], scalar=sm[:, 10, c8:c8 + 1], in1=Gp[:, c8, :],
                                    op0=ALU.mult, op1=ALU.mult), reads=[t_numS, t_sm, t_Gp], writes=[hn_t])
                                for j in range(2):
                                    op(PE, lambda c8=c8, j=j, hn_h=hn_h: nc.tensor.transpose(
                                        out=PT[j][0][:, c8, :], in_=hn_h[:, j * 128:(j + 1) * 128], identity=ident[:]),
                                       reads=[hn_t, t_ident], writes=[PT[j][1]])
                            for j in range(2):
                                op(ACT, lambda j=j, cs=cs: nc.scalar.copy(
                                    out=bufA[:, 2 * h + j, cs * 128:(cs + 4) * 128],
                                    in_=PT[j][0][:, 0:4, :].rearrange("p a b -> p (a b)")),
                                   reads=[PT[j][1]], writes=[t_bufA])
                    ws.release(s3i)
                    ws.release(s4i)
                kb.end_scratch_phase()
            if first:
                tap("hbT", bufA[:], [128, KC, S], BF16, [t_bufA])

            if upto == "M":
                break

            ys = es.enter_context(ExitStack())
            bufY, t_bufY = kb.scratch(ys, "bufY", [128, KC, S], BF16)
            proj_n[0] = 6
            with ExitStack() as ms:
                sgs = [kb.scratch(ms, f"d_sg{i}", [128, 512], F32) for i in range(2)]
                for oc in range(8):
                    si, u, t_u = ws.next()
                    for tc in range(4):
                        py_h, py_t = proj_bank()
                        for kc in range(KC):
                            op(PE, lambda kc=kc, tc=tc, py_h=py_h: nc.tensor.matmul(
                                py_h[:, 0:512], lhsT=u[:, kc, 0:128], rhs=bufA[:, kc, tc * 512:(tc + 1) * 512],
                                start=(kc == 0), stop=(kc == KC - 1)), reads=[t_u, t_bufA], writes=[py_t], inc=(kc == KC - 1))
                        pg_h, pg_t = proj_bank()
                        for kc in range(KC):
                            op(PE, lambda kc=kc, tc=tc, pg_h=pg_h: nc.tensor.matmul(
                                pg_h[:, 0:512], lhsT=u[:, kc, 128:256], rhs=xnT[:, kc, tc * 512:(tc + 1) * 512],
                                start=(kc == 0), stop=(kc == KC - 1)), reads=[t_u, t_xnT], writes=[pg_t], inc=(kc == KC - 1))
                        sg_h, sg_t = sgs[tc % 2]
                        op(ACT, lambda pg_h=pg_h, sg_h=sg_h, oc=oc: nc.scalar.activation(
                            out=sg_h[:], in_=pg_h[:, 0:512], func=AF.Tanh, bias=bgh[:, 8 + oc:9 + oc], scale=0.5),
                           reads=[pg_t, t_bgh], writes=[sg_t])
                        op(DVE, lambda py_h=py_h, sg_h=sg_h, oc=oc, tc=tc: nc.vector.scalar_tensor_tensor(
                            out=bufY[:, oc, tc * 512:(tc + 1) * 512], in0=sg_h[:], scalar=1.0, in1=py_h[:, 0:512],
                            op0=ALU.add, op1=ALU.mult), reads=[py_t, sg_t], writes=[t_bufY])
                    ws.release(si)
                kb.end_scratch_phase()
            if first:
                tap("ybg", bufY[:], [128, KC, S], BF16, [t_bufY])

            if upto == "MD":
                break

            proj_n[0] = 2
            with ExitStack() as ms:
                fq = [kb.scratch(ms, f"f_q{i}", [128, S], BF16) for i in range(2)]
                fk = [kb.scratch(ms, f"f_k{i}", [128, S], BF16) for i in range(2)]
                fv = [kb.scratch(ms, f"f_v{i}", [128, NT, 130], BF16) for i in range(2)]
                fzs = [kb.scratch(ms, f"f_z{i}", [128, NT, 128], BF16) for i in range(2)]
                bms = [kb.scratch(ms, f"f_bm{i}", [128, NT, NT], F32) for i in range(2)]
                ptb = [kb.scratch(ms, f"f_pt{i}", [128, 512], BF16) for i in range(NPTB)]
                obs = [kb.scratch(ms, f"f_ob{i}", [128, 4, 128], BF16) for i in range(2)]
                rin, t_rin = kb.scratch(ms, "f_rin", [128, 16], F32)
                tzs = [kb.scratch(ms, f"f_tz{i}", [128, 128], F32) for i in range(4)]
                for i in range(2):
                    op(DVE, lambda i=i: nc.vector.memset(fv[i][0][:, :, 128:129], 2.0), writes=[fv[i][1]])
                sc = float(128 ** -0.5)
                sb_i = 0
                for h in range(8):
                    s1i, u1, t_u1 = ws.next()
                    s2i, u2, t_u2 = ws.next()
                    qT, t_qT = fq[h % 2]
                    kT, t_kT = fk[h % 2]
                    V, t_V = fv[h % 2]
                    FZ, t_FZ = fzs[h % 2]
                    bm, t_bm = bms[h % 2]
                    for which, (dst, t_dst) in enumerate(((qT, t_qT), (kT, t_kT))):
                        for tc in range(4):
                            pp_h, pp_t = proj_bank()
                            for kc in range(KC):
                                op(PE, lambda kc=kc, tc=tc, which=which, pp_h=pp_h: nc.tensor.matmul(
                                    pp_h[:, 0:512], lhsT=u1[:, kc, which * 128:(which + 1) * 128],
                                    rhs=xnT[:, kc, tc * 512:(tc + 1) * 512], start=(kc == 0), stop=(kc == KC - 1)),
                                   reads=[t_u1, t_xnT], writes=[pp_t], inc=(kc == KC - 1))
                            op(DVE, lambda tc=tc, pp_h=pp_h, dst=dst: nc.vector.tensor_copy(
                                out=dst[:, tc * 512:(tc + 1) * 512], in_=pp_h[:, 0:512]), reads=[pp_t], writes=[t_dst])
                    ws.release(s1i)
                    for tb in range(NT):
                        pp_h, pp_t = proj_bank()
                        for kc in range(KC):
                            op(PE, lambda kc=kc, tb=tb, pp_h=pp_h: nc.tensor.matmul(
                                pp_h[:, 0:256], lhsT=xnT[:, kc, tb * 128:(tb + 1) * 128], rhs=u2[:, kc, :],
                                start=(kc == 0), stop=(kc == KC - 1)), reads=[t_u2, t_xnT], writes=[pp_t], inc=(kc == KC - 1))
                        op(DVE, lambda tb=tb, pp_h=pp_h: nc.vector.tensor_copy(out=V[:, tb, 0:128], in_=pp_h[:, 0:128]),
                           reads=[pp_t], writes=[t_V])
                        tz_h, tz_t = tzs[tb % 4]
                        op(ACT, lambda pp_h=pp_h, tz_h=tz_h: nc.scalar.activation(out=tz_h[:], in_=pp_h[:, 128:256],
                                                                                  func=AF.Tanh, scale=0.5),
                           reads=[pp_t], writes=[tz_t])
                        op(DVE, lambda tb=tb, pp_h=pp_h, tz_h=tz_h: nc.vector.scalar_tensor_tensor(
                            out=FZ[:, tb, :], in0=tz_h[:], scalar=1.0, in1=pp_h[:, 128:256], op0=ALU.add, op1=ALU.mult),
                           reads=[pp_t, tz_t], writes=[t_FZ])
                    ws.release(s2i)
                    for j in range(NT):
                        op(DVE, lambda j=j: nc.vector.tensor_scalar(out=bm[:, j, :], in0=RR[:, :, h],
                                                                    scalar1=cc[:, j, h:h + 1], scalar2=None,
                                                                    op0=ALU.subtract), reads=[t_RR, t_cc], writes=[t_bm])
                    steps = [(I, j) for I in range(4) for j in range(4 * I + 4)]

                    def emit_ST(n):
                        I, j = steps[n]
                        i0 = max(j, 4 * I)
                        ps_h, ps_t = PB[2 + n % 2]
                        N = (4 * I + 4 - i0) * 128
                        op(PE, lambda: nc.tensor.matmul(ps_h[:, 0:N], lhsT=kT[:, j * 128:(j + 1) * 128],
                                                        rhs=qT[:, i0 * 128:(4 * I + 4) * 128], start=True, stop=True),
                           reads=[t_kT, t_qT], writes=[ps_t])

                    def emit_exp(n):
                        I, j = steps[n]
                        i0 = max(j, 4 * I)
                        ps_h, ps_t = PB[2 + n % 2]
                        p_h, p_t = ptb[n % NPTB]
                        for i in range(i0, 4 * I + 4):
                            lo = (i - i0) * 128
                            op(ACT, lambda i=i, lo=lo: nc.scalar.activation(
                                out=p_h[:, lo:lo + 128], in_=ps_h[:, lo:lo + 128], func=AF.Exp, scale=sc,
                                bias=bm[:, j, i:i + 1]), reads=[ps_t, t_bm], writes=[p_t])
                        if j >= 4 * I:
                            op(DVE, lambda: nc.vector.tensor_tensor(out=p_h[:, 0:128], in0=p_h[:, 0:128], in1=maskb[:],
                                                                    op=ALU.mult), reads=[p_t, t_maskb], writes=[p_t])

                    def emit_PV(n):
                        I, j = steps[n]
                        i0 = max(j, 4 * I)
                        p_h, p_t = ptb[n % NPTB]
                        for i in range(i0, 4 * I + 4):
                            il = i - 4 * I
                            lo = (i - i0) * 128
                            po_h, po_t = PB[4 + il // 2]
                            off = (il % 2) * 129
                            op(PE, lambda i=i, lo=lo, po_h=po_h, off=off, il=il: nc.tensor.matmul(
                                po_h[:, off:off + 129], lhsT=p_h[:, lo:lo + 128], rhs=V[:, j, 0:129],
                                start=(j == 0 and il % 2 == 0), stop=(j == i), skip_group_check=True),
                               reads=[p_t, t_V], writes=[po_t])
                            if j == i:
                                ob_h, ob_t = obs[I % 2]
                                op(DVE, lambda po_h=po_h, off=off, i=i: nc.vector.reciprocal(
                                    out=rin[:, i:i + 1], in_=po_h[:, off + 128:off + 129]), reads=[po_t], writes=[t_rin])
                                op(DVE, lambda po_h=po_h, off=off, i=i, il=il, ob_h=ob_h: nc.vector.scalar_tensor_tensor(
                                    out=ob_h[:, il, :], in0=po_h[:, off:off + 128], scalar=rin[:, i:i + 1], in1=FZ[:, i, :],
                                    op0=ALU.mult, op1=ALU.mult), reads=[po_t, t_rin, t_FZ], writes=[ob_t])
                                pt_h, pt_t = PT[I % 2]
                                pb0 = 0
                                op(PE, lambda il=il, ob_h=ob_h, pt_h=pt_h, pb0=pb0: nc.tensor.transpose(
                                    out=pt_h[:, pb0 + il, :], in_=ob_h[:, il, :], identity=ident[:]),
                                   reads=[ob_t, t_ident], writes=[pt_t])
                                if il == 3:
                                    op(DVE, lambda pt_h=pt_h, I=I: nc.vector.tensor_copy(
                                        out=bufA[:, h, I * 512:(I + 1) * 512],
                                        in_=pt_h[:, pb0:pb0 + 4, :].rearrange("p a b -> p (a b)")),
                                       reads=[pt_t], writes=[t_bufA])

                    emit_ST(0)
                    for n in range(len(steps)):
                        emit_exp(n)
                        if n + 1 < len(steps):
                            emit_ST(n + 1)
                        emit_PV(n)
                kb.end_scratch_phase()
            if first:
                tap("oaT", bufA[:], [128, KC, S], BF16, [t_bufA])

            if upto == "F":
                break

            proj_n[0] = 6
            with ExitStack() as ms:
                sgs = [kb.scratch(ms, f"e_sg{i}", [128, 512], F32) for i in range(2)]
                for oc in range(8):
                    si, u, t_u = ws.next()
                    for tc in range(4):
                        py_h, py_t = proj_bank()
                        for kc in range(KC):
                            op(PE, lambda kc=kc, tc=tc, py_h=py_h: nc.tensor.matmul(
                                py_h[:, 0:512], lhsT=u[:, kc, 0:128], rhs=bufA[:, kc, tc * 512:(tc + 1) * 512],
                                start=(kc == 0), stop=(kc == KC - 1)), reads=[t_u, t_bufA], writes=[py_t], inc=(kc == KC - 1))
                        pg_h, pg_t = proj_bank()
                        for kc in range(KC):
                            op(PE, lambda kc=kc, tc=tc, pg_h=pg_h: nc.tensor.matmul(
                                pg_h[:, 0:512], lhsT=u[:, kc, 128:256], rhs=xnT[:, kc, tc * 512:(tc + 1) * 512],
                                start=(kc == 0), stop=(kc == KC - 1)), reads=[t_u, t_xnT], writes=[pg_t], inc=(kc == KC - 1))
                        sg_h, sg_t = sgs[tc % 2]
                        op(ACT, lambda pg_h=pg_h, sg_h=sg_h, oc=oc: nc.scalar.activation(
                            out=sg_h[:], in_=pg_h[:, 0:512], func=AF.Tanh, bias=bgh[:, oc:oc + 1], scale=0.5),
                           reads=[pg_t, t_bgh], writes=[sg_t])
                        op(DVE, lambda py_h=py_h, sg_h=sg_h: nc.vector.scalar_tensor_tensor(
                            out=sg_h[:], in0=sg_h[:], scalar=1.0, in1=py_h[:, 0:512], op0=ALU.add, op1=ALU.mult),
                           reads=[py_t, sg_t], writes=[sg_t])
                        op(DVE, lambda sg_h=sg_h, oc=oc, tc=tc: nc.vector.tensor_tensor(
                            out=bufY[:, oc, tc * 512:(tc + 1) * 512], in0=sg_h[:], in1=bufY[:, oc, tc * 512:(tc + 1) * 512],
                            op=ALU.add), reads=[sg_t, t_bufY], writes=[t_bufY])
                    ws.release(si)
                kb.end_scratch_phase()
            if first:
                tap("yT", bufY[:], [128, KC, S], BF16, [t_bufY])

            if upto == "FD":
                break

            def _phase_O(seq=seq, bufY=bufY, t_bufY=t_bufY, ys=ys):
                dma(SP, out=gX[:], in_=final_g.partition_broadcast(128), writes=[t_gX])
                us = [ws.next() for _ in range(4)]
                for tb in range(NT):
                    xt_h, xt_t = xts[tb % NXT]
                    dma(SP, out=xt_h[:], in_=x_d[seq, tb * 128:(tb + 1) * 128, :], writes=[xt_t])
                    banks = [proj_bank(), proj_bank()]
                    for hf in range(2):
                        po_h, po_t = banks[hf]
                        for n in range(2):
                            si, u, t_u = us[2 * hf + n]
                            for kc in range(KC):
                                op(PE, lambda kc=kc, n=n, u=u, po_h=po_h: nc.tensor.matmul(
                                    po_h[:, n * 256:(n + 1) * 256], lhsT=bufY[:, kc, tb * 128:(tb + 1) * 128], rhs=u[:, kc, :],
                                    start=(kc == 0), stop=(kc == KC - 1)), reads=[t_u, t_bufY], writes=[po_t],
                                   inc=(kc == KC - 1 and n == 1))
                        op(DVE, lambda hf=hf, po_h=po_h: nc.vector.scalar_tensor_tensor(
                            out=xt_h[:, hf * 512:(hf + 1) * 512], in0=po_h[:, 0:512], scalar=0.5,
                            in1=xt_h[:, hf * 512:(hf + 1) * 512], op0=ALU.mult, op1=ALU.add),
                           reads=[po_t, xt_t], writes=[xt_t])
                    xn_h, xn_t = xns[tb % NXT]
                    rmsnorm_tail(xt_h, xt_t, tb, gF, t_gF, xt_h[:], [xt_t], xn_h[:], [xn_t])
                    dma(SP, out=y_d[seq, tb * 128:(tb + 1) * 128, :], in_=xt_h[:], reads=[xt_t])
                for (si, u, t_u) in us:
                    ws.release(si)
                kb.scr_tiles.append(t_bufY)
                kb.end_scratch_phase()
                ys.close()

            pending_O = _phase_O

        if pending_O is not None:
            pending_O()
            pending_O = None
        if kb.mode == "defer":
            kb.flush()
            build.n_inst = kb.n_inst
            build.nsem = kb.nsem
            build.est_us = kb.est_makespan
        out_costs = [o.cost for o in kb.ops]
    return nc, dbg_out, out_costs


def host_inputs(inputs):
    f = lambda a: np.ascontiguousarray(np.asarray(a, dtype=np.float32))
    gb = np.concatenate([f(inputs["b_fox_f"])[0], f(inputs["b_ml_i"])[0], f(inputs["b_ml_f"])[0]])
    gbias = np.ascontiguousarray(np.broadcast_to(gb[None, None, :], (128, NT, 16)))
    convw = np.ascontiguousarray(f(inputs["conv_w"])[0].reshape(4, 8, 128).transpose(2, 1, 0))
    convb = np.ascontiguousarray(f(inputs["conv_b"])[0].reshape(8, 128).T)
    bgate = np.ascontiguousarray(f(inputs["b_gate"])[0].reshape(16, 128).T)
    return {
        "w_in": f(inputs["w_in"])[0],
        "w_fox_down": f(inputs["w_fox_down"])[0],
        "w_ml_down": f(inputs["w_ml_down"])[0],
        "w_out": f(inputs["w_out"])[0],
        "norm_g": f(inputs["norm_g"])[0],
        "final_g": f(inputs["final_g"]),
        "ml_norm_g": f(inputs["ml_norm_g"])[0],
        "gbias": gbias,
        "convw": convw,
        "convb": convb,
        "bgate": bgate,
    }


def kernel(**inputs):
    x = np.ascontiguousarray(np.asarray(inputs["x"], dtype=np.float32))
    B = x.shape[0]
    nseq = B // NCORES
    shared = host_inputs(inputs)
    nc, _ = build(nseq)
    in_maps = []
    for c in range(NCORES):
        m = dict(shared)
        m["x"] = x[c * nseq:(c + 1) * nseq]
        in_maps.append(m)
    res = run_bass_kernel_spmd(nc, in_maps, core_ids=list(range(NCORES)))
    out = np.concatenate([np.asarray(r["y"]) for r in res.results], axis=0)
    return out.astype(np.float32, copy=False)
```

```python
import types
import numpy as np
from contextlib import ExitStack
import concourse.bass as bass
import concourse.mybir as mybir
from concourse.bass_utils import run_bass_kernel_spmd

F32 = mybir.dt.float32
BF16 = mybir.dt.bfloat16
AF = mybir.ActivationFunctionType
ALU = mybir.AluOpType

S = 2048
D = 1024
NT = 16
KC = 8
NCORES = 8
EPS = 1e-6
C_FQ, C_FK, C_FV, C_FF, C_FZ = 0, 1024, 2048, 3072, 3080
C_MQ, C_MK, C_MV, C_MI, C_MF, C_MO, C_MZ = 4104, 4616, 5128, 6152, 6156, 6160, 7184
C_GA, C_GB = 8208, 9232
IN_W = 10256
SEM_LIMIT = 30000
SLACK_US = 0.0
PE_MAXG = 2
NPTB = 3
NXT = 4
MDB = 1
NCB = 2
NTH = 1
M_PROJ_BANKS = [0, 1, 5]
NSLOT = 6


class SemObj:
    __slots__ = ("h", "owner", "val")

    def __init__(self, h, owner):
        self.h = h
        self.owner = owner
        self.val = 0


class Tile:
    __slots__ = ("w", "r", "name", "psum")

    def __init__(self, name, init=None, psum=False):
        self.name = name
        self.w = list(init) if init else []
        self.r = []
        self.psum = psum


class Eng:
    def __init__(self, kb, name, h, is_pe=False):
        self.kb = kb
        self.name = name
        self.h = h
        self.is_pe = is_pe
        self.waited = {}
        self.cur = None

    def sem(self):
        if self.cur is None or self.cur.val >= SEM_LIMIT:
            self.cur = self.kb.new_sem(self)
        return self.cur


class Op:
    __slots__ = ("id", "eng", "fns", "deps", "cost", "is_dma", "tok", "start", "ninstr")

    def __init__(self, oid, eng, is_dma):
        self.id = oid
        self.eng = eng
        self.fns = []
        self.deps = []
        self.cost = 0.0
        self.is_dma = is_dma
        self.tok = None
        self.start = 0.0


def _snap(fn):
    if fn.__closure__ is None:
        return fn
    cells = []
    for c in fn.__closure__:
        try:
            cells.append(types.CellType(c.cell_contents))
        except ValueError:
            cells.append(c)
    return types.FunctionType(fn.__code__, fn.__globals__, fn.__name__, fn.__defaults__, tuple(cells))


def _free_elems(acc):
    try:
        ap = acc.ap
        n = 1
        for (_, c) in ap[1:]:
            n *= c
        return max(int(n), 1)
    except Exception:
        return 64


class KB:
    def __init__(self, nc, es, costs=None):
        self.nc = nc
        self.es = es
        self.mode = "measure" if costs is None else "defer"
        self.costs_in = costs
        self.nsem = 0
        self.PE = Eng(self, "pe", nc.tensor, is_pe=True)
        self.ACT = Eng(self, "act", nc.scalar)
        self.DVE = Eng(self, "dve", nc.vector)
        self.POOL = Eng(self, "pool", nc.gpsimd)
        self.SP = Eng(self, "sp", nc.sync)
        self.engs = [self.PE, self.ACT, self.DVE, self.POOL, self.SP]
        self.dma_pool = {}
        self.dma_rr = {}
        self.scr_tokens = []
        self.scr_tiles = []
        self.scr_id = 0
        self.n_inst = 0
        self.ops = []
        self.open_pe = None

    def new_sem(self, owner):
        self.nsem += 1
        h = self.es.enter_context(self.nc.semaphore(f"s{self.nsem}"))
        return SemObj(h, owner)

    def _deps(self, o, reads, writes):
        eng = o.eng
        ops = self.ops
        d = o.deps
        for t in reads:
            d.extend(t.w)
            if t.psum:
                d.extend(r for r in t.r if ops[r].eng is not eng)
        for t in writes:
            d.extend(t.w)
            d.extend(t.r)

    def _touch(self, oid, reads, writes):
        for t in reads:
            if not t.r or t.r[-1] != oid:
                t.r.append(oid)
        for t in writes:
            t.w = [oid]
            t.r = []

    def _cost_of(self, eng, ins):
        try:
            i = ins.ins
            n = _free_elems(i.outs[0])
            if eng.is_pe:
                f32 = False
                try:
                    f32 = (i.ins[0].dtype == F32) and not getattr(i, "is_transpose", False)
                except Exception:
                    pass
                return (max(n, 200) * (4 if f32 else 1) * 0.5 + 8.0) / 1000.0
            if eng is self.ACT:
                return 0.22 + n * 0.00083
            if eng is self.DVE:
                return 0.10 + n * 0.00104
            return 0.25 + n * 0.002
        except Exception:
            return 0.3

    def op(self, eng, fn, reads=(), writes=(), inc=True):
        if eng.is_pe and self.open_pe is not None:
            o = self.open_pe
        else:
            assert self.open_pe is None, "non-PE op recorded while a PE group is open"
            o = Op(len(self.ops), eng, False)
            self.ops.append(o)
            if eng.is_pe and not inc:
                self.open_pe = o
        if eng.is_pe:
            o.ninstr = getattr(o, "ninstr", 0) + 1
            if inc or o.ninstr >= PE_MAXG:
                self.open_pe = None
        self._deps(o, reads, writes)
        self._touch(o.id, reads, writes)
        if self.mode == "measure":
            ins = fn()
            o.cost += self._cost_of(eng, ins)
            return ins
        o.fns.append(_snap(fn))
        return None

    def dma(self, q, out, in_, reads=(), writes=()):
        assert self.open_pe is None
        o = Op(len(self.ops), q, True)
        self.ops.append(o)
        self._deps(o, reads, writes)
        self._touch(o.id, reads, writes)
        if self.mode == "measure":
            q.h.dma_start(out=out, in_=in_)
            try:
                n = 1
                for (_, c) in out.ap:
                    n *= c
            except Exception:
                n = 131072
            o.cost = 2.0 + n * 4 / 150e3
            return None
        o.fns.append(lambda: q.h.dma_start(out=out, in_=in_))
        return None

    def sb(self, name, shape, dt, es=None):
        h = (es or self.es).enter_context(self.nc.sbuf_tensor(name, shape, dt))
        return h, Tile(name)

    def scratch(self, es, name, shape, dt):
        self.scr_id += 1
        h = es.enter_context(self.nc.sbuf_tensor(f"{name}_{self.scr_id}", shape, dt))
        t = Tile(name, self.scr_tokens)
        self.scr_tiles.append(t)
        return h, t

    def end_scratch_phase(self):
        acc = set(self.scr_tokens)
        for t in self.scr_tiles:
            acc.update(t.w)
            acc.update(t.r)
        self.scr_tokens = sorted(acc)
        self.scr_tiles = []

    def schedule(self):
        ops = self.ops
        n = len(ops)
        costs = self.costs_in
        assert len(costs) == n, (len(costs), n)
        succ = [[] for _ in range(n)]
        for o in ops:
            o.deps = sorted(set(d for d in o.deps if d != o.id))
            for d in o.deps:
                succ[d].append(o.id)
        pend = {e: [] for e in self.engs}
        for o in ops:
            pend[o.eng].append(o.id)
        done = [False] * n
        fin = [0.0] * n
        efree = {e: 0.0 for e in self.engs}
        best = {e: None for e in self.engs}
        dirty = set(self.engs)
        HOP, SAME, W = 0.30, 0.06, 400
        SLACK = SLACK_US
        order = []
        INF = float("inf")
        while len(order) < n:
            for e in dirty:
                lst = pend[e]
                be, bi, bpos = INF, -1, -1
                ef = efree[e]
                lim = min(W, len(lst))
                for pos in range(lim):
                    oid = lst[pos]
                    rt = ef
                    ok = True
                    for d in ops[oid].deps:
                        if not done[d]:
                            ok = False
                            break
                        t = fin[d] + (SAME if ops[d].eng is e else HOP)
                        if t > rt:
                            rt = t
                    if ok and rt < be - SLACK:
                        be, bi, bpos = rt, oid, pos
                        if rt <= ef:
                            break
                best[e] = (be, bi, bpos) if bi >= 0 else None
            dirty.clear()
            ce, cb = None, None
            for e in self.engs:
                b = best[e]
                if b is not None and (cb is None or b[0] < cb[0] or (b[0] == cb[0] and b[1] < cb[1])):
                    ce, cb = e, b
            assert ce is not None, "scheduler deadlock"
            st, oid, pos = cb
            o = ops[oid]
            o.start = st
            if o.is_dma:
                efree[ce] = st + 0.15
                fin[oid] = st + costs[oid]
            else:
                efree[ce] = st + costs[oid]
                fin[oid] = efree[ce]
            done[oid] = True
            del pend[ce][pos]
            order.append(o)
            dirty.add(ce)
            for sidx in succ[oid]:
                dirty.add(ops[sidx].eng)
        self.est_makespan = max(fin) if fin else 0.0
        return order

    def flush(self):
        order = self.schedule()
        ops = self.ops
        for o in order:
            eng = o.eng
            for d in o.deps:
                p = ops[d]
                if eng.is_pe and p.eng is eng and not p.is_dma:
                    continue
                s, v = p.tok
                if eng.waited.get(s, 0) >= v:
                    continue
                eng.h.wait_ge(s.h, v)
                eng.waited[s] = v
                self.n_inst += 1
            if o.is_dma:
                pool = self.dma_pool.setdefault(eng, [])
                if len(pool) < 8:
                    dsem = self.new_sem(None)
                    pool.append(dsem)
                else:
                    i = self.dma_rr.get(eng, 0)
                    dsem = pool[i % len(pool)]
                    self.dma_rr[eng] = i + 1
                if dsem.val and eng.waited.get(dsem, 0) < dsem.val:
                    eng.h.wait_ge(dsem.h, dsem.val)
                    eng.waited[dsem] = dsem.val
                    self.n_inst += 1
                ins = o.fns[0]()
                dsem.val += 16
                ins.then_inc(dsem.h, 16)
                o.tok = (dsem, dsem.val)
            else:
                ins = None
                for fn in o.fns:
                    ins = fn()
                    self.n_inst += 1
                s = eng.sem()
                s.val += 1
                ins.then_inc(s.h, 1)
                o.tok = (s, s.val)
            self.n_inst += 1
        SP = self.SP
        for q, pool in self.dma_pool.items():
            for dsem in pool:
                if dsem.val and SP.waited.get(dsem, 0) < dsem.val:
                    self.nc.sync.wait_ge(dsem.h, dsem.val)
                    SP.waited[dsem] = dsem.val


class WStream:
    def __init__(self, kb, slots, plan):
        self.kb = kb
        self.slots = slots
        self.plan = plan
        self.issued = 0
        self.taken = 0
        self.free = [True] * len(slots)

    def _pump(self):
        while self.issued < len(self.plan):
            si = self.issued % len(self.slots)
            if not self.free[si]:
                break
            h, t = self.slots[si]
            for (dst, src, c0, n) in self.plan[self.issued]:
                self.kb.dma(self.kb.POOL, out=h[:, :, dst:dst + n], in_=src[:, :, c0:c0 + n], writes=[t])
            self.free[si] = False
            self.issued += 1

    def next(self):
        self._pump()
        assert self.taken < self.issued, "weight stream stalled (slot not released)"
        si = self.taken % len(self.slots)
        self.taken += 1
        h, t = self.slots[si]
        return si, h, t

    def release(self, si):
        self.free[si] = True
        self._pump()


def unit_plan(nseq, wv, wmd, wfd, wout):
    plan = []
    for _ in range(nseq):
        for h in range(4):
            plan.append([(0, wv, C_MQ + h * 128, 128), (128, wv, C_MK + h * 128, 128)])
            plan.append([(0, wv, C_MV + h * 256, 256)])
            plan.append([(0, wv, C_MO + h * 256, 256)])
            plan.append([(0, wv, C_MZ + h * 256, 256)])
        for oc in range(8):
            plan.append([(0, wmd, oc * 128, 128), (128, wv, C_GB + oc * 128, 128)])
        for h in range(8):
            plan.append([(0, wv, C_FQ + h * 128, 128), (128, wv, C_FK + h * 128, 128)])
            plan.append([(0, wv, C_FV + h * 128, 128), (128, wv, C_FZ + h * 128, 128)])
        for oc in range(8):
            plan.append([(0, wfd, oc * 128, 128), (128, wv, C_GA + oc * 128, 128)])
        for u in range(4):
            plan.append([(0, wout, u * 256, 256)])
    return plan


def build(nseq=2, dbg=None, upto=None):
    _, _, costs = _build_pass(nseq, dbg, upto, None)
    nc, dbg_out, _ = _build_pass(nseq, dbg, upto, costs)
    return nc, dbg_out


def _build_pass(nseq, dbg, upto, costs):
    dbg = dbg or set()
    nc = bass.Bass("TRN2", target_bir_lowering=False)
    x_d = nc.dram_tensor("x", [nseq, S, D], F32, kind="ExternalInput").ap()
    w_in = nc.dram_tensor("w_in", [D, IN_W], F32, kind="ExternalInput").ap()
    w_fd = nc.dram_tensor("w_fox_down", [D, D], F32, kind="ExternalInput").ap()
    w_md = nc.dram_tensor("w_ml_down", [D, D], F32, kind="ExternalInput").ap()
    w_o = nc.dram_tensor("w_out", [D, D], F32, kind="ExternalInput").ap()
    norm_g = nc.dram_tensor("norm_g", [D], F32, kind="ExternalInput").ap()
    final_g = nc.dram_tensor("final_g", [D], F32, kind="ExternalInput").ap()
    ml_g = nc.dram_tensor("ml_norm_g", [D], F32, kind="ExternalInput").ap()
    gbias_d = nc.dram_tensor("gbias", [128, NT, 16], F32, kind="ExternalInput").ap()
    convw_d = nc.dram_tensor("convw", [128, 8, 4], F32, kind="ExternalInput").ap()
    convb_d = nc.dram_tensor("convb", [128, 8], F32, kind="ExternalInput").ap()
    bgate_d = nc.dram_tensor("bgate", [128, 16], F32, kind="ExternalInput").ap()
    y_d = nc.dram_tensor("y", [nseq, S, D], F32, kind="ExternalOutput").ap()
    dbg_out = {}

    def dbg_tensor(name, shape, dt):
        dbg_out[name] = nc.dram_tensor("dbg_" + name, shape, dt, kind="ExternalOutput").ap()
        return dbg_out[name]

    wv = w_in.rearrange("(kc p) c -> p kc c", p=128)
    wfd = w_fd.rearrange("(kc p) c -> p kc c", p=128)
    wmd = w_md.rearrange("(kc p) c -> p kc c", p=128)
    wout = w_o.rearrange("(kc p) c -> p kc c", p=128)

    with ExitStack() as es:
        kb = KB(nc, es, costs)
        PE, ACT, DVE, POOL, SP = kb.PE, kb.ACT, kb.DVE, kb.POOL, kb.SP
        op, dma = kb.op, kb.dma

        xnT, t_xnT = kb.sb("xnT", [128, KC, S], BF16)
        bufA, t_bufA = kb.sb("bufA", [128, KC, S], BF16)
        slots = [kb.sb(f"wslot{i}", [128, KC, 256], BF16) for i in range(NSLOT)]
        wg, t_wg = kb.sb("wg", [128, KC, 16], BF16)
        gX, t_gX = kb.sb("gX", [128, D], F32)
        gN, t_gN = kb.sb("gN", [128, D], F32)
        gF = gM = gX
        t_gF = t_gM = t_gX
        gbias, t_gbias = kb.sb("gbias_sb", [128, NT, 16], F32)
        cw, t_cw = kb.sb("cw", [128, 8, 4], F32)
        cb, t_cb = kb.sb("cb", [128, 8], F32)
        bg, t_bg = kb.sb("bg", [128, 16], F32)
        cbh, t_cbh = kb.sb("cbh", [128, 8], F32)
        bgh, t_bgh = kb.sb("bgh", [128, 16], F32)
        epsT, t_eps = kb.sb("epsT", [128, 1], F32)
        ident, t_ident = kb.sb("ident", [128, 128], BF16)
        onesb, t_onesb = kb.sb("onesb", [128, 128], BF16)
        maskb, t_maskb = kb.sb("maskb", [128, 128], BF16)
        maskf, t_maskf = kb.sb("maskf", [128, 128], F32)
        onesf, t_onesf = kb.sb("onesf", [128, 128], F32)
        hmid, t_hmid = kb.sb("hmid", [128, 128], F32)
        xts = [kb.sb(f"xt{i}", [128, D], F32) for i in range(NXT)]
        xns = [kb.sb(f"xn{i}", [128, D], BF16) for i in range(NXT)]
        stats = [kb.sb(f"stat{i}", [128, 4], F32) for i in range(NXT)]
        zs, t_zs = kb.sb("zs", [128, NT, 16], F32)
        lf, t_lf = kb.sb("lf", [128, NT, 16], F32)
        tot, t_tot = kb.sb("tot", [128, NT, 16], F32)
        bc, t_bc = kb.sb("bc", [128, NT, 16], F32)
        half, t_half = kb.sb("half", [128, NT, 16], F32)
        pre, t_pre = kb.sb("pre", [128, NT, 16], F32)
        cc, t_cc = kb.sb("cc", [128, NT, 16], F32)
        RR, t_RR = kb.sb("RR", [128, NT, 16], F32)
        ew, t_ew = kb.sb("ew", [128, NT, 4], F32)
        eb, t_eb = kb.sb("eb", [128, NT, 4], F32)
        aa, t_aa = kb.sb("aa", [128, NT, 4], F32)
        gtmp, t_gtmp = kb.sb("gtmp", [128, NT, 16], F32)

        PB = []
        for i in range(6):
            h = es.enter_context(nc.psum_tensor(f"pb{i}", [128, 512], F32))
            PB.append((h, Tile(f"pb{i}", psum=True)))
        PT = []
        for i in range(2):
            h = es.enter_context(nc.psum_tensor(f"pt{i}", [128, 8, 128], BF16))
            PT.append((h, Tile(f"pt{i}", psum=True)))

        dma(SP, out=gbias[:], in_=gbias_d[:, :, :], writes=[t_gbias])
        dma(SP, out=cw[:], in_=convw_d[:, :, :], writes=[t_cw])
        dma(SP, out=cb[:], in_=convb_d[:, :], writes=[t_cb])
        dma(SP, out=bg[:], in_=bgate_d[:, :], writes=[t_bg])
        dma(POOL, out=wg[:, :, 0:8], in_=wv[:, :, C_FF:C_FF + 8], writes=[t_wg])
        dma(POOL, out=wg[:, :, 8:16], in_=wv[:, :, C_MI:C_MI + 8], writes=[t_wg])
        op(DVE, lambda: nc.vector.memset(epsT[:], EPS), writes=[t_eps])
        op(DVE, lambda: nc.vector.tensor_scalar(out=cbh[:], in0=cb[:], scalar1=0.5, scalar2=None, op0=ALU.mult),
           reads=[t_cb], writes=[t_cbh])
        op(DVE, lambda: nc.vector.tensor_scalar(out=bgh[:], in0=bg[:], scalar1=0.5, scalar2=None, op0=ALU.mult),
           reads=[t_bg], writes=[t_bgh])
        op(POOL, lambda: nc.gpsimd.memset(onesb[:], 1.0), writes=[t_onesb])
        op(POOL, lambda: nc.gpsimd.memset(onesf[:], 1.0), writes=[t_onesf])
        op(POOL, lambda: nc.gpsimd.memset(hmid[:], 0.0), writes=[t_hmid])
        op(POOL, lambda: nc.gpsimd.memset(hmid[0:64, :], 1.0), writes=[t_hmid])
        op(POOL, lambda: nc.gpsimd.affine_select(out=ident[:], in_=onesb[:], pattern=[[1, 128]],
                                                 compare_op=ALU.is_equal, fill=0.0, base=0,
                                                 channel_multiplier=-1), reads=[t_onesb], writes=[t_ident])
        op(POOL, lambda: nc.gpsimd.affine_select(out=maskb[:], in_=onesb[:], pattern=[[1, 128]],
                                                 compare_op=ALU.is_ge, fill=0.0, base=0,
                                                 channel_multiplier=-1), reads=[t_onesb], writes=[t_maskb])
        op(POOL, lambda: nc.gpsimd.affine_select(out=maskf[:], in_=onesf[:], pattern=[[1, 128]],
                                                 compare_op=ALU.is_ge, fill=0.0, base=0,
                                                 channel_multiplier=-1), reads=[t_onesf], writes=[t_maskf])

        ws = WStream(kb, slots, unit_plan(nseq, wv, wmd, wfd, wout))
        proj_rr = [0]
        proj_n = [2]

        proj_idx = [None]

        def proj_bank():
            proj_rr[0] += 1
            if proj_idx[0] is not None:
                return PB[proj_idx[0][proj_rr[0] % len(proj_idx[0])]]
            return PB[proj_rr[0] % proj_n[0]]

        def tap(name, src_ap, shape, dt, tiles):
            if name in dbg:
                d = dbg_tensor(name, shape, dt)
                dma(SP, out=d, in_=src_ap, reads=tiles)

        def rmsnorm_tail(xt_h, xt_t, si, g_h, g_t, out_ap, out_tiles, sq_out_ap, sq_out_tiles):
            st_h, st_t = stats[si % NXT]
            op(ACT, lambda: nc.scalar.activation(out=sq_out_ap, in_=xt_h[:], func=AF.Square,
                                                 accum_out=st_h[:, 0:1]),
               reads=[xt_t], writes=sq_out_tiles + [st_t])
            op(ACT, lambda: nc.scalar.activation(out=st_h[:, 1:2], in_=st_h[:, 0:1], func=AF.Sqrt,
                                                 scale=1.0 / D, bias=epsT[:]),
               reads=[st_t, t_eps], writes=[st_t])
            op(DVE, lambda: nc.vector.reciprocal(out=st_h[:, 2:3], in_=st_h[:, 1:2]), reads=[st_t], writes=[st_t])
            op(DVE, lambda: nc.vector.scalar_tensor_tensor(out=out_ap, in0=xt_h[:], scalar=st_h[:, 2:3],
                                                           in1=g_h[:], op0=ALU.mult, op1=ALU.mult),
               reads=[xt_t, st_t, g_t], writes=out_tiles)

        pending_O = None
        for seq in range(nseq):
            first = (seq == 0)
            proj_n[0] = 2
            if first:
                dma(SP, out=gN[:], in_=norm_g.partition_broadcast(128), writes=[t_gN])
            for tb in range(NT):
                xt_h, xt_t = xts[tb % NXT]
                xn_h, xn_t = xns[tb % NXT]
                dma(SP, out=xt_h[:], in_=x_d[seq, tb * 128:(tb + 1) * 128, :], writes=[xt_t])
                rmsnorm_tail(xt_h, xt_t, tb, gN, t_gN, xn_h[:], [xn_t], xn_h[:], [xn_t])
                pt_h, pt_t = PT[tb % 2]
                for kc in range(KC):
                    op(PE, lambda kc=kc: nc.tensor.transpose(out=pt_h[:, kc, :], in_=xn_h[:, kc * 128:(kc + 1) * 128],
                                                             identity=ident[:]),
                       reads=[xn_t, t_ident], writes=[pt_t], inc=(kc == KC - 1))
                op(ACT, lambda: nc.scalar.copy(out=xnT[:, :, tb * 128:(tb + 1) * 128], in_=pt_h[:]),
                   reads=[pt_t], writes=[t_xnT])
            if first:
                tap("xnT", xnT[:], [128, KC, S], BF16, [t_xnT])

            if upto == "P0":
                break

            pz_h, pz_t = PB[2]
            for tb in range(NT):
                for kc in range(KC):
                    op(PE, lambda kc=kc, tb=tb: nc.tensor.matmul(pz_h[:, tb * 16:(tb + 1) * 16],
                                                                 lhsT=xnT[:, kc, tb * 128:(tb + 1) * 128],
                                                                 rhs=wg[:, kc, :], start=(kc == 0), stop=(kc == KC - 1)),
                       reads=[t_xnT, t_wg], writes=[pz_t], inc=(kc == KC - 1 and tb == NT - 1))
            op(DVE, lambda: nc.vector.tensor_tensor(out=zs[:].rearrange("p a b -> p (a b)"), in0=pz_h[:, 0:256],
                                                    in1=gbias[:].rearrange("p a b -> p (a b)"), op=ALU.add),
               reads=[pz_t, t_gbias], writes=[t_zs])
            if upto == "G1":
                tap("lf", zs[:], [128, NT, 16], F32, [t_zs])
                break
            op(ACT, lambda: nc.scalar.activation(out=gtmp[:], in_=zs[:], func=AF.Exp, scale=-1.0),
               reads=[t_zs], writes=[t_gtmp])
            op(ACT, lambda: nc.scalar.activation(out=gtmp[:], in_=gtmp[:], func=AF.Ln, bias=1.0, scale=1.0),
               reads=[t_gtmp], writes=[t_gtmp])
            op(DVE, lambda: nc.vector.tensor_scalar(out=lf[:], in0=gtmp[:], scalar1=-1.0, scalar2=None, op0=ALU.mult),
               reads=[t_gtmp], writes=[t_lf])
            if upto == "G2":
                tap("lf", lf[:], [128, NT, 16], F32, [t_lf])
                break
            lf2 = lf[:].rearrange("p a b -> p (a b)")
            for (lhs_h, lhs_t, dst_h, dst_t, pbi) in ((onesf, t_onesf, tot, t_tot, 3), (maskf, t_maskf, bc, t_bc, 4),
                                                       (hmid, t_hmid, half, t_half, 5)):
                pp_h, pp_t = PB[pbi]
                op(PE, lambda lhs_h=lhs_h, pp_h=pp_h: nc.tensor.matmul(pp_h[:, 0:256], lhsT=lhs_h[:], rhs=lf2,
                                                                       start=True, stop=True),
                   reads=[lhs_t, t_lf], writes=[pp_t])
                op(DVE, lambda dst_h=dst_h, pp_h=pp_h: nc.vector.tensor_copy(out=dst_h[:].rearrange("p a b -> p (a b)"),
                                                                             in_=pp_h[:, 0:256]),
                   reads=[pp_t], writes=[dst_t])
            if upto == "G3":
                tap("lf", lf[:], [128, NT, 16], F32, [t_lf])
                tap("cc", bc[:], [128, NT, 16], F32, [t_bc])
                break
            op(DVE, lambda: nc.vector.memset(pre[:, 0, :], 0.0), writes=[t_pre])
            for tb in range(1, NT):
                op(DVE, lambda tb=tb: nc.vector.tensor_tensor(out=pre[:, tb, :], in0=pre[:, tb - 1, :],
                                                              in1=tot[:, tb - 1, :], op=ALU.add),
                   reads=[t_pre, t_tot], writes=[t_pre])
            op(DVE, lambda: nc.vector.tensor_tensor(out=cc[:], in0=bc[:], in1=pre[:], op=ALU.add),
               reads=[t_bc, t_pre], writes=[t_cc])
            op(DVE, lambda: nc.vector.tensor_tensor(out=RR[:], in0=half[:], in1=pre[:], op=ALU.add),
               reads=[t_half, t_pre], writes=[t_RR])
            if upto == "G4":
                tap("cc", cc[:], [128, NT, 16], F32, [t_cc])
                break
            op(DVE, lambda: nc.vector.tensor_tensor(out=ew[:], in0=zs[:, :, 8:12], in1=bc[:, :, 12:16], op=ALU.subtract),
               reads=[t_zs, t_bc], writes=[t_ew])
            if upto == "G5":
                tap("cc", cc[:], [128, NT, 16], F32, [t_cc, t_ew])
                break
            op(ACT, lambda: nc.scalar.activation(out=ew[:], in_=ew[:], func=AF.Exp), reads=[t_ew], writes=[t_ew])
            if upto == "G6":
                tap("cc", cc[:], [128, NT, 16], F32, [t_cc, t_ew])
                break
            op(ACT, lambda: nc.scalar.activation(out=eb[:], in_=bc[:, :, 12:16], func=AF.Exp), reads=[t_bc], writes=[t_eb])
            op(DVE, lambda: nc.vector.tensor_scalar(out=eb[:], in0=eb[:], scalar1=float(128 ** -0.5), scalar2=None,
                                                    op0=ALU.mult), reads=[t_eb], writes=[t_eb])
            if upto == "G7":
                tap("cc", cc[:], [128, NT, 16], F32, [t_cc, t_ew, t_eb])
                break
            op(ACT, lambda: nc.scalar.activation(out=aa[:], in_=tot[:, :, 12:16], func=AF.Exp), reads=[t_tot], writes=[t_aa])
            if first:
                tap("lf", lf[:], [128, NT, 16], F32, [t_lf])
                tap("cc", cc[:], [128, NT, 16], F32, [t_cc])
                tap("RR", RR[:], [128, NT, 16], F32, [t_RR])
                tap("ew", ew[:], [128, NT, 4], F32, [t_ew])
                tap("eb", eb[:], [128, NT, 4], F32, [t_eb])
                tap("aa", aa[:], [128, NT, 4], F32, [t_aa])

            if upto == "G":
                break

            if pending_O is not None:
                pending_O()
                pending_O = None
            proj_idx[0] = M_PROJ_BANKS
            dma(SP, out=gX[:], in_=ml_g.partition_broadcast(128), writes=[t_gX])
            with ExitStack() as ms:
                mq2 = [kb.scratch(ms, f"m_qT{i}", [128, S], BF16) for i in range(MDB)]
                mk2 = [kb.scratch(ms, f"m_kT{i}", [128, S], BF16) for i in range(MDB)]
                mc2 = [kb.scratch(ms, f"m_cbuf{i}", [128, S + 4], F32) for i in range(NCB)]
                accs = [kb.scratch(ms, f"m_acc{i}", [128, 1024], F32) for i in range(2)]
                ths = [kb.scratch(ms, f"m_th{i}", [128, 1024], F32) for i in range(NTH)]
                kTok, t_kTok = kb.scratch(ms, "m_kTok", [128, NT, 128], BF16)
                vaug, t_vaug = kb.scratch(ms, "m_vaug", [128, NT, 258], BF16)
                Crun = [kb.scratch(ms, f"m_C{i}", [128, 258], F32) for i in range(2)]
                Cbf, t_Cbf = kb.scratch(ms, "m_Cbf", [128, NT, 258], BF16)
                numS, t_numS = kb.scratch(ms, "m_numS", [128, 4, 258], F32)
                Gp, t_Gp = kb.scratch(ms, "m_Gp", [128, 4, 256], BF16)
                STm = [kb.scratch(ms, f"m_STm{i}", [128, 128], BF16) for i in range(2)]
                hn = [kb.scratch(ms, f"m_hn{i}", [128, 256], BF16) for i in range(2)]
                s12 = [kb.scratch(ms, f"m_s12{i}", [128, 512], F32) for i in range(2)]
                sm, t_sm = kb.scratch(ms, "m_sm", [128, 12, 4], F32)
                junk, t_junk = kb.scratch(ms, "m_junk", [128, 256], BF16)
                for (cb_h, cb_t) in mc2:
                    op(DVE, lambda cb_h=cb_h: nc.vector.memset(cb_h[:, 0:4], 0.0), writes=[cb_t])
                for h in range(4):
                    qT, t_qT = mq2[h % MDB]
                    kT, t_kT = mk2[h % MDB]
                    s1i, u1, t_u1 = ws.next()
                    s2i, u2, t_u2 = ws.next()
                    s3i, u3, t_u3 = ws.next()
                    s4i, u4, t_u4 = ws.next()
                    for which, (dst, t_dst) in enumerate(((qT, t_qT), (kT, t_kT))):
                        chunk = which * 4 + h
                        cbuf, t_cbuf = mc2[which % len(mc2)]
                        for tc in range(4):
                            pp_h, pp_t = proj_bank()
                            for kc in range(KC):
                                op(PE, lambda kc=kc, tc=tc, which=which, pp_h=pp_h: nc.tensor.matmul(
                                    pp_h[:, 0:512], lhsT=u1[:, kc, which * 128:(which + 1) * 128],
                                    rhs=xnT[:, kc, tc * 512:(tc + 1) * 512], start=(kc == 0), stop=(kc == KC - 1)),
                                   reads=[t_u1, t_xnT], writes=[pp_t], inc=(kc == KC - 1))
                            op(ACT, lambda tc=tc, pp_h=pp_h: nc.scalar.copy(out=cbuf[:, 4 + tc * 512:4 + (tc + 1) * 512],
                                                                           in_=pp_h[:, 0:512]),
                               reads=[pp_t], writes=[t_cbuf])
                        for hf in range(2):
                            o = hf * 1024
                            ac_h, ac_t = accs[hf]
                            op(DVE, lambda o=o, ac_h=ac_h, chunk=chunk: nc.vector.tensor_scalar(
                                out=ac_h[:], in0=cbuf[:, 4 + o:4 + o + 1024], scalar1=cw[:, chunk, 3:4], scalar2=None,
                                op0=ALU.mult), reads=[t_cbuf, t_cw], writes=[ac_t])
                            for j in (2, 1, 0):
                                op(DVE, lambda o=o, ac_h=ac_h, chunk=chunk, j=j: nc.vector.scalar_tensor_tensor(
                                    out=ac_h[:], in0=cbuf[:, 1 + j + o:1 + j + o + 1024], scalar=cw[:, chunk, j:j + 1],
                                    in1=ac_h[:], op0=ALU.mult, op1=ALU.add), reads=[t_cbuf, t_cw, ac_t], writes=[ac_t])
                            th_h, th_t = ths[hf % NTH]
                            op(ACT, lambda ac_h=ac_h, chunk=chunk, th_h=th_h: nc.scalar.activation(
                                out=th_h[:], in_=ac_h[:], func=AF.Tanh, bias=cbh[:, chunk:chunk + 1], scale=0.5),
                               reads=[ac_t, t_cbh], writes=[th_t])
                            op(DVE, lambda th_h=th_h: nc.vector.tensor_scalar(
                                out=th_h[:], in0=th_h[:], scalar1=0.5, scalar2=0.5, op0=ALU.mult, op1=ALU.add),
                               reads=[th_t], writes=[th_t])
                            op(DVE, lambda o=o, ac_h=ac_h, chunk=chunk, dst=dst, th_h=th_h: nc.vector.scalar_tensor_tensor(
                                out=dst[:, o:o + 1024], in0=ac_h[:], scalar=cb[:, chunk:chunk + 1], in1=th_h[:],
                                op0=ALU.add, op1=ALU.mult), reads=[ac_t, t_cb, th_t], writes=[t_dst])
                    ws.release(s1i)
                    for g in range(2):
                        pt_h, pt_t = PT[g]
                        for c8 in range(8):
                            c = g * 8 + c8
                            op(PE, lambda c=c, c8=c8, pt_h=pt_h: nc.tensor.transpose(
                                out=pt_h[:, c8, :], in_=kT[:, c * 128:(c + 1) * 128], identity=ident[:]),
                               reads=[t_kT, t_ident], writes=[pt_t], inc=(c8 == 7))
                        for c8 in range(8):
                            c = g * 8 + c8
                            op(ACT, lambda c=c, c8=c8, pt_h=pt_h: nc.scalar.activation(
                                out=kTok[:, c, :], in_=pt_h[:, c8, :], func=AF.Copy, scale=aa[:, c, h:h + 1]),
                               reads=[pt_t, t_aa], writes=[t_kTok])
                    for tb in range(NT):
                        pp_h, pp_t = proj_bank()
                        for kc in range(KC):
                            op(PE, lambda kc=kc, tb=tb, pp_h=pp_h: nc.tensor.matmul(
                                pp_h[:, 0:256], lhsT=xnT[:, kc, tb * 128:(tb + 1) * 128], rhs=u2[:, kc, :],
                                start=(kc == 0), stop=(kc == KC - 1)),
                               reads=[t_u2, t_xnT], writes=[pp_t], inc=(kc == KC - 1))
                        op(ACT, lambda tb=tb, pp_h=pp_h: nc.scalar.activation(
                            out=vaug[:, tb, 0:256], in_=pp_h[:, 0:256], func=AF.Copy, scale=ew[:, tb, h:h + 1]),
                           reads=[pp_t, t_ew], writes=[t_vaug])
                    ws.release(s2i)
                    op(DVE, lambda: nc.vector.tensor_copy(out=vaug[:, :, 256:257], in_=ew[:, :, h:h + 1]),
                       reads=[t_ew], writes=[t_vaug])
                    dcb = (PB[5], PB[2])
                    for c in range(NT - 1):
                        pd_h, pd_t = dcb[c % 2]
                        op(PE, lambda c=c, pd_h=pd_h: nc.tensor.matmul(pd_h[:, 0:257], lhsT=kTok[:, c, :],
                                                                       rhs=vaug[:, c, 0:257], start=True, stop=True),
                           reads=[t_kTok, t_vaug], writes=[pd_t])
                        cn_h, cn_t = Crun[(c + 1) % 2]
                        cp_h, cp_t = Crun[c % 2]
                        if c == 0:
                            op(DVE, lambda pd_h=pd_h, cn_h=cn_h: nc.vector.tensor_copy(out=cn_h[:, 0:257], in_=pd_h[:, 0:257]),
                               reads=[pd_t], writes=[cn_t])
                        else:
                            op(DVE, lambda c=c, pd_h=pd_h, cn_h=cn_h, cp_h=cp_h: nc.vector.scalar_tensor_tensor(
                                out=cn_h[:, 0:257], in0=cp_h[:, 0:257], scalar=aa[:, c, h:h + 1], in1=pd_h[:, 0:257],
                                op0=ALU.mult, op1=ALU.add), reads=[cp_t, pd_t, t_aa], writes=[cn_t])
                        op(ACT, lambda c=c, cn_h=cn_h: nc.scalar.copy(out=Cbf[:, c + 1, 0:257], in_=cn_h[:, 0:257]),
                           reads=[cn_t], writes=[t_Cbf])
                    for c in range(NT):
                        ps_h, ps_t = PB[2]
                        op(PE, lambda c=c: nc.tensor.matmul(ps_h[:, 0:128], lhsT=kT[:, c * 128:(c + 1) * 128],
                                                            rhs=qT[:, c * 128:(c + 1) * 128], start=True, stop=True),
                           reads=[t_kT, t_qT], writes=[ps_t])
                        st_h, st_t = STm[c % 2]
                        op(DVE, lambda st_h=st_h: nc.vector.tensor_tensor(out=st_h[:], in0=ps_h[:, 0:128], in1=maskf[:],
                                                                          op=ALU.mult),
                           reads=[ps_t, t_maskf], writes=[st_t])
                        pn_h, pn_t = PB[3 + c % 2]
                        op(PE, lambda c=c, pn_h=pn_h, st_h=st_h: nc.tensor.matmul(
                            pn_h[:, 0:257], lhsT=st_h[:], rhs=vaug[:, c, 0:257], start=True, stop=(c == 0)),
                           reads=[st_t, t_vaug], writes=[pn_t], inc=(c == 0))
                        if c > 0:
                            op(PE, lambda c=c, pn_h=pn_h: nc.tensor.matmul(
                                pn_h[:, 0:257], lhsT=qT[:, c * 128:(c + 1) * 128], rhs=Cbf[:, c, 0:257],
                                start=False, stop=True), reads=[t_qT, t_Cbf], writes=[pn_t])
                        op(ACT, lambda c=c, pn_h=pn_h: nc.scalar.copy(out=numS[:, c % 4, 0:257], in_=pn_h[:, 0:257]),
                           reads=[pn_t], writes=[t_numS])
                        pp_h, pp_t = proj_bank()
                        for gi, (ug, t_ug) in enumerate(((u3, t_u3), (u4, t_u4))):
                            for kc in range(KC):
                                op(PE, lambda kc=kc, c=c, gi=gi, ug=ug, pp_h=pp_h: nc.tensor.matmul(
                                    pp_h[:, gi * 256:(gi + 1) * 256], lhsT=xnT[:, kc, c * 128:(c + 1) * 128],
                                    rhs=ug[:, kc, :], start=(kc == 0), stop=(kc == KC - 1)),
                                   reads=[t_ug, t_xnT], writes=[pp_t], inc=(kc == KC - 1 and gi == 1))
                        sg_h, sg_t = s12[c % 2]
                        op(ACT, lambda pp_h=pp_h, sg_h=sg_h: nc.scalar.activation(out=sg_h[:, 0:512], in_=pp_h[:, 0:512],
                                                                                  func=AF.Tanh, scale=0.5),
                           reads=[pp_t], writes=[sg_t])
                        op(DVE, lambda pp_h=pp_h, sg_h=sg_h: nc.vector.scalar_tensor_tensor(
                            out=sg_h[:, 256:512], in0=sg_h[:, 256:512], scalar=1.0, in1=pp_h[:, 256:512],
                            op0=ALU.add, op1=ALU.mult), reads=[sg_t, pp_t], writes=[sg_t])
                        op(DVE, lambda sg_h=sg_h: nc.vector.scalar_tensor_tensor(
                            out=sg_h[:, 0:256], in0=sg_h[:, 0:256], scalar=1.0, in1=sg_h[:, 256:512],
                            op0=ALU.add, op1=ALU.mult), reads=[sg_t], writes=[sg_t])
                        op(DVE, lambda c=c, sg_h=sg_h: nc.vector.tensor_tensor(out=Gp[:, c % 4, :], in0=sg_h[:, 0:256],
                                                                               in1=gM[:, h * 256:(h + 1) * 256], op=ALU.mult),
                           reads=[sg_t, t_gM], writes=[t_Gp])
                        if c % 4 == 3:
                            cs = c - 3
                            den = numS[:, :, 256]
                            ebs = eb[:, cs:cs + 4, h]
                            op(DVE, lambda: nc.vector.tensor_tensor(out=sm[:, 0, :], in0=den, in1=ebs, op=ALU.mult),
                               reads=[t_numS, t_eb], writes=[t_sm])
                            op(DVE, lambda: nc.vector.scalar_tensor_tensor(out=sm[:, 1, :], in0=sm[:, 0, :], scalar=-1.0,
                                                                           in1=sm[:, 0, :], op0=ALU.mult, op1=ALU.max),
                               reads=[t_sm], writes=[t_sm])
                            op(DVE, lambda: nc.vector.tensor_scalar(out=sm[:, 2, :], in0=sm[:, 1, :], scalar1=1.0,
                                                                    scalar2=None, op0=ALU.max), reads=[t_sm], writes=[t_sm])
                            op(DVE, lambda: nc.vector.reciprocal(out=sm[:, 3, :], in_=sm[:, 2, :]), reads=[t_sm], writes=[t_sm])
                            op(DVE, lambda: nc.vector.tensor_tensor(out=sm[:, 4, :], in0=sm[:, 3, :], in1=ebs, op=ALU.mult),
                               reads=[t_sm, t_eb], writes=[t_sm])
                            for c8 in range(4):
                                op(ACT, lambda c8=c8: nc.scalar.activation(out=junk[:], in_=numS[:, c8, 0:256],
                                                                           func=AF.Square, accum_out=sm[:, 5, c8:c8 + 1]),
                                   reads=[t_numS], writes=[t_junk, t_sm])
                            op(DVE, lambda: nc.vector.tensor_tensor(out=sm[:, 6, :], in0=sm[:, 4, :], in1=sm[:, 4, :],
                                                                    op=ALU.mult), reads=[t_sm], writes=[t_sm])
                            op(DVE, lambda: nc.vector.tensor_tensor(out=sm[:, 7, :], in0=sm[:, 6, :], in1=sm[:, 5, :],
                                                                    op=ALU.mult), reads=[t_sm], writes=[t_sm])
                            op(ACT, lambda: nc.scalar.activation(out=sm[:, 8, :], in_=sm[:, 7, :], func=AF.Sqrt,
                                                                 scale=1.0 / 256, bias=epsT[:]),
                               reads=[t_sm, t_eps], writes=[t_sm])
                            op(DVE, lambda: nc.vector.reciprocal(out=sm[:, 9, :], in_=sm[:, 8, :]), reads=[t_sm], writes=[t_sm])
                            op(DVE, lambda: nc.vector.scalar_tensor_tensor(out=sm[:, 10, :], in0=sm[:, 9, :], scalar=0.25,
                                                                           in1=sm[:, 4, :], op0=ALU.mult, op1=ALU.mult),
                               reads=[t_sm], writes=[t_sm])
                            for c8 in range(4):
                                hn_h, hn_t = hn[c8 % 2]
                                op(DVE, lambda c8=c8, hn_h=hn_h: nc.vector.scalar_tensor_tensor(
                                    out=hn_h[:], in0=numS[:, c8, 0:256], scalar=sm[:, 10, c8:c8 + 1], in1=Gp[:, c8, :],
                                    op0=ALU.mult, op1=ALU.mult), reads=[t_numS, t_sm, t_Gp], writes=[hn_t])
                                for j in range(2):
                                    op(PE, lambda c8=c8, j=j, hn_h=hn_h: nc.tensor.transpose(
                                        out=PT[j][0][:, c8, :], in_=hn_h[:, j * 128:(j + 1) * 128], identity=ident[:]),
                                       reads=[hn_t, t_ident], writes=[PT[j][1]])
                            for j in range(2):
                                op(ACT, lambda j=j, cs=cs: nc.scalar.copy(
                                    out=bufA[:, 2 * h + j, cs * 128:(cs + 4) * 128],
                                    in_=PT[j][0][:, 0:4, :].rearrange("p a b -> p (a b)")),
                                   reads=[PT[j][1]], writes=[t_bufA])
                    ws.release(s3i)
                    ws.release(s4i)
                kb.end_scratch_phase()
            if first:
                tap("hbT", bufA[:], [128, KC, S], BF16, [t_bufA])

            if upto == "M":
                break

            ys = es.enter_context(ExitStack())
            bufY, t_bufY = kb.scratch(ys, "bufY", [128, KC, S], BF16)
            proj_n[0] = 6
            proj_idx[0] = None
            with ExitStack() as ms:
                sgs = [kb.scratch(ms, f"d_sg{i}", [128, 512], F32) for i in range(2)]
                for oc in range(8):
                    si, u, t_u = ws.next()
                    for tc in range(4):
                        py_h, py_t = proj_bank()
                        for kc in range(KC):
                            op(PE, lambda kc=kc, tc=tc, py_h=py_h: nc.tensor.matmul(
                                py_h[:, 0:512], lhsT=u[:, kc, 0:128], rhs=bufA[:, kc, tc * 512:(tc + 1) * 512],
                                start=(kc == 0), stop=(kc == KC - 1)), reads=[t_u, t_bufA], writes=[py_t], inc=(kc == KC - 1))
                        pg_h, pg_t = proj_bank()
                        for kc in range(KC):
                            op(PE, lambda kc=kc, tc=tc, pg_h=pg_h: nc.tensor.matmul(
                                pg_h[:, 0:512], lhsT=u[:, kc, 128:256], rhs=xnT[:, kc, tc * 512:(tc + 1) * 512],
                                start=(kc == 0), stop=(kc == KC - 1)), reads=[t_u, t_xnT], writes=[pg_t], inc=(kc == KC - 1))
                        sg_h, sg_t = sgs[tc % 2]
                        op(ACT, lambda pg_h=pg_h, sg_h=sg_h, oc=oc: nc.scalar.activation(
                            out=sg_h[:], in_=pg_h[:, 0:512], func=AF.Tanh, bias=bgh[:, 8 + oc:9 + oc], scale=0.5),
                           reads=[pg_t, t_bgh], writes=[sg_t])
                        op(DVE, lambda py_h=py_h, sg_h=sg_h, oc=oc, tc=tc: nc.vector.scalar_tensor_tensor(
                            out=bufY[:, oc, tc * 512:(tc + 1) * 512], in0=sg_h[:], scalar=1.0, in1=py_h[:, 0:512],
                            op0=ALU.add, op1=ALU.mult), reads=[py_t, sg_t], writes=[t_bufY])
                    ws.release(si)
                kb.end_scratch_phase()
            if first:
                tap("ybg", bufY[:], [128, KC, S], BF16, [t_bufY])

            if upto == "MD":
                break

            proj_n[0] = 2
            with ExitStack() as ms:
                fq = [kb.scratch(ms, f"f_q{i}", [128, S], BF16) for i in range(2)]
                fk = [kb.scratch(ms, f"f_k{i}", [128, S], BF16) for i in range(2)]
                fv = [kb.scratch(ms, f"f_v{i}", [128, NT, 130], BF16) for i in range(2)]
                fzs = [kb.scratch(ms, f"f_z{i}", [128, NT, 128], BF16) for i in range(2)]
                bms = [kb.scratch(ms, f"f_bm{i}", [128, NT, NT], F32) for i in range(2)]
                ptb = [kb.scratch(ms, f"f_pt{i}", [128, 512], BF16) for i in range(NPTB)]
                obs = [kb.scratch(ms, f"f_ob{i}", [128, 4, 128], BF16) for i in range(2)]
                rin, t_rin = kb.scratch(ms, "f_rin", [128, 16], F32)
                tzs = [kb.scratch(ms, f"f_tz{i}", [128, 128], F32) for i in range(4)]
                for i in range(2):
                    op(DVE, lambda i=i: nc.vector.memset(fv[i][0][:, :, 128:129], 2.0), writes=[fv[i][1]])
                sc = float(128 ** -0.5)
                sb_i = 0
                for h in range(8):
                    s1i, u1, t_u1 = ws.next()
                    s2i, u2, t_u2 = ws.next()
                    qT, t_qT = fq[h % 2]
                    kT, t_kT = fk[h % 2]
                    V, t_V = fv[h % 2]
                    FZ, t_FZ = fzs[h % 2]
                    bm, t_bm = bms[h % 2]
                    for which, (dst, t_dst) in enumerate(((qT, t_qT), (kT, t_kT))):
                        for tc in range(4):
                            pp_h, pp_t = proj_bank()
                            for kc in range(KC):
                                op(PE, lambda kc=kc, tc=tc, which=which, pp_h=pp_h: nc.tensor.matmul(
                                    pp_h[:, 0:512], lhsT=u1[:, kc, which * 128:(which + 1) * 128],
                                    rhs=xnT[:, kc, tc * 512:(tc + 1) * 512], start=(kc == 0), stop=(kc == KC - 1)),
                                   reads=[t_u1, t_xnT], writes=[pp_t], inc=(kc == KC - 1))
                            op(DVE, lambda tc=tc, pp_h=pp_h, dst=dst: nc.vector.tensor_copy(
                                out=dst[:, tc * 512:(tc + 1) * 512], in_=pp_h[:, 0:512]), reads=[pp_t], writes=[t_dst])
                    ws.release(s1i)
                    for tb in range(NT):
                        pp_h, pp_t = proj_bank()
                        for kc in range(KC):
                            op(PE, lambda kc=kc, tb=tb, pp_h=pp_h: nc.tensor.matmul(
                                pp_h[:, 0:256], lhsT=xnT[:, kc, tb * 128:(tb + 1) * 128], rhs=u2[:, kc, :],
                                start=(kc == 0), stop=(kc == KC - 1)), reads=[t_u2, t_xnT], writes=[pp_t], inc=(kc == KC - 1))
                        op(DVE, lambda tb=tb, pp_h=pp_h: nc.vector.tensor_copy(out=V[:, tb, 0:128], in_=pp_h[:, 0:128]),
                           reads=[pp_t], writes=[t_V])
                        tz_h, tz_t = tzs[tb % 4]
                        op(ACT, lambda pp_h=pp_h, tz_h=tz_h: nc.scalar.activation(out=tz_h[:], in_=pp_h[:, 128:256],
                                                                                  func=AF.Tanh, scale=0.5),
                           reads=[pp_t], writes=[tz_t])
                        op(DVE, lambda tb=tb, pp_h=pp_h, tz_h=tz_h: nc.vector.scalar_tensor_tensor(
                            out=FZ[:, tb, :], in0=tz_h[:], scalar=1.0, in1=pp_h[:, 128:256], op0=ALU.add, op1=ALU.mult),
                           reads=[pp_t, tz_t], writes=[t_FZ])
                    ws.release(s2i)
                    for j in range(NT):
                        op(DVE, lambda j=j: nc.vector.tensor_scalar(out=bm[:, j, :], in0=RR[:, :, h],
                                                                    scalar1=cc[:, j, h:h + 1], scalar2=None,
                                                                    op0=ALU.subtract), reads=[t_RR, t_cc], writes=[t_bm])
                    steps = [(I, j) for I in range(4) for j in range(4 * I + 4)]

                    def emit_ST(n):
                        I, j = steps[n]
                        i0 = max(j, 4 * I)
                        ps_h, ps_t = PB[2 + n % 2]
                        N = (4 * I + 4 - i0) * 128
                        op(PE, lambda: nc.tensor.matmul(ps_h[:, 0:N], lhsT=kT[:, j * 128:(j + 1) * 128],
                                                        rhs=qT[:, i0 * 128:(4 * I + 4) * 128], start=True, stop=True),
                           reads=[t_kT, t_qT], writes=[ps_t])

                    def emit_exp(n):
                        I, j = steps[n]
                        i0 = max(j, 4 * I)
                        ps_h, ps_t = PB[2 + n % 2]
                        p_h, p_t = ptb[n % NPTB]
                        for i in range(i0, 4 * I + 4):
                            lo = (i - i0) * 128
                            op(ACT, lambda i=i, lo=lo: nc.scalar.activation(
                                out=p_h[:, lo:lo + 128], in_=ps_h[:, lo:lo + 128], func=AF.Exp, scale=sc,
                                bias=bm[:, j, i:i + 1]), reads=[ps_t, t_bm], writes=[p_t])
                        if j >= 4 * I:
                            op(DVE, lambda: nc.vector.tensor_tensor(out=p_h[:, 0:128], in0=p_h[:, 0:128], in1=maskb[:],
                                                                    op=ALU.mult), reads=[p_t, t_maskb], writes=[p_t])

                    def emit_PV(n):
                        I, j = steps[n]
                        i0 = max(j, 4 * I)
                        p_h, p_t = ptb[n % NPTB]
                        for i in range(i0, 4 * I + 4):
                            il = i - 4 * I
                            lo = (i - i0) * 128
                            po_h, po_t = PB[4 + il // 2]
                            off = (il % 2) * 129
                            op(PE, lambda i=i, lo=lo, po_h=po_h, off=off, il=il: nc.tensor.matmul(
                                po_h[:, off:off + 129], lhsT=p_h[:, lo:lo + 128], rhs=V[:, j, 0:129],
                                start=(j == 0 and il % 2 == 0), stop=(j == i), skip_group_check=True),
                               reads=[p_t, t_V], writes=[po_t])
                            if j == i:
                                ob_h, ob_t = obs[I % 2]
                                op(DVE, lambda po_h=po_h, off=off, i=i: nc.vector.reciprocal(
                                    out=rin[:, i:i + 1], in_=po_h[:, off + 128:off + 129]), reads=[po_t], writes=[t_rin])
                                op(DVE, lambda po_h=po_h, off=off, i=i, il=il, ob_h=ob_h: nc.vector.scalar_tensor_tensor(
                                    out=ob_h[:, il, :], in0=po_h[:, off:off + 128], scalar=rin[:, i:i + 1], in1=FZ[:, i, :],
                                    op0=ALU.mult, op1=ALU.mult), reads=[po_t, t_rin, t_FZ], writes=[ob_t])
                                pt_h, pt_t = PT[I % 2]
                                pb0 = 0
                                op(PE, lambda il=il, ob_h=ob_h, pt_h=pt_h, pb0=pb0: nc.tensor.transpose(
                                    out=pt_h[:, pb0 + il, :], in_=ob_h[:, il, :], identity=ident[:]),
                                   reads=[ob_t, t_ident], writes=[pt_t])
                                if il == 3:
                                    op(DVE, lambda pt_h=pt_h, I=I: nc.vector.tensor_copy(
                                        out=bufA[:, h, I * 512:(I + 1) * 512],
                                        in_=pt_h[:, pb0:pb0 + 4, :].rearrange("p a b -> p (a b)")),
                                       reads=[pt_t], writes=[t_bufA])

                    emit_ST(0)
                    for n in range(len(steps)):
                        emit_exp(n)
                        if n + 1 < len(steps):
                            emit_ST(n + 1)
                        emit_PV(n)
                kb.end_scratch_phase()
            if first:
                tap("oaT", bufA[:], [128, KC, S], BF16, [t_bufA])

            if upto == "F":
                break

            proj_n[0] = 6
            with ExitStack() as ms:
                sgs = [kb.scratch(ms, f"e_sg{i}", [128, 512], F32) for i in range(2)]
                for oc in range(8):
                    si, u, t_u = ws.next()
                    for tc in range(4):
                        py_h, py_t = proj_bank()
                        for kc in range(KC):
                            op(PE, lambda kc=kc, tc=tc, py_h=py_h: nc.tensor.matmul(
                                py_h[:, 0:512], lhsT=u[:, kc, 0:128], rhs=bufA[:, kc, tc * 512:(tc + 1) * 512],
                                start=(kc == 0), stop=(kc == KC - 1)), reads=[t_u, t_bufA], writes=[py_t], inc=(kc == KC - 1))
                        pg_h, pg_t = proj_bank()
                        for kc in range(KC):
                            op(PE, lambda kc=kc, tc=tc, pg_h=pg_h: nc.tensor.matmul(
                                pg_h[:, 0:512], lhsT=u[:, kc, 128:256], rhs=xnT[:, kc, tc * 512:(tc + 1) * 512],
                                start=(kc == 0), stop=(kc == KC - 1)), reads=[t_u, t_xnT], writes=[pg_t], inc=(kc == KC - 1))
                        sg_h, sg_t = sgs[tc % 2]
                        op(ACT, lambda pg_h=pg_h, sg_h=sg_h, oc=oc: nc.scalar.activation(
                            out=sg_h[:], in_=pg_h[:, 0:512], func=AF.Tanh, bias=bgh[:, oc:oc + 1], scale=0.5),
                           reads=[pg_t, t_bgh], writes=[sg_t])
                        op(DVE, lambda py_h=py_h, sg_h=sg_h: nc.vector.scalar_tensor_tensor(
                            out=sg_h[:], in0=sg_h[:], scalar=1.0, in1=py_h[:, 0:512], op0=ALU.add, op1=ALU.mult),
                           reads=[py_t, sg_t], writes=[sg_t])
                        op(DVE, lambda sg_h=sg_h, oc=oc, tc=tc: nc.vector.tensor_tensor(
                            out=bufY[:, oc, tc * 512:(tc + 1) * 512], in0=sg_h[:], in1=bufY[:, oc, tc * 512:(tc + 1) * 512],
                            op=ALU.add), reads=[sg_t, t_bufY], writes=[t_bufY])
                    ws.release(si)
                kb.end_scratch_phase()
            if first:
                tap("yT", bufY[:], [128, KC, S], BF16, [t_bufY])

            if upto == "FD":
                break

            def _phase_O(seq=seq, bufY=bufY, t_bufY=t_bufY, ys=ys):
                dma(SP, out=gX[:], in_=final_g.partition_broadcast(128), writes=[t_gX])
                us = [ws.next() for _ in range(4)]
                for tb in range(NT):
                    xt_h, xt_t = xts[tb % NXT]
                    dma(SP, out=xt_h[:], in_=x_d[seq, tb * 128:(tb + 1) * 128, :], writes=[xt_t])
                    banks = [proj_bank(), proj_bank()]
                    for hf in range(2):
                        po_h, po_t = banks[hf]
                        for n in range(2):
                            si, u, t_u = us[2 * hf + n]
                            for kc in range(KC):
                                op(PE, lambda kc=kc, n=n, u=u, po_h=po_h: nc.tensor.matmul(
                                    po_h[:, n * 256:(n + 1) * 256], lhsT=bufY[:, kc, tb * 128:(tb + 1) * 128], rhs=u[:, kc, :],
                                    start=(kc == 0), stop=(kc == KC - 1)), reads=[t_u, t_bufY], writes=[po_t],
                                   inc=(kc == KC - 1 and n == 1))
                        op(DVE, lambda hf=hf, po_h=po_h: nc.vector.scalar_tensor_tensor(
                            out=xt_h[:, hf * 512:(hf + 1) * 512], in0=po_h[:, 0:512], scalar=0.5,
                            in1=xt_h[:, hf * 512:(hf + 1) * 512], op0=ALU.mult, op1=ALU.add),
                           reads=[po_t, xt_t], writes=[xt_t])
                    xn_h, xn_t = xns[tb % NXT]
                    rmsnorm_tail(xt_h, xt_t, tb, gF, t_gF, xt_h[:], [xt_t], xn_h[:], [xn_t])
                    dma(SP, out=y_d[seq, tb * 128:(tb + 1) * 128, :], in_=xt_h[:], reads=[xt_t])
                for (si, u, t_u) in us:
                    ws.release(si)
                kb.scr_tiles.append(t_bufY)
                kb.end_scratch_phase()
                ys.close()

            pending_O = _phase_O

        if pending_O is not None:
            pending_O()
            pending_O = None
        if kb.mode == "defer":
            kb.flush()
            build.n_inst = kb.n_inst
            build.nsem = kb.nsem
            build.est_us = kb.est_makespan
        out_costs = [o.cost for o in kb.ops]
    return nc, dbg_out, out_costs


def host_inputs(inputs):
    f = lambda a: np.ascontiguousarray(np.asarray(a, dtype=np.float32))
    gb = np.concatenate([f(inputs["b_fox_f"])[0], f(inputs["b_ml_i"])[0], f(inputs["b_ml_f"])[0]])
    gbias = np.ascontiguousarray(np.broadcast_to(gb[None, None, :], (128, NT, 16)))
    convw = np.ascontiguousarray(f(inputs["conv_w"])[0].reshape(4, 8, 128).transpose(2, 1, 0))
    convb = np.ascontiguousarray(f(inputs["conv_b"])[0].reshape(8, 128).T)
    bgate = np.ascontiguousarray(f(inputs["b_gate"])[0].reshape(16, 128).T)
    return {
        "w_in": f(inputs["w_in"])[0],
        "w_fox_down": f(inputs["w_fox_down"])[0],
        "w_ml_down": f(inputs["w_ml_down"])[0],
        "w_out": f(inputs["w_out"])[0],
        "norm_g": f(inputs["norm_g"])[0],
        "final_g": f(inputs["final_g"]),
        "ml_norm_g": f(inputs["ml_norm_g"])[0],
        "gbias": gbias,
        "convw": convw,
        "convb": convb,
        "bgate": bgate,
    }


def kernel(**inputs):
    x = np.ascontiguousarray(np.asarray(inputs["x"], dtype=np.float32))
    B = x.shape[0]
    nseq = B // NCORES
    shared = host_inputs(inputs)
    nc, _ = build(nseq)
    in_maps = []
    for c in range(NCORES):
        m = dict(shared)
        m["x"] = x[c * nseq:(c + 1) * nseq]
        in_maps.append(m)
    res = run_bass_kernel_spmd(nc, in_maps, core_ids=list(range(NCORES)))
    out = np.concatenate([np.asarray(r["y"]) for r in res.results], axis=0)
    return out.astype(np.float32, copy=False)
```

```python
import types
import numpy as np
from contextlib import ExitStack
import concourse.bass as bass
import concourse.mybir as mybir
from concourse.bass_utils import run_bass_kernel_spmd

F32 = mybir.dt.float32
BF16 = mybir.dt.bfloat16
AF = mybir.ActivationFunctionType
ALU = mybir.AluOpType

S = 2048
D = 1024
NT = 16
KC = 8
NCORES = 8
EPS = 1e-6
C_FQ, C_FK, C_FV, C_FF, C_FZ = 0, 1024, 2048, 3072, 3080
C_MQ, C_MK, C_MV, C_MI, C_MF, C_MO, C_MZ = 4104, 4616, 5128, 6152, 6156, 6160, 7184
C_GA, C_GB = 8208, 9232
IN_W = 10256
SEM_LIMIT = 30000
SLACK_US = 0.0
PE_MAXG = 2
NPTB = 3
NXT = 4
MDB = 1
NCB = 2
NTH = 1
NQB = 2
NXN = 2
M_PROJ_BANKS = [0, 1, 5]
NSLOT = 6


class SemObj:
    __slots__ = ("h", "owner", "val")

    def __init__(self, h, owner):
        self.h = h
        self.owner = owner
        self.val = 0


class Tile:
    __slots__ = ("w", "r", "name", "psum")

    def __init__(self, name, init=None, psum=False):
        self.name = name
        self.w = list(init) if init else []
        self.r = []
        self.psum = psum


class Eng:
    def __init__(self, kb, name, h, is_pe=False):
        self.kb = kb
        self.name = name
        self.h = h
        self.is_pe = is_pe
        self.waited = {}
        self.cur = None

    def sem(self):
        if self.cur is None or self.cur.val >= SEM_LIMIT:
            self.cur = self.kb.new_sem(self)
        return self.cur


class Op:
    __slots__ = ("id", "eng", "fns", "deps", "cost", "is_dma", "tok", "start", "ninstr", "aset")

    def __init__(self, oid, eng, is_dma):
        self.id = oid
        self.eng = eng
        self.fns = []
        self.deps = []
        self.cost = 0.0
        self.is_dma = is_dma
        self.tok = None
        self.start = 0.0
        self.aset = None


def _snap(fn):
    if fn.__closure__ is None:
        return fn
    cells = []
    for c in fn.__closure__:
        try:
            cells.append(types.CellType(c.cell_contents))
        except ValueError:
            cells.append(c)
    return types.FunctionType(fn.__code__, fn.__globals__, fn.__name__, fn.__defaults__, tuple(cells))


def _free_elems(acc):
    try:
        ap = acc.ap
        n = 1
        for (_, c) in ap[1:]:
            n *= c
        return max(int(n), 1)
    except Exception:
        return 64


class KB:
    def __init__(self, nc, es, costs=None):
        self.nc = nc
        self.es = es
        self.mode = "measure" if costs is None else "defer"
        self.costs_in = costs
        self.nsem = 0
        self.PE = Eng(self, "pe", nc.tensor, is_pe=True)
        self.ACT = Eng(self, "act", nc.scalar)
        self.DVE = Eng(self, "dve", nc.vector)
        self.POOL = Eng(self, "pool", nc.gpsimd)
        self.SP = Eng(self, "sp", nc.sync)
        self.engs = [self.PE, self.ACT, self.DVE, self.POOL, self.SP]
        self.dma_pool = {}
        self.dma_rr = {}
        self.scr_tokens = []
        self.scr_tiles = []
        self.scr_id = 0
        self.n_inst = 0
        self.ops = []
        self.open_pe = None

    def new_sem(self, owner):
        self.nsem += 1
        h = self.es.enter_context(self.nc.semaphore(f"s{self.nsem}"))
        return SemObj(h, owner)

    def _deps(self, o, reads, writes):
        eng = o.eng
        ops = self.ops
        d = o.deps
        for t in reads:
            d.extend(t.w)
            if t.psum:
                d.extend(r for r in t.r if ops[r].eng is not eng)
        for t in writes:
            d.extend(t.w)
            d.extend(t.r)

    def _touch(self, oid, reads, writes):
        for t in reads:
            if not t.r or t.r[-1] != oid:
                t.r.append(oid)
        for t in writes:
            t.w = [oid]
            t.r = []

    def _cost_of(self, eng, ins):
        try:
            i = ins.ins
            n = _free_elems(i.outs[0])
            if eng.is_pe:
                f32 = False
                try:
                    f32 = (i.ins[0].dtype == F32) and not getattr(i, "is_transpose", False)
                except Exception:
                    pass
                return (max(n, 200) * (4 if f32 else 1) * 0.5 + 8.0) / 1000.0
            if eng is self.ACT:
                return 0.22 + n * 0.00083
            if eng is self.DVE:
                return 0.10 + n * 0.00104
            return 0.25 + n * 0.002
        except Exception:
            return 0.3

    def op(self, eng, fn, reads=(), writes=(), inc=True):
        if eng.is_pe and self.open_pe is not None:
            o = self.open_pe
        else:
            assert self.open_pe is None, "non-PE op recorded while a PE group is open"
            o = Op(len(self.ops), eng, False)
            self.ops.append(o)
            if eng.is_pe and not inc:
                self.open_pe = o
        if eng.is_pe:
            o.ninstr = getattr(o, "ninstr", 0) + 1
            if inc or o.ninstr >= PE_MAXG:
                self.open_pe = None
        self._deps(o, reads, writes)
        self._touch(o.id, reads, writes)
        if self.mode == "measure":
            ins = fn()
            o.cost += self._cost_of(eng, ins)
            if eng is self.ACT:
                try:
                    fname = str(ins.ins.func)
                except Exception:
                    fname = ""
                if "Exp" in fname or "Tanh" in fname:
                    o.aset = "A"
                elif "Sqrt" in fname:
                    o.aset = "S"
                elif "Ln" in fname:
                    o.aset = "L"
            return ins
        o.fns.append(_snap(fn))
        return None

    def dma(self, q, out, in_, reads=(), writes=()):
        assert self.open_pe is None
        o = Op(len(self.ops), q, True)
        self.ops.append(o)
        self._deps(o, reads, writes)
        self._touch(o.id, reads, writes)
        if self.mode == "measure":
            q.h.dma_start(out=out, in_=in_)
            try:
                n = 1
                for (_, c) in out.ap:
                    n *= c
            except Exception:
                n = 131072
            o.cost = 2.0 + n * 4 / 150e3
            return None
        o.fns.append(lambda: q.h.dma_start(out=out, in_=in_))
        return None

    def sb(self, name, shape, dt, es=None):
        h = (es or self.es).enter_context(self.nc.sbuf_tensor(name, shape, dt))
        return h, Tile(name)

    def scratch(self, es, name, shape, dt):
        self.scr_id += 1
        h = es.enter_context(self.nc.sbuf_tensor(f"{name}_{self.scr_id}", shape, dt))
        t = Tile(name, self.scr_tokens)
        self.scr_tiles.append(t)
        return h, t

    def end_scratch_phase(self):
        acc = set(self.scr_tokens)
        for t in self.scr_tiles:
            acc.update(t.w)
            acc.update(t.r)
        self.scr_tokens = sorted(acc)
        self.scr_tiles = []

    def schedule(self):
        ops = self.ops
        n = len(ops)
        assert len(self.costs_in) == n, (len(self.costs_in), n)
        costs = [c for (c, _) in self.costs_in]
        asets = [a for (_, a) in self.costs_in]
        act = self.ACT
        cur_set = [None]
        TL = 0.0
        succ = [[] for _ in range(n)]
        for o in ops:
            o.deps = sorted(set(d for d in o.deps if d != o.id))
            for d in o.deps:
                succ[d].append(o.id)
        pend = {e: [] for e in self.engs}
        for o in ops:
            pend[o.eng].append(o.id)
        done = [False] * n
        fin = [0.0] * n
        efree = {e: 0.0 for e in self.engs}
        best = {e: None for e in self.engs}
        dirty = set(self.engs)
        HOP, SAME, W = 0.30, 0.06, 400
        SLACK = SLACK_US
        order = []
        INF = float("inf")
        while len(order) < n:
            for e in dirty:
                lst = pend[e]
                be, bi, bpos = INF, -1, -1
                ef = efree[e]
                lim = min(W, len(lst))
                for pos in range(lim):
                    oid = lst[pos]
                    rt = ef
                    ok = True
                    for d in ops[oid].deps:
                        if not done[d]:
                            ok = False
                            break
                        t = fin[d] + (SAME if ops[d].eng is e else HOP)
                        if t > rt:
                            rt = t
                    if ok and e is act and asets[oid] is not None and cur_set[0] is not None and asets[oid] != cur_set[0]:
                        rt += TL
                    if ok and rt < be - SLACK:
                        be, bi, bpos = rt, oid, pos
                        if rt <= ef:
                            break
                best[e] = (be, bi, bpos) if bi >= 0 else None
            dirty.clear()
            ce, cb = None, None
            for e in self.engs:
                b = best[e]
                if b is not None and (cb is None or b[0] < cb[0] or (b[0] == cb[0] and b[1] < cb[1])):
                    ce, cb = e, b
            assert ce is not None, "scheduler deadlock"
            st, oid, pos = cb
            o = ops[oid]
            o.start = st
            if o.is_dma:
                efree[ce] = st + 0.15
                fin[oid] = st + costs[oid]
            else:
                efree[ce] = st + costs[oid]
                fin[oid] = efree[ce]
                if ce is act and asets[oid] is not None:
                    cur_set[0] = asets[oid]
            done[oid] = True
            del pend[ce][pos]
            order.append(o)
            dirty.add(ce)
            for sidx in succ[oid]:
                dirty.add(ops[sidx].eng)
        self.est_makespan = max(fin) if fin else 0.0
        return order

    def flush(self):
        order = self.schedule()
        ops = self.ops
        for o in order:
            eng = o.eng
            for d in o.deps:
                p = ops[d]
                if eng.is_pe and p.eng is eng and not p.is_dma:
                    continue
                s, v = p.tok
                if eng.waited.get(s, 0) >= v:
                    continue
                eng.h.wait_ge(s.h, v)
                eng.waited[s] = v
                self.n_inst += 1
            if o.is_dma:
                pool = self.dma_pool.setdefault(eng, [])
                if len(pool) < 8:
                    dsem = self.new_sem(None)
                    pool.append(dsem)
                else:
                    i = self.dma_rr.get(eng, 0)
                    dsem = pool[i % len(pool)]
                    self.dma_rr[eng] = i + 1
                if dsem.val and eng.waited.get(dsem, 0) < dsem.val:
                    eng.h.wait_ge(dsem.h, dsem.val)
                    eng.waited[dsem] = dsem.val
                    self.n_inst += 1
                ins = o.fns[0]()
                dsem.val += 16
                ins.then_inc(dsem.h, 16)
                o.tok = (dsem, dsem.val)
            else:
                ins = None
                for fn in o.fns:
                    ins = fn()
                    self.n_inst += 1
                s = eng.sem()
                s.val += 1
                ins.then_inc(s.h, 1)
                o.tok = (s, s.val)
            self.n_inst += 1
        SP = self.SP
        for q, pool in self.dma_pool.items():
            for dsem in pool:
                if dsem.val and SP.waited.get(dsem, 0) < dsem.val:
                    self.nc.sync.wait_ge(dsem.h, dsem.val)
                    SP.waited[dsem] = dsem.val


class WStream:
    def __init__(self, kb, slots, plan):
        self.kb = kb
        self.slots = slots
        self.plan = plan
        self.issued = 0
        self.taken = 0
        self.free = [True] * len(slots)

    def _pump(self):
        while self.issued < len(self.plan):
            si = self.issued % len(self.slots)
            if not self.free[si]:
                break
            h, t = self.slots[si]
            for (dst, src, c0, n) in self.plan[self.issued]:
                self.kb.dma(self.kb.POOL, out=h[:, :, dst:dst + n], in_=src[:, :, c0:c0 + n], writes=[t])
            self.free[si] = False
            self.issued += 1

    def next(self):
        self._pump()
        assert self.taken < self.issued, "weight stream stalled (slot not released)"
        si = self.taken % len(self.slots)
        self.taken += 1
        h, t = self.slots[si]
        return si, h, t

    def release(self, si):
        self.free[si] = True
        self._pump()


def unit_plan(nseq, wv, wmd, wfd, wout):
    plan = []
    for _ in range(nseq):
        for h in range(4):
            plan.append([(0, wv, C_MQ + h * 128, 128), (128, wv, C_MK + h * 128, 128)])
            plan.append([(0, wv, C_MV + h * 256, 256)])
            plan.append([(0, wv, C_MO + h * 256, 256)])
            plan.append([(0, wv, C_MZ + h * 256, 256)])
        for oc in range(8):
            plan.append([(0, wmd, oc * 128, 128), (128, wv, C_GB + oc * 128, 128)])
        for h in range(8):
            plan.append([(0, wv, C_FQ + h * 128, 128), (128, wv, C_FK + h * 128, 128)])
            plan.append([(0, wv, C_FV + h * 128, 128), (128, wv, C_FZ + h * 128, 128)])
        for oc in range(8):
            plan.append([(0, wfd, oc * 128, 128), (128, wv, C_GA + oc * 128, 128)])
        for u in range(4):
            plan.append([(0, wout, u * 256, 256)])
    return plan


def build(nseq=2, dbg=None, upto=None):
    _, _, costs = _build_pass(nseq, dbg, upto, None)
    nc, dbg_out, _ = _build_pass(nseq, dbg, upto, costs)
    return nc, dbg_out


def _build_pass(nseq, dbg, upto, costs):
    dbg = dbg or set()
    nc = bass.Bass("TRN2", target_bir_lowering=False)
    x_d = nc.dram_tensor("x", [nseq, S, D], F32, kind="ExternalInput").ap()
    w_in = nc.dram_tensor("w_in", [D, IN_W], F32, kind="ExternalInput").ap()
    w_fd = nc.dram_tensor("w_fox_down", [D, D], F32, kind="ExternalInput").ap()
    w_md = nc.dram_tensor("w_ml_down", [D, D], F32, kind="ExternalInput").ap()
    w_o = nc.dram_tensor("w_out", [D, D], F32, kind="ExternalInput").ap()
    norm_g = nc.dram_tensor("norm_g", [D], F32, kind="ExternalInput").ap()
    final_g = nc.dram_tensor("final_g", [D], F32, kind="ExternalInput").ap()
    ml_g = nc.dram_tensor("ml_norm_g", [D], F32, kind="ExternalInput").ap()
    gbias_d = nc.dram_tensor("gbias", [128, NT, 16], F32, kind="ExternalInput").ap()
    convw_d = nc.dram_tensor("convw", [128, 8, 4], F32, kind="ExternalInput").ap()
    convb_d = nc.dram_tensor("convb", [128, 8], F32, kind="ExternalInput").ap()
    bgate_d = nc.dram_tensor("bgate", [128, 16], F32, kind="ExternalInput").ap()
    y_d = nc.dram_tensor("y", [nseq, S, D], F32, kind="ExternalOutput").ap()
    dbg_out = {}

    def dbg_tensor(name, shape, dt):
        dbg_out[name] = nc.dram_tensor("dbg_" + name, shape, dt, kind="ExternalOutput").ap()
        return dbg_out[name]

    wv = w_in.rearrange("(kc p) c -> p kc c", p=128)
    wfd = w_fd.rearrange("(kc p) c -> p kc c", p=128)
    wmd = w_md.rearrange("(kc p) c -> p kc c", p=128)
    wout = w_o.rearrange("(kc p) c -> p kc c", p=128)

    with ExitStack() as es:
        kb = KB(nc, es, costs)
        PE, ACT, DVE, POOL, SP = kb.PE, kb.ACT, kb.DVE, kb.POOL, kb.SP
        op, dma = kb.op, kb.dma

        xnT, t_xnT = kb.sb("xnT", [128, KC, S], BF16)
        bufA, t_bufA = kb.sb("bufA", [128, KC, S], BF16)
        slots = [kb.sb(f"wslot{i}", [128, KC, 256], BF16) for i in range(NSLOT)]
        wg, t_wg = kb.sb("wg", [128, KC, 16], BF16)
        gX, t_gX = kb.sb("gX", [128, D], F32)
        gN, t_gN = kb.sb("gN", [128, D], F32)
        gF = gM = gX
        t_gF = t_gM = t_gX
        gbias, t_gbias = kb.sb("gbias_sb", [128, NT, 16], F32)
        cw, t_cw = kb.sb("cw", [128, 8, 4], F32)
        cb, t_cb = kb.sb("cb", [128, 8], F32)
        bg, t_bg = kb.sb("bg", [128, 16], F32)
        cbh, t_cbh = kb.sb("cbh", [128, 8], F32)
        bgh, t_bgh = kb.sb("bgh", [128, 16], F32)
        epsT, t_eps = kb.sb("epsT", [128, 1], F32)
        ident, t_ident = kb.sb("ident", [128, 128], BF16)
        onesb, t_onesb = kb.sb("onesb", [128, 128], BF16)
        maskb, t_maskb = kb.sb("maskb", [128, 128], BF16)
        maskf, t_maskf = kb.sb("maskf", [128, 128], F32)
        onesf, t_onesf = kb.sb("onesf", [128, 128], F32)
        hmid, t_hmid = kb.sb("hmid", [128, 128], F32)
        xts = [kb.sb(f"xt{i}", [128, D], F32) for i in range(NXT)]
        xns = [kb.sb(f"xn{i}", [128, D], BF16) for i in range(NXN)]
        stats = [kb.sb(f"stat{i}", [128, 4], F32) for i in range(NXT)]
        zs, t_zs = kb.sb("zs", [128, NT, 16], F32)
        lf, t_lf = kb.sb("lf", [128, NT, 16], F32)
        tot, t_tot = kb.sb("tot", [128, NT, 16], F32)
        bc, t_bc = kb.sb("bc", [128, NT, 16], F32)
        half, t_half = kb.sb("half", [128, NT, 16], F32)
        pre, t_pre = kb.sb("pre", [128, NT, 16], F32)
        cc, t_cc = kb.sb("cc", [128, NT, 16], F32)
        RR, t_RR = kb.sb("RR", [128, NT, 16], F32)
        ew, t_ew = kb.sb("ew", [128, NT, 4], F32)
        eb, t_eb = kb.sb("eb", [128, NT, 4], F32)
        aa, t_aa = kb.sb("aa", [128, NT, 4], F32)
        gtmp, t_gtmp = kb.sb("gtmp", [128, NT, 16], F32)

        PB = []
        for i in range(6):
            h = es.enter_context(nc.psum_tensor(f"pb{i}", [128, 512], F32))
            PB.append((h, Tile(f"pb{i}", psum=True)))
        PT = []
        for i in range(2):
            h = es.enter_context(nc.psum_tensor(f"pt{i}", [128, 8, 128], BF16))
            PT.append((h, Tile(f"pt{i}", psum=True)))

        dma(SP, out=gbias[:], in_=gbias_d[:, :, :], writes=[t_gbias])
        dma(SP, out=cw[:], in_=convw_d[:, :, :], writes=[t_cw])
        dma(SP, out=cb[:], in_=convb_d[:, :], writes=[t_cb])
        dma(SP, out=bg[:], in_=bgate_d[:, :], writes=[t_bg])
        dma(POOL, out=wg[:, :, 0:8], in_=wv[:, :, C_FF:C_FF + 8], writes=[t_wg])
        dma(POOL, out=wg[:, :, 8:16], in_=wv[:, :, C_MI:C_MI + 8], writes=[t_wg])
        op(DVE, lambda: nc.vector.memset(epsT[:], EPS), writes=[t_eps])
        op(DVE, lambda: nc.vector.tensor_scalar(out=cbh[:], in0=cb[:], scalar1=0.5, scalar2=None, op0=ALU.mult),
           reads=[t_cb], writes=[t_cbh])
        op(DVE, lambda: nc.vector.tensor_scalar(out=bgh[:], in0=bg[:], scalar1=0.5, scalar2=None, op0=ALU.mult),
           reads=[t_bg], writes=[t_bgh])
        op(POOL, lambda: nc.gpsimd.memset(onesb[:], 1.0), writes=[t_onesb])
        op(POOL, lambda: nc.gpsimd.memset(onesf[:], 1.0), writes=[t_onesf])
        op(POOL, lambda: nc.gpsimd.memset(hmid[:], 0.0), writes=[t_hmid])
        op(POOL, lambda: nc.gpsimd.memset(hmid[0:64, :], 1.0), writes=[t_hmid])
        op(POOL, lambda: nc.gpsimd.affine_select(out=ident[:], in_=onesb[:], pattern=[[1, 128]],
                                                 compare_op=ALU.is_equal, fill=0.0, base=0,
                                                 channel_multiplier=-1), reads=[t_onesb], writes=[t_ident])
        op(POOL, lambda: nc.gpsimd.affine_select(out=maskb[:], in_=onesb[:], pattern=[[1, 128]],
                                                 compare_op=ALU.is_ge, fill=0.0, base=0,
                                                 channel_multiplier=-1), reads=[t_onesb], writes=[t_maskb])
        op(POOL, lambda: nc.gpsimd.affine_select(out=maskf[:], in_=onesf[:], pattern=[[1, 128]],
                                                 compare_op=ALU.is_ge, fill=0.0, base=0,
                                                 channel_multiplier=-1), reads=[t_onesf], writes=[t_maskf])

        ws = WStream(kb, slots, unit_plan(nseq, wv, wmd, wfd, wout))
        proj_rr = [0]
        proj_n = [2]

        proj_idx = [None]

        def proj_bank():
            proj_rr[0] += 1
            if proj_idx[0] is not None:
                return PB[proj_idx[0][proj_rr[0] % len(proj_idx[0])]]
            return PB[proj_rr[0] % proj_n[0]]

        def tap(name, src_ap, shape, dt, tiles):
            if name in dbg:
                d = dbg_tensor(name, shape, dt)
                dma(SP, out=d, in_=src_ap, reads=tiles)

        def rmsnorm_tail(xt_h, xt_t, si, g_h, g_t, out_ap, out_tiles, sq_out_ap, sq_out_tiles):
            st_h, st_t = stats[si % NXT]
            op(ACT, lambda: nc.scalar.activation(out=sq_out_ap, in_=xt_h[:], func=AF.Square,
                                                 accum_out=st_h[:, 0:1]),
               reads=[xt_t], writes=sq_out_tiles + [st_t])
            op(ACT, lambda: nc.scalar.activation(out=st_h[:, 1:2], in_=st_h[:, 0:1], func=AF.Sqrt,
                                                 scale=1.0 / D, bias=epsT[:]),
               reads=[st_t, t_eps], writes=[st_t])
            op(DVE, lambda: nc.vector.reciprocal(out=st_h[:, 2:3], in_=st_h[:, 1:2]), reads=[st_t], writes=[st_t])
            op(DVE, lambda: nc.vector.scalar_tensor_tensor(out=out_ap, in0=xt_h[:], scalar=st_h[:, 2:3],
                                                           in1=g_h[:], op0=ALU.mult, op1=ALU.mult),
               reads=[xt_t, st_t, g_t], writes=out_tiles)

        pending_O = None
        for seq in range(nseq):
            first = (seq == 0)
            proj_n[0] = 2
            if first:
                dma(SP, out=gN[:], in_=norm_g.partition_broadcast(128), writes=[t_gN])
            for tb in range(NT):
                xt_h, xt_t = xts[tb % NXT]
                xn_h, xn_t = xns[tb % NXN]
                dma(SP, out=xt_h[:], in_=x_d[seq, tb * 128:(tb + 1) * 128, :], writes=[xt_t])
                rmsnorm_tail(xt_h, xt_t, tb, gN, t_gN, xn_h[:], [xn_t], xn_h[:], [xn_t])
                pt_h, pt_t = PT[tb % 2]
                for kc in range(KC):
                    op(PE, lambda kc=kc: nc.tensor.transpose(out=pt_h[:, kc, :], in_=xn_h[:, kc * 128:(kc + 1) * 128],
                                                             identity=ident[:]),
                       reads=[xn_t, t_ident], writes=[pt_t], inc=(kc == KC - 1))
                op(ACT, lambda: nc.scalar.copy(out=xnT[:, :, tb * 128:(tb + 1) * 128], in_=pt_h[:]),
                   reads=[pt_t], writes=[t_xnT])
            if first:
                tap("xnT", xnT[:], [128, KC, S], BF16, [t_xnT])

            if upto == "P0":
                break

            pz_h, pz_t = PB[2]
            for tb in range(NT):
                for kc in range(KC):
                    op(PE, lambda kc=kc, tb=tb: nc.tensor.matmul(pz_h[:, tb * 16:(tb + 1) * 16],
                                                                 lhsT=xnT[:, kc, tb * 128:(tb + 1) * 128],
                                                                 rhs=wg[:, kc, :], start=(kc == 0), stop=(kc == KC - 1)),
                       reads=[t_xnT, t_wg], writes=[pz_t], inc=(kc == KC - 1 and tb == NT - 1))
            op(DVE, lambda: nc.vector.tensor_tensor(out=zs[:].rearrange("p a b -> p (a b)"), in0=pz_h[:, 0:256],
                                                    in1=gbias[:].rearrange("p a b -> p (a b)"), op=ALU.add),
               reads=[pz_t, t_gbias], writes=[t_zs])
            if upto == "G1":
                tap("lf", zs[:], [128, NT, 16], F32, [t_zs])
                break
            op(ACT, lambda: nc.scalar.activation(out=gtmp[:], in_=zs[:], func=AF.Exp, scale=-1.0),
               reads=[t_zs], writes=[t_gtmp])
            op(ACT, lambda: nc.scalar.activation(out=gtmp[:], in_=gtmp[:], func=AF.Ln, bias=1.0, scale=1.0),
               reads=[t_gtmp], writes=[t_gtmp])
            op(DVE, lambda: nc.vector.tensor_scalar(out=lf[:], in0=gtmp[:], scalar1=-1.0, scalar2=None, op0=ALU.mult),
               reads=[t_gtmp], writes=[t_lf])
            if upto == "G2":
                tap("lf", lf[:], [128, NT, 16], F32, [t_lf])
                break
            lf2 = lf[:].rearrange("p a b -> p (a b)")
            for (lhs_h, lhs_t, dst_h, dst_t, pbi) in ((onesf, t_onesf, tot, t_tot, 3), (maskf, t_maskf, bc, t_bc, 4),
                                                       (hmid, t_hmid, half, t_half, 5)):
                pp_h, pp_t = PB[pbi]
                op(PE, lambda lhs_h=lhs_h, pp_h=pp_h: nc.tensor.matmul(pp_h[:, 0:256], lhsT=lhs_h[:], rhs=lf2,
                                                                       start=True, stop=True),
                   reads=[lhs_t, t_lf], writes=[pp_t])
                op(DVE, lambda dst_h=dst_h, pp_h=pp_h: nc.vector.tensor_copy(out=dst_h[:].rearrange("p a b -> p (a b)"),
                                                                             in_=pp_h[:, 0:256]),
                   reads=[pp_t], writes=[dst_t])
            if upto == "G3":
                tap("lf", lf[:], [128, NT, 16], F32, [t_lf])
                tap("cc", bc[:], [128, NT, 16], F32, [t_bc])
                break
            op(DVE, lambda: nc.vector.memset(pre[:, 0, :], 0.0), writes=[t_pre])
            for tb in range(1, NT):
                op(DVE, lambda tb=tb: nc.vector.tensor_tensor(out=pre[:, tb, :], in0=pre[:, tb - 1, :],
                                                              in1=tot[:, tb - 1, :], op=ALU.add),
                   reads=[t_pre, t_tot], writes=[t_pre])
            op(DVE, lambda: nc.vector.tensor_tensor(out=cc[:], in0=bc[:], in1=pre[:], op=ALU.add),
               reads=[t_bc, t_pre], writes=[t_cc])
            op(DVE, lambda: nc.vector.tensor_tensor(out=RR[:], in0=half[:], in1=pre[:], op=ALU.add),
               reads=[t_half, t_pre], writes=[t_RR])
            if upto == "G4":
                tap("cc", cc[:], [128, NT, 16], F32, [t_cc])
                break
            op(DVE, lambda: nc.vector.tensor_tensor(out=ew[:], in0=zs[:, :, 8:12], in1=bc[:, :, 12:16], op=ALU.subtract),
               reads=[t_zs, t_bc], writes=[t_ew])
            if upto == "G5":
                tap("cc", cc[:], [128, NT, 16], F32, [t_cc, t_ew])
                break
            op(ACT, lambda: nc.scalar.activation(out=ew[:], in_=ew[:], func=AF.Exp), reads=[t_ew], writes=[t_ew])
            if upto == "G6":
                tap("cc", cc[:], [128, NT, 16], F32, [t_cc, t_ew])
                break
            op(ACT, lambda: nc.scalar.activation(out=eb[:], in_=bc[:, :, 12:16], func=AF.Exp), reads=[t_bc], writes=[t_eb])
            op(DVE, lambda: nc.vector.tensor_scalar(out=eb[:], in0=eb[:], scalar1=float(128 ** -0.5), scalar2=None,
                                                    op0=ALU.mult), reads=[t_eb], writes=[t_eb])
            if upto == "G7":
                tap("cc", cc[:], [128, NT, 16], F32, [t_cc, t_ew, t_eb])
                break
            op(ACT, lambda: nc.scalar.activation(out=aa[:], in_=tot[:, :, 12:16], func=AF.Exp), reads=[t_tot], writes=[t_aa])
            if first:
                tap("lf", lf[:], [128, NT, 16], F32, [t_lf])
                tap("cc", cc[:], [128, NT, 16], F32, [t_cc])
                tap("RR", RR[:], [128, NT, 16], F32, [t_RR])
                tap("ew", ew[:], [128, NT, 4], F32, [t_ew])
                tap("eb", eb[:], [128, NT, 4], F32, [t_eb])
                tap("aa", aa[:], [128, NT, 4], F32, [t_aa])

            if upto == "G":
                break

            if pending_O is not None:
                pending_O()
                pending_O = None
            proj_idx[0] = M_PROJ_BANKS
            dma(SP, out=gX[:], in_=ml_g.partition_broadcast(128), writes=[t_gX])
            with ExitStack() as ms:
                mq2 = [kb.scratch(ms, f"m_qT{i}", [128, S], BF16) for i in range(MDB)]
                mk2 = [kb.scratch(ms, f"m_kT{i}", [128, S], BF16) for i in range(MDB)]
                mc2 = [kb.scratch(ms, f"m_cbuf{i}", [128, S + 4], F32) for i in range(NCB)]
                accs = [kb.scratch(ms, f"m_acc{i}", [128, 1024], F32) for i in range(2)]
                ths = [kb.scratch(ms, f"m_th{i}", [128, 1024], F32) for i in range(NTH)]
                kTok, t_kTok = kb.scratch(ms, "m_kTok", [128, NT, 128], BF16)
                vaug, t_vaug = kb.scratch(ms, "m_vaug", [128, NT, 258], BF16)
                Crun = [kb.scratch(ms, f"m_C{i}", [128, 258], F32) for i in range(2)]
                Cbf, t_Cbf = kb.scratch(ms, "m_Cbf", [128, NT, 258], BF16)
                numS2 = [kb.scratch(ms, f"m_numS{i}", [128, 4, 258], F32) for i in range(NQB)]
                Gp2 = [kb.scratch(ms, f"m_Gp{i}", [128, 4, 256], BF16) for i in range(NQB)]
                STm = [kb.scratch(ms, f"m_STm{i}", [128, 128], BF16) for i in range(2)]
                hn = [kb.scratch(ms, f"m_hn{i}", [128, 256], BF16) for i in range(2)]
                s12 = [kb.scratch(ms, f"m_s12{i}", [128, 512], F32) for i in range(2)]
                sm2 = [kb.scratch(ms, f"m_sm{i}", [128, 12, 4], F32) for i in range(NQB)]
                junk, t_junk = kb.scratch(ms, "m_junk", [128, 256], BF16)
                for (cb_h, cb_t) in mc2:
                    op(DVE, lambda cb_h=cb_h: nc.vector.memset(cb_h[:, 0:4], 0.0), writes=[cb_t])
                for h in range(4):
                    qT, t_qT = mq2[h % MDB]
                    kT, t_kT = mk2[h % MDB]
                    s1i, u1, t_u1 = ws.next()
                    s2i, u2, t_u2 = ws.next()
                    s3i, u3, t_u3 = ws.next()
                    s4i, u4, t_u4 = ws.next()
                    for which, (dst, t_dst) in enumerate(((qT, t_qT), (kT, t_kT))):
                        chunk = which * 4 + h
                        cbuf, t_cbuf = mc2[which % len(mc2)]
                        for tc in range(4):
                            pp_h, pp_t = proj_bank()
                            for kc in range(KC):
                                op(PE, lambda kc=kc, tc=tc, which=which, pp_h=pp_h: nc.tensor.matmul(
                                    pp_h[:, 0:512], lhsT=u1[:, kc, which * 128:(which + 1) * 128],
                                    rhs=xnT[:, kc, tc * 512:(tc + 1) * 512], start=(kc == 0), stop=(kc == KC - 1)),
                                   reads=[t_u1, t_xnT], writes=[pp_t], inc=(kc == KC - 1))
                            op(ACT, lambda tc=tc, pp_h=pp_h: nc.scalar.copy(out=cbuf[:, 4 + tc * 512:4 + (tc + 1) * 512],
                                                                           in_=pp_h[:, 0:512]),
                               reads=[pp_t], writes=[t_cbuf])
                        for hf in range(2):
                            o = hf * 1024
                            ac_h, ac_t = accs[hf]
                            op(DVE, lambda o=o, ac_h=ac_h, chunk=chunk: nc.vector.tensor_scalar(
                                out=ac_h[:], in0=cbuf[:, 4 + o:4 + o + 1024], scalar1=cw[:, chunk, 3:4], scalar2=None,
                                op0=ALU.mult), reads=[t_cbuf, t_cw], writes=[ac_t])
                            for j in (2, 1, 0):
                                op(DVE, lambda o=o, ac_h=ac_h, chunk=chunk, j=j: nc.vector.scalar_tensor_tensor(
                                    out=ac_h[:], in0=cbuf[:, 1 + j + o:1 + j + o + 1024], scalar=cw[:, chunk, j:j + 1],
                                    in1=ac_h[:], op0=ALU.mult, op1=ALU.add), reads=[t_cbuf, t_cw, ac_t], writes=[ac_t])
                            th_h, th_t = ths[hf % NTH]
                            op(ACT, lambda ac_h=ac_h, chunk=chunk, th_h=th_h: nc.scalar.activation(
                                out=th_h[:], in_=ac_h[:], func=AF.Tanh, bias=cbh[:, chunk:chunk + 1], scale=0.5),
                               reads=[ac_t, t_cbh], writes=[th_t])
                            op(DVE, lambda th_h=th_h: nc.vector.tensor_scalar(
                                out=th_h[:], in0=th_h[:], scalar1=0.5, scalar2=0.5, op0=ALU.mult, op1=ALU.add),
                               reads=[th_t], writes=[th_t])
                            op(DVE, lambda o=o, ac_h=ac_h, chunk=chunk, dst=dst, th_h=th_h: nc.vector.scalar_tensor_tensor(
                                out=dst[:, o:o + 1024], in0=ac_h[:], scalar=cb[:, chunk:chunk + 1], in1=th_h[:],
                                op0=ALU.add, op1=ALU.mult), reads=[ac_t, t_cb, th_t], writes=[t_dst])
                    ws.release(s1i)
                    for g in range(2):
                        pt_h, pt_t = PT[g]
                        for c8 in range(8):
                            c = g * 8 + c8
                            op(PE, lambda c=c, c8=c8, pt_h=pt_h: nc.tensor.transpose(
                                out=pt_h[:, c8, :], in_=kT[:, c * 128:(c + 1) * 128], identity=ident[:]),
                               reads=[t_kT, t_ident], writes=[pt_t], inc=(c8 == 7))
                        for c8 in range(8):
                            c = g * 8 + c8
                            op(ACT, lambda c=c, c8=c8, pt_h=pt_h: nc.scalar.activation(
                                out=kTok[:, c, :], in_=pt_h[:, c8, :], func=AF.Copy, scale=aa[:, c, h:h + 1]),
                               reads=[pt_t, t_aa], writes=[t_kTok])
                    for tb in range(NT):
                        pp_h, pp_t = proj_bank()
                        for kc in range(KC):
                            op(PE, lambda kc=kc, tb=tb, pp_h=pp_h: nc.tensor.matmul(
                                pp_h[:, 0:256], lhsT=xnT[:, kc, tb * 128:(tb + 1) * 128], rhs=u2[:, kc, :],
                                start=(kc == 0), stop=(kc == KC - 1)),
                               reads=[t_u2, t_xnT], writes=[pp_t], inc=(kc == KC - 1))
                        op(ACT, lambda tb=tb, pp_h=pp_h: nc.scalar.activation(
                            out=vaug[:, tb, 0:256], in_=pp_h[:, 0:256], func=AF.Copy, scale=ew[:, tb, h:h + 1]),
                           reads=[pp_t, t_ew], writes=[t_vaug])
                    ws.release(s2i)
                    op(DVE, lambda: nc.vector.tensor_copy(out=vaug[:, :, 256:257], in_=ew[:, :, h:h + 1]),
                       reads=[t_ew], writes=[t_vaug])
                    dcb = (PB[5], PB[2])
                    for c in range(NT - 1):
                        pd_h, pd_t = dcb[c % 2]
                        op(PE, lambda c=c, pd_h=pd_h: nc.tensor.matmul(pd_h[:, 0:257], lhsT=kTok[:, c, :],
                                                                       rhs=vaug[:, c, 0:257], start=True, stop=True),
                           reads=[t_kTok, t_vaug], writes=[pd_t])
                        cn_h, cn_t = Crun[(c + 1) % 2]
                        cp_h, cp_t = Crun[c % 2]
                        if c == 0:
                            op(DVE, lambda pd_h=pd_h, cn_h=cn_h: nc.vector.tensor_copy(out=cn_h[:, 0:257], in_=pd_h[:, 0:257]),
                               reads=[pd_t], writes=[cn_t])
                        else:
                            op(DVE, lambda c=c, pd_h=pd_h, cn_h=cn_h, cp_h=cp_h: nc.vector.scalar_tensor_tensor(
                                out=cn_h[:, 0:257], in0=cp_h[:, 0:257], scalar=aa[:, c, h:h + 1], in1=pd_h[:, 0:257],
                                op0=ALU.mult, op1=ALU.add), reads=[cp_t, pd_t, t_aa], writes=[cn_t])
                        op(ACT, lambda c=c, cn_h=cn_h: nc.scalar.copy(out=Cbf[:, c + 1, 0:257], in_=cn_h[:, 0:257]),
                           reads=[cn_t], writes=[t_Cbf])
                    for c in range(NT):
                        numS, t_numS = numS2[(c // 4) % NQB]
                        Gp, t_Gp = Gp2[(c // 4) % NQB]
                        sm, t_sm = sm2[(c // 4) % NQB]
                        ps_h, ps_t = PB[2]
                        op(PE, lambda c=c: nc.tensor.matmul(ps_h[:, 0:128], lhsT=kT[:, c * 128:(c + 1) * 128],
                                                            rhs=qT[:, c * 128:(c + 1) * 128], start=True, stop=True),
                           reads=[t_kT, t_qT], writes=[ps_t])
                        st_h, st_t = STm[c % 2]
                        op(DVE, lambda st_h=st_h: nc.vector.tensor_tensor(out=st_h[:], in0=ps_h[:, 0:128], in1=maskf[:],
                                                                          op=ALU.mult),
                           reads=[ps_t, t_maskf], writes=[st_t])
                        pn_h, pn_t = PB[3 + c % 2]
                        op(PE, lambda c=c, pn_h=pn_h, st_h=st_h: nc.tensor.matmul(
                            pn_h[:, 0:257], lhsT=st_h[:], rhs=vaug[:, c, 0:257], start=True, stop=(c == 0)),
                           reads=[st_t, t_vaug], writes=[pn_t], inc=(c == 0))
                        if c > 0:
                            op(PE, lambda c=c, pn_h=pn_h: nc.tensor.matmul(
                                pn_h[:, 0:257], lhsT=qT[:, c * 128:(c + 1) * 128], rhs=Cbf[:, c, 0:257],
                                start=False, stop=True), reads=[t_qT, t_Cbf], writes=[pn_t])
                        op(ACT, lambda c=c, pn_h=pn_h: nc.scalar.copy(out=numS[:, c % 4, 0:257], in_=pn_h[:, 0:257]),
                           reads=[pn_t], writes=[t_numS])
                        pp_h, pp_t = proj_bank()
                        for gi, (ug, t_ug) in enumerate(((u3, t_u3), (u4, t_u4))):
                            for kc in range(KC):
                                op(PE, lambda kc=kc, c=c, gi=gi, ug=ug, pp_h=pp_h: nc.tensor.matmul(
                                    pp_h[:, gi * 256:(gi + 1) * 256], lhsT=xnT[:, kc, c * 128:(c + 1) * 128],
                                    rhs=ug[:, kc, :], start=(kc == 0), stop=(kc == KC - 1)),
                                   reads=[t_ug, t_xnT], writes=[pp_t], inc=(kc == KC - 1 and gi == 1))
                        sg_h, sg_t = s12[c % 2]
                        op(ACT, lambda pp_h=pp_h, sg_h=sg_h: nc.scalar.activation(out=sg_h[:, 0:512], in_=pp_h[:, 0:512],
                                                                                  func=AF.Tanh, scale=0.5),
                           reads=[pp_t], writes=[sg_t])
                        op(DVE, lambda pp_h=pp_h, sg_h=sg_h: nc.vector.scalar_tensor_tensor(
                            out=sg_h[:, 256:512], in0=sg_h[:, 256:512], scalar=1.0, in1=pp_h[:, 256:512],
                            op0=ALU.add, op1=ALU.mult), reads=[sg_t, pp_t], writes=[sg_t])
                        op(DVE, lambda sg_h=sg_h: nc.vector.scalar_tensor_tensor(
                            out=sg_h[:, 0:256], in0=sg_h[:, 0:256], scalar=1.0, in1=sg_h[:, 256:512],
                            op0=ALU.add, op1=ALU.mult), reads=[sg_t], writes=[sg_t])
                        op(DVE, lambda c=c, sg_h=sg_h: nc.vector.tensor_tensor(out=Gp[:, c % 4, :], in0=sg_h[:, 0:256],
                                                                               in1=gM[:, h * 256:(h + 1) * 256], op=ALU.mult),
                           reads=[sg_t, t_gM], writes=[t_Gp])
                        if c % 4 == 3:
                            cs = c - 3
                            den = numS[:, :, 256]
                            ebs = eb[:, cs:cs + 4, h]
                            op(DVE, lambda: nc.vector.tensor_tensor(out=sm[:, 0, :], in0=den, in1=ebs, op=ALU.mult),
                               reads=[t_numS, t_eb], writes=[t_sm])
                            op(DVE, lambda: nc.vector.scalar_tensor_tensor(out=sm[:, 1, :], in0=sm[:, 0, :], scalar=-1.0,
                                                                           in1=sm[:, 0, :], op0=ALU.mult, op1=ALU.max),
                               reads=[t_sm], writes=[t_sm])
                            op(DVE, lambda: nc.vector.tensor_scalar(out=sm[:, 2, :], in0=sm[:, 1, :], scalar1=1.0,
                                                                    scalar2=None, op0=ALU.max), reads=[t_sm], writes=[t_sm])
                            op(DVE, lambda: nc.vector.reciprocal(out=sm[:, 3, :], in_=sm[:, 2, :]), reads=[t_sm], writes=[t_sm])
                            op(DVE, lambda: nc.vector.tensor_tensor(out=sm[:, 4, :], in0=sm[:, 3, :], in1=ebs, op=ALU.mult),
                               reads=[t_sm, t_eb], writes=[t_sm])
                            for c8 in range(4):
                                op(ACT, lambda c8=c8: nc.scalar.activation(out=junk[:], in_=numS[:, c8, 0:256],
                                                                           func=AF.Square, accum_out=sm[:, 5, c8:c8 + 1]),
                                   reads=[t_numS], writes=[t_junk, t_sm])
                            op(DVE, lambda: nc.vector.tensor_tensor(out=sm[:, 6, :], in0=sm[:, 4, :], in1=sm[:, 4, :],
                                                                    op=ALU.mult), reads=[t_sm], writes=[t_sm])
                            op(DVE, lambda: nc.vector.tensor_tensor(out=sm[:, 7, :], in0=sm[:, 6, :], in1=sm[:, 5, :],
                                                                    op=ALU.mult), reads=[t_sm], writes=[t_sm])
                            op(ACT, lambda: nc.scalar.activation(out=sm[:, 8, :], in_=sm[:, 7, :], func=AF.Sqrt,
                                                                 scale=1.0 / 256, bias=epsT[:]),
                               reads=[t_sm, t_eps], writes=[t_sm])
                            op(DVE, lambda: nc.vector.reciprocal(out=sm[:, 9, :], in_=sm[:, 8, :]), reads=[t_sm], writes=[t_sm])
                            op(DVE, lambda: nc.vector.scalar_tensor_tensor(out=sm[:, 10, :], in0=sm[:, 9, :], scalar=0.25,
                                                                           in1=sm[:, 4, :], op0=ALU.mult, op1=ALU.mult),
                               reads=[t_sm], writes=[t_sm])
                            for c8 in range(4):
                                hn_h, hn_t = hn[c8 % 2]
                                op(DVE, lambda c8=c8, hn_h=hn_h: nc.vector.scalar_tensor_tensor(
                                    out=hn_h[:], in0=numS[:, c8, 0:256], scalar=sm[:, 10, c8:c8 + 1], in1=Gp[:, c8, :],
                                    op0=ALU.mult, op1=ALU.mult), reads=[t_numS, t_sm, t_Gp], writes=[hn_t])
                                for j in range(2):
                                    op(PE, lambda c8=c8, j=j, hn_h=hn_h: nc.tensor.transpose(
                                        out=PT[j][0][:, c8, :], in_=hn_h[:, j * 128:(j + 1) * 128], identity=ident[:]),
                                       reads=[hn_t, t_ident], writes=[PT[j][1]])
                            for j in range(2):
                                op(ACT, lambda j=j, cs=cs: nc.scalar.copy(
                                    out=bufA[:, 2 * h + j, cs * 128:(cs + 4) * 128],
                                    in_=PT[j][0][:, 0:4, :].rearrange("p a b -> p (a b)")),
                                   reads=[PT[j][1]], writes=[t_bufA])
                    ws.release(s3i)
                    ws.release(s4i)
                kb.end_scratch_phase()
            if first:
                tap("hbT", bufA[:], [128, KC, S], BF16, [t_bufA])

            if upto == "M":
                break

            ys = es.enter_context(ExitStack())
            bufY, t_bufY = kb.scratch(ys, "bufY", [128, KC, S], BF16)
            proj_n[0] = 6
            proj_idx[0] = None
            with ExitStack() as ms:
                sgs = [kb.scratch(ms, f"d_sg{i}", [128, 512], F32) for i in range(2)]
                for oc in range(8):
                    si, u, t_u = ws.next()
                    for tc in range(4):
                        py_h, py_t = proj_bank()
                        for kc in range(KC):
                            op(PE, lambda kc=kc, tc=tc, py_h=py_h: nc.tensor.matmul(
                                py_h[:, 0:512], lhsT=u[:, kc, 0:128], rhs=bufA[:, kc, tc * 512:(tc + 1) * 512],
                                start=(kc == 0), stop=(kc == KC - 1)), reads=[t_u, t_bufA], writes=[py_t], inc=(kc == KC - 1))
                        pg_h, pg_t = proj_bank()
                        for kc in range(KC):
                            op(PE, lambda kc=kc, tc=tc, pg_h=pg_h: nc.tensor.matmul(
                                pg_h[:, 0:512], lhsT=u[:, kc, 128:256], rhs=xnT[:, kc, tc * 512:(tc + 1) * 512],
                                start=(kc == 0), stop=(kc == KC - 1)), reads=[t_u, t_xnT], writes=[pg_t], inc=(kc == KC - 1))
                        sg_h, sg_t = sgs[tc % 2]
                        op(ACT, lambda pg_h=pg_h, sg_h=sg_h, oc=oc: nc.scalar.activation(
                            out=sg_h[:], in_=pg_h[:, 0:512], func=AF.Tanh, bias=bgh[:, 8 + oc:9 + oc], scale=0.5),
                           reads=[pg_t, t_bgh], writes=[sg_t])
                        op(DVE, lambda py_h=py_h, sg_h=sg_h, oc=oc, tc=tc: nc.vector.scalar_tensor_tensor(
                            out=bufY[:, oc, tc * 512:(tc + 1) * 512], in0=sg_h[:], scalar=1.0, in1=py_h[:, 0:512],
                            op0=ALU.add, op1=ALU.mult), reads=[py_t, sg_t], writes=[t_bufY])
                    ws.release(si)
                kb.end_scratch_phase()
            if first:
                tap("ybg", bufY[:], [128, KC, S], BF16, [t_bufY])

            if upto == "MD":
                break

            proj_n[0] = 2
            with ExitStack() as ms:
                fq = [kb.scratch(ms, f"f_q{i}", [128, S], BF16) for i in range(2)]
                fk = [kb.scratch(ms, f"f_k{i}", [128, S], BF16) for i in range(2)]
                fv = [kb.scratch(ms, f"f_v{i}", [128, NT, 130], BF16) for i in range(2)]
                fzs = [kb.scratch(ms, f"f_z{i}", [128, NT, 128], BF16) for i in range(2)]
                bms = [kb.scratch(ms, f"f_bm{i}", [128, NT, NT], F32) for i in range(2)]
                ptb = [kb.scratch(ms, f"f_pt{i}", [128, 512], BF16) for i in range(NPTB)]
                obs = [kb.scratch(ms, f"f_ob{i}", [128, 4, 128], BF16) for i in range(2)]
                rin, t_rin = kb.scratch(ms, "f_rin", [128, 16], F32)
                tzs = [kb.scratch(ms, f"f_tz{i}", [128, 128], F32) for i in range(4)]
                for i in range(2):
                    op(DVE, lambda i=i: nc.vector.memset(fv[i][0][:, :, 128:129], 2.0), writes=[fv[i][1]])
                sc = float(128 ** -0.5)
                sb_i = 0
                for h in range(8):
                    s1i, u1, t_u1 = ws.next()
                    s2i, u2, t_u2 = ws.next()
                    qT, t_qT = fq[h % 2]
                    kT, t_kT = fk[h % 2]
                    V, t_V = fv[h % 2]
                    FZ, t_FZ = fzs[h % 2]
                    bm, t_bm = bms[h % 2]
                    for which, (dst, t_dst) in enumerate(((qT, t_qT), (kT, t_kT))):
                        for tc in range(4):
                            pp_h, pp_t = proj_bank()
                            for kc in range(KC):
                                op(PE, lambda kc=kc, tc=tc, which=which, pp_h=pp_h: nc.tensor.matmul(
                                    pp_h[:, 0:512], lhsT=u1[:, kc, which * 128:(which + 1) * 128],
                                    rhs=xnT[:, kc, tc * 512:(tc + 1) * 512], start=(kc == 0), stop=(kc == KC - 1)),
                                   reads=[t_u1, t_xnT], writes=[pp_t], inc=(kc == KC - 1))
                            op(DVE, lambda tc=tc, pp_h=pp_h, dst=dst: nc.vector.tensor_copy(
                                out=dst[:, tc * 512:(tc + 1) * 512], in_=pp_h[:, 0:512]), reads=[pp_t], writes=[t_dst])
                    ws.release(s1i)
                    for tb in range(NT):
                        pp_h, pp_t = proj_bank()
                        for kc in range(KC):
                            op(PE, lambda kc=kc, tb=tb, pp_h=pp_h: nc.tensor.matmul(
                                pp_h[:, 0:256], lhsT=xnT[:, kc, tb * 128:(tb + 1) * 128], rhs=u2[:, kc, :],
                                start=(kc == 0), stop=(kc == KC - 1)), reads=[t_u2, t_xnT], writes=[pp_t], inc=(kc == KC - 1))
                        op(DVE, lambda tb=tb, pp_h=pp_h: nc.vector.tensor_copy(out=V[:, tb, 0:128], in_=pp_h[:, 0:128]),
                           reads=[pp_t], writes=[t_V])
                        tz_h, tz_t = tzs[tb % 4]
                        op(ACT, lambda pp_h=pp_h, tz_h=tz_h: nc.scalar.activation(out=tz_h[:], in_=pp_h[:, 128:256],
                                                                                  func=AF.Tanh, scale=0.5),
                           reads=[pp_t], writes=[tz_t])
                        op(DVE, lambda tb=tb, pp_h=pp_h, tz_h=tz_h: nc.vector.scalar_tensor_tensor(
                            out=FZ[:, tb, :], in0=tz_h[:], scalar=1.0, in1=pp_h[:, 128:256], op0=ALU.add, op1=ALU.mult),
                           reads=[pp_t, tz_t], writes=[t_FZ])
                    ws.release(s2i)
                    for j in range(NT):
                        op(DVE, lambda j=j: nc.vector.tensor_scalar(out=bm[:, j, :], in0=RR[:, :, h],
                                                                    scalar1=cc[:, j, h:h + 1], scalar2=None,
                                                                    op0=ALU.subtract), reads=[t_RR, t_cc], writes=[t_bm])
                    steps = [(I, j) for I in range(4) for j in range(4 * I + 4)]

                    def emit_ST(n):
                        I, j = steps[n]
                        i0 = max(j, 4 * I)
                        ps_h, ps_t = PB[2 + n % 2]
                        N = (4 * I + 4 - i0) * 128
                        op(PE, lambda: nc.tensor.matmul(ps_h[:, 0:N], lhsT=kT[:, j * 128:(j + 1) * 128],
                                                        rhs=qT[:, i0 * 128:(4 * I + 4) * 128], start=True, stop=True),
                           reads=[t_kT, t_qT], writes=[ps_t])

                    def emit_exp(n):
                        I, j = steps[n]
                        i0 = max(j, 4 * I)
                        ps_h, ps_t = PB[2 + n % 2]
                        p_h, p_t = ptb[n % NPTB]
                        for i in range(i0, 4 * I + 4):
                            lo = (i - i0) * 128
                            op(ACT, lambda i=i, lo=lo: nc.scalar.activation(
                                out=p_h[:, lo:lo + 128], in_=ps_h[:, lo:lo + 128], func=AF.Exp, scale=sc,
                                bias=bm[:, j, i:i + 1]), reads=[ps_t, t_bm], writes=[p_t])
                        if j >= 4 * I:
                            op(DVE, lambda: nc.vector.tensor_tensor(out=p_h[:, 0:128], in0=p_h[:, 0:128], in1=maskb[:],
                                                                    op=ALU.mult), reads=[p_t, t_maskb], writes=[p_t])

                    def emit_PV(n):
                        I, j = steps[n]
                        i0 = max(j, 4 * I)
                        p_h, p_t = ptb[n % NPTB]
                        for i in range(i0, 4 * I + 4):
                            il = i - 4 * I
                            lo = (i - i0) * 128
                            po_h, po_t = PB[4 + il // 2]
                            off = (il % 2) * 129
                            op(PE, lambda i=i, lo=lo, po_h=po_h, off=off, il=il: nc.tensor.matmul(
                                po_h[:, off:off + 129], lhsT=p_h[:, lo:lo + 128], rhs=V[:, j, 0:129],
                                start=(j == 0 and il % 2 == 0), stop=(j == i), skip_group_check=True),
                               reads=[p_t, t_V], writes=[po_t])
                            if j == i:
                                ob_h, ob_t = obs[I % 2]
                                op(DVE, lambda po_h=po_h, off=off, i=i: nc.vector.reciprocal(
                                    out=rin[:, i:i + 1], in_=po_h[:, off + 128:off + 129]), reads=[po_t], writes=[t_rin])
                                op(DVE, lambda po_h=po_h, off=off, i=i, il=il, ob_h=ob_h: nc.vector.scalar_tensor_tensor(
                                    out=ob_h[:, il, :], in0=po_h[:, off:off + 128], scalar=rin[:, i:i + 1], in1=FZ[:, i, :],
                                    op0=ALU.mult, op1=ALU.mult), reads=[po_t, t_rin, t_FZ], writes=[ob_t])
                                pt_h, pt_t = PT[I % 2]
                                pb0 = 0
                                op(PE, lambda il=il, ob_h=ob_h, pt_h=pt_h, pb0=pb0: nc.tensor.transpose(
                                    out=pt_h[:, pb0 + il, :], in_=ob_h[:, il, :], identity=ident[:]),
                                   reads=[ob_t, t_ident], writes=[pt_t])
                                if il == 3:
                                    op(DVE, lambda pt_h=pt_h, I=I: nc.vector.tensor_copy(
                                        out=bufA[:, h, I * 512:(I + 1) * 512],
                                        in_=pt_h[:, pb0:pb0 + 4, :].rearrange("p a b -> p (a b)")),
                                       reads=[pt_t], writes=[t_bufA])

                    emit_ST(0)
                    for n in range(len(steps)):
                        emit_exp(n)
                        if n + 1 < len(steps):
                            emit_ST(n + 1)
                        emit_PV(n)
                kb.end_scratch_phase()
            if first:
                tap("oaT", bufA[:], [128, KC, S], BF16, [t_bufA])

            if upto == "F":
                break

            proj_n[0] = 6
            with ExitStack() as ms:
                sgs = [kb.scratch(ms, f"e_sg{i}", [128, 512], F32) for i in range(2)]
                for oc in range(8):
                    si, u, t_u = ws.next()
                    for tc in range(4):
                        py_h, py_t = proj_bank()
                        for kc in range(KC):
                            op(PE, lambda kc=kc, tc=tc, py_h=py_h: nc.tensor.matmul(
                                py_h[:, 0:512], lhsT=u[:, kc, 0:128], rhs=bufA[:, kc, tc * 512:(tc + 1) * 512],
                                start=(kc == 0), stop=(kc == KC - 1)), reads=[t_u, t_bufA], writes=[py_t], inc=(kc == KC - 1))
                        pg_h, pg_t = proj_bank()
                        for kc in range(KC):
                            op(PE, lambda kc=kc, tc=tc, pg_h=pg_h: nc.tensor.matmul(
                                pg_h[:, 0:512], lhsT=u[:, kc, 128:256], rhs=xnT[:, kc, tc * 512:(tc + 1) * 512],
                                start=(kc == 0), stop=(kc == KC - 1)), reads=[t_u, t_xnT], writes=[pg_t], inc=(kc == KC - 1))
                        sg_h, sg_t = sgs[tc % 2]
                        op(ACT, lambda pg_h=pg_h, sg_h=sg_h, oc=oc: nc.scalar.activation(
                            out=sg_h[:], in_=pg_h[:, 0:512], func=AF.Tanh, bias=bgh[:, oc:oc + 1], scale=0.5),
                           reads=[pg_t, t_bgh], writes=[sg_t])
                        op(DVE, lambda py_h=py_h, sg_h=sg_h: nc.vector.scalar_tensor_tensor(
                            out=sg_h[:], in0=sg_h[:], scalar=1.0, in1=py_h[:, 0:512], op0=ALU.add, op1=ALU.mult),
                           reads=[py_t, sg_t], writes=[sg_t])
                        op(DVE, lambda sg_h=sg_h, oc=oc, tc=tc: nc.vector.tensor_tensor(
                            out=bufY[:, oc, tc * 512:(tc + 1) * 512], in0=sg_h[:], in1=bufY[:, oc, tc * 512:(tc + 1) * 512],
                            op=ALU.add), reads=[sg_t, t_bufY], writes=[t_bufY])
                    ws.release(si)
                kb.end_scratch_phase()
            if first:
                tap("yT", bufY[:], [128, KC, S], BF16, [t_bufY])

            if upto == "FD":
                break

            def _phase_O(seq=seq, bufY=bufY, t_bufY=t_bufY, ys=ys):
                dma(SP, out=gX[:], in_=final_g.partition_broadcast(128), writes=[t_gX])
                us = [ws.next() for _ in range(4)]
                for tb in range(NT):
                    xt_h, xt_t = xts[tb % NXT]
                    dma(SP, out=xt_h[:], in_=x_d[seq, tb * 128:(tb + 1) * 128, :], writes=[xt_t])
                    banks = [proj_bank(), proj_bank()]
                    for hf in range(2):
                        po_h, po_t = banks[hf]
                        for n in range(2):
                            si, u, t_u = us[2 * hf + n]
                            for kc in range(KC):
                                op(PE, lambda kc=kc, n=n, u=u, po_h=po_h: nc.tensor.matmul(
                                    po_h[:, n * 256:(n + 1) * 256], lhsT=bufY[:, kc, tb * 128:(tb + 1) * 128], rhs=u[:, kc, :],
                                    start=(kc == 0), stop=(kc == KC - 1)), reads=[t_u, t_bufY], writes=[po_t],
                                   inc=(kc == KC - 1 and n == 1))
                        op(DVE, lambda hf=hf, po_h=po_h: nc.vector.scalar_tensor_tensor(
                            out=xt_h[:, hf * 512:(hf + 1) * 512], in0=po_h[:, 0:512], scalar=0.5,
                            in1=xt_h[:, hf * 512:(hf + 1) * 512], op0=ALU.mult, op1=ALU.add),
                           reads=[po_t, xt_t], writes=[xt_t])
                    xn_h, xn_t = xns[tb % NXN]
                    rmsnorm_tail(xt_h, xt_t, tb, gF, t_gF, xt_h[:], [xt_t], xn_h[:], [xn_t])
                    dma(SP, out=y_d[seq, tb * 128:(tb + 1) * 128, :], in_=xt_h[:], reads=[xt_t])
                for (si, u, t_u) in us:
                    ws.release(si)
                kb.scr_tiles.append(t_bufY)
                kb.end_scratch_phase()
                ys.close()

            pending_O = _phase_O

        if pending_O is not None:
            pending_O()
            pending_O = None
        if kb.mode == "defer":
            kb.flush()
            build.n_inst = kb.n_inst
            build.nsem = kb.nsem
            build.est_us = kb.est_makespan
        out_costs = [(o.cost, o.aset) for o in kb.ops]
    return nc, dbg_out, out_costs


def host_inputs(inputs):
    f = lambda a: np.ascontiguousarray(np.asarray(a, dtype=np.float32))
    gb = np.concatenate([f(inputs["b_fox_f"])[0], f(inputs["b_ml_i"])[0], f(inputs["b_ml_f"])[0]])
    gbias = np.ascontiguousarray(np.broadcast_to(gb[None, None, :], (128, NT, 16)))
    convw = np.ascontiguousarray(f(inputs["conv_w"])[0].reshape(4, 8, 128).transpose(2, 1, 0))
    convb = np.ascontiguousarray(f(inputs["conv_b"])[0].reshape(8, 128).T)
    bgate = np.ascontiguousarray(f(inputs["b_gate"])[0].reshape(16, 128).T)
    return {
        "w_in": f(inputs["w_in"])[0],
        "w_fox_down": f(inputs["w_fox_down"])[0],
        "w_ml_down": f(inputs["w_ml_down"])[0],
        "w_out": f(inputs["w_out"])[0],
        "norm_g": f(inputs["norm_g"])[0],
        "final_g": f(inputs["final_g"]),
        "ml_norm_g": f(inputs["ml_norm_g"])[0],
        "gbias": gbias,
        "convw": convw,
        "convb": convb,
        "bgate": bgate,
    }


def kernel(**inputs):
    x = np.ascontiguousarray(np.asarray(inputs["x"], dtype=np.float32))
    B = x.shape[0]
    nseq = B // NCORES
    shared = host_inputs(inputs)
    nc, _ = build(nseq)
    in_maps = []
    for c in range(NCORES):
        m = dict(shared)
        m["x"] = x[c * nseq:(c + 1) * nseq]
        in_maps.append(m)
    res = run_bass_kernel_spmd(nc, in_maps, core_ids=list(range(NCORES)))
    out = np.concatenate([np.asarray(r["y"]) for r in res.results], axis=0)
    return out.astype(np.float32, copy=False)
```

```python
import types
import numpy as np
from contextlib import ExitStack
import concourse.bass as bass
import concourse.mybir as mybir
from concourse.bass_utils import run_bass_kernel_spmd

F32 = mybir.dt.float32
BF16 = mybir.dt.bfloat16
AF = mybir.ActivationFunctionType
ALU = mybir.AluOpType

S = 2048
D = 1024
NT = 16
KC = 8
NCORES = 8
EPS = 1e-6
C_FQ, C_FK, C_FV, C_FF, C_FZ = 0, 1024, 2048, 3072, 3080
C_MQ, C_MK, C_MV, C_MI, C_MF, C_MO, C_MZ = 4104, 4616, 5128, 6152, 6156, 6160, 7184
C_GA, C_GB = 8208, 9232
IN_W = 10256
SEM_LIMIT = 30000
SLACK_US = 0.0
PE_MAXG = 2
NPTB = 3
NXT = 4
MDB = 1
NCB = 2
NTH = 2
NQB = 2
NVA = 1
NXN = 3
M_PROJ_BANKS = [0, 1, 5]
NSLOT = 6


class SemObj:
    __slots__ = ("h", "owner", "val")

    def __init__(self, h, owner):
        self.h = h
        self.owner = owner
        self.val = 0


class Tile:
    __slots__ = ("w", "r", "name", "psum")

    def __init__(self, name, init=None, psum=False):
        self.name = name
        self.w = list(init) if init else []
        self.r = []
        self.psum = psum


class Eng:
    def __init__(self, kb, name, h, is_pe=False):
        self.kb = kb
        self.name = name
        self.h = h
        self.is_pe = is_pe
        self.waited = {}
        self.cur = None

    def sem(self):
        if self.cur is None or self.cur.val >= SEM_LIMIT:
            self.cur = self.kb.new_sem(self)
        return self.cur


class Op:
    __slots__ = ("id", "eng", "fns", "deps", "cost", "is_dma", "tok", "start", "ninstr", "aset")

    def __init__(self, oid, eng, is_dma):
        self.id = oid
        self.eng = eng
        self.fns = []
        self.deps = []
        self.cost = 0.0
        self.is_dma = is_dma
        self.tok = None
        self.start = 0.0
        self.aset = None


def _snap(fn):
    if fn.__closure__ is None:
        return fn
    cells = []
    for c in fn.__closure__:
        try:
            cells.append(types.CellType(c.cell_contents))
        except ValueError:
            cells.append(c)
    return types.FunctionType(fn.__code__, fn.__globals__, fn.__name__, fn.__defaults__, tuple(cells))


def _free_elems(acc):
    try:
        ap = acc.ap
        n = 1
        for (_, c) in ap[1:]:
            n *= c
        return max(int(n), 1)
    except Exception:
        return 64


class KB:
    def __init__(self, nc, es, costs=None):
        self.nc = nc
        self.es = es
        self.mode = "measure" if costs is None else "defer"
        self.costs_in = costs
        self.nsem = 0
        self.PE = Eng(self, "pe", nc.tensor, is_pe=True)
        self.ACT = Eng(self, "act", nc.scalar)
        self.DVE = Eng(self, "dve", nc.vector)
        self.POOL = Eng(self, "pool", nc.gpsimd)
        self.SP = Eng(self, "sp", nc.sync)
        self.engs = [self.PE, self.ACT, self.DVE, self.POOL, self.SP]
        self.dma_pool = {}
        self.dma_rr = {}
        self.scr_tokens = []
        self.scr_tiles = []
        self.scr_id = 0
        self.n_inst = 0
        self.ops = []
        self.open_pe = None

    def new_sem(self, owner):
        self.nsem += 1
        h = self.es.enter_context(self.nc.semaphore(f"s{self.nsem}"))
        return SemObj(h, owner)

    def _deps(self, o, reads, writes):
        eng = o.eng
        ops = self.ops
        d = o.deps
        for t in reads:
            d.extend(t.w)
            if t.psum:
                d.extend(r for r in t.r if ops[r].eng is not eng)
        for t in writes:
            d.extend(t.w)
            d.extend(t.r)

    def _touch(self, oid, reads, writes):
        for t in reads:
            if not t.r or t.r[-1] != oid:
                t.r.append(oid)
        for t in writes:
            t.w = [oid]
            t.r = []

    def _cost_of(self, eng, ins):
        try:
            i = ins.ins
            n = _free_elems(i.outs[0])
            if eng.is_pe:
                f32 = False
                try:
                    f32 = (i.ins[0].dtype == F32) and not getattr(i, "is_transpose", False)
                except Exception:
                    pass
                return (max(n, 200) * (4 if f32 else 1) * 0.5 + 8.0) / 1000.0
            if eng is self.ACT:
                return 0.22 + n * 0.00083
            if eng is self.DVE:
                return 0.10 + n * 0.00104
            return 0.25 + n * 0.002
        except Exception:
            return 0.3

    def op(self, eng, fn, reads=(), writes=(), inc=True):
        if eng.is_pe and self.open_pe is not None:
            o = self.open_pe
        else:
            assert self.open_pe is None, "non-PE op recorded while a PE group is open"
            o = Op(len(self.ops), eng, False)
            self.ops.append(o)
            if eng.is_pe and not inc:
                self.open_pe = o
        if eng.is_pe:
            o.ninstr = getattr(o, "ninstr", 0) + 1
            if inc or o.ninstr >= PE_MAXG:
                self.open_pe = None
        self._deps(o, reads, writes)
        self._touch(o.id, reads, writes)
        if self.mode == "measure":
            ins = fn()
            o.cost += self._cost_of(eng, ins)
            if eng is self.ACT:
                try:
                    fname = str(ins.ins.func)
                except Exception:
                    fname = ""
                if "Exp" in fname or "Tanh" in fname:
                    o.aset = "A"
                elif "Sqrt" in fname:
                    o.aset = "S"
                elif "Ln" in fname:
                    o.aset = "L"
            return ins
        o.fns.append(_snap(fn))
        return None

    def dma(self, q, out, in_, reads=(), writes=()):
        assert self.open_pe is None
        o = Op(len(self.ops), q, True)
        self.ops.append(o)
        self._deps(o, reads, writes)
        self._touch(o.id, reads, writes)
        if self.mode == "measure":
            q.h.dma_start(out=out, in_=in_)
            try:
                n = 1
                for (_, c) in out.ap:
                    n *= c
            except Exception:
                n = 131072
            o.cost = 2.0 + n * 4 / 150e3
            return None
        o.fns.append(lambda: q.h.dma_start(out=out, in_=in_))
        return None

    def sb(self, name, shape, dt, es=None):
        h = (es or self.es).enter_context(self.nc.sbuf_tensor(name, shape, dt))
        return h, Tile(name)

    def scratch(self, es, name, shape, dt):
        self.scr_id += 1
        h = es.enter_context(self.nc.sbuf_tensor(f"{name}_{self.scr_id}", shape, dt))
        t = Tile(name, self.scr_tokens)
        self.scr_tiles.append(t)
        return h, t

    def end_scratch_phase(self):
        acc = set(self.scr_tokens)
        for t in self.scr_tiles:
            acc.update(t.w)
            acc.update(t.r)
        self.scr_tokens = sorted(acc)
        self.scr_tiles = []

    def schedule(self):
        ops = self.ops
        n = len(ops)
        assert len(self.costs_in) == n, (len(self.costs_in), n)
        costs = [c for (c, _) in self.costs_in]
        asets = [a for (_, a) in self.costs_in]
        act = self.ACT
        cur_set = [None]
        TL = 0.0
        succ = [[] for _ in range(n)]
        for o in ops:
            o.deps = sorted(set(d for d in o.deps if d != o.id))
            for d in o.deps:
                succ[d].append(o.id)
        pend = {e: [] for e in self.engs}
        for o in ops:
            pend[o.eng].append(o.id)
        done = [False] * n
        fin = [0.0] * n
        efree = {e: 0.0 for e in self.engs}
        best = {e: None for e in self.engs}
        dirty = set(self.engs)
        HOP, SAME, W = 0.30, 0.06, 400
        SLACK = SLACK_US
        order = []
        INF = float("inf")
        while len(order) < n:
            for e in dirty:
                lst = pend[e]
                be, bi, bpos = INF, -1, -1
                ef = efree[e]
                lim = min(W, len(lst))
                for pos in range(lim):
                    oid = lst[pos]
                    rt = ef
                    ok = True
                    for d in ops[oid].deps:
                        if not done[d]:
                            ok = False
                            break
                        t = fin[d] + (SAME if ops[d].eng is e else HOP)
                        if t > rt:
                            rt = t
                    if ok and e is act and asets[oid] is not None and cur_set[0] is not None and asets[oid] != cur_set[0]:
                        rt += TL
                    if ok and rt < be - SLACK:
                        be, bi, bpos = rt, oid, pos
                        if rt <= ef:
                            break
                best[e] = (be, bi, bpos) if bi >= 0 else None
            dirty.clear()
            ce, cb = None, None
            for e in self.engs:
                b = best[e]
                if b is not None and (cb is None or b[0] < cb[0] or (b[0] == cb[0] and b[1] < cb[1])):
                    ce, cb = e, b
            assert ce is not None, "scheduler deadlock"
            st, oid, pos = cb
            o = ops[oid]
            o.start = st
            if o.is_dma:
                efree[ce] = st + 0.15
                fin[oid] = st + costs[oid]
            else:
                efree[ce] = st + costs[oid]
                fin[oid] = efree[ce]
                if ce is act and asets[oid] is not None:
                    cur_set[0] = asets[oid]
            done[oid] = True
            del pend[ce][pos]
            order.append(o)
            dirty.add(ce)
            for sidx in succ[oid]:
                dirty.add(ops[sidx].eng)
        self.est_makespan = max(fin) if fin else 0.0
        return order

    def flush(self):
        order = self.schedule()
        ops = self.ops
        for o in order:
            eng = o.eng
            for d in o.deps:
                p = ops[d]
                if eng.is_pe and p.eng is eng and not p.is_dma:
                    continue
                s, v = p.tok
                if eng.waited.get(s, 0) >= v:
                    continue
                eng.h.wait_ge(s.h, v)
                eng.waited[s] = v
                self.n_inst += 1
            if o.is_dma:
                pool = self.dma_pool.setdefault(eng, [])
                if len(pool) < 8:
                    dsem = self.new_sem(None)
                    pool.append(dsem)
                else:
                    i = self.dma_rr.get(eng, 0)
                    dsem = pool[i % len(pool)]
                    self.dma_rr[eng] = i + 1
                if dsem.val and eng.waited.get(dsem, 0) < dsem.val:
                    eng.h.wait_ge(dsem.h, dsem.val)
                    eng.waited[dsem] = dsem.val
                    self.n_inst += 1
                ins = o.fns[0]()
                dsem.val += 16
                ins.then_inc(dsem.h, 16)
                o.tok = (dsem, dsem.val)
            else:
                ins = None
                for fn in o.fns:
                    ins = fn()
                    self.n_inst += 1
                s = eng.sem()
                s.val += 1
                ins.then_inc(s.h, 1)
                o.tok = (s, s.val)
            self.n_inst += 1
        SP = self.SP
        for q, pool in self.dma_pool.items():
            for dsem in pool:
                if dsem.val and SP.waited.get(dsem, 0) < dsem.val:
                    self.nc.sync.wait_ge(dsem.h, dsem.val)
                    SP.waited[dsem] = dsem.val


class WStream:
    def __init__(self, kb, slots, plan):
        self.kb = kb
        self.slots = slots
        self.plan = plan
        self.issued = 0
        self.taken = 0
        self.free = [True] * len(slots)

    def _pump(self):
        while self.issued < len(self.plan):
            si = self.issued % len(self.slots)
            if not self.free[si]:
                break
            h, t = self.slots[si]
            for (dst, src, c0, n) in self.plan[self.issued]:
                self.kb.dma(self.kb.POOL, out=h[:, :, dst:dst + n], in_=src[:, :, c0:c0 + n], writes=[t])
            self.free[si] = False
            self.issued += 1

    def next(self):
        self._pump()
        assert self.taken < self.issued, "weight stream stalled (slot not released)"
        si = self.taken % len(self.slots)
        self.taken += 1
        h, t = self.slots[si]
        return si, h, t

    def release(self, si):
        self.free[si] = True
        self._pump()


def unit_plan(nseq, wv, wmd, wfd, wout):
    plan = []
    for _ in range(nseq):
        for h in range(4):
            plan.append([(0, wv, C_MQ + h * 128, 128), (128, wv, C_MK + h * 128, 128)])
            plan.append([(0, wv, C_MV + h * 256, 256)])
            plan.append([(0, wv, C_MO + h * 256, 256)])
            plan.append([(0, wv, C_MZ + h * 256, 256)])
        for oc in range(8):
            plan.append([(0, wmd, oc * 128, 128), (128, wv, C_GB + oc * 128, 128)])
        for h in range(8):
            plan.append([(0, wv, C_FQ + h * 128, 128), (128, wv, C_FK + h * 128, 128)])
            plan.append([(0, wv, C_FV + h * 128, 128), (128, wv, C_FZ + h * 128, 128)])
        for oc in range(8):
            plan.append([(0, wfd, oc * 128, 128), (128, wv, C_GA + oc * 128, 128)])
        for u in range(4):
            plan.append([(0, wout, u * 256, 256)])
    return plan


def build(nseq=2, dbg=None, upto=None):
    _, _, costs = _build_pass(nseq, dbg, upto, None)
    nc, dbg_out, _ = _build_pass(nseq, dbg, upto, costs)
    return nc, dbg_out


def _build_pass(nseq, dbg, upto, costs):
    dbg = dbg or set()
    nc = bass.Bass("TRN2", target_bir_lowering=False)
    x_d = nc.dram_tensor("x", [nseq, S, D], F32, kind="ExternalInput").ap()
    w_in = nc.dram_tensor("w_in", [D, IN_W], F32, kind="ExternalInput").ap()
    w_fd = nc.dram_tensor("w_fox_down", [D, D], F32, kind="ExternalInput").ap()
    w_md = nc.dram_tensor("w_ml_down", [D, D], F32, kind="ExternalInput").ap()
    w_o = nc.dram_tensor("w_out", [D, D], F32, kind="ExternalInput").ap()
    norm_g = nc.dram_tensor("norm_g", [D], F32, kind="ExternalInput").ap()
    final_g = nc.dram_tensor("final_g", [D], F32, kind="ExternalInput").ap()
    ml_g = nc.dram_tensor("ml_norm_g", [D], F32, kind="ExternalInput").ap()
    gbias_d = nc.dram_tensor("gbias", [128, NT, 16], F32, kind="ExternalInput").ap()
    convw_d = nc.dram_tensor("convw", [128, 8, 4], F32, kind="ExternalInput").ap()
    convb_d = nc.dram_tensor("convb", [128, 8], F32, kind="ExternalInput").ap()
    bgate_d = nc.dram_tensor("bgate", [128, 16], F32, kind="ExternalInput").ap()
    y_d = nc.dram_tensor("y", [nseq, S, D], F32, kind="ExternalOutput").ap()
    dbg_out = {}

    def dbg_tensor(name, shape, dt):
        dbg_out[name] = nc.dram_tensor("dbg_" + name, shape, dt, kind="ExternalOutput").ap()
        return dbg_out[name]

    wv = w_in.rearrange("(kc p) c -> p kc c", p=128)
    wfd = w_fd.rearrange("(kc p) c -> p kc c", p=128)
    wmd = w_md.rearrange("(kc p) c -> p kc c", p=128)
    wout = w_o.rearrange("(kc p) c -> p kc c", p=128)

    with ExitStack() as es:
        kb = KB(nc, es, costs)
        PE, ACT, DVE, POOL, SP = kb.PE, kb.ACT, kb.DVE, kb.POOL, kb.SP
        op, dma = kb.op, kb.dma

        xnT, t_xnT = kb.sb("xnT", [128, KC, S], BF16)
        bufA, t_bufA = kb.sb("bufA", [128, KC, S], BF16)
        slots = [kb.sb(f"wslot{i}", [128, KC, 256], BF16) for i in range(NSLOT)]
        wg, t_wg = kb.sb("wg", [128, KC, 16], BF16)
        gX, t_gX = kb.sb("gX", [128, D], F32)
        gN, t_gN = kb.sb("gN", [128, D], F32)
        gF = gM = gX
        t_gF = t_gM = t_gX
        gbias, t_gbias = kb.sb("gbias_sb", [128, NT, 16], F32)
        cw, t_cw = kb.sb("cw", [128, 8, 4], F32)
        cb, t_cb = kb.sb("cb", [128, 8], F32)
        bg, t_bg = kb.sb("bg", [128, 16], F32)
        cbh, t_cbh = kb.sb("cbh", [128, 8], F32)
        bgh, t_bgh = kb.sb("bgh", [128, 16], F32)
        epsT, t_eps = kb.sb("epsT", [128, 1], F32)
        ident, t_ident = kb.sb("ident", [128, 128], BF16)
        onesb, t_onesb = kb.sb("onesb", [128, 128], BF16)
        maskb, t_maskb = kb.sb("maskb", [128, 128], BF16)
        maskf, t_maskf = kb.sb("maskf", [128, 128], F32)
        onesf, t_onesf = kb.sb("onesf", [128, 128], F32)
        hmid, t_hmid = kb.sb("hmid", [128, 128], F32)
        xts = [kb.sb(f"xt{i}", [128, D], F32) for i in range(NXT)]
        xns = [kb.sb(f"xn{i}", [128, D], BF16) for i in range(NXN)]
        stats = [kb.sb(f"stat{i}", [128, 4], F32) for i in range(NXT)]
        cc, t_cc = kb.sb("cc", [128, NT, 16], F32)
        RR, t_RR = kb.sb("RR", [128, NT, 16], F32)
        ew, t_ew = kb.sb("ew", [128, NT, 4], F32)
        eb, t_eb = kb.sb("eb", [128, NT, 4], F32)
        aa, t_aa = kb.sb("aa", [128, NT, 4], F32)

        PB = []
        for i in range(6):
            h = es.enter_context(nc.psum_tensor(f"pb{i}", [128, 512], F32))
            PB.append((h, Tile(f"pb{i}", psum=True)))
        PT = []
        for i in range(2):
            h = es.enter_context(nc.psum_tensor(f"pt{i}", [128, 8, 128], BF16))
            PT.append((h, Tile(f"pt{i}", psum=True)))

        dma(SP, out=gbias[:], in_=gbias_d[:, :, :], writes=[t_gbias])
        dma(SP, out=cw[:], in_=convw_d[:, :, :], writes=[t_cw])
        dma(SP, out=cb[:], in_=convb_d[:, :], writes=[t_cb])
        dma(SP, out=bg[:], in_=bgate_d[:, :], writes=[t_bg])
        dma(POOL, out=wg[:, :, 0:8], in_=wv[:, :, C_FF:C_FF + 8], writes=[t_wg])
        dma(POOL, out=wg[:, :, 8:16], in_=wv[:, :, C_MI:C_MI + 8], writes=[t_wg])
        op(DVE, lambda: nc.vector.memset(epsT[:], EPS), writes=[t_eps])
        op(DVE, lambda: nc.vector.tensor_scalar(out=cbh[:], in0=cb[:], scalar1=0.5, scalar2=None, op0=ALU.mult),
           reads=[t_cb], writes=[t_cbh])
        op(DVE, lambda: nc.vector.tensor_scalar(out=bgh[:], in0=bg[:], scalar1=0.5, scalar2=None, op0=ALU.mult),
           reads=[t_bg], writes=[t_bgh])
        op(POOL, lambda: nc.gpsimd.memset(onesb[:], 1.0), writes=[t_onesb])
        op(POOL, lambda: nc.gpsimd.memset(onesf[:], 1.0), writes=[t_onesf])
        op(POOL, lambda: nc.gpsimd.memset(hmid[:], 0.0), writes=[t_hmid])
        op(POOL, lambda: nc.gpsimd.memset(hmid[0:64, :], 1.0), writes=[t_hmid])
        op(POOL, lambda: nc.gpsimd.affine_select(out=ident[:], in_=onesb[:], pattern=[[1, 128]],
                                                 compare_op=ALU.is_equal, fill=0.0, base=0,
                                                 channel_multiplier=-1), reads=[t_onesb], writes=[t_ident])
        op(POOL, lambda: nc.gpsimd.affine_select(out=maskb[:], in_=onesb[:], pattern=[[1, 128]],
                                                 compare_op=ALU.is_ge, fill=0.0, base=0,
                                                 channel_multiplier=-1), reads=[t_onesb], writes=[t_maskb])
        op(POOL, lambda: nc.gpsimd.affine_select(out=maskf[:], in_=onesf[:], pattern=[[1, 128]],
                                                 compare_op=ALU.is_ge, fill=0.0, base=0,
                                                 channel_multiplier=-1), reads=[t_onesf], writes=[t_maskf])

        ws = WStream(kb, slots, unit_plan(nseq, wv, wmd, wfd, wout))
        proj_rr = [0]
        proj_n = [2]

        proj_idx = [None]

        def proj_bank():
            proj_rr[0] += 1
            if proj_idx[0] is not None:
                return PB[proj_idx[0][proj_rr[0] % len(proj_idx[0])]]
            return PB[proj_rr[0] % proj_n[0]]

        def tap(name, src_ap, shape, dt, tiles):
            if name in dbg:
                d = dbg_tensor(name, shape, dt)
                dma(SP, out=d, in_=src_ap, reads=tiles)

        def rmsnorm_tail(xt_h, xt_t, si, g_h, g_t, out_ap, out_tiles, sq_out_ap, sq_out_tiles):
            st_h, st_t = stats[si % NXT]
            op(ACT, lambda: nc.scalar.activation(out=sq_out_ap, in_=xt_h[:], func=AF.Square,
                                                 accum_out=st_h[:, 0:1]),
               reads=[xt_t], writes=sq_out_tiles + [st_t])
            op(ACT, lambda: nc.scalar.activation(out=st_h[:, 1:2], in_=st_h[:, 0:1], func=AF.Sqrt,
                                                 scale=1.0 / D, bias=epsT[:]),
               reads=[st_t, t_eps], writes=[st_t])
            op(DVE, lambda: nc.vector.reciprocal(out=st_h[:, 2:3], in_=st_h[:, 1:2]), reads=[st_t], writes=[st_t])
            op(DVE, lambda: nc.vector.scalar_tensor_tensor(out=out_ap, in0=xt_h[:], scalar=st_h[:, 2:3],
                                                           in1=g_h[:], op0=ALU.mult, op1=ALU.mult),
               reads=[xt_t, st_t, g_t], writes=out_tiles)

        pending_O = None
        for seq in range(nseq):
            first = (seq == 0)
            proj_n[0] = 2
            if first:
                dma(SP, out=gN[:], in_=norm_g.partition_broadcast(128), writes=[t_gN])
            for tb in range(NT):
                xt_h, xt_t = xts[tb % NXT]
                xn_h, xn_t = xns[tb % NXN]
                dma(SP, out=xt_h[:], in_=x_d[seq, tb * 128:(tb + 1) * 128, :], writes=[xt_t])
                rmsnorm_tail(xt_h, xt_t, tb, gN, t_gN, xn_h[:], [xn_t], xn_h[:], [xn_t])
                pt_h, pt_t = PT[tb % 2]
                for kc in range(KC):
                    op(PE, lambda kc=kc: nc.tensor.transpose(out=pt_h[:, kc, :], in_=xn_h[:, kc * 128:(kc + 1) * 128],
                                                             identity=ident[:]),
                       reads=[xn_t, t_ident], writes=[pt_t], inc=(kc == KC - 1))
                op(ACT, lambda: nc.scalar.copy(out=xnT[:, :, tb * 128:(tb + 1) * 128], in_=pt_h[:]),
                   reads=[pt_t], writes=[t_xnT])
            if first:
                tap("xnT", xnT[:], [128, KC, S], BF16, [t_xnT])

            if upto == "P0":
                break

            gs = es.enter_context(ExitStack())
            zs, t_zs = kb.scratch(gs, "zs", [128, NT, 16], F32)
            lf, t_lf = kb.scratch(gs, "lf", [128, NT, 16], F32)
            tot, t_tot = kb.scratch(gs, "tot", [128, NT, 16], F32)
            bc, t_bc = kb.scratch(gs, "bc", [128, NT, 16], F32)
            half, t_half = kb.scratch(gs, "half", [128, NT, 16], F32)
            pre, t_pre = kb.scratch(gs, "pre", [128, NT, 16], F32)
            gtmp, t_gtmp = kb.scratch(gs, "gtmp", [128, NT, 16], F32)
            pz_h, pz_t = PB[2]
            for tb in range(NT):
                for kc in range(KC):
                    op(PE, lambda kc=kc, tb=tb: nc.tensor.matmul(pz_h[:, tb * 16:(tb + 1) * 16],
                                                                 lhsT=xnT[:, kc, tb * 128:(tb + 1) * 128],
                                                                 rhs=wg[:, kc, :], start=(kc == 0), stop=(kc == KC - 1)),
                       reads=[t_xnT, t_wg], writes=[pz_t], inc=(kc == KC - 1 and tb == NT - 1))
            op(DVE, lambda: nc.vector.tensor_tensor(out=zs[:].rearrange("p a b -> p (a b)"), in0=pz_h[:, 0:256],
                                                    in1=gbias[:].rearrange("p a b -> p (a b)"), op=ALU.add),
               reads=[pz_t, t_gbias], writes=[t_zs])
            if upto == "G1":
                tap("lf", zs[:], [128, NT, 16], F32, [t_zs])
                break
            op(ACT, lambda: nc.scalar.activation(out=gtmp[:], in_=zs[:], func=AF.Exp, scale=-1.0),
               reads=[t_zs], writes=[t_gtmp])
            op(ACT, lambda: nc.scalar.activation(out=gtmp[:], in_=gtmp[:], func=AF.Ln, bias=1.0, scale=1.0),
               reads=[t_gtmp], writes=[t_gtmp])
            op(DVE, lambda: nc.vector.tensor_scalar(out=lf[:], in0=gtmp[:], scalar1=-1.0, scalar2=None, op0=ALU.mult),
               reads=[t_gtmp], writes=[t_lf])
            if upto == "G2":
                tap("lf", lf[:], [128, NT, 16], F32, [t_lf])
                break
            lf2 = lf[:].rearrange("p a b -> p (a b)")
            for (lhs_h, lhs_t, dst_h, dst_t, pbi) in ((onesf, t_onesf, tot, t_tot, 3), (maskf, t_maskf, bc, t_bc, 4),
                                                       (hmid, t_hmid, half, t_half, 5)):
                pp_h, pp_t = PB[pbi]
                op(PE, lambda lhs_h=lhs_h, pp_h=pp_h: nc.tensor.matmul(pp_h[:, 0:256], lhsT=lhs_h[:], rhs=lf2,
                                                                       start=True, stop=True),
                   reads=[lhs_t, t_lf], writes=[pp_t])
                op(DVE, lambda dst_h=dst_h, pp_h=pp_h: nc.vector.tensor_copy(out=dst_h[:].rearrange("p a b -> p (a b)"),
                                                                             in_=pp_h[:, 0:256]),
                   reads=[pp_t], writes=[dst_t])
            if upto == "G3":
                tap("lf", lf[:], [128, NT, 16], F32, [t_lf])
                tap("cc", bc[:], [128, NT, 16], F32, [t_bc])
                break
            op(DVE, lambda: nc.vector.memset(pre[:, 0, :], 0.0), writes=[t_pre])
            for tb in range(1, NT):
                op(DVE, lambda tb=tb: nc.vector.tensor_tensor(out=pre[:, tb, :], in0=pre[:, tb - 1, :],
                                                              in1=tot[:, tb - 1, :], op=ALU.add),
                   reads=[t_pre, t_tot], writes=[t_pre])
            op(DVE, lambda: nc.vector.tensor_tensor(out=cc[:], in0=bc[:], in1=pre[:], op=ALU.add),
               reads=[t_bc, t_pre], writes=[t_cc])
            op(DVE, lambda: nc.vector.tensor_tensor(out=RR[:], in0=half[:], in1=pre[:], op=ALU.add),
               reads=[t_half, t_pre], writes=[t_RR])
            if upto == "G4":
                tap("cc", cc[:], [128, NT, 16], F32, [t_cc])
                break
            op(DVE, lambda: nc.vector.tensor_tensor(out=ew[:], in0=zs[:, :, 8:12], in1=bc[:, :, 12:16], op=ALU.subtract),
               reads=[t_zs, t_bc], writes=[t_ew])
            if upto == "G5":
                tap("cc", cc[:], [128, NT, 16], F32, [t_cc, t_ew])
                break
            op(ACT, lambda: nc.scalar.activation(out=ew[:], in_=ew[:], func=AF.Exp), reads=[t_ew], writes=[t_ew])
            if upto == "G6":
                tap("cc", cc[:], [128, NT, 16], F32, [t_cc, t_ew])
                break
            op(ACT, lambda: nc.scalar.activation(out=eb[:], in_=bc[:, :, 12:16], func=AF.Exp), reads=[t_bc], writes=[t_eb])
            op(DVE, lambda: nc.vector.tensor_scalar(out=eb[:], in0=eb[:], scalar1=float(128 ** -0.5), scalar2=None,
                                                    op0=ALU.mult), reads=[t_eb], writes=[t_eb])
            if upto == "G7":
                tap("cc", cc[:], [128, NT, 16], F32, [t_cc, t_ew, t_eb])
                break
            op(ACT, lambda: nc.scalar.activation(out=aa[:], in_=tot[:, :, 12:16], func=AF.Exp), reads=[t_tot], writes=[t_aa])
            if first:
                tap("lf", lf[:], [128, NT, 16], F32, [t_lf])
                tap("cc", cc[:], [128, NT, 16], F32, [t_cc])
                tap("RR", RR[:], [128, NT, 16], F32, [t_RR])
                tap("ew", ew[:], [128, NT, 4], F32, [t_ew])
                tap("eb", eb[:], [128, NT, 4], F32, [t_eb])
                tap("aa", aa[:], [128, NT, 4], F32, [t_aa])

            if upto == "G":
                break
            kb.end_scratch_phase()
            gs.close()

            if pending_O is not None:
                pending_O()
                pending_O = None
            proj_idx[0] = M_PROJ_BANKS
            dma(SP, out=gX[:], in_=ml_g.partition_broadcast(128), writes=[t_gX])
            with ExitStack() as ms:
                mq2 = [kb.scratch(ms, f"m_qT{i}", [128, S], BF16) for i in range(MDB)]
                mk2 = [kb.scratch(ms, f"m_kT{i}", [128, S], BF16) for i in range(MDB)]
                mc2 = [kb.scratch(ms, f"m_cbuf{i}", [128, S + 4], F32) for i in range(NCB)]
                accs = [kb.scratch(ms, f"m_acc{i}", [128, 1024], F32) for i in range(2)]
                ths = [kb.scratch(ms, f"m_th{i}", [128, 1024], F32) for i in range(NTH)]
                kTok, t_kTok = kb.scratch(ms, "m_kTok", [128, NT, 128], BF16)
                vaug2 = [kb.scratch(ms, f"m_vaug{i}", [128, NT, 258], BF16) for i in range(NVA)]
                Crun = [kb.scratch(ms, f"m_C{i}", [128, 258], F32) for i in range(2)]
                Cbf, t_Cbf = kb.scratch(ms, "m_Cbf", [128, NT, 258], BF16)
                numS2 = [kb.scratch(ms, f"m_numS{i}", [128, 4, 258], F32) for i in range(NQB)]
                Gp2 = [kb.scratch(ms, f"m_Gp{i}", [128, 4, 256], BF16) for i in range(NQB)]
                STm = [kb.scratch(ms, f"m_STm{i}", [128, 128], BF16) for i in range(2)]
                hn = [kb.scratch(ms, f"m_hn{i}", [128, 256], BF16) for i in range(2)]
                s12 = [kb.scratch(ms, f"m_s12{i}", [128, 512], F32) for i in range(2)]
                sm2 = [kb.scratch(ms, f"m_sm{i}", [128, 12, 4], F32) for i in range(NQB)]
                junk, t_junk = kb.scratch(ms, "m_junk", [128, 256], BF16)
                for (cb_h, cb_t) in mc2:
                    op(DVE, lambda cb_h=cb_h: nc.vector.memset(cb_h[:, 0:4], 0.0), writes=[cb_t])
                for h in range(4):
                    vaug, t_vaug = vaug2[h % NVA]
                    qT, t_qT = mq2[h % MDB]
                    kT, t_kT = mk2[h % MDB]
                    s1i, u1, t_u1 = ws.next()
                    s2i, u2, t_u2 = ws.next()
                    s3i, u3, t_u3 = ws.next()
                    s4i, u4, t_u4 = ws.next()
                    for which, (dst, t_dst) in enumerate(((qT, t_qT), (kT, t_kT))):
                        chunk = which * 4 + h
                        cbuf, t_cbuf = mc2[which % len(mc2)]
                        for tc in range(4):
                            pp_h, pp_t = proj_bank()
                            for kc in range(KC):
                                op(PE, lambda kc=kc, tc=tc, which=which, pp_h=pp_h: nc.tensor.matmul(
                                    pp_h[:, 0:512], lhsT=u1[:, kc, which * 128:(which + 1) * 128],
                                    rhs=xnT[:, kc, tc * 512:(tc + 1) * 512], start=(kc == 0), stop=(kc == KC - 1)),
                                   reads=[t_u1, t_xnT], writes=[pp_t], inc=(kc == KC - 1))
                            op(ACT, lambda tc=tc, pp_h=pp_h: nc.scalar.copy(out=cbuf[:, 4 + tc * 512:4 + (tc + 1) * 512],
                                                                           in_=pp_h[:, 0:512]),
                               reads=[pp_t], writes=[t_cbuf])
                        for hf in range(2):
                            o = hf * 1024
                            ac_h, ac_t = accs[hf]
                            op(DVE, lambda o=o, ac_h=ac_h, chunk=chunk: nc.vector.tensor_scalar(
                                out=ac_h[:], in0=cbuf[:, 4 + o:4 + o + 1024], scalar1=cw[:, chunk, 3:4], scalar2=None,
                                op0=ALU.mult), reads=[t_cbuf, t_cw], writes=[ac_t])
                            for j in (2, 1, 0):
                                op(DVE, lambda o=o, ac_h=ac_h, chunk=chunk, j=j: nc.vector.scalar_tensor_tensor(
                                    out=ac_h[:], in0=cbuf[:, 1 + j + o:1 + j + o + 1024], scalar=cw[:, chunk, j:j + 1],
                                    in1=ac_h[:], op0=ALU.mult, op1=ALU.add), reads=[t_cbuf, t_cw, ac_t], writes=[ac_t])
                            th_h, th_t = ths[hf % NTH]
                            op(ACT, lambda ac_h=ac_h, chunk=chunk, th_h=th_h: nc.scalar.activation(
                                out=th_h[:], in_=ac_h[:], func=AF.Tanh, bias=cbh[:, chunk:chunk + 1], scale=0.5),
                               reads=[ac_t, t_cbh], writes=[th_t])
                            op(DVE, lambda th_h=th_h: nc.vector.tensor_scalar(
                                out=th_h[:], in0=th_h[:], scalar1=0.5, scalar2=0.5, op0=ALU.mult, op1=ALU.add),
                               reads=[th_t], writes=[th_t])
                            op(DVE, lambda o=o, ac_h=ac_h, chunk=chunk, dst=dst, th_h=th_h: nc.vector.scalar_tensor_tensor(
                                out=dst[:, o:o + 1024], in0=ac_h[:], scalar=cb[:, chunk:chunk + 1], in1=th_h[:],
                                op0=ALU.add, op1=ALU.mult), reads=[ac_t, t_cb, th_t], writes=[t_dst])
                    ws.release(s1i)
                    for g in range(2):
                        pt_h, pt_t = PT[g]
                        for c8 in range(8):
                            c = g * 8 + c8
                            op(PE, lambda c=c, c8=c8, pt_h=pt_h: nc.tensor.transpose(
                                out=pt_h[:, c8, :], in_=kT[:, c * 128:(c + 1) * 128], identity=ident[:]),
                               reads=[t_kT, t_ident], writes=[pt_t], inc=(c8 == 7))
                        for c8 in range(8):
                            c = g * 8 + c8
                            op(ACT, lambda c=c, c8=c8, pt_h=pt_h: nc.scalar.activation(
                                out=kTok[:, c, :], in_=pt_h[:, c8, :], func=AF.Copy, scale=aa[:, c, h:h + 1]),
                               reads=[pt_t, t_aa], writes=[t_kTok])
                    for tb in range(NT):
                        pp_h, pp_t = proj_bank()
                        for kc in range(KC):
                            op(PE, lambda kc=kc, tb=tb, pp_h=pp_h: nc.tensor.matmul(
                                pp_h[:, 0:256], lhsT=xnT[:, kc, tb * 128:(tb + 1) * 128], rhs=u2[:, kc, :],
                                start=(kc == 0), stop=(kc == KC - 1)),
                               reads=[t_u2, t_xnT], writes=[pp_t], inc=(kc == KC - 1))
                        op(ACT, lambda tb=tb, pp_h=pp_h: nc.scalar.activation(
                            out=vaug[:, tb, 0:256], in_=pp_h[:, 0:256], func=AF.Copy, scale=ew[:, tb, h:h + 1]),
                           reads=[pp_t, t_ew], writes=[t_vaug])
                    ws.release(s2i)
                    op(DVE, lambda: nc.vector.tensor_copy(out=vaug[:, :, 256:257], in_=ew[:, :, h:h + 1]),
                       reads=[t_ew], writes=[t_vaug])
                    dcb = (PB[5], PB[2])
                    for c in range(NT - 1):
                        pd_h, pd_t = dcb[c % 2]
                        op(PE, lambda c=c, pd_h=pd_h: nc.tensor.matmul(pd_h[:, 0:257], lhsT=kTok[:, c, :],
                                                                       rhs=vaug[:, c, 0:257], start=True, stop=True),
                           reads=[t_kTok, t_vaug], writes=[pd_t])
                        cn_h, cn_t = Crun[(c + 1) % 2]
                        cp_h, cp_t = Crun[c % 2]
                        if c == 0:
                            op(DVE, lambda pd_h=pd_h, cn_h=cn_h: nc.vector.tensor_copy(out=cn_h[:, 0:257], in_=pd_h[:, 0:257]),
                               reads=[pd_t], writes=[cn_t])
                        else:
                            op(DVE, lambda c=c, pd_h=pd_h, cn_h=cn_h, cp_h=cp_h: nc.vector.scalar_tensor_tensor(
                                out=cn_h[:, 0:257], in0=cp_h[:, 0:257], scalar=aa[:, c, h:h + 1], in1=pd_h[:, 0:257],
                                op0=ALU.mult, op1=ALU.add), reads=[cp_t, pd_t, t_aa], writes=[cn_t])
                        op(ACT, lambda c=c, cn_h=cn_h: nc.scalar.copy(out=Cbf[:, c + 1, 0:257], in_=cn_h[:, 0:257]),
                           reads=[cn_t], writes=[t_Cbf])
                    for c in range(NT):
                        numS, t_numS = numS2[(c // 4) % NQB]
                        Gp, t_Gp = Gp2[(c // 4) % NQB]
                        sm, t_sm = sm2[(c // 4) % NQB]
                        ps_h, ps_t = PB[2]
                        op(PE, lambda c=c: nc.tensor.matmul(ps_h[:, 0:128], lhsT=kT[:, c * 128:(c + 1) * 128],
                                                            rhs=qT[:, c * 128:(c + 1) * 128], start=True, stop=True),
                           reads=[t_kT, t_qT], writes=[ps_t])
                        st_h, st_t = STm[c % 2]
                        op(DVE, lambda st_h=st_h: nc.vector.tensor_tensor(out=st_h[:], in0=ps_h[:, 0:128], in1=maskf[:],
                                                                          op=ALU.mult),
                           reads=[ps_t, t_maskf], writes=[st_t])
                        pn_h, pn_t = PB[3 + c % 2]
                        op(PE, lambda c=c, pn_h=pn_h, st_h=st_h: nc.tensor.matmul(
                            pn_h[:, 0:257], lhsT=st_h[:], rhs=vaug[:, c, 0:257], start=True, stop=(c == 0)),
                           reads=[st_t, t_vaug], writes=[pn_t], inc=(c == 0))
                        if c > 0:
                            op(PE, lambda c=c, pn_h=pn_h: nc.tensor.matmul(
                                pn_h[:, 0:257], lhsT=qT[:, c * 128:(c + 1) * 128], rhs=Cbf[:, c, 0:257],
                                start=False, stop=True), reads=[t_qT, t_Cbf], writes=[pn_t])
                        op(ACT, lambda c=c, pn_h=pn_h: nc.scalar.copy(out=numS[:, c % 4, 0:257], in_=pn_h[:, 0:257]),
                           reads=[pn_t], writes=[t_numS])
                        pp_h, pp_t = proj_bank()
                        for gi, (ug, t_ug) in enumerate(((u3, t_u3), (u4, t_u4))):
                            for kc in range(KC):
                                op(PE, lambda kc=kc, c=c, gi=gi, ug=ug, pp_h=pp_h: nc.tensor.matmul(
                                    pp_h[:, gi * 256:(gi + 1) * 256], lhsT=xnT[:, kc, c * 128:(c + 1) * 128],
                                    rhs=ug[:, kc, :], start=(kc == 0), stop=(kc == KC - 1)),
                                   reads=[t_ug, t_xnT], writes=[pp_t], inc=(kc == KC - 1 and gi == 1))
                        sg_h, sg_t = s12[c % 2]
                        op(ACT, lambda pp_h=pp_h, sg_h=sg_h: nc.scalar.activation(out=sg_h[:, 0:512], in_=pp_h[:, 0:512],
                                                                                  func=AF.Tanh, scale=0.5),
                           reads=[pp_t], writes=[sg_t])
                        op(DVE, lambda pp_h=pp_h, sg_h=sg_h: nc.vector.scalar_tensor_tensor(
                            out=sg_h[:, 256:512], in0=sg_h[:, 256:512], scalar=1.0, in1=pp_h[:, 256:512],
                            op0=ALU.add, op1=ALU.mult), reads=[sg_t, pp_t], writes=[sg_t])
                        op(DVE, lambda sg_h=sg_h: nc.vector.scalar_tensor_tensor(
                            out=sg_h[:, 0:256], in0=sg_h[:, 0:256], scalar=1.0, in1=sg_h[:, 256:512],
                            op0=ALU.add, op1=ALU.mult), reads=[sg_t], writes=[sg_t])
                        op(DVE, lambda c=c, sg_h=sg_h: nc.vector.tensor_tensor(out=Gp[:, c % 4, :], in0=sg_h[:, 0:256],
                                                                               in1=gM[:, h * 256:(h + 1) * 256], op=ALU.mult),
                           reads=[sg_t, t_gM], writes=[t_Gp])
                        if c % 4 == 3:
                            cs = c - 3
                            den = numS[:, :, 256]
                            ebs = eb[:, cs:cs + 4, h]
                            op(DVE, lambda: nc.vector.tensor_tensor(out=sm[:, 0, :], in0=den, in1=ebs, op=ALU.mult),
                               reads=[t_numS, t_eb], writes=[t_sm])
                            op(DVE, lambda: nc.vector.scalar_tensor_tensor(out=sm[:, 1, :], in0=sm[:, 0, :], scalar=-1.0,
                                                                           in1=sm[:, 0, :], op0=ALU.mult, op1=ALU.max),
                               reads=[t_sm], writes=[t_sm])
                            op(DVE, lambda: nc.vector.tensor_scalar(out=sm[:, 2, :], in0=sm[:, 1, :], scalar1=1.0,
                                                                    scalar2=None, op0=ALU.max), reads=[t_sm], writes=[t_sm])
                            op(DVE, lambda: nc.vector.reciprocal(out=sm[:, 3, :], in_=sm[:, 2, :]), reads=[t_sm], writes=[t_sm])
                            op(DVE, lambda: nc.vector.tensor_tensor(out=sm[:, 4, :], in0=sm[:, 3, :], in1=ebs, op=ALU.mult),
                               reads=[t_sm, t_eb], writes=[t_sm])
                            for c8 in range(4):
                                op(ACT, lambda c8=c8: nc.scalar.activation(out=junk[:], in_=numS[:, c8, 0:256],
                                                                           func=AF.Square, accum_out=sm[:, 5, c8:c8 + 1]),
                                   reads=[t_numS], writes=[t_junk, t_sm])
                            op(DVE, lambda: nc.vector.tensor_tensor(out=sm[:, 6, :], in0=sm[:, 4, :], in1=sm[:, 4, :],
                                                                    op=ALU.mult), reads=[t_sm], writes=[t_sm])
                            op(DVE, lambda: nc.vector.tensor_tensor(out=sm[:, 7, :], in0=sm[:, 6, :], in1=sm[:, 5, :],
                                                                    op=ALU.mult), reads=[t_sm], writes=[t_sm])
                            op(ACT, lambda: nc.scalar.activation(out=sm[:, 8, :], in_=sm[:, 7, :], func=AF.Sqrt,
                                                                 scale=1.0 / 256, bias=epsT[:]),
                               reads=[t_sm, t_eps], writes=[t_sm])
                            op(DVE, lambda: nc.vector.reciprocal(out=sm[:, 9, :], in_=sm[:, 8, :]), reads=[t_sm], writes=[t_sm])
                            op(DVE, lambda: nc.vector.scalar_tensor_tensor(out=sm[:, 10, :], in0=sm[:, 9, :], scalar=0.25,
                                                                           in1=sm[:, 4, :], op0=ALU.mult, op1=ALU.mult),
                               reads=[t_sm], writes=[t_sm])
                            for c8 in range(4):
                                hn_h, hn_t = hn[c8 % 2]
                                op(DVE, lambda c8=c8, hn_h=hn_h: nc.vector.scalar_tensor_tensor(
                                    out=hn_h[:], in0=numS[:, c8, 0:256], scalar=sm[:, 10, c8:c8 + 1], in1=Gp[:, c8, :],
                                    op0=ALU.mult, op1=ALU.mult), reads=[t_numS, t_sm, t_Gp], writes=[hn_t])
                                for j in range(2):
                                    op(PE, lambda c8=c8, j=j, hn_h=hn_h: nc.tensor.transpose(
                                        out=PT[j][0][:, c8, :], in_=hn_h[:, j * 128:(j + 1) * 128], identity=ident[:]),
                                       reads=[hn_t, t_ident], writes=[PT[j][1]])
                            for j in range(2):
                                op(ACT, lambda j=j, cs=cs: nc.scalar.copy(
                                    out=bufA[:, 2 * h + j, cs * 128:(cs + 4) * 128],
                                    in_=PT[j][0][:, 0:4, :].rearrange("p a b -> p (a b)")),
                                   reads=[PT[j][1]], writes=[t_bufA])
                    ws.release(s3i)
                    ws.release(s4i)
                kb.end_scratch_phase()
            if first:
                tap("hbT", bufA[:], [128, KC, S], BF16, [t_bufA])

            if upto == "M":
                break

            ys = es.enter_context(ExitStack())
            bufY, t_bufY = kb.scratch(ys, "bufY", [128, KC, S], BF16)
            proj_n[0] = 6
            proj_idx[0] = None
            with ExitStack() as ms:
                sgs = [kb.scratch(ms, f"d_sg{i}", [128, 512], F32) for i in range(2)]
                for oc in range(8):
                    si, u, t_u = ws.next()
                    for tc in range(4):
                        py_h, py_t = proj_bank()
                        for kc in range(KC):
                            op(PE, lambda kc=kc, tc=tc, py_h=py_h: nc.tensor.matmul(
                                py_h[:, 0:512], lhsT=u[:, kc, 0:128], rhs=bufA[:, kc, tc * 512:(tc + 1) * 512],
                                start=(kc == 0), stop=(kc == KC - 1)), reads=[t_u, t_bufA], writes=[py_t], inc=(kc == KC - 1))
                        pg_h, pg_t = proj_bank()
                        for kc in range(KC):
                            op(PE, lambda kc=kc, tc=tc, pg_h=pg_h: nc.tensor.matmul(
                                pg_h[:, 0:512], lhsT=u[:, kc, 128:256], rhs=xnT[:, kc, tc * 512:(tc + 1) * 512],
                                start=(kc == 0), stop=(kc == KC - 1)), reads=[t_u, t_xnT], writes=[pg_t], inc=(kc == KC - 1))
                        sg_h, sg_t = sgs[tc % 2]
                        op(ACT, lambda pg_h=pg_h, sg_h=sg_h, oc=oc: nc.scalar.activation(
                            out=sg_h[:], in_=pg_h[:, 0:512], func=AF.Tanh, bias=bgh[:, 8 + oc:9 + oc], scale=0.5),
                           reads=[pg_t, t_bgh], writes=[sg_t])
                        op(DVE, lambda py_h=py_h, sg_h=sg_h, oc=oc, tc=tc: nc.vector.scalar_tensor_tensor(
                            out=bufY[:, oc, tc * 512:(tc + 1) * 512], in0=sg_h[:], scalar=1.0, in1=py_h[:, 0:512],
                            op0=ALU.add, op1=ALU.mult), reads=[py_t, sg_t], writes=[t_bufY])
                    ws.release(si)
                kb.end_scratch_phase()
            if first:
                tap("ybg", bufY[:], [128, KC, S], BF16, [t_bufY])

            if upto == "MD":
                break

            proj_n[0] = 2
            with ExitStack() as ms:
                fq = [kb.scratch(ms, f"f_q{i}", [128, S], BF16) for i in range(2)]
                fk = [kb.scratch(ms, f"f_k{i}", [128, S], BF16) for i in range(2)]
                fv = [kb.scratch(ms, f"f_v{i}", [128, NT, 130], BF16) for i in range(2)]
                fzs = [kb.scratch(ms, f"f_z{i}", [128, NT, 128], BF16) for i in range(2)]
                bms = [kb.scratch(ms, f"f_bm{i}", [128, NT, NT], F32) for i in range(2)]
                ptb = [kb.scratch(ms, f"f_pt{i}", [128, 512], BF16) for i in range(NPTB)]
                obs = [kb.scratch(ms, f"f_ob{i}", [128, 4, 128], BF16) for i in range(2)]
                rin, t_rin = kb.scratch(ms, "f_rin", [128, 16], F32)
                tzs = [kb.scratch(ms, f"f_tz{i}", [128, 128], F32) for i in range(4)]
                for i in range(2):
                    op(DVE, lambda i=i: nc.vector.memset(fv[i][0][:, :, 128:129], 2.0), writes=[fv[i][1]])
                sc = float(128 ** -0.5)
                sb_i = 0
                for h in range(8):
                    s1i, u1, t_u1 = ws.next()
                    s2i, u2, t_u2 = ws.next()
                    qT, t_qT = fq[h % 2]
                    kT, t_kT = fk[h % 2]
                    V, t_V = fv[h % 2]
                    FZ, t_FZ = fzs[h % 2]
                    bm, t_bm = bms[h % 2]
                    for which, (dst, t_dst) in enumerate(((qT, t_qT), (kT, t_kT))):
                        for tc in range(4):
                            pp_h, pp_t = proj_bank()
                            for kc in range(KC):
                                op(PE, lambda kc=kc, tc=tc, which=which, pp_h=pp_h: nc.tensor.matmul(
                                    pp_h[:, 0:512], lhsT=u1[:, kc, which * 128:(which + 1) * 128],
                                    rhs=xnT[:, kc, tc * 512:(tc + 1) * 512], start=(kc == 0), stop=(kc == KC - 1)),
                                   reads=[t_u1, t_xnT], writes=[pp_t], inc=(kc == KC - 1))
                            op(DVE, lambda tc=tc, pp_h=pp_h, dst=dst: nc.vector.tensor_copy(
                                out=dst[:, tc * 512:(tc + 1) * 512], in_=pp_h[:, 0:512]), reads=[pp_t], writes=[t_dst])
                    ws.release(s1i)
                    for tb in range(NT):
                        pp_h, pp_t = proj_bank()
                        for kc in range(KC):
                            op(PE, lambda kc=kc, tb=tb, pp_h=pp_h: nc.tensor.matmul(
                                pp_h[:, 0:256], lhsT=xnT[:, kc, tb * 128:(tb + 1) * 128], rhs=u2[:, kc, :],
                                start=(kc == 0), stop=(kc == KC - 1)), reads=[t_u2, t_xnT], writes=[pp_t], inc=(kc == KC - 1))
                        op(DVE, lambda tb=tb, pp_h=pp_h: nc.vector.tensor_copy(out=V[:, tb, 0:128], in_=pp_h[:, 0:128]),
                           reads=[pp_t], writes=[t_V])
                        tz_h, tz_t = tzs[tb % 4]
                        op(ACT, lambda pp_h=pp_h, tz_h=tz_h: nc.scalar.activation(out=tz_h[:], in_=pp_h[:, 128:256],
                                                                                  func=AF.Tanh, scale=0.5),
                           reads=[pp_t], writes=[tz_t])
                        op(DVE, lambda tb=tb, pp_h=pp_h, tz_h=tz_h: nc.vector.scalar_tensor_tensor(
                            out=FZ[:, tb, :], in0=tz_h[:], scalar=1.0, in1=pp_h[:, 128:256], op0=ALU.add, op1=ALU.mult),
                           reads=[pp_t, tz_t], writes=[t_FZ])
                    ws.release(s2i)
                    for j in range(NT):
                        op(DVE, lambda j=j: nc.vector.tensor_scalar(out=bm[:, j, :], in0=RR[:, :, h],
                                                                    scalar1=cc[:, j, h:h + 1], scalar2=None,
                                                                    op0=ALU.subtract), reads=[t_RR, t_cc], writes=[t_bm])
                    steps = [(I, j) for I in range(4) for j in range(4 * I + 4)]

                    def emit_ST(n):
                        I, j = steps[n]
                        i0 = max(j, 4 * I)
                        ps_h, ps_t = PB[2 + n % 2]
                        N = (4 * I + 4 - i0) * 128
                        op(PE, lambda: nc.tensor.matmul(ps_h[:, 0:N], lhsT=kT[:, j * 128:(j + 1) * 128],
                                                        rhs=qT[:, i0 * 128:(4 * I + 4) * 128], start=True, stop=True),
                           reads=[t_kT, t_qT], writes=[ps_t])

                    def emit_exp(n):
                        I, j = steps[n]
                        i0 = max(j, 4 * I)
                        ps_h, ps_t = PB[2 + n % 2]
                        p_h, p_t = ptb[n % NPTB]
                        for i in range(i0, 4 * I + 4):
                            lo = (i - i0) * 128
                            op(ACT, lambda i=i, lo=lo: nc.scalar.activation(
                                out=p_h[:, lo:lo + 128], in_=ps_h[:, lo:lo + 128], func=AF.Exp, scale=sc,
                                bias=bm[:, j, i:i + 1]), reads=[ps_t, t_bm], writes=[p_t])
                        if j >= 4 * I:
                            op(DVE, lambda: nc.vector.tensor_tensor(out=p_h[:, 0:128], in0=p_h[:, 0:128], in1=maskb[:],
                                                                    op=ALU.mult), reads=[p_t, t_maskb], writes=[p_t])

                    def emit_PV(n):
                        I, j = steps[n]
                        i0 = max(j, 4 * I)
                        p_h, p_t = ptb[n % NPTB]
                        for i in range(i0, 4 * I + 4):
                            il = i - 4 * I
                            lo = (i - i0) * 128
                            po_h, po_t = PB[4 + il // 2]
                            off = (il % 2) * 129
                            op(PE, lambda i=i, lo=lo, po_h=po_h, off=off, il=il: nc.tensor.matmul(
                                po_h[:, off:off + 129], lhsT=p_h[:, lo:lo + 128], rhs=V[:, j, 0:129],
                                start=(j == 0 and il % 2 == 0), stop=(j == i), skip_group_check=True),
                               reads=[p_t, t_V], writes=[po_t])
                            if j == i:
                                ob_h, ob_t = obs[I % 2]
                                op(DVE, lambda po_h=po_h, off=off, i=i: nc.vector.reciprocal(
                                    out=rin[:, i:i + 1], in_=po_h[:, off + 128:off + 129]), reads=[po_t], writes=[t_rin])
                                op(DVE, lambda po_h=po_h, off=off, i=i, il=il, ob_h=ob_h: nc.vector.scalar_tensor_tensor(
                                    out=ob_h[:, il, :], in0=po_h[:, off:off + 128], scalar=rin[:, i:i + 1], in1=FZ[:, i, :],
                                    op0=ALU.mult, op1=ALU.mult), reads=[po_t, t_rin, t_FZ], writes=[ob_t])
                                pt_h, pt_t = PT[I % 2]
                                pb0 = 0
                                op(PE, lambda il=il, ob_h=ob_h, pt_h=pt_h, pb0=pb0: nc.tensor.transpose(
                                    out=pt_h[:, pb0 + il, :], in_=ob_h[:, il, :], identity=ident[:]),
                                   reads=[ob_t, t_ident], writes=[pt_t])
                                if il == 3:
                                    op(DVE, lambda pt_h=pt_h, I=I: nc.vector.tensor_copy(
                                        out=bufA[:, h, I * 512:(I + 1) * 512],
                                        in_=pt_h[:, pb0:pb0 + 4, :].rearrange("p a b -> p (a b)")),
                                       reads=[pt_t], writes=[t_bufA])

                    emit_ST(0)
                    for n in range(len(steps)):
                        emit_exp(n)
                        if n + 1 < len(steps):
                            emit_ST(n + 1)
                        emit_PV(n)
                kb.end_scratch_phase()
            if first:
                tap("oaT", bufA[:], [128, KC, S], BF16, [t_bufA])

            if upto == "F":
                break

            proj_n[0] = 6
            with ExitStack() as ms:
                sgs = [kb.scratch(ms, f"e_sg{i}", [128, 512], F32) for i in range(2)]
                for oc in range(8):
                    si, u, t_u = ws.next()
                    for tc in range(4):
                        py_h, py_t = proj_bank()
                        for kc in range(KC):
                            op(PE, lambda kc=kc, tc=tc, py_h=py_h: nc.tensor.matmul(
                                py_h[:, 0:512], lhsT=u[:, kc, 0:128], rhs=bufA[:, kc, tc * 512:(tc + 1) * 512],
                                start=(kc == 0), stop=(kc == KC - 1)), reads=[t_u, t_bufA], writes=[py_t], inc=(kc == KC - 1))
                        pg_h, pg_t = proj_bank()
                        for kc in range(KC):
                            op(PE, lambda kc=kc, tc=tc, pg_h=pg_h: nc.tensor.matmul(
                                pg_h[:, 0:512], lhsT=u[:, kc, 128:256], rhs=xnT[:, kc, tc * 512:(tc + 1) * 512],
                                start=(kc == 0), stop=(kc == KC - 1)), reads=[t_u, t_xnT], writes=[pg_t], inc=(kc == KC - 1))
                        sg_h, sg_t = sgs[tc % 2]
                        op(ACT, lambda pg_h=pg_h, sg_h=sg_h, oc=oc: nc.scalar.activation(
                            out=sg_h[:], in_=pg_h[:, 0:512], func=AF.Tanh, bias=bgh[:, oc:oc + 1], scale=0.5),
                           reads=[pg_t, t_bgh], writes=[sg_t])
                        op(DVE, lambda py_h=py_h, sg_h=sg_h: nc.vector.scalar_tensor_tensor(
                            out=sg_h[:], in0=sg_h[:], scalar=1.0, in1=py_h[:, 0:512], op0=ALU.add, op1=ALU.mult),
                           reads=[py_t, sg_t], writes=[sg_t])
                        op(DVE, lambda sg_h=sg_h, oc=oc, tc=tc: nc.vector.tensor_tensor(
                            out=bufY[:, oc, tc * 512:(tc + 1) * 512], in0=sg_h[:], in1=bufY[:, oc, tc * 512:(tc + 1) * 512],
                            op=ALU.add), reads=[sg_t, t_bufY], writes=[t_bufY])
                    ws.release(si)
                kb.end_scratch_phase()
            if first:
                tap("yT", bufY[:], [128, KC, S], BF16, [t_bufY])

            if upto == "FD":
                break

            def _phase_O(seq=seq, bufY=bufY, t_bufY=t_bufY, ys=ys):
                dma(SP, out=gX[:], in_=final_g.partition_broadcast(128), writes=[t_gX])
                us = [ws.next() for _ in range(4)]
                for tb in range(NT):
                    xt_h, xt_t = xts[tb % NXT]
                    dma(SP, out=xt_h[:], in_=x_d[seq, tb * 128:(tb + 1) * 128, :], writes=[xt_t])
                    banks = [proj_bank(), proj_bank()]
                    for hf in range(2):
                        po_h, po_t = banks[hf]
                        for n in range(2):
                            si, u, t_u = us[2 * hf + n]
                            for kc in range(KC):
                                op(PE, lambda kc=kc, n=n, u=u, po_h=po_h: nc.tensor.matmul(
                                    po_h[:, n * 256:(n + 1) * 256], lhsT=bufY[:, kc, tb * 128:(tb + 1) * 128], rhs=u[:, kc, :],
                                    start=(kc == 0), stop=(kc == KC - 1)), reads=[t_u, t_bufY], writes=[po_t],
                                   inc=(kc == KC - 1 and n == 1))
                        op(DVE, lambda hf=hf, po_h=po_h: nc.vector.scalar_tensor_tensor(
                            out=xt_h[:, hf * 512:(hf + 1) * 512], in0=po_h[:, 0:512], scalar=0.5,
                            in1=xt_h[:, hf * 512:(hf + 1) * 512], op0=ALU.mult, op1=ALU.add),
                           reads=[po_t, xt_t], writes=[xt_t])
                    xn_h, xn_t = xns[tb % NXN]
                    rmsnorm_tail(xt_h, xt_t, tb, gF, t_gF, xt_h[:], [xt_t], xn_h[:], [xn_t])
                    dma(SP, out=y_d[seq, tb * 128:(tb + 1) * 128, :], in_=xt_h[:], reads=[xt_t])
                for (si, u, t_u) in us:
                    ws.release(si)
                kb.scr_tiles.append(t_bufY)
                kb.end_scratch_phase()
                ys.close()

            pending_O = _phase_O

        if pending_O is not None:
            pending_O()
            pending_O = None
        if kb.mode == "defer":
            kb.flush()
            build.n_inst = kb.n_inst
            build.nsem = kb.nsem
            build.est_us = kb.est_makespan
        out_costs = [(o.cost, o.aset) for o in kb.ops]
    return nc, dbg_out, out_costs


def host_inputs(inputs):
    f = lambda a: np.ascontiguousarray(np.asarray(a, dtype=np.float32))
    gb = np.concatenate([f(inputs["b_fox_f"])[0], f(inputs["b_ml_i"])[0], f(inputs["b_ml_f"])[0]])
    gbias = np.ascontiguousarray(np.broadcast_to(gb[None, None, :], (128, NT, 16)))
    convw = np.ascontiguousarray(f(inputs["conv_w"])[0].reshape(4, 8, 128).transpose(2, 1, 0))
    convb = np.ascontiguousarray(f(inputs["conv_b"])[0].reshape(8, 128).T)
    bgate = np.ascontiguousarray(f(inputs["b_gate"])[0].reshape(16, 128).T)
    return {
        "w_in": f(inputs["w_in"])[0],
        "w_fox_down": f(inputs["w_fox_down"])[0],
        "w_ml_down": f(inputs["w_ml_down"])[0],
        "w_out": f(inputs["w_out"])[0],
        "norm_g": f(inputs["norm_g"])[0],
        "final_g": f(inputs["final_g"]),
        "ml_norm_g": f(inputs["ml_norm_g"])[0],
        "gbias": gbias,
        "convw": convw,
        "convb": convb,
        "bgate": bgate,
    }


def kernel(**inputs):
    x = np.ascontiguousarray(np.asarray(inputs["x"], dtype=np.float32))
    B = x.shape[0]
    nseq = B // NCORES
    shared = host_inputs(inputs)
    nc, _ = build(nseq)
    in_maps = []
    for c in range(NCORES):
        m = dict(shared)
        m["x"] = x[c * nseq:(c + 1) * nseq]
        in_maps.append(m)
    res = run_bass_kernel_spmd(nc, in_maps, core_ids=list(range(NCORES)))
    out = np.concatenate([np.asarray(r["y"]) for r in res.results], axis=0)
    return out.astype(np.float32, copy=False)
```

```python
import types
import numpy as np
from contextlib import ExitStack
import concourse.bass as bass
import concourse.mybir as mybir
from concourse.bass_utils import run_bass_kernel_spmd

F32 = mybir.dt.float32
BF16 = mybir.dt.bfloat16
AF = mybir.ActivationFunctionType
ALU = mybir.AluOpType

S = 2048
D = 1024
NT = 16
KC = 8
NCORES = 8
EPS = 1e-6
C_FQ, C_FK, C_FV, C_FF, C_FZ = 0, 1024, 2048, 3072, 3080
C_MQ, C_MK, C_MV, C_MI, C_MF, C_MO, C_MZ = 4104, 4616, 5128, 6152, 6156, 6160, 7184
C_GA, C_GB = 8208, 9232
IN_W = 10256
SEM_LIMIT = 30000
SLACK_US = 0.0
PE_MAXG = 2
NPTB = 3
NXT = 4
MDB = 1
NCB = 2
NTH = 1
NQB = 2
NXN = 2
M_PROJ_BANKS = [0, 1, 5]
NSLOT = 6


class SemObj:
    __slots__ = ("h", "owner", "val")

    def __init__(self, h, owner):
        self.h = h
        self.owner = owner
        self.val = 0


class Tile:
    __slots__ = ("w", "r", "name", "psum")

    def __init__(self, name, init=None, psum=False):
        self.name = name
        self.w = list(init) if init else []
        self.r = []
        self.psum = psum


class Eng:
    def __init__(self, kb, name, h, is_pe=False):
        self.kb = kb
        self.name = name
        self.h = h
        self.is_pe = is_pe
        self.waited = {}
        self.cur = None

    def sem(self):
        if self.cur is None or self.cur.val >= SEM_LIMIT:
            self.cur = self.kb.new_sem(self)
        return self.cur


class Op:
    __slots__ = ("id", "eng", "fns", "deps", "cost", "is_dma", "tok", "start", "ninstr", "aset")

    def __init__(self, oid, eng, is_dma):
        self.id = oid
        self.eng = eng
        self.fns = []
        self.deps = []
        self.cost = 0.0
        self.is_dma = is_dma
        self.tok = None
        self.start = 0.0
        self.aset = None


def _snap(fn):
    if fn.__closure__ is None:
        return fn
    cells = []
    for c in fn.__closure__:
        try:
            cells.append(types.CellType(c.cell_contents))
        except ValueError:
            cells.append(c)
    return types.FunctionType(fn.__code__, fn.__globals__, fn.__name__, fn.__defaults__, tuple(cells))


def _free_elems(acc):
    try:
        ap = acc.ap
        n = 1
        for (_, c) in ap[1:]:
            n *= c
        return max(int(n), 1)
    except Exception:
        return 64


class KB:
    def __init__(self, nc, es, costs=None):
        self.nc = nc
        self.es = es
        self.mode = "measure" if costs is None else "defer"
        self.costs_in = costs
        self.nsem = 0
        self.PE = Eng(self, "pe", nc.tensor, is_pe=True)
        self.ACT = Eng(self, "act", nc.scalar)
        self.DVE = Eng(self, "dve", nc.vector)
        self.POOL = Eng(self, "pool", nc.gpsimd)
        self.SP = Eng(self, "sp", nc.sync)
        self.engs = [self.PE, self.ACT, self.DVE, self.POOL, self.SP]
        self.dma_pool = {}
        self.dma_rr = {}
        self.scr_tokens = []
        self.scr_tiles = []
        self.scr_id = 0
        self.n_inst = 0
        self.ops = []
        self.open_pe = None

    def new_sem(self, owner):
        self.nsem += 1
        h = self.es.enter_context(self.nc.semaphore(f"s{self.nsem}"))
        return SemObj(h, owner)

    def _deps(self, o, reads, writes):
        eng = o.eng
        ops = self.ops
        d = o.deps
        for t in reads:
            d.extend(t.w)
            if t.psum:
                d.extend(r for r in t.r if ops[r].eng is not eng)
        for t in writes:
            d.extend(t.w)
            d.extend(t.r)

    def _touch(self, oid, reads, writes):
        for t in reads:
            if not t.r or t.r[-1] != oid:
                t.r.append(oid)
        for t in writes:
            t.w = [oid]
            t.r = []

    def _cost_of(self, eng, ins):
        try:
            i = ins.ins
            n = _free_elems(i.outs[0])
            if eng.is_pe:
                f32 = False
                try:
                    f32 = (i.ins[0].dtype == F32) and not getattr(i, "is_transpose", False)
                except Exception:
                    pass
                return (max(n, 200) * (4 if f32 else 1) * 0.5 + 8.0) / 1000.0
            if eng is self.ACT:
                return 0.22 + n * 0.00083
            if eng is self.DVE:
                return 0.10 + n * 0.00104
            return 0.25 + n * 0.002
        except Exception:
            return 0.3

    def op(self, eng, fn, reads=(), writes=(), inc=True):
        if eng.is_pe and self.open_pe is not None:
            o = self.open_pe
        else:
            assert self.open_pe is None, "non-PE op recorded while a PE group is open"
            o = Op(len(self.ops), eng, False)
            self.ops.append(o)
            if eng.is_pe and not inc:
                self.open_pe = o
        if eng.is_pe:
            o.ninstr = getattr(o, "ninstr", 0) + 1
            if inc or o.ninstr >= PE_MAXG:
                self.open_pe = None
        self._deps(o, reads, writes)
        self._touch(o.id, reads, writes)
        if self.mode == "measure":
            ins = fn()
            o.cost += self._cost_of(eng, ins)
            if eng is self.ACT:
                try:
                    fname = str(ins.ins.func)
                except Exception:
                    fname = ""
                if "Exp" in fname or "Tanh" in fname:
                    o.aset = "A"
                elif "Sqrt" in fname:
                    o.aset = "S"
                elif "Ln" in fname:
                    o.aset = "L"
            return ins
        o.fns.append(_snap(fn))
        return None

    def dma(self, q, out, in_, reads=(), writes=()):
        assert self.open_pe is None
        o = Op(len(self.ops), q, True)
        self.ops.append(o)
        self._deps(o, reads, writes)
        self._touch(o.id, reads, writes)
        if self.mode == "measure":
            q.h.dma_start(out=out, in_=in_)
            try:
                n = 1
                for (_, c) in out.ap:
                    n *= c
            except Exception:
                n = 131072
            o.cost = 2.0 + n * 4 / 150e3
            return None
        o.fns.append(lambda: q.h.dma_start(out=out, in_=in_))
        return None

    def sb(self, name, shape, dt, es=None):
        h = (es or self.es).enter_context(self.nc.sbuf_tensor(name, shape, dt))
        return h, Tile(name)

    def scratch(self, es, name, shape, dt):
        self.scr_id += 1
        h = es.enter_context(self.nc.sbuf_tensor(f"{name}_{self.scr_id}", shape, dt))
        t = Tile(name, self.scr_tokens)
        self.scr_tiles.append(t)
        return h, t

    def end_scratch_phase(self):
        acc = set(self.scr_tokens)
        for t in self.scr_tiles:
            acc.update(t.w)
            acc.update(t.r)
        self.scr_tokens = sorted(acc)
        self.scr_tiles = []

    def schedule(self):
        ops = self.ops
        n = len(ops)
        assert len(self.costs_in) == n, (len(self.costs_in), n)
        costs = [c for (c, _) in self.costs_in]
        asets = [a for (_, a) in self.costs_in]
        act = self.ACT
        cur_set = [None]
        TL = 0.0
        succ = [[] for _ in range(n)]
        for o in ops:
            o.deps = sorted(set(d for d in o.deps if d != o.id))
            for d in o.deps:
                succ[d].append(o.id)
        pend = {e: [] for e in self.engs}
        for o in ops:
            pend[o.eng].append(o.id)
        done = [False] * n
        fin = [0.0] * n
        efree = {e: 0.0 for e in self.engs}
        best = {e: None for e in self.engs}
        dirty = set(self.engs)
        HOP, SAME, W = 0.30, 0.25, 400
        SLACK = SLACK_US
        order = []
        INF = float("inf")
        while len(order) < n:
            for e in dirty:
                lst = pend[e]
                be, bi, bpos = INF, -1, -1
                ef = efree[e]
                lim = min(W, len(lst))
                for pos in range(lim):
                    oid = lst[pos]
                    rt = ef
                    ok = True
                    for d in ops[oid].deps:
                        if not done[d]:
                            ok = False
                            break
                        t = fin[d] + (SAME if ops[d].eng is e else HOP)
                        if t > rt:
                            rt = t
                    if ok and e is act and asets[oid] is not None and cur_set[0] is not None and asets[oid] != cur_set[0]:
                        rt += TL
                    if ok and rt < be - SLACK:
                        be, bi, bpos = rt, oid, pos
                        if rt <= ef:
                            break
                best[e] = (be, bi, bpos) if bi >= 0 else None
            dirty.clear()
            ce, cb = None, None
            for e in self.engs:
                b = best[e]
                if b is not None and (cb is None or b[0] < cb[0] or (b[0] == cb[0] and b[1] < cb[1])):
                    ce, cb = e, b
            assert ce is not None, "scheduler deadlock"
            st, oid, pos = cb
            o = ops[oid]
            o.start = st
            if o.is_dma:
                efree[ce] = st + 0.15
                fin[oid] = st + costs[oid]
            else:
                efree[ce] = st + costs[oid]
                fin[oid] = efree[ce]
                if ce is act and asets[oid] is not None:
                    cur_set[0] = asets[oid]
            done[oid] = True
            del pend[ce][pos]
            order.append(o)
            dirty.add(ce)
            for sidx in succ[oid]:
                dirty.add(ops[sidx].eng)
        self.est_makespan = max(fin) if fin else 0.0
        return order

    def flush(self):
        order = self.schedule()
        ops = self.ops
        for o in order:
            eng = o.eng
            for d in o.deps:
                p = ops[d]
                if eng.is_pe and p.eng is eng and not p.is_dma:
                    continue
                s, v = p.tok
                if eng.waited.get(s, 0) >= v:
                    continue
                eng.h.wait_ge(s.h, v)
                eng.waited[s] = v
                self.n_inst += 1
            if o.is_dma:
                pool = self.dma_pool.setdefault(eng, [])
                if len(pool) < 8:
                    dsem = self.new_sem(None)
                    pool.append(dsem)
                else:
                    i = self.dma_rr.get(eng, 0)
                    dsem = pool[i % len(pool)]
                    self.dma_rr[eng] = i + 1
                if dsem.val and eng.waited.get(dsem, 0) < dsem.val:
                    eng.h.wait_ge(dsem.h, dsem.val)
                    eng.waited[dsem] = dsem.val
                    self.n_inst += 1
                ins = o.fns[0]()
                dsem.val += 16
                ins.then_inc(dsem.h, 16)
                o.tok = (dsem, dsem.val)
            else:
                ins = None
                for fn in o.fns:
                    ins = fn()
                    self.n_inst += 1
                s = eng.sem()
                s.val += 1
                ins.then_inc(s.h, 1)
                o.tok = (s, s.val)
            self.n_inst += 1
        SP = self.SP
        for q, pool in self.dma_pool.items():
            for dsem in pool:
                if dsem.val and SP.waited.get(dsem, 0) < dsem.val:
                    self.nc.sync.wait_ge(dsem.h, dsem.val)
                    SP.waited[dsem] = dsem.val


class WStream:
    def __init__(self, kb, slots, plan):
        self.kb = kb
        self.slots = slots
        self.plan = plan
        self.issued = 0
        self.taken = 0
        self.free = [True] * len(slots)

    def _pump(self):
        while self.issued < len(self.plan):
            si = self.issued % len(self.slots)
            if not self.free[si]:
                break
            h, t = self.slots[si]
            for (dst, src, c0, n) in self.plan[self.issued]:
                self.kb.dma(self.kb.POOL, out=h[:, :, dst:dst + n], in_=src[:, :, c0:c0 + n], writes=[t])
            self.free[si] = False
            self.issued += 1

    def next(self):
        self._pump()
        assert self.taken < self.issued, "weight stream stalled (slot not released)"
        si = self.taken % len(self.slots)
        self.taken += 1
        h, t = self.slots[si]
        return si, h, t

    def release(self, si):
        self.free[si] = True
        self._pump()


def unit_plan(nseq, wv, wmd, wfd, wout):
    plan = []
    for _ in range(nseq):
        for h in range(4):
            plan.append([(0, wv, C_MQ + h * 128, 128), (128, wv, C_MK + h * 128, 128)])
            plan.append([(0, wv, C_MV + h * 256, 256)])
            plan.append([(0, wv, C_MO + h * 256, 256)])
            plan.append([(0, wv, C_MZ + h * 256, 256)])
        for oc in range(8):
            plan.append([(0, wmd, oc * 128, 128), (128, wv, C_GB + oc * 128, 128)])
        for h in range(8):
            plan.append([(0, wv, C_FQ + h * 128, 128), (128, wv, C_FK + h * 128, 128)])
            plan.append([(0, wv, C_FV + h * 128, 128), (128, wv, C_FZ + h * 128, 128)])
        for oc in range(8):
            plan.append([(0, wfd, oc * 128, 128), (128, wv, C_GA + oc * 128, 128)])
        for u in range(4):
            plan.append([(0, wout, u * 256, 256)])
    return plan


def build(nseq=2, dbg=None, upto=None):
    _, _, costs = _build_pass(nseq, dbg, upto, None)
    nc, dbg_out, _ = _build_pass(nseq, dbg, upto, costs)
    return nc, dbg_out


def _build_pass(nseq, dbg, upto, costs):
    dbg = dbg or set()
    nc = bass.Bass("TRN2", target_bir_lowering=False)
    x_d = nc.dram_tensor("x", [nseq, S, D], F32, kind="ExternalInput").ap()
    w_in = nc.dram_tensor("w_in", [D, IN_W], F32, kind="ExternalInput").ap()
    w_fd = nc.dram_tensor("w_fox_down", [D, D], F32, kind="ExternalInput").ap()
    w_md = nc.dram_tensor("w_ml_down", [D, D], F32, kind="ExternalInput").ap()
    w_o = nc.dram_tensor("w_out", [D, D], F32, kind="ExternalInput").ap()
    norm_g = nc.dram_tensor("norm_g", [D], F32, kind="ExternalInput").ap()
    final_g = nc.dram_tensor("final_g", [D], F32, kind="ExternalInput").ap()
    ml_g = nc.dram_tensor("ml_norm_g", [D], F32, kind="ExternalInput").ap()
    gbias_d = nc.dram_tensor("gbias", [128, NT, 16], F32, kind="ExternalInput").ap()
    convw_d = nc.dram_tensor("convw", [128, 8, 4], F32, kind="ExternalInput").ap()
    convb_d = nc.dram_tensor("convb", [128, 8], F32, kind="ExternalInput").ap()
    bgate_d = nc.dram_tensor("bgate", [128, 16], F32, kind="ExternalInput").ap()
    y_d = nc.dram_tensor("y", [nseq, S, D], F32, kind="ExternalOutput").ap()
    dbg_out = {}

    def dbg_tensor(name, shape, dt):
        dbg_out[name] = nc.dram_tensor("dbg_" + name, shape, dt, kind="ExternalOutput").ap()
        return dbg_out[name]

    wv = w_in.rearrange("(kc p) c -> p kc c", p=128)
    wfd = w_fd.rearrange("(kc p) c -> p kc c", p=128)
    wmd = w_md.rearrange("(kc p) c -> p kc c", p=128)
    wout = w_o.rearrange("(kc p) c -> p kc c", p=128)

    with ExitStack() as es:
        kb = KB(nc, es, costs)
        PE, ACT, DVE, POOL, SP = kb.PE, kb.ACT, kb.DVE, kb.POOL, kb.SP
        op, dma = kb.op, kb.dma

        xnT, t_xnT = kb.sb("xnT", [128, KC, S], BF16)
        bufA, t_bufA = kb.sb("bufA", [128, KC, S], BF16)
        slots = [kb.sb(f"wslot{i}", [128, KC, 256], BF16) for i in range(NSLOT)]
        wg, t_wg = kb.sb("wg", [128, KC, 16], BF16)
        gX, t_gX = kb.sb("gX", [128, D], F32)
        gN, t_gN = kb.sb("gN", [128, D], F32)
        gF = gM = gX
        t_gF = t_gM = t_gX
        gbias, t_gbias = kb.sb("gbias_sb", [128, NT, 16], F32)
        cw, t_cw = kb.sb("cw", [128, 8, 4], F32)
        cb, t_cb = kb.sb("cb", [128, 8], F32)
        bg, t_bg = kb.sb("bg", [128, 16], F32)
        cbh, t_cbh = kb.sb("cbh", [128, 8], F32)
        bgh, t_bgh = kb.sb("bgh", [128, 16], F32)
        epsT, t_eps = kb.sb("epsT", [128, 1], F32)
        ident, t_ident = kb.sb("ident", [128, 128], BF16)
        onesb, t_onesb = kb.sb("onesb", [128, 128], BF16)
        maskb, t_maskb = kb.sb("maskb", [128, 128], BF16)
        maskf, t_maskf = kb.sb("maskf", [128, 128], F32)
        onesf, t_onesf = kb.sb("onesf", [128, 128], F32)
        hmid, t_hmid = kb.sb("hmid", [128, 128], F32)
        xts = [kb.sb(f"xt{i}", [128, D], F32) for i in range(NXT)]
        xns = [kb.sb(f"xn{i}", [128, D], BF16) for i in range(NXN)]
        stats = [kb.sb(f"stat{i}", [128, 4], F32) for i in range(NXT)]
        zs, t_zs = kb.sb("zs", [128, NT, 16], F32)
        lf, t_lf = kb.sb("lf", [128, NT, 16], F32)
        tot, t_tot = kb.sb("tot", [128, NT, 16], F32)
        bc, t_bc = kb.sb("bc", [128, NT, 16], F32)
        half, t_half = kb.sb("half", [128, NT, 16], F32)
        pre, t_pre = kb.sb("pre", [128, NT, 16], F32)
        cc, t_cc = kb.sb("cc", [128, NT, 16], F32)
        RR, t_RR = kb.sb("RR", [128, NT, 16], F32)
        ew, t_ew = kb.sb("ew", [128, NT, 4], F32)
        eb, t_eb = kb.sb("eb", [128, NT, 4], F32)
        aa, t_aa = kb.sb("aa", [128, NT, 4], F32)
        gtmp, t_gtmp = kb.sb("gtmp", [128, NT, 16], F32)

        PB = []
        for i in range(6):
            h = es.enter_context(nc.psum_tensor(f"pb{i}", [128, 512], F32))
            PB.append((h, Tile(f"pb{i}", psum=True)))
        PT = []
        for i in range(2):
            h = es.enter_context(nc.psum_tensor(f"pt{i}", [128, 8, 128], BF16))
            PT.append((h, Tile(f"pt{i}", psum=True)))

        dma(SP, out=gbias[:], in_=gbias_d[:, :, :], writes=[t_gbias])
        dma(SP, out=cw[:], in_=convw_d[:, :, :], writes=[t_cw])
        dma(SP, out=cb[:], in_=convb_d[:, :], writes=[t_cb])
        dma(SP, out=bg[:], in_=bgate_d[:, :], writes=[t_bg])
        dma(POOL, out=wg[:, :, 0:8], in_=wv[:, :, C_FF:C_FF + 8], writes=[t_wg])
        dma(POOL, out=wg[:, :, 8:16], in_=wv[:, :, C_MI:C_MI + 8], writes=[t_wg])
        op(DVE, lambda: nc.vector.memset(epsT[:], EPS), writes=[t_eps])
        op(DVE, lambda: nc.vector.tensor_scalar(out=cbh[:], in0=cb[:], scalar1=0.5, scalar2=None, op0=ALU.mult),
           reads=[t_cb], writes=[t_cbh])
        op(DVE, lambda: nc.vector.tensor_scalar(out=bgh[:], in0=bg[:], scalar1=0.5, scalar2=None, op0=ALU.mult),
           reads=[t_bg], writes=[t_bgh])
        op(POOL, lambda: nc.gpsimd.memset(onesb[:], 1.0), writes=[t_onesb])
        op(POOL, lambda: nc.gpsimd.memset(onesf[:], 1.0), writes=[t_onesf])
        op(POOL, lambda: nc.gpsimd.memset(hmid[:], 0.0), writes=[t_hmid])
        op(POOL, lambda: nc.gpsimd.memset(hmid[0:64, :], 1.0), writes=[t_hmid])
        op(POOL, lambda: nc.gpsimd.affine_select(out=ident[:], in_=onesb[:], pattern=[[1, 128]],
                                                 compare_op=ALU.is_equal, fill=0.0, base=0,
                                                 channel_multiplier=-1), reads=[t_onesb], writes=[t_ident])
        op(POOL, lambda: nc.gpsimd.affine_select(out=maskb[:], in_=onesb[:], pattern=[[1, 128]],
                                                 compare_op=ALU.is_ge, fill=0.0, base=0,
                                                 channel_multiplier=-1), reads=[t_onesb], writes=[t_maskb])
        op(POOL, lambda: nc.gpsimd.affine_select(out=maskf[:], in_=onesf[:], pattern=[[1, 128]],
                                                 compare_op=ALU.is_ge, fill=0.0, base=0,
                                                 channel_multiplier=-1), reads=[t_onesf], writes=[t_maskf])

        ws = WStream(kb, slots, unit_plan(nseq, wv, wmd, wfd, wout))
        proj_rr = [0]
        proj_n = [2]

        proj_idx = [None]

        def proj_bank():
            proj_rr[0] += 1
            if proj_idx[0] is not None:
                return PB[proj_idx[0][proj_rr[0] % len(proj_idx[0])]]
            return PB[proj_rr[0] % proj_n[0]]

        def tap(name, src_ap, shape, dt, tiles):
            if name in dbg:
                d = dbg_tensor(name, shape, dt)
                dma(SP, out=d, in_=src_ap, reads=tiles)

        def rmsnorm_tail(xt_h, xt_t, si, g_h, g_t, out_ap, out_tiles, sq_out_ap, sq_out_tiles):
            st_h, st_t = stats[si % NXT]
            op(ACT, lambda: nc.scalar.activation(out=sq_out_ap, in_=xt_h[:], func=AF.Square,
                                                 accum_out=st_h[:, 0:1]),
               reads=[xt_t], writes=sq_out_tiles + [st_t])
            op(ACT, lambda: nc.scalar.activation(out=st_h[:, 1:2], in_=st_h[:, 0:1], func=AF.Sqrt,
                                                 scale=1.0 / D, bias=epsT[:]),
               reads=[st_t, t_eps], writes=[st_t])
            op(DVE, lambda: nc.vector.reciprocal(out=st_h[:, 2:3], in_=st_h[:, 1:2]), reads=[st_t], writes=[st_t])
            op(DVE, lambda: nc.vector.scalar_tensor_tensor(out=out_ap, in0=xt_h[:], scalar=st_h[:, 2:3],
                                                           in1=g_h[:], op0=ALU.mult, op1=ALU.mult),
               reads=[xt_t, st_t, g_t], writes=out_tiles)

        pending_O = None
        for seq in range(nseq):
            first = (seq == 0)
            proj_n[0] = 2
            if first:
                dma(SP, out=gN[:], in_=norm_g.partition_broadcast(128), writes=[t_gN])
            for tb in range(NT):
                xt_h, xt_t = xts[tb % NXT]
                xn_h, xn_t = xns[tb % NXN]
                dma(SP, out=xt_h[:], in_=x_d[seq, tb * 128:(tb + 1) * 128, :], writes=[xt_t])
                rmsnorm_tail(xt_h, xt_t, tb, gN, t_gN, xn_h[:], [xn_t], xn_h[:], [xn_t])
                pt_h, pt_t = PT[tb % 2]
                for kc in range(KC):
                    op(PE, lambda kc=kc: nc.tensor.transpose(out=pt_h[:, kc, :], in_=xn_h[:, kc * 128:(kc + 1) * 128],
                                                             identity=ident[:]),
                       reads=[xn_t, t_ident], writes=[pt_t], inc=(kc == KC - 1))
                op(ACT, lambda: nc.scalar.copy(out=xnT[:, :, tb * 128:(tb + 1) * 128], in_=pt_h[:]),
                   reads=[pt_t], writes=[t_xnT])
            if first:
                tap("xnT", xnT[:], [128, KC, S], BF16, [t_xnT])

            if upto == "P0":
                break

            pz_h, pz_t = PB[2]
            for tb in range(NT):
                for kc in range(KC):
                    op(PE, lambda kc=kc, tb=tb: nc.tensor.matmul(pz_h[:, tb * 16:(tb + 1) * 16],
                                                                 lhsT=xnT[:, kc, tb * 128:(tb + 1) * 128],
                                                                 rhs=wg[:, kc, :], start=(kc == 0), stop=(kc == KC - 1)),
                       reads=[t_xnT, t_wg], writes=[pz_t], inc=(kc == KC - 1 and tb == NT - 1))
            op(DVE, lambda: nc.vector.tensor_tensor(out=zs[:].rearrange("p a b -> p (a b)"), in0=pz_h[:, 0:256],
                                                    in1=gbias[:].rearrange("p a b -> p (a b)"), op=ALU.add),
               reads=[pz_t, t_gbias], writes=[t_zs])
            if upto == "G1":
                tap("lf", zs[:], [128, NT, 16], F32, [t_zs])
                break
            op(ACT, lambda: nc.scalar.activation(out=gtmp[:], in_=zs[:], func=AF.Exp, scale=-1.0),
               reads=[t_zs], writes=[t_gtmp])
            op(ACT, lambda: nc.scalar.activation(out=gtmp[:], in_=gtmp[:], func=AF.Ln, bias=1.0, scale=1.0),
               reads=[t_gtmp], writes=[t_gtmp])
            op(DVE, lambda: nc.vector.tensor_scalar(out=lf[:], in0=gtmp[:], scalar1=-1.0, scalar2=None, op0=ALU.mult),
               reads=[t_gtmp], writes=[t_lf])
            if upto == "G2":
                tap("lf", lf[:], [128, NT, 16], F32, [t_lf])
                break
            lf2 = lf[:].rearrange("p a b -> p (a b)")
            for (lhs_h, lhs_t, dst_h, dst_t, pbi) in ((onesf, t_onesf, tot, t_tot, 3), (maskf, t_maskf, bc, t_bc, 4),
                                                       (hmid, t_hmid, half, t_half, 5)):
                pp_h, pp_t = PB[pbi]
                op(PE, lambda lhs_h=lhs_h, pp_h=pp_h: nc.tensor.matmul(pp_h[:, 0:256], lhsT=lhs_h[:], rhs=lf2,
                                                                       start=True, stop=True),
                   reads=[lhs_t, t_lf], writes=[pp_t])
                op(DVE, lambda dst_h=dst_h, pp_h=pp_h: nc.vector.tensor_copy(out=dst_h[:].rearrange("p a b -> p (a b)"),
                                                                             in_=pp_h[:, 0:256]),
                   reads=[pp_t], writes=[dst_t])
            if upto == "G3":
                tap("lf", lf[:], [128, NT, 16], F32, [t_lf])
                tap("cc", bc[:], [128, NT, 16], F32, [t_bc])
                break
            op(DVE, lambda: nc.vector.memset(pre[:, 0, :], 0.0), writes=[t_pre])
            for tb in range(1, NT):
                op(DVE, lambda tb=tb: nc.vector.tensor_tensor(out=pre[:, tb, :], in0=pre[:, tb - 1, :],
                                                              in1=tot[:, tb - 1, :], op=ALU.add),
                   reads=[t_pre, t_tot], writes=[t_pre])
            op(DVE, lambda: nc.vector.tensor_tensor(out=cc[:], in0=bc[:], in1=pre[:], op=ALU.add),
               reads=[t_bc, t_pre], writes=[t_cc])
            op(DVE, lambda: nc.vector.tensor_tensor(out=RR[:], in0=half[:], in1=pre[:], op=ALU.add),
               reads=[t_half, t_pre], writes=[t_RR])
            if upto == "G4":
                tap("cc", cc[:], [128, NT, 16], F32, [t_cc])
                break
            op(DVE, lambda: nc.vector.tensor_tensor(out=ew[:], in0=zs[:, :, 8:12], in1=bc[:, :, 12:16], op=ALU.subtract),
               reads=[t_zs, t_bc], writes=[t_ew])
            if upto == "G5":
                tap("cc", cc[:], [128, NT, 16], F32, [t_cc, t_ew])
                break
            op(ACT, lambda: nc.scalar.activation(out=ew[:], in_=ew[:], func=AF.Exp), reads=[t_ew], writes=[t_ew])
            if upto == "G6":
                tap("cc", cc[:], [128, NT, 16], F32, [t_cc, t_ew])
                break
            op(ACT, lambda: nc.scalar.activation(out=eb[:], in_=bc[:, :, 12:16], func=AF.Exp), reads=[t_bc], writes=[t_eb])
            op(DVE, lambda: nc.vector.tensor_scalar(out=eb[:], in0=eb[:], scalar1=float(128 ** -0.5), scalar2=None,
                                                    op0=ALU.mult), reads=[t_eb], writes=[t_eb])
            if upto == "G7":
                tap("cc", cc[:], [128, NT, 16], F32, [t_cc, t_ew, t_eb])
                break
            op(ACT, lambda: nc.scalar.activation(out=aa[:], in_=tot[:, :, 12:16], func=AF.Exp), reads=[t_tot], writes=[t_aa])
            if first:
                tap("lf", lf[:], [128, NT, 16], F32, [t_lf])
                tap("cc", cc[:], [128, NT, 16], F32, [t_cc])
                tap("RR", RR[:], [128, NT, 16], F32, [t_RR])
                tap("ew", ew[:], [128, NT, 4], F32, [t_ew])
                tap("eb", eb[:], [128, NT, 4], F32, [t_eb])
                tap("aa", aa[:], [128, NT, 4], F32, [t_aa])

            if upto == "G":
                break

            if pending_O is not None:
                pending_O()
                pending_O = None
            proj_idx[0] = M_PROJ_BANKS
            dma(SP, out=gX[:], in_=ml_g.partition_broadcast(128), writes=[t_gX])
            with ExitStack() as ms:
                mq2 = [kb.scratch(ms, f"m_qT{i}", [128, S], BF16) for i in range(MDB)]
                mk2 = [kb.scratch(ms, f"m_kT{i}", [128, S], BF16) for i in range(MDB)]
                mc2 = [kb.scratch(ms, f"m_cbuf{i}", [128, S + 4], F32) for i in range(NCB)]
                accs = [kb.scratch(ms, f"m_acc{i}", [128, 1024], F32) for i in range(2)]
                ths = [kb.scratch(ms, f"m_th{i}", [128, 1024], F32) for i in range(NTH)]
                kTok, t_kTok = kb.scratch(ms, "m_kTok", [128, NT, 128], BF16)
                vaug, t_vaug = kb.scratch(ms, "m_vaug", [128, NT, 258], BF16)
                Crun = [kb.scratch(ms, f"m_C{i}", [128, 258], F32) for i in range(2)]
                Cbf, t_Cbf = kb.scratch(ms, "m_Cbf", [128, NT, 258], BF16)
                numS2 = [kb.scratch(ms, f"m_numS{i}", [128, 4, 258], F32) for i in range(NQB)]
                Gp2 = [kb.scratch(ms, f"m_Gp{i}", [128, 4, 256], BF16) for i in range(NQB)]
                STm = [kb.scratch(ms, f"m_STm{i}", [128, 128], BF16) for i in range(2)]
                hn = [kb.scratch(ms, f"m_hn{i}", [128, 256], BF16) for i in range(2)]
                s12 = [kb.scratch(ms, f"m_s12{i}", [128, 512], F32) for i in range(2)]
                sm2 = [kb.scratch(ms, f"m_sm{i}", [128, 12, 4], F32) for i in range(NQB)]
                junk, t_junk = kb.scratch(ms, "m_junk", [128, 256], BF16)
                for (cb_h, cb_t) in mc2:
                    op(DVE, lambda cb_h=cb_h: nc.vector.memset(cb_h[:, 0:4], 0.0), writes=[cb_t])
                for h in range(4):
                    qT, t_qT = mq2[h % MDB]
                    kT, t_kT = mk2[h % MDB]
                    s1i, u1, t_u1 = ws.next()
                    s2i, u2, t_u2 = ws.next()
                    s3i, u3, t_u3 = ws.next()
                    s4i, u4, t_u4 = ws.next()
                    for which, (dst, t_dst) in enumerate(((qT, t_qT), (kT, t_kT))):
                        chunk = which * 4 + h
                        cbuf, t_cbuf = mc2[which % len(mc2)]
                        for tc in range(4):
                            pp_h, pp_t = proj_bank()
                            for kc in range(KC):
                                op(PE, lambda kc=kc, tc=tc, which=which, pp_h=pp_h: nc.tensor.matmul(
                                    pp_h[:, 0:512], lhsT=u1[:, kc, which * 128:(which + 1) * 128],
                                    rhs=xnT[:, kc, tc * 512:(tc + 1) * 512], start=(kc == 0), stop=(kc == KC - 1)),
                                   reads=[t_u1, t_xnT], writes=[pp_t], inc=(kc == KC - 1))
                            op(ACT, lambda tc=tc, pp_h=pp_h: nc.scalar.copy(out=cbuf[:, 4 + tc * 512:4 + (tc + 1) * 512],
                                                                           in_=pp_h[:, 0:512]),
                               reads=[pp_t], writes=[t_cbuf])
                        for hf in range(2):
                            o = hf * 1024
                            ac_h, ac_t = accs[hf]
                            op(DVE, lambda o=o, ac_h=ac_h, chunk=chunk: nc.vector.tensor_scalar(
                                out=ac_h[:], in0=cbuf[:, 4 + o:4 + o + 1024], scalar1=cw[:, chunk, 3:4], scalar2=None,
                                op0=ALU.mult), reads=[t_cbuf, t_cw], writes=[ac_t])
                            for j in (2, 1, 0):
                                op(DVE, lambda o=o, ac_h=ac_h, chunk=chunk, j=j: nc.vector.scalar_tensor_tensor(
                                    out=ac_h[:], in0=cbuf[:, 1 + j + o:1 + j + o + 1024], scalar=cw[:, chunk, j:j + 1],
                                    in1=ac_h[:], op0=ALU.mult, op1=ALU.add), reads=[t_cbuf, t_cw, ac_t], writes=[ac_t])
                            th_h, th_t = ths[hf % NTH]
                            op(ACT, lambda ac_h=ac_h, chunk=chunk, th_h=th_h: nc.scalar.activation(
                                out=th_h[:], in_=ac_h[:], func=AF.Tanh, bias=cbh[:, chunk:chunk + 1], scale=0.5),
                               reads=[ac_t, t_cbh], writes=[th_t])
                            op(DVE, lambda th_h=th_h: nc.vector.tensor_scalar(
                                out=th_h[:], in0=th_h[:], scalar1=0.5, scalar2=0.5, op0=ALU.mult, op1=ALU.add),
                               reads=[th_t], writes=[th_t])
                            op(DVE, lambda o=o, ac_h=ac_h, chunk=chunk, dst=dst, th_h=th_h: nc.vector.scalar_tensor_tensor(
                                out=dst[:, o:o + 1024], in0=ac_h[:], scalar=cb[:, chunk:chunk + 1], in1=th_h[:],
                                op0=ALU.add, op1=ALU.mult), reads=[ac_t, t_cb, th_t], writes=[t_dst])
                    ws.release(s1i)
                    for g in range(2):
                        pt_h, pt_t = PT[g]
                        for c8 in range(8):
                            c = g * 8 + c8
                            op(PE, lambda c=c, c8=c8, pt_h=pt_h: nc.tensor.transpose(
                                out=pt_h[:, c8, :], in_=kT[:, c * 128:(c + 1) * 128], identity=ident[:]),
                               reads=[t_kT, t_ident], writes=[pt_t], inc=(c8 == 7))
                        for c8 in range(8):
                            c = g * 8 + c8
                            op(ACT, lambda c=c, c8=c8, pt_h=pt_h: nc.scalar.activation(
                                out=kTok[:, c, :], in_=pt_h[:, c8, :], func=AF.Copy, scale=aa[:, c, h:h + 1]),
                               reads=[pt_t, t_aa], writes=[t_kTok])
                    for tb in range(NT):
                        pp_h, pp_t = proj_bank()
                        for kc in range(KC):
                            op(PE, lambda kc=kc, tb=tb, pp_h=pp_h: nc.tensor.matmul(
                                pp_h[:, 0:256], lhsT=xnT[:, kc, tb * 128:(tb + 1) * 128], rhs=u2[:, kc, :],
                                start=(kc == 0), stop=(kc == KC - 1)),
                               reads=[t_u2, t_xnT], writes=[pp_t], inc=(kc == KC - 1))
                        op(ACT, lambda tb=tb, pp_h=pp_h: nc.scalar.activation(
                            out=vaug[:, tb, 0:256], in_=pp_h[:, 0:256], func=AF.Copy, scale=ew[:, tb, h:h + 1]),
                           reads=[pp_t, t_ew], writes=[t_vaug])
                    ws.release(s2i)
                    op(DVE, lambda: nc.vector.tensor_copy(out=vaug[:, :, 256:257], in_=ew[:, :, h:h + 1]),
                       reads=[t_ew], writes=[t_vaug])
                    dcb = (PB[5], PB[2])
                    for c in range(NT - 1):
                        pd_h, pd_t = dcb[c % 2]
                        op(PE, lambda c=c, pd_h=pd_h: nc.tensor.matmul(pd_h[:, 0:257], lhsT=kTok[:, c, :],
                                                                       rhs=vaug[:, c, 0:257], start=True, stop=True),
                           reads=[t_kTok, t_vaug], writes=[pd_t])
                        cn_h, cn_t = Crun[(c + 1) % 2]
                        cp_h, cp_t = Crun[c % 2]
                        if c == 0:
                            op(DVE, lambda pd_h=pd_h, cn_h=cn_h: nc.vector.tensor_copy(out=cn_h[:, 0:257], in_=pd_h[:, 0:257]),
                               reads=[pd_t], writes=[cn_t])
                        else:
                            op(DVE, lambda c=c, pd_h=pd_h, cn_h=cn_h, cp_h=cp_h: nc.vector.scalar_tensor_tensor(
                                out=cn_h[:, 0:257], in0=cp_h[:, 0:257], scalar=aa[:, c, h:h + 1], in1=pd_h[:, 0:257],
                                op0=ALU.mult, op1=ALU.add), reads=[cp_t, pd_t, t_aa], writes=[cn_t])
                        op(ACT, lambda c=c, cn_h=cn_h: nc.scalar.copy(out=Cbf[:, c + 1, 0:257], in_=cn_h[:, 0:257]),
                           reads=[cn_t], writes=[t_Cbf])
                    for c in range(NT):
                        numS, t_numS = numS2[(c // 4) % NQB]
                        Gp, t_Gp = Gp2[(c // 4) % NQB]
                        sm, t_sm = sm2[(c // 4) % NQB]
                        ps_h, ps_t = PB[2]
                        op(PE, lambda c=c: nc.tensor.matmul(ps_h[:, 0:128], lhsT=kT[:, c * 128:(c + 1) * 128],
                                                            rhs=qT[:, c * 128:(c + 1) * 128], start=True, stop=True),
                           reads=[t_kT, t_qT], writes=[ps_t])
                        st_h, st_t = STm[c % 2]
                        op(DVE, lambda st_h=st_h: nc.vector.tensor_tensor(out=st_h[:], in0=ps_h[:, 0:128], in1=maskf[:],
                                                                          op=ALU.mult),
                           reads=[ps_t, t_maskf], writes=[st_t])
                        pn_h, pn_t = PB[3 + c % 2]
                        op(PE, lambda c=c, pn_h=pn_h, st_h=st_h: nc.tensor.matmul(
                            pn_h[:, 0:257], lhsT=st_h[:], rhs=vaug[:, c, 0:257], start=True, stop=(c == 0)),
                           reads=[st_t, t_vaug], writes=[pn_t], inc=(c == 0))
                        if c > 0:
                            op(PE, lambda c=c, pn_h=pn_h: nc.tensor.matmul(
                                pn_h[:, 0:257], lhsT=qT[:, c * 128:(c + 1) * 128], rhs=Cbf[:, c, 0:257],
                                start=False, stop=True), reads=[t_qT, t_Cbf], writes=[pn_t])
                        op(ACT, lambda c=c, pn_h=pn_h: nc.scalar.copy(out=numS[:, c % 4, 0:257], in_=pn_h[:, 0:257]),
                           reads=[pn_t], writes=[t_numS])
                        pp_h, pp_t = proj_bank()
                        for gi, (ug, t_ug) in enumerate(((u3, t_u3), (u4, t_u4))):
                            for kc in range(KC):
                                op(PE, lambda kc=kc, c=c, gi=gi, ug=ug, pp_h=pp_h: nc.tensor.matmul(
                                    pp_h[:, gi * 256:(gi + 1) * 256], lhsT=xnT[:, kc, c * 128:(c + 1) * 128],
                                    rhs=ug[:, kc, :], start=(kc == 0), stop=(kc == KC - 1)),
                                   reads=[t_ug, t_xnT], writes=[pp_t], inc=(kc == KC - 1 and gi == 1))
                        sg_h, sg_t = s12[c % 2]
                        op(ACT, lambda pp_h=pp_h, sg_h=sg_h: nc.scalar.activation(out=sg_h[:, 0:512], in_=pp_h[:, 0:512],
                                                                                  func=AF.Tanh, scale=0.5),
                           reads=[pp_t], writes=[sg_t])
                        op(DVE, lambda pp_h=pp_h, sg_h=sg_h: nc.vector.scalar_tensor_tensor(
                            out=sg_h[:, 256:512], in0=sg_h[:, 256:512], scalar=1.0, in1=pp_h[:, 256:512],
                            op0=ALU.add, op1=ALU.mult), reads=[sg_t, pp_t], writes=[sg_t])
                        op(DVE, lambda sg_h=sg_h: nc.vector.scalar_tensor_tensor(
                            out=sg_h[:, 0:256], in0=sg_h[:, 0:256], scalar=1.0, in1=sg_h[:, 256:512],
                            op0=ALU.add, op1=ALU.mult), reads=[sg_t], writes=[sg_t])
                        op(DVE, lambda c=c, sg_h=sg_h: nc.vector.tensor_tensor(out=Gp[:, c % 4, :], in0=sg_h[:, 0:256],
                                                                               in1=gM[:, h * 256:(h + 1) * 256], op=ALU.mult),
                           reads=[sg_t, t_gM], writes=[t_Gp])
                        if c % 4 == 3:
                            cs = c - 3
                            den = numS[:, :, 256]
                            ebs = eb[:, cs:cs + 4, h]
                            op(DVE, lambda: nc.vector.tensor_tensor(out=sm[:, 0, :], in0=den, in1=ebs, op=ALU.mult),
                               reads=[t_numS, t_eb], writes=[t_sm])
                            op(DVE, lambda: nc.vector.scalar_tensor_tensor(out=sm[:, 1, :], in0=sm[:, 0, :], scalar=-1.0,
                                                                           in1=sm[:, 0, :], op0=ALU.mult, op1=ALU.max),
                               reads=[t_sm], writes=[t_sm])
                            op(DVE, lambda: nc.vector.tensor_scalar(out=sm[:, 2, :], in0=sm[:, 1, :], scalar1=1.0,
                                                                    scalar2=None, op0=ALU.max), reads=[t_sm], writes=[t_sm])
                            op(DVE, lambda: nc.vector.reciprocal(out=sm[:, 3, :], in_=sm[:, 2, :]), reads=[t_sm], writes=[t_sm])
                            op(DVE, lambda: nc.vector.tensor_tensor(out=sm[:, 4, :], in0=sm[:, 3, :], in1=ebs, op=ALU.mult),
                               reads=[t_sm, t_eb], writes=[t_sm])
                            for c8 in range(4):
                                op(ACT, lambda c8=c8: nc.scalar.activation(out=junk[:], in_=numS[:, c8, 0:256],
                                                                           func=AF.Square, accum_out=sm[:, 5, c8:c8 + 1]),
                                   reads=[t_numS], writes=[t_junk, t_sm])
                            op(DVE, lambda: nc.vector.tensor_tensor(out=sm[:, 6, :], in0=sm[:, 4, :], in1=sm[:, 4, :],
                                                                    op=ALU.mult), reads=[t_sm], writes=[t_sm])
                            op(DVE, lambda: nc.vector.tensor_tensor(out=sm[:, 7, :], in0=sm[:, 6, :], in1=sm[:, 5, :],
                                                                    op=ALU.mult), reads=[t_sm], writes=[t_sm])
                            op(ACT, lambda: nc.scalar.activation(out=sm[:, 8, :], in_=sm[:, 7, :], func=AF.Sqrt,
                                                                 scale=1.0 / 256, bias=epsT[:]),
                               reads=[t_sm, t_eps], writes=[t_sm])
                            op(DVE, lambda: nc.vector.reciprocal(out=sm[:, 9, :], in_=sm[:, 8, :]), reads=[t_sm], writes=[t_sm])
                            op(DVE, lambda: nc.vector.scalar_tensor_tensor(out=sm[:, 10, :], in0=sm[:, 9, :], scalar=0.25,
                                                                           in1=sm[:, 4, :], op0=ALU.mult, op1=ALU.mult),
                               reads=[t_sm], writes=[t_sm])
                            for c8 in range(4):
                                hn_h, hn_t = hn[c8 % 2]
                                op(DVE, lambda c8=c8, hn_h=hn_h: nc.vector.scalar_tensor_tensor(
                                    out=hn_h[:], in0=numS[:, c8, 0:256], scalar=sm[:, 10, c8:c8 + 1], in1=Gp[:, c8, :],
                                    op0=ALU.mult, op1=ALU.mult), reads=[t_numS, t_sm, t_Gp], writes=[hn_t])
                                for j in range(2):
                                    op(PE, lambda c8=c8, j=j, hn_h=hn_h: nc.tensor.transpose(
                                        out=PT[j][0][:, c8, :], in_=hn_h[:, j * 128:(j + 1) * 128], identity=ident[:]),
                                       reads=[hn_t, t_ident], writes=[PT[j][1]])
                            for j in range(2):
                                op(ACT, lambda j=j, cs=cs: nc.scalar.copy(
                                    out=bufA[:, 2 * h + j, cs * 128:(cs + 4) * 128],
                                    in_=PT[j][0][:, 0:4, :].rearrange("p a b -> p (a b)")),
                                   reads=[PT[j][1]], writes=[t_bufA])
                    ws.release(s3i)
                    ws.release(s4i)
                kb.end_scratch_phase()
            if first:
                tap("hbT", bufA[:], [128, KC, S], BF16, [t_bufA])

            if upto == "M":
                break

            ys = es.enter_context(ExitStack())
            bufY, t_bufY = kb.scratch(ys, "bufY", [128, KC, S], BF16)
            proj_n[0] = 6
            proj_idx[0] = None
            with ExitStack() as ms:
                sgs = [kb.scratch(ms, f"d_sg{i}", [128, 512], F32) for i in range(2)]
                for oc in range(8):
                    si, u, t_u = ws.next()
                    for tc in range(4):
                        py_h, py_t = proj_bank()
                        for kc in range(KC):
                            op(PE, lambda kc=kc, tc=tc, py_h=py_h: nc.tensor.matmul(
                                py_h[:, 0:512], lhsT=u[:, kc, 0:128], rhs=bufA[:, kc, tc * 512:(tc + 1) * 512],
                                start=(kc == 0), stop=(kc == KC - 1)), reads=[t_u, t_bufA], writes=[py_t], inc=(kc == KC - 1))
                        pg_h, pg_t = proj_bank()
                        for kc in range(KC):
                            op(PE, lambda kc=kc, tc=tc, pg_h=pg_h: nc.tensor.matmul(
                                pg_h[:, 0:512], lhsT=u[:, kc, 128:256], rhs=xnT[:, kc, tc * 512:(tc + 1) * 512],
                                start=(kc == 0), stop=(kc == KC - 1)), reads=[t_u, t_xnT], writes=[pg_t], inc=(kc == KC - 1))
                        sg_h, sg_t = sgs[tc % 2]
                        op(ACT, lambda pg_h=pg_h, sg_h=sg_h, oc=oc: nc.scalar.activation(
                            out=sg_h[:], in_=pg_h[:, 0:512], func=AF.Tanh, bias=bgh[:, 8 + oc:9 + oc], scale=0.5),
                           reads=[pg_t, t_bgh], writes=[sg_t])
                        op(DVE, lambda py_h=py_h, sg_h=sg_h, oc=oc, tc=tc: nc.vector.scalar_tensor_tensor(
                            out=bufY[:, oc, tc * 512:(tc + 1) * 512], in0=sg_h[:], scalar=1.0, in1=py_h[:, 0:512],
                            op0=ALU.add, op1=ALU.mult), reads=[py_t, sg_t], writes=[t_bufY])
                    ws.release(si)
                kb.end_scratch_phase()
            if first:
                tap("ybg", bufY[:], [128, KC, S], BF16, [t_bufY])

            if upto == "MD":
                break

            proj_n[0] = 2
            with ExitStack() as ms:
                fq = [kb.scratch(ms, f"f_q{i}", [128, S], BF16) for i in range(2)]
                fk = [kb.scratch(ms, f"f_k{i}", [128, S], BF16) for i in range(2)]
                fv = [kb.scratch(ms, f"f_v{i}", [128, NT, 130], BF16) for i in range(2)]
                fzs = [kb.scratch(ms, f"f_z{i}", [128, NT, 128], BF16) for i in range(2)]
                bms = [kb.scratch(ms, f"f_bm{i}", [128, NT, NT], F32) for i in range(2)]
                ptb = [kb.scratch(ms, f"f_pt{i}", [128, 512], BF16) for i in range(NPTB)]
                obs = [kb.scratch(ms, f"f_ob{i}", [128, 4, 128], BF16) for i in range(2)]
                rin, t_rin = kb.scratch(ms, "f_rin", [128, 16], F32)
                tzs = [kb.scratch(ms, f"f_tz{i}", [128, 128], F32) for i in range(4)]
                for i in range(2):
                    op(DVE, lambda i=i: nc.vector.memset(fv[i][0][:, :, 128:129], 2.0), writes=[fv[i][1]])
                sc = float(128 ** -0.5)
                sb_i = 0
                for h in range(8):
                    s1i, u1, t_u1 = ws.next()
                    s2i, u2, t_u2 = ws.next()
                    qT, t_qT = fq[h % 2]
                    kT, t_kT = fk[h % 2]
                    V, t_V = fv[h % 2]
                    FZ, t_FZ = fzs[h % 2]
                    bm, t_bm = bms[h % 2]
                    for which, (dst, t_dst) in enumerate(((qT, t_qT), (kT, t_kT))):
                        for tc in range(4):
                            pp_h, pp_t = proj_bank()
                            for kc in range(KC):
                                op(PE, lambda kc=kc, tc=tc, which=which, pp_h=pp_h: nc.tensor.matmul(
                                    pp_h[:, 0:512], lhsT=u1[:, kc, which * 128:(which + 1) * 128],
                                    rhs=xnT[:, kc, tc * 512:(tc + 1) * 512], start=(kc == 0), stop=(kc == KC - 1)),
                                   reads=[t_u1, t_xnT], writes=[pp_t], inc=(kc == KC - 1))
                            op(DVE, lambda tc=tc, pp_h=pp_h, dst=dst: nc.vector.tensor_copy(
                                out=dst[:, tc * 512:(tc + 1) * 512], in_=pp_h[:, 0:512]), reads=[pp_t], writes=[t_dst])
                    ws.release(s1i)
                    for tb in range(NT):
                        pp_h, pp_t = proj_bank()
                        for kc in range(KC):
                            op(PE, lambda kc=kc, tb=tb, pp_h=pp_h: nc.tensor.matmul(
                                pp_h[:, 0:256], lhsT=xnT[:, kc, tb * 128:(tb + 1) * 128], rhs=u2[:, kc, :],
                                start=(kc == 0), stop=(kc == KC - 1)), reads=[t_u2, t_xnT], writes=[pp_t], inc=(kc == KC - 1))
                        op(DVE, lambda tb=tb, pp_h=pp_h: nc.vector.tensor_copy(out=V[:, tb, 0:128], in_=pp_h[:, 0:128]),
                           reads=[pp_t], writes=[t_V])
                        tz_h, tz_t = tzs[tb % 4]
                        op(ACT, lambda pp_h=pp_h, tz_h=tz_h: nc.scalar.activation(out=tz_h[:], in_=pp_h[:, 128:256],
                                                                                  func=AF.Tanh, scale=0.5),
                           reads=[pp_t], writes=[tz_t])
                        op(DVE, lambda tb=tb, pp_h=pp_h, tz_h=tz_h: nc.vector.scalar_tensor_tensor(
                            out=FZ[:, tb, :], in0=tz_h[:], scalar=1.0, in1=pp_h[:, 128:256], op0=ALU.add, op1=ALU.mult),
                           reads=[pp_t, tz_t], writes=[t_FZ])
                    ws.release(s2i)
                    for j in range(NT):
                        op(DVE, lambda j=j: nc.vector.tensor_scalar(out=bm[:, j, :], in0=RR[:, :, h],
                                                                    scalar1=cc[:, j, h:h + 1], scalar2=None,
                                                                    op0=ALU.subtract), reads=[t_RR, t_cc], writes=[t_bm])
                    steps = [(I, j) for I in range(4) for j in range(4 * I + 4)]

                    def emit_ST(n):
                        I, j = steps[n]
                        i0 = max(j, 4 * I)
                        ps_h, ps_t = PB[2 + n % 2]
                        N = (4 * I + 4 - i0) * 128
                        op(PE, lambda: nc.tensor.matmul(ps_h[:, 0:N], lhsT=kT[:, j * 128:(j + 1) * 128],
                                                        rhs=qT[:, i0 * 128:(4 * I + 4) * 128], start=True, stop=True),
                           reads=[t_kT, t_qT], writes=[ps_t])

                    def emit_exp(n):
                        I, j = steps[n]
                        i0 = max(j, 4 * I)
                        ps_h, ps_t = PB[2 + n % 2]
                        p_h, p_t = ptb[n % NPTB]
                        for i in range(i0, 4 * I + 4):
                            lo = (i - i0) * 128
                            op(ACT, lambda i=i, lo=lo: nc.scalar.activation(
                                out=p_h[:, lo:lo + 128], in_=ps_h[:, lo:lo + 128], func=AF.Exp, scale=sc,
                                bias=bm[:, j, i:i + 1]), reads=[ps_t, t_bm], writes=[p_t])
                        if j >= 4 * I:
                            op(DVE, lambda: nc.vector.tensor_tensor(out=p_h[:, 0:128], in0=p_h[:, 0:128], in1=maskb[:],
                                                                    op=ALU.mult), reads=[p_t, t_maskb], writes=[p_t])

                    def emit_PV(n):
                        I, j = steps[n]
                        i0 = max(j, 4 * I)
                        p_h, p_t = ptb[n % NPTB]
                        for i in range(i0, 4 * I + 4):
                            il = i - 4 * I
                            lo = (i - i0) * 128
                            po_h, po_t = PB[4 + il // 2]
                            off = (il % 2) * 129
                            op(PE, lambda i=i, lo=lo, po_h=po_h, off=off, il=il: nc.tensor.matmul(
                                po_h[:, off:off + 129], lhsT=p_h[:, lo:lo + 128], rhs=V[:, j, 0:129],
                                start=(j == 0 and il % 2 == 0), stop=(j == i), skip_group_check=True),
                               reads=[p_t, t_V], writes=[po_t])
                            if j == i:
                                ob_h, ob_t = obs[I % 2]
                                op(DVE, lambda po_h=po_h, off=off, i=i: nc.vector.reciprocal(
                                    out=rin[:, i:i + 1], in_=po_h[:, off + 128:off + 129]), reads=[po_t], writes=[t_rin])
                                op(DVE, lambda po_h=po_h, off=off, i=i, il=il, ob_h=ob_h: nc.vector.scalar_tensor_tensor(
                                    out=ob_h[:, il, :], in0=po_h[:, off:off + 128], scalar=rin[:, i:i + 1], in1=FZ[:, i, :],
                                    op0=ALU.mult, op1=ALU.mult), reads=[po_t, t_rin, t_FZ], writes=[ob_t])
                                pt_h, pt_t = PT[I % 2]
                                pb0 = 0
                                op(PE, lambda il=il, ob_h=ob_h, pt_h=pt_h, pb0=pb0: nc.tensor.transpose(
                                    out=pt_h[:, pb0 + il, :], in_=ob_h[:, il, :], identity=ident[:]),
                                   reads=[ob_t, t_ident], writes=[pt_t])
                                if il == 3:
                                    op(DVE, lambda pt_h=pt_h, I=I: nc.vector.tensor_copy(
                                        out=bufA[:, h, I * 512:(I + 1) * 512],
                                        in_=pt_h[:, pb0:pb0 + 4, :].rearrange("p a b -> p (a b)")),
                                       reads=[pt_t], writes=[t_bufA])

                    emit_ST(0)
                    for n in range(len(steps)):
                        emit_exp(n)
                        if n + 1 < len(steps):
                            emit_ST(n + 1)
                        emit_PV(n)
                kb.end_scratch_phase()
            if first:
                tap("oaT", bufA[:], [128, KC, S], BF16, [t_bufA])

            if upto == "F":
                break

            proj_n[0] = 6
            with ExitStack() as ms:
                sgs = [kb.scratch(ms, f"e_sg{i}", [128, 512], F32) for i in range(2)]
                for oc in range(8):
                    si, u, t_u = ws.next()
                    for tc in range(4):
                        py_h, py_t = proj_bank()
                        for kc in range(KC):
                            op(PE, lambda kc=kc, tc=tc, py_h=py_h: nc.tensor.matmul(
                                py_h[:, 0:512], lhsT=u[:, kc, 0:128], rhs=bufA[:, kc, tc * 512:(tc + 1) * 512],
                                start=(kc == 0), stop=(kc == KC - 1)), reads=[t_u, t_bufA], writes=[py_t], inc=(kc == KC - 1))
                        pg_h, pg_t = proj_bank()
                        for kc in range(KC):
                            op(PE, lambda kc=kc, tc=tc, pg_h=pg_h: nc.tensor.matmul(
                                pg_h[:, 0:512], lhsT=u[:, kc, 128:256], rhs=xnT[:, kc, tc * 512:(tc + 1) * 512],
                                start=(kc == 0), stop=(kc == KC - 1)), reads=[t_u, t_xnT], writes=[pg_t], inc=(kc == KC - 1))
                        sg_h, sg_t = sgs[tc % 2]
                        op(ACT, lambda pg_h=pg_h, sg_h=sg_h, oc=oc: nc.scalar.activation(
                            out=sg_h[:], in_=pg_h[:, 0:512], func=AF.Tanh, bias=bgh[:, oc:oc + 1], scale=0.5),
                           reads=[pg_t, t_bgh], writes=[sg_t])
                        op(DVE, lambda py_h=py_h, sg_h=sg_h: nc.vector.scalar_tensor_tensor(
                            out=sg_h[:], in0=sg_h[:], scalar=1.0, in1=py_h[:, 0:512], op0=ALU.add, op1=ALU.mult),
                           reads=[py_t, sg_t], writes=[sg_t])
                        op(DVE, lambda sg_h=sg_h, oc=oc, tc=tc: nc.vector.tensor_tensor(
                            out=bufY[:, oc, tc * 512:(tc + 1) * 512], in0=sg_h[:], in1=bufY[:, oc, tc * 512:(tc + 1) * 512],
                            op=ALU.add), reads=[sg_t, t_bufY], writes=[t_bufY])
                    ws.release(si)
                kb.end_scratch_phase()
            if first:
                tap("yT", bufY[:], [128, KC, S], BF16, [t_bufY])

            if upto == "FD":
                break

            def _phase_O(seq=seq, bufY=bufY, t_bufY=t_bufY, ys=ys):
                dma(SP, out=gX[:], in_=final_g.partition_broadcast(128), writes=[t_gX])
                us = [ws.next() for _ in range(4)]
                for tb in range(NT):
                    xt_h, xt_t = xts[tb % NXT]
                    dma(SP, out=xt_h[:], in_=x_d[seq, tb * 128:(tb + 1) * 128, :], writes=[xt_t])
                    banks = [proj_bank(), proj_bank()]
                    for hf in range(2):
                        po_h, po_t = banks[hf]
                        for n in range(2):
                            si, u, t_u = us[2 * hf + n]
                            for kc in range(KC):
                                op(PE, lambda kc=kc, n=n, u=u, po_h=po_h: nc.tensor.matmul(
                                    po_h[:, n * 256:(n + 1) * 256], lhsT=bufY[:, kc, tb * 128:(tb + 1) * 128], rhs=u[:, kc, :],
                                    start=(kc == 0), stop=(kc == KC - 1)), reads=[t_u, t_bufY], writes=[po_t],
                                   inc=(kc == KC - 1 and n == 1))
                        op(DVE, lambda hf=hf, po_h=po_h: nc.vector.scalar_tensor_tensor(
                            out=xt_h[:, hf * 512:(hf + 1) * 512], in0=po_h[:, 0:512], scalar=0.5,
                            in1=xt_h[:, hf * 512:(hf + 1) * 512], op0=ALU.mult, op1=ALU.add),
                           reads=[po_t, xt_t], writes=[xt_t])
                    xn_h, xn_t = xns[tb % NXN]
                    rmsnorm_tail(xt_h, xt_t, tb, gF, t_gF, xt_h[:], [xt_t], xn_h[:], [xn_t])
                    dma(SP, out=y_d[seq, tb * 128:(tb + 1) * 128, :], in_=xt_h[:], reads=[xt_t])
                for (si, u, t_u) in us:
                    ws.release(si)
                kb.scr_tiles.append(t_bufY)
                kb.end_scratch_phase()
                ys.close()

            pending_O = _phase_O

        if pending_O is not None:
            pending_O()
            pending_O = None
        if kb.mode == "defer":
            kb.flush()
            build.n_inst = kb.n_inst
            build.nsem = kb.nsem
            build.est_us = kb.est_makespan
        out_costs = [(o.cost, o.aset) for o in kb.ops]
    return nc, dbg_out, out_costs


def host_inputs(inputs):
    f = lambda a: np.ascontiguousarray(np.asarray(a, dtype=np.float32))
    gb = np.concatenate([f(inputs["b_fox_f"])[0], f(inputs["b_ml_i"])[0], f(inputs["b_ml_f"])[0]])
    gbias = np.ascontiguousarray(np.broadcast_to(gb[None, None, :], (128, NT, 16)))
    convw = np.ascontiguousarray(f(inputs["conv_w"])[0].reshape(4, 8, 128).transpose(2, 1, 0))
    convb = np.ascontiguousarray(f(inputs["conv_b"])[0].reshape(8, 128).T)
    bgate = np.ascontiguousarray(f(inputs["b_gate"])[0].reshape(16, 128).T)
    return {
        "w_in": f(inputs["w_in"])[0],
        "w_fox_down": f(inputs["w_fox_down"])[0],
        "w_ml_down": f(inputs["w_ml_down"])[0],
        "w_out": f(inputs["w_out"])[0],
        "norm_g": f(inputs["norm_g"])[0],
        "final_g": f(inputs["final_g"]),
        "ml_norm_g": f(inputs["ml_norm_g"])[0],
        "gbias": gbias,
        "convw": convw,
        "convb": convb,
        "bgate": bgate,
    }


def kernel(**inputs):
    x = np.ascontiguousarray(np.asarray(inputs["x"], dtype=np.float32))
    B = x.shape[0]
    nseq = B // NCORES
    shared = host_inputs(inputs)
    nc, _ = build(nseq)
    in_maps = []
    for c in range(NCORES):
        m = dict(shared)
        m["x"] = x[c * nseq:(c + 1) * nseq]
        in_maps.append(m)
    res = run_bass_kernel_spmd(nc, in_maps, core_ids=list(range(NCORES)))
    out = np.concatenate([np.asarray(r["y"]) for r in res.results], axis=0)
    return out.astype(np.float32, copy=False)
```

```python
import types
import numpy as np
from contextlib import ExitStack
import concourse.bass as bass
import concourse.mybir as mybir
from concourse.bass_utils import run_bass_kernel_spmd

F32 = mybir.dt.float32
BF16 = mybir.dt.bfloat16
AF = mybir.ActivationFunctionType
ALU = mybir.AluOpType

S = 2048
D = 1024
NT = 16
KC = 8
NCORES = 8
EPS = 1e-6
C_FQ, C_FK, C_FV, C_FF, C_FZ = 0, 1024, 2048, 3072, 3080
C_MQ, C_MK, C_MV, C_MI, C_MF, C_MO, C_MZ = 4104, 4616, 5128, 6152, 6156, 6160, 7184
C_GA, C_GB = 8208, 9232
IN_W = 10256
SEM_LIMIT = 30000
SLACK_US = 0.0
PE_MAXG = 2
NPTB = 4
NXT = 4
MDB = 1
NCB = 2
NTH = 1
NQB = 2
NXN = 2
M_PROJ_BANKS = [0, 1, 5]
NSLOT = 6


class SemObj:
    __slots__ = ("h", "owner", "val")

    def __init__(self, h, owner):
        self.h = h
        self.owner = owner
        self.val = 0


class Tile:
    __slots__ = ("w", "r", "name", "psum")

    def __init__(self, name, init=None, psum=False):
        self.name = name
        self.w = list(init) if init else []
        self.r = []
        self.psum = psum


class Eng:
    def __init__(self, kb, name, h, is_pe=False):
        self.kb = kb
        self.name = name
        self.h = h
        self.is_pe = is_pe
        self.waited = {}
        self.cur = None

    def sem(self):
        if self.cur is None or self.cur.val >= SEM_LIMIT:
            self.cur = self.kb.new_sem(self)
        return self.cur


class Op:
    __slots__ = ("id", "eng", "fns", "deps", "cost", "is_dma", "tok", "start", "ninstr", "aset")

    def __init__(self, oid, eng, is_dma):
        self.id = oid
        self.eng = eng
        self.fns = []
        self.deps = []
        self.cost = 0.0
        self.is_dma = is_dma
        self.tok = None
        self.start = 0.0
        self.aset = None


def _snap(fn):
    if fn.__closure__ is None:
        return fn
    cells = []
    for c in fn.__closure__:
        try:
            cells.append(types.CellType(c.cell_contents))
        except ValueError:
            cells.append(c)
    return types.FunctionType(fn.__code__, fn.__globals__, fn.__name__, fn.__defaults__, tuple(cells))


def _free_elems(acc):
    try:
        ap = acc.ap
        n = 1
        for (_, c) in ap[1:]:
            n *= c
        return max(int(n), 1)
    except Exception:
        return 64


class KB:
    def __init__(self, nc, es, costs=None):
        self.nc = nc
        self.es = es
        self.mode = "measure" if costs is None else "defer"
        self.costs_in = costs
        self.nsem = 0
        self.PE = Eng(self, "pe", nc.tensor, is_pe=True)
        self.ACT = Eng(self, "act", nc.scalar)
        self.DVE = Eng(self, "dve", nc.vector)
        self.POOL = Eng(self, "pool", nc.gpsimd)
        self.SP = Eng(self, "sp", nc.sync)
        self.engs = [self.PE, self.ACT, self.DVE, self.POOL, self.SP]
        self.dma_pool = {}
        self.dma_rr = {}
        self.scr_tokens = []
        self.scr_tiles = []
        self.scr_id = 0
        self.n_inst = 0
        self.ops = []
        self.open_pe = None

    def new_sem(self, owner):
        self.nsem += 1
        h = self.es.enter_context(self.nc.semaphore(f"s{self.nsem}"))
        return SemObj(h, owner)

    def _deps(self, o, reads, writes):
        eng = o.eng
        ops = self.ops
        d = o.deps
        for t in reads:
            d.extend(t.w)
            if t.psum:
                d.extend(r for r in t.r if ops[r].eng is not eng)
        for t in writes:
            d.extend(t.w)
            d.extend(t.r)

    def _touch(self, oid, reads, writes):
        for t in reads:
            if not t.r or t.r[-1] != oid:
                t.r.append(oid)
        for t in writes:
            t.w = [oid]
            t.r = []

    def _cost_of(self, eng, ins):
        try:
            i = ins.ins
            n = _free_elems(i.outs[0])
            if eng.is_pe:
                f32 = False
                try:
                    f32 = (i.ins[0].dtype == F32) and not getattr(i, "is_transpose", False)
                except Exception:
                    pass
                return (max(n, 200) * (4 if f32 else 1) * 0.5 + 8.0) / 1000.0
            if eng is self.ACT:
                return 0.22 + n * 0.00083
            if eng is self.DVE:
                return 0.10 + n * 0.00104
            return 0.25 + n * 0.002
        except Exception:
            return 0.3

    def op(self, eng, fn, reads=(), writes=(), inc=True):
        if eng.is_pe and self.open_pe is not None:
            o = self.open_pe
        else:
            assert self.open_pe is None, "non-PE op recorded while a PE group is open"
            o = Op(len(self.ops), eng, False)
            self.ops.append(o)
            if eng.is_pe and not inc:
                self.open_pe = o
        if eng.is_pe:
            o.ninstr = getattr(o, "ninstr", 0) + 1
            if inc or o.ninstr >= PE_MAXG:
                self.open_pe = None
        self._deps(o, reads, writes)
        self._touch(o.id, reads, writes)
        if self.mode == "measure":
            ins = fn()
            o.cost += self._cost_of(eng, ins)
            if eng is self.ACT:
                try:
                    fname = str(ins.ins.func)
                except Exception:
                    fname = ""
                if "Exp" in fname or "Tanh" in fname:
                    o.aset = "A"
                elif "Sqrt" in fname:
                    o.aset = "S"
                elif "Ln" in fname:
                    o.aset = "L"
            return ins
        o.fns.append(_snap(fn))
        return None

    def dma(self, q, out, in_, reads=(), writes=()):
        assert self.open_pe is None
        o = Op(len(self.ops), q, True)
        self.ops.append(o)
        self._deps(o, reads, writes)
        self._touch(o.id, reads, writes)
        if self.mode == "measure":
            q.h.dma_start(out=out, in_=in_)
            try:
                n = 1
                for (_, c) in out.ap:
                    n *= c
            except Exception:
                n = 131072
            o.cost = 2.0 + n * 4 / 150e3
            return None
        o.fns.append(lambda: q.h.dma_start(out=out, in_=in_))
        return None

    def sb(self, name, shape, dt, es=None):
        h = (es or self.es).enter_context(self.nc.sbuf_tensor(name, shape, dt))
        return h, Tile(name)

    def scratch(self, es, name, shape, dt):
        self.scr_id += 1
        h = es.enter_context(self.nc.sbuf_tensor(f"{name}_{self.scr_id}", shape, dt))
        t = Tile(name, self.scr_tokens)
        self.scr_tiles.append(t)
        return h, t

    def end_scratch_phase(self):
        acc = set(self.scr_tokens)
        for t in self.scr_tiles:
            acc.update(t.w)
            acc.update(t.r)
        self.scr_tokens = sorted(acc)
        self.scr_tiles = []

    def schedule(self):
        ops = self.ops
        n = len(ops)
        assert len(self.costs_in) == n, (len(self.costs_in), n)
        costs = [c for (c, _) in self.costs_in]
        asets = [a for (_, a) in self.costs_in]
        act = self.ACT
        cur_set = [None]
        TL = 0.0
        succ = [[] for _ in range(n)]
        for o in ops:
            o.deps = sorted(set(d for d in o.deps if d != o.id))
            for d in o.deps:
                succ[d].append(o.id)
        pend = {e: [] for e in self.engs}
        for o in ops:
            pend[o.eng].append(o.id)
        done = [False] * n
        fin = [0.0] * n
        efree = {e: 0.0 for e in self.engs}
        best = {e: None for e in self.engs}
        dirty = set(self.engs)
        HOP, SAME, W = 0.30, 0.25, 400
        SLACK = SLACK_US
        order = []
        INF = float("inf")
        while len(order) < n:
            for e in dirty:
                lst = pend[e]
                be, bi, bpos = INF, -1, -1
                ef = efree[e]
                lim = min(W, len(lst))
                for pos in range(lim):
                    oid = lst[pos]
                    rt = ef
                    ok = True
                    for d in ops[oid].deps:
                        if not done[d]:
                            ok = False
                            break
                        t = fin[d] + (SAME if ops[d].eng is e else HOP)
                        if t > rt:
                            rt = t
                    if ok and e is act and asets[oid] is not None and cur_set[0] is not None and asets[oid] != cur_set[0]:
                        rt += TL
                    if ok and rt < be - SLACK:
                        be, bi, bpos = rt, oid, pos
                        if rt <= ef:
                            break
                best[e] = (be, bi, bpos) if bi >= 0 else None
            dirty.clear()
            ce, cb = None, None
            for e in self.engs:
                b = best[e]
                if b is not None and (cb is None or b[0] < cb[0] or (b[0] == cb[0] and b[1] < cb[1])):
                    ce, cb = e, b
            assert ce is not None, "scheduler deadlock"
            st, oid, pos = cb
            o = ops[oid]
            o.start = st
            if o.is_dma:
                efree[ce] = st + 0.15
                fin[oid] = st + costs[oid]
            else:
                efree[ce] = st + costs[oid]
                fin[oid] = efree[ce]
                if ce is act and asets[oid] is not None:
                    cur_set[0] = asets[oid]
            done[oid] = True
            del pend[ce][pos]
            order.append(o)
            dirty.add(ce)
            for sidx in succ[oid]:
                dirty.add(ops[sidx].eng)
        self.est_makespan = max(fin) if fin else 0.0
        return order

    def flush(self):
        order = self.schedule()
        ops = self.ops
        for o in order:
            eng = o.eng
            for d in o.deps:
                p = ops[d]
                if eng.is_pe and p.eng is eng and not p.is_dma:
                    continue
                s, v = p.tok
                if eng.waited.get(s, 0) >= v:
                    continue
                eng.h.wait_ge(s.h, v)
                eng.waited[s] = v
                self.n_inst += 1
            if o.is_dma:
                pool = self.dma_pool.setdefault(eng, [])
                if len(pool) < 8:
                    dsem = self.new_sem(None)
                    pool.append(dsem)
                else:
                    i = self.dma_rr.get(eng, 0)
                    dsem = pool[i % len(pool)]
                    self.dma_rr[eng] = i + 1
                if dsem.val and eng.waited.get(dsem, 0) < dsem.val:
                    eng.h.wait_ge(dsem.h, dsem.val)
                    eng.waited[dsem] = dsem.val
                    self.n_inst += 1
                ins = o.fns[0]()
                dsem.val += 16
                ins.then_inc(dsem.h, 16)
                o.tok = (dsem, dsem.val)
            else:
                ins = None
                for fn in o.fns:
                    ins = fn()
                    self.n_inst += 1
                s = eng.sem()
                s.val += 1
                ins.then_inc(s.h, 1)
                o.tok = (s, s.val)
            self.n_inst += 1
        SP = self.SP
        for q, pool in self.dma_pool.items():
            for dsem in pool:
                if dsem.val and SP.waited.get(dsem, 0) < dsem.val:
                    self.nc.sync.wait_ge(dsem.h, dsem.val)
                    SP.waited[dsem] = dsem.val


class WStream:
    def __init__(self, kb, slots, plan):
        self.kb = kb
        self.slots = slots
        self.plan = plan
        self.issued = 0
        self.taken = 0
        self.free = [True] * len(slots)

    def _pump(self):
        while self.issued < len(self.plan):
            si = self.issued % len(self.slots)
            if not self.free[si]:
                break
            h, t = self.slots[si]
            for (dst, src, c0, n) in self.plan[self.issued]:
                self.kb.dma(self.kb.POOL, out=h[:, :, dst:dst + n], in_=src[:, :, c0:c0 + n], writes=[t])
            self.free[si] = False
            self.issued += 1

    def next(self):
        self._pump()
        assert self.taken < self.issued, "weight stream stalled (slot not released)"
        si = self.taken % len(self.slots)
        self.taken += 1
        h, t = self.slots[si]
        return si, h, t

    def release(self, si):
        self.free[si] = True
        self._pump()


def unit_plan(nseq, wv, wmd, wfd, wout):
    plan = []
    for _ in range(nseq):
        for h in range(4):
            plan.append([(0, wv, C_MQ + h * 128, 128), (128, wv, C_MK + h * 128, 128)])
            plan.append([(0, wv, C_MV + h * 256, 256)])
            plan.append([(0, wv, C_MO + h * 256, 256)])
            plan.append([(0, wv, C_MZ + h * 256, 256)])
        for oc in range(8):
            plan.append([(0, wmd, oc * 128, 128), (128, wv, C_GB + oc * 128, 128)])
        for h in range(8):
            plan.append([(0, wv, C_FQ + h * 128, 128), (128, wv, C_FK + h * 128, 128)])
            plan.append([(0, wv, C_FV + h * 128, 128), (128, wv, C_FZ + h * 128, 128)])
        for oc in range(8):
            plan.append([(0, wfd, oc * 128, 128), (128, wv, C_GA + oc * 128, 128)])
        for u in range(4):
            plan.append([(0, wout, u * 256, 256)])
    return plan


def build(nseq=2, dbg=None, upto=None):
    _, _, costs = _build_pass(nseq, dbg, upto, None)
    nc, dbg_out, _ = _build_pass(nseq, dbg, upto, costs)
    return nc, dbg_out


def _build_pass(nseq, dbg, upto, costs):
    dbg = dbg or set()
    nc = bass.Bass("TRN2", target_bir_lowering=False)
    x_d = nc.dram_tensor("x", [nseq, S, D], F32, kind="ExternalInput").ap()
    w_in = nc.dram_tensor("w_in", [D, IN_W], F32, kind="ExternalInput").ap()
    w_fd = nc.dram_tensor("w_fox_down", [D, D], F32, kind="ExternalInput").ap()
    w_md = nc.dram_tensor("w_ml_down", [D, D], F32, kind="ExternalInput").ap()
    w_o = nc.dram_tensor("w_out", [D, D], F32, kind="ExternalInput").ap()
    norm_g = nc.dram_tensor("norm_g", [D], F32, kind="ExternalInput").ap()
    final_g = nc.dram_tensor("final_g", [D], F32, kind="ExternalInput").ap()
    ml_g = nc.dram_tensor("ml_norm_g", [D], F32, kind="ExternalInput").ap()
    gbias_d = nc.dram_tensor("gbias", [128, NT, 16], F32, kind="ExternalInput").ap()
    convw_d = nc.dram_tensor("convw", [128, 8, 4], F32, kind="ExternalInput").ap()
    convb_d = nc.dram_tensor("convb", [128, 8], F32, kind="ExternalInput").ap()
    bgate_d = nc.dram_tensor("bgate", [128, 16], F32, kind="ExternalInput").ap()
    y_d = nc.dram_tensor("y", [nseq, S, D], F32, kind="ExternalOutput").ap()
    dbg_out = {}

    def dbg_tensor(name, shape, dt):
        dbg_out[name] = nc.dram_tensor("dbg_" + name, shape, dt, kind="ExternalOutput").ap()
        return dbg_out[name]

    wv = w_in.rearrange("(kc p) c -> p kc c", p=128)
    wfd = w_fd.rearrange("(kc p) c -> p kc c", p=128)
    wmd = w_md.rearrange("(kc p) c -> p kc c", p=128)
    wout = w_o.rearrange("(kc p) c -> p kc c", p=128)

    with ExitStack() as es:
        kb = KB(nc, es, costs)
        PE, ACT, DVE, POOL, SP = kb.PE, kb.ACT, kb.DVE, kb.POOL, kb.SP
        op, dma = kb.op, kb.dma

        xnT, t_xnT = kb.sb("xnT", [128, KC, S], BF16)
        bufA, t_bufA = kb.sb("bufA", [128, KC, S], BF16)
        slots = [kb.sb(f"wslot{i}", [128, KC, 256], BF16) for i in range(NSLOT)]
        wg, t_wg = kb.sb("wg", [128, KC, 16], BF16)
        gX, t_gX = kb.sb("gX", [128, D], F32)
        gN, t_gN = kb.sb("gN", [128, D], F32)
        gF = gM = gX
        t_gF = t_gM = t_gX
        gbias, t_gbias = kb.sb("gbias_sb", [128, NT, 16], F32)
        cw, t_cw = kb.sb("cw", [128, 8, 4], F32)
        cb, t_cb = kb.sb("cb", [128, 8], F32)
        bg, t_bg = kb.sb("bg", [128, 16], F32)
        cbh, t_cbh = kb.sb("cbh", [128, 8], F32)
        bgh, t_bgh = kb.sb("bgh", [128, 16], F32)
        epsT, t_eps = kb.sb("epsT", [128, 1], F32)
        ident, t_ident = kb.sb("ident", [128, 128], BF16)
        onesb, t_onesb = kb.sb("onesb", [128, 128], BF16)
        maskb, t_maskb = kb.sb("maskb", [128, 128], BF16)
        maskf, t_maskf = kb.sb("maskf", [128, 128], F32)
        onesf, t_onesf = kb.sb("onesf", [128, 128], F32)
        hmid, t_hmid = kb.sb("hmid", [128, 128], F32)
        xts = [kb.sb(f"xt{i}", [128, D], F32) for i in range(NXT)]
        xns = [kb.sb(f"xn{i}", [128, D], BF16) for i in range(NXN)]
        stats = [kb.sb(f"stat{i}", [128, 4], F32) for i in range(NXT)]
        zs, t_zs = kb.sb("zs", [128, NT, 16], F32)
        lf, t_lf = kb.sb("lf", [128, NT, 16], F32)
        tot, t_tot = kb.sb("tot", [128, NT, 16], F32)
        bc, t_bc = kb.sb("bc", [128, NT, 16], F32)
        half, t_half = kb.sb("half", [128, NT, 16], F32)
        pre, t_pre = kb.sb("pre", [128, NT, 16], F32)
        cc, t_cc = kb.sb("cc", [128, NT, 16], F32)
        RR, t_RR = kb.sb("RR", [128, NT, 16], F32)
        ew, t_ew = kb.sb("ew", [128, NT, 4], F32)
        eb, t_eb = kb.sb("eb", [128, NT, 4], F32)
        aa, t_aa = kb.sb("aa", [128, NT, 4], F32)
        gtmp, t_gtmp = kb.sb("gtmp", [128, NT, 16], F32)

        PB = []
        for i in range(6):
            h = es.enter_context(nc.psum_tensor(f"pb{i}", [128, 512], F32))
            PB.append((h, Tile(f"pb{i}", psum=True)))
        PT = []
        for i in range(2):
            h = es.enter_context(nc.psum_tensor(f"pt{i}", [128, 8, 128], BF16))
            PT.append((h, Tile(f"pt{i}", psum=True)))

        dma(SP, out=gbias[:], in_=gbias_d[:, :, :], writes=[t_gbias])
        dma(SP, out=cw[:], in_=convw_d[:, :, :], writes=[t_cw])
        dma(SP, out=cb[:], in_=convb_d[:, :], writes=[t_cb])
        dma(SP, out=bg[:], in_=bgate_d[:, :], writes=[t_bg])
        dma(POOL, out=wg[:, :, 0:8], in_=wv[:, :, C_FF:C_FF + 8], writes=[t_wg])
        dma(POOL, out=wg[:, :, 8:16], in_=wv[:, :, C_MI:C_MI + 8], writes=[t_wg])
        op(DVE, lambda: nc.vector.memset(epsT[:], EPS), writes=[t_eps])
        op(DVE, lambda: nc.vector.tensor_scalar(out=cbh[:], in0=cb[:], scalar1=0.5, scalar2=None, op0=ALU.mult),
           reads=[t_cb], writes=[t_cbh])
        op(DVE, lambda: nc.vector.tensor_scalar(out=bgh[:], in0=bg[:], scalar1=0.5, scalar2=None, op0=ALU.mult),
           reads=[t_bg], writes=[t_bgh])
        op(POOL, lambda: nc.gpsimd.memset(onesb[:], 1.0), writes=[t_onesb])
        op(POOL, lambda: nc.gpsimd.memset(onesf[:], 1.0), writes=[t_onesf])
        op(POOL, lambda: nc.gpsimd.memset(hmid[:], 0.0), writes=[t_hmid])
        op(POOL, lambda: nc.gpsimd.memset(hmid[0:64, :], 1.0), writes=[t_hmid])
        op(POOL, lambda: nc.gpsimd.affine_select(out=ident[:], in_=onesb[:], pattern=[[1, 128]],
                                                 compare_op=ALU.is_equal, fill=0.0, base=0,
                                                 channel_multiplier=-1), reads=[t_onesb], writes=[t_ident])
        op(POOL, lambda: nc.gpsimd.affine_select(out=maskb[:], in_=onesb[:], pattern=[[1, 128]],
                                                 compare_op=ALU.is_ge, fill=0.0, base=0,
                                                 channel_multiplier=-1), reads=[t_onesb], writes=[t_maskb])
        op(POOL, lambda: nc.gpsimd.affine_select(out=maskf[:], in_=onesf[:], pattern=[[1, 128]],
                                                 compare_op=ALU.is_ge, fill=0.0, base=0,
                                                 channel_multiplier=-1), reads=[t_onesf], writes=[t_maskf])

        ws = WStream(kb, slots, unit_plan(nseq, wv, wmd, wfd, wout))
        proj_rr = [0]
        proj_n = [2]

        proj_idx = [None]

        def proj_bank():
            proj_rr[0] += 1
            if proj_idx[0] is not None:
                return PB[proj_idx[0][proj_rr[0] % len(proj_idx[0])]]
            return PB[proj_rr[0] % proj_n[0]]

        def tap(name, src_ap, shape, dt, tiles):
            if name in dbg:
                d = dbg_tensor(name, shape, dt)
                dma(SP, out=d, in_=src_ap, reads=tiles)

        def rmsnorm_tail(xt_h, xt_t, si, g_h, g_t, out_ap, out_tiles, sq_out_ap, sq_out_tiles):
            st_h, st_t = stats[si % NXT]
            op(ACT, lambda: nc.scalar.activation(out=sq_out_ap, in_=xt_h[:], func=AF.Square,
                                                 accum_out=st_h[:, 0:1]),
               reads=[xt_t], writes=sq_out_tiles + [st_t])
            op(ACT, lambda: nc.scalar.activation(out=st_h[:, 1:2], in_=st_h[:, 0:1], func=AF.Sqrt,
                                                 scale=1.0 / D, bias=epsT[:]),
               reads=[st_t, t_eps], writes=[st_t])
            op(DVE, lambda: nc.vector.reciprocal(out=st_h[:, 2:3], in_=st_h[:, 1:2]), reads=[st_t], writes=[st_t])
            op(DVE, lambda: nc.vector.scalar_tensor_tensor(out=out_ap, in0=xt_h[:], scalar=st_h[:, 2:3],
                                                           in1=g_h[:], op0=ALU.mult, op1=ALU.mult),
               reads=[xt_t, st_t, g_t], writes=out_tiles)

        pending_O = None
        for seq in range(nseq):
            first = (seq == 0)
            proj_n[0] = 2
            if first:
                dma(SP, out=gN[:], in_=norm_g.partition_broadcast(128), writes=[t_gN])
            for tb in range(NT):
                xt_h, xt_t = xts[tb % NXT]
                xn_h, xn_t = xns[tb % NXN]
                dma(SP, out=xt_h[:], in_=x_d[seq, tb * 128:(tb + 1) * 128, :], writes=[xt_t])
                rmsnorm_tail(xt_h, xt_t, tb, gN, t_gN, xn_h[:], [xn_t], xn_h[:], [xn_t])
                pt_h, pt_t = PT[tb % 2]
                for kc in range(KC):
                    op(PE, lambda kc=kc: nc.tensor.transpose(out=pt_h[:, kc, :], in_=xn_h[:, kc * 128:(kc + 1) * 128],
                                                             identity=ident[:]),
                       reads=[xn_t, t_ident], writes=[pt_t], inc=(kc == KC - 1))
                op(ACT, lambda: nc.scalar.copy(out=xnT[:, :, tb * 128:(tb + 1) * 128], in_=pt_h[:]),
                   reads=[pt_t], writes=[t_xnT])
            if first:
                tap("xnT", xnT[:], [128, KC, S], BF16, [t_xnT])

            if upto == "P0":
                break

            pz_h, pz_t = PB[2]
            for tb in range(NT):
                for kc in range(KC):
                    op(PE, lambda kc=kc, tb=tb: nc.tensor.matmul(pz_h[:, tb * 16:(tb + 1) * 16],
                                                                 lhsT=xnT[:, kc, tb * 128:(tb + 1) * 128],
                                                                 rhs=wg[:, kc, :], start=(kc == 0), stop=(kc == KC - 1)),
                       reads=[t_xnT, t_wg], writes=[pz_t], inc=(kc == KC - 1 and tb == NT - 1))
            op(DVE, lambda: nc.vector.tensor_tensor(out=zs[:].rearrange("p a b -> p (a b)"), in0=pz_h[:, 0:256],
                                                    in1=gbias[:].rearrange("p a b -> p (a b)"), op=ALU.add),
               reads=[pz_t, t_gbias], writes=[t_zs])
            if upto == "G1":
                tap("lf", zs[:], [128, NT, 16], F32, [t_zs])
                break
            op(ACT, lambda: nc.scalar.activation(out=gtmp[:], in_=zs[:], func=AF.Exp, scale=-1.0),
               reads=[t_zs], writes=[t_gtmp])
            op(ACT, lambda: nc.scalar.activation(out=gtmp[:], in_=gtmp[:], func=AF.Ln, bias=1.0, scale=1.0),
               reads=[t_gtmp], writes=[t_gtmp])
            op(DVE, lambda: nc.vector.tensor_scalar(out=lf[:], in0=gtmp[:], scalar1=-1.0, scalar2=None, op0=ALU.mult),
               reads=[t_gtmp], writes=[t_lf])
            if upto == "G2":
                tap("lf", lf[:], [128, NT, 16], F32, [t_lf])
                break
            lf2 = lf[:].rearrange("p a b -> p (a b)")
            for (lhs_h, lhs_t, dst_h, dst_t, pbi) in ((onesf, t_onesf, tot, t_tot, 3), (maskf, t_maskf, bc, t_bc, 4),
                                                       (hmid, t_hmid, half, t_half, 5)):
                pp_h, pp_t = PB[pbi]
                op(PE, lambda lhs_h=lhs_h, pp_h=pp_h: nc.tensor.matmul(pp_h[:, 0:256], lhsT=lhs_h[:], rhs=lf2,
                                                                       start=True, stop=True),
                   reads=[lhs_t, t_lf], writes=[pp_t])
                op(DVE, lambda dst_h=dst_h, pp_h=pp_h: nc.vector.tensor_copy(out=dst_h[:].rearrange("p a b -> p (a b)"),
                                                                             in_=pp_h[:, 0:256]),
                   reads=[pp_t], writes=[dst_t])
            if upto == "G3":
                tap("lf", lf[:], [128, NT, 16], F32, [t_lf])
                tap("cc", bc[:], [128, NT, 16], F32, [t_bc])
                break
            op(DVE, lambda: nc.vector.memset(pre[:, 0, :], 0.0), writes=[t_pre])
            for tb in range(1, NT):
                op(DVE, lambda tb=tb: nc.vector.tensor_tensor(out=pre[:, tb, :], in0=pre[:, tb - 1, :],
                                                              in1=tot[:, tb - 1, :], op=ALU.add),
                   reads=[t_pre, t_tot], writes=[t_pre])
            op(DVE, lambda: nc.vector.tensor_tensor(out=cc[:], in0=bc[:], in1=pre[:], op=ALU.add),
               reads=[t_bc, t_pre], writes=[t_cc])
            op(DVE, lambda: nc.vector.tensor_tensor(out=RR[:], in0=half[:], in1=pre[:], op=ALU.add),
               reads=[t_half, t_pre], writes=[t_RR])
            if upto == "G4":
                tap("cc", cc[:], [128, NT, 16], F32, [t_cc])
                break
            op(DVE, lambda: nc.vector.tensor_tensor(out=ew[:], in0=zs[:, :, 8:12], in1=bc[:, :, 12:16], op=ALU.subtract),
               reads=[t_zs, t_bc], writes=[t_ew])
            if upto == "G5":
                tap("cc", cc[:], [128, NT, 16], F32, [t_cc, t_ew])
                break
            op(ACT, lambda: nc.scalar.activation(out=ew[:], in_=ew[:], func=AF.Exp), reads=[t_ew], writes=[t_ew])
            if upto == "G6":
                tap("cc", cc[:], [128, NT, 16], F32, [t_cc, t_ew])
                break
            op(ACT, lambda: nc.scalar.activation(out=eb[:], in_=bc[:, :, 12:16], func=AF.Exp), reads=[t_bc], writes=[t_eb])
            op(DVE, lambda: nc.vector.tensor_scalar(out=eb[:], in0=eb[:], scalar1=float(128 ** -0.5), scalar2=None,
                                                    op0=ALU.mult), reads=[t_eb], writes=[t_eb])
            if upto == "G7":
                tap("cc", cc[:], [128, NT, 16], F32, [t_cc, t_ew, t_eb])
                break
            op(ACT, lambda: nc.scalar.activation(out=aa[:], in_=tot[:, :, 12:16], func=AF.Exp), reads=[t_tot], writes=[t_aa])
            if first:
                tap("lf", lf[:], [128, NT, 16], F32, [t_lf])
                tap("cc", cc[:], [128, NT, 16], F32, [t_cc])
                tap("RR", RR[:], [128, NT, 16], F32, [t_RR])
                tap("ew", ew[:], [128, NT, 4], F32, [t_ew])
                tap("eb", eb[:], [128, NT, 4], F32, [t_eb])
                tap("aa", aa[:], [128, NT, 4], F32, [t_aa])

            if upto == "G":
                break

            if pending_O is not None:
                pending_O()
                pending_O = None
            proj_idx[0] = M_PROJ_BANKS
            dma(SP, out=gX[:], in_=ml_g.partition_broadcast(128), writes=[t_gX])
            with ExitStack() as ms:
                mq2 = [kb.scratch(ms, f"m_qT{i}", [128, S], BF16) for i in range(MDB)]
                mk2 = [kb.scratch(ms, f"m_kT{i}", [128, S], BF16) for i in range(MDB)]
                mc2 = [kb.scratch(ms, f"m_cbuf{i}", [128, S + 4], F32) for i in range(NCB)]
                accs = [kb.scratch(ms, f"m_acc{i}", [128, 1024], F32) for i in range(2)]
                ths = [kb.scratch(ms, f"m_th{i}", [128, 1024], F32) for i in range(NTH)]
                kTok, t_kTok = kb.scratch(ms, "m_kTok", [128, NT, 128], BF16)
                vaug, t_vaug = kb.scratch(ms, "m_vaug", [128, NT, 258], BF16)
                Crun = [kb.scratch(ms, f"m_C{i}", [128, 258], F32) for i in range(2)]
                Cbf, t_Cbf = kb.scratch(ms, "m_Cbf", [128, NT, 258], BF16)
                numS2 = [kb.scratch(ms, f"m_numS{i}", [128, 4, 258], F32) for i in range(NQB)]
                Gp2 = [kb.scratch(ms, f"m_Gp{i}", [128, 4, 256], BF16) for i in range(NQB)]
                STm = [kb.scratch(ms, f"m_STm{i}", [128, 128], BF16) for i in range(2)]
                hn = [kb.scratch(ms, f"m_hn{i}", [128, 256], BF16) for i in range(2)]
                s12 = [kb.scratch(ms, f"m_s12{i}", [128, 512], F32) for i in range(2)]
                sm2 = [kb.scratch(ms, f"m_sm{i}", [128, 12, 4], F32) for i in range(NQB)]
                junk, t_junk = kb.scratch(ms, "m_junk", [128, 256], BF16)
                for (cb_h, cb_t) in mc2:
                    op(DVE, lambda cb_h=cb_h: nc.vector.memset(cb_h[:, 0:4], 0.0), writes=[cb_t])
                for h in range(4):
                    qT, t_qT = mq2[h % MDB]
                    kT, t_kT = mk2[h % MDB]
                    s1i, u1, t_u1 = ws.next()
                    s2i, u2, t_u2 = ws.next()
                    s3i, u3, t_u3 = ws.next()
                    s4i, u4, t_u4 = ws.next()
                    for which, (dst, t_dst) in enumerate(((qT, t_qT), (kT, t_kT))):
                        chunk = which * 4 + h
                        cbuf, t_cbuf = mc2[which % len(mc2)]
                        for tc in range(4):
                            pp_h, pp_t = proj_bank()
                            for kc in range(KC):
                                op(PE, lambda kc=kc, tc=tc, which=which, pp_h=pp_h: nc.tensor.matmul(
                                    pp_h[:, 0:512], lhsT=u1[:, kc, which * 128:(which + 1) * 128],
                                    rhs=xnT[:, kc, tc * 512:(tc + 1) * 512], start=(kc == 0), stop=(kc == KC - 1)),
                                   reads=[t_u1, t_xnT], writes=[pp_t], inc=(kc == KC - 1))
                            op(ACT, lambda tc=tc, pp_h=pp_h: nc.scalar.copy(out=cbuf[:, 4 + tc * 512:4 + (tc + 1) * 512],
                                                                           in_=pp_h[:, 0:512]),
                               reads=[pp_t], writes=[t_cbuf])
                        for hf in range(2):
                            o = hf * 1024
                            ac_h, ac_t = accs[hf]
                            op(DVE, lambda o=o, ac_h=ac_h, chunk=chunk: nc.vector.tensor_scalar(
                                out=ac_h[:], in0=cbuf[:, 4 + o:4 + o + 1024], scalar1=cw[:, chunk, 3:4], scalar2=None,
                                op0=ALU.mult), reads=[t_cbuf, t_cw], writes=[ac_t])
                            for j in (2, 1, 0):
                                op(DVE, lambda o=o, ac_h=ac_h, chunk=chunk, j=j: nc.vector.scalar_tensor_tensor(
                                    out=ac_h[:], in0=cbuf[:, 1 + j + o:1 + j + o + 1024], scalar=cw[:, chunk, j:j + 1],
                                    in1=ac_h[:], op0=ALU.mult, op1=ALU.add), reads=[t_cbuf, t_cw, ac_t], writes=[ac_t])
                            th_h, th_t = ths[hf % NTH]
                            op(ACT, lambda ac_h=ac_h, chunk=chunk, th_h=th_h: nc.scalar.activation(
                                out=th_h[:], in_=ac_h[:], func=AF.Tanh, bias=cbh[:, chunk:chunk + 1], scale=0.5),
                               reads=[ac_t, t_cbh], writes=[th_t])
                            op(DVE, lambda th_h=th_h: nc.vector.tensor_scalar(
                                out=th_h[:], in0=th_h[:], scalar1=0.5, scalar2=0.5, op0=ALU.mult, op1=ALU.add),
                               reads=[th_t], writes=[th_t])
                            op(DVE, lambda o=o, ac_h=ac_h, chunk=chunk, dst=dst, th_h=th_h: nc.vector.scalar_tensor_tensor(
                                out=dst[:, o:o + 1024], in0=ac_h[:], scalar=cb[:, chunk:chunk + 1], in1=th_h[:],
                                op0=ALU.add, op1=ALU.mult), reads=[ac_t, t_cb, th_t], writes=[t_dst])
                    ws.release(s1i)
                    for g in range(2):
                        pt_h, pt_t = PT[g]
                        for c8 in range(8):
                            c = g * 8 + c8
                            op(PE, lambda c=c, c8=c8, pt_h=pt_h: nc.tensor.transpose(
                                out=pt_h[:, c8, :], in_=kT[:, c * 128:(c + 1) * 128], identity=ident[:]),
                               reads=[t_kT, t_ident], writes=[pt_t], inc=(c8 == 7))
                        for c8 in range(8):
                            c = g * 8 + c8
                            op(ACT, lambda c=c, c8=c8, pt_h=pt_h: nc.scalar.activation(
                                out=kTok[:, c, :], in_=pt_h[:, c8, :], func=AF.Copy, scale=aa[:, c, h:h + 1]),
                               reads=[pt_t, t_aa], writes=[t_kTok])
                    for tb in range(NT):
                        pp_h, pp_t = proj_bank()
                        for kc in range(KC):
                            op(PE, lambda kc=kc, tb=tb, pp_h=pp_h: nc.tensor.matmul(
                                pp_h[:, 0:256], lhsT=xnT[:, kc, tb * 128:(tb + 1) * 128], rhs=u2[:, kc, :],
                                start=(kc == 0), stop=(kc == KC - 1)),
                               reads=[t_u2, t_xnT], writes=[pp_t], inc=(kc == KC - 1))
                        op(ACT, lambda tb=tb, pp_h=pp_h: nc.scalar.activation(
                            out=vaug[:, tb, 0:256], in_=pp_h[:, 0:256], func=AF.Copy, scale=ew[:, tb, h:h + 1]),
                           reads=[pp_t, t_ew], writes=[t_vaug])
                    ws.release(s2i)
                    op(DVE, lambda: nc.vector.tensor_copy(out=vaug[:, :, 256:257], in_=ew[:, :, h:h + 1]),
                       reads=[t_ew], writes=[t_vaug])
                    dcb = (PB[5], PB[2])
                    for c in range(NT - 1):
                        pd_h, pd_t = dcb[c % 2]
                        op(PE, lambda c=c, pd_h=pd_h: nc.tensor.matmul(pd_h[:, 0:257], lhsT=kTok[:, c, :],
                                                                       rhs=vaug[:, c, 0:257], start=True, stop=True),
                           reads=[t_kTok, t_vaug], writes=[pd_t])
                        cn_h, cn_t = Crun[(c + 1) % 2]
                        cp_h, cp_t = Crun[c % 2]
                        if c == 0:
                            op(DVE, lambda pd_h=pd_h, cn_h=cn_h: nc.vector.tensor_copy(out=cn_h[:, 0:257], in_=pd_h[:, 0:257]),
                               reads=[pd_t], writes=[cn_t])
                        else:
                            op(DVE, lambda c=c, pd_h=pd_h, cn_h=cn_h, cp_h=cp_h: nc.vector.scalar_tensor_tensor(
                                out=cn_h[:, 0:257], in0=cp_h[:, 0:257], scalar=aa[:, c, h:h + 1], in1=pd_h[:, 0:257],
                                op0=ALU.mult, op1=ALU.add), reads=[cp_t, pd_t, t_aa], writes=[cn_t])
                        op(ACT, lambda c=c, cn_h=cn_h: nc.scalar.copy(out=Cbf[:, c + 1, 0:257], in_=cn_h[:, 0:257]),
                           reads=[cn_t], writes=[t_Cbf])
                    for c in range(NT):
                        numS, t_numS = numS2[(c // 4) % NQB]
                        Gp, t_Gp = Gp2[(c // 4) % NQB]
                        sm, t_sm = sm2[(c // 4) % NQB]
                        ps_h, ps_t = PB[2]
                        op(PE, lambda c=c: nc.tensor.matmul(ps_h[:, 0:128], lhsT=kT[:, c * 128:(c + 1) * 128],
                                                            rhs=qT[:, c * 128:(c + 1) * 128], start=True, stop=True),
                           reads=[t_kT, t_qT], writes=[ps_t])
                        st_h, st_t = STm[c % 2]
                        op(DVE, lambda st_h=st_h: nc.vector.tensor_tensor(out=st_h[:], in0=ps_h[:, 0:128], in1=maskf[:],
                                                                          op=ALU.mult),
                           reads=[ps_t, t_maskf], writes=[st_t])
                        pn_h, pn_t = PB[3 + c % 2]
                        op(PE, lambda c=c, pn_h=pn_h, st_h=st_h: nc.tensor.matmul(
                            pn_h[:, 0:257], lhsT=st_h[:], rhs=vaug[:, c, 0:257], start=True, stop=(c == 0)),
                           reads=[st_t, t_vaug], writes=[pn_t], inc=(c == 0))
                        if c > 0:
                            op(PE, lambda c=c, pn_h=pn_h: nc.tensor.matmul(
                                pn_h[:, 0:257], lhsT=qT[:, c * 128:(c + 1) * 128], rhs=Cbf[:, c, 0:257],
                                start=False, stop=True), reads=[t_qT, t_Cbf], writes=[pn_t])
                        op(ACT, lambda c=c, pn_h=pn_h: nc.scalar.copy(out=numS[:, c % 4, 0:257], in_=pn_h[:, 0:257]),
                           reads=[pn_t], writes=[t_numS])
                        pp_h, pp_t = proj_bank()
                        for gi, (ug, t_ug) in enumerate(((u3, t_u3), (u4, t_u4))):
                            for kc in range(KC):
                                op(PE, lambda kc=kc, c=c, gi=gi, ug=ug, pp_h=pp_h: nc.tensor.matmul(
                                    pp_h[:, gi * 256:(gi + 1) * 256], lhsT=xnT[:, kc, c * 128:(c + 1) * 128],
                                    rhs=ug[:, kc, :], start=(kc == 0), stop=(kc == KC - 1)),
                                   reads=[t_ug, t_xnT], writes=[pp_t], inc=(kc == KC - 1 and gi == 1))
                        sg_h, sg_t = s12[c % 2]
                        op(ACT, lambda pp_h=pp_h, sg_h=sg_h: nc.scalar.activation(out=sg_h[:, 0:512], in_=pp_h[:, 0:512],
                                                                                  func=AF.Tanh, scale=0.5),
                           reads=[pp_t], writes=[sg_t])
                        op(DVE, lambda pp_h=pp_h, sg_h=sg_h: nc.vector.scalar_tensor_tensor(
                            out=sg_h[:, 256:512], in0=sg_h[:, 256:512], scalar=1.0, in1=pp_h[:, 256:512],
                            op0=ALU.add, op1=ALU.mult), reads=[sg_t, pp_t], writes=[sg_t])
                        op(DVE, lambda sg_h=sg_h: nc.vector.scalar_tensor_tensor(
                            out=sg_h[:, 0:256], in0=sg_h[:, 0:256], scalar=1.0, in1=sg_h[:, 256:512],
                            op0=ALU.add, op1=ALU.mult), reads=[sg_t], writes=[sg_t])
                        op(DVE, lambda c=c, sg_h=sg_h: nc.vector.tensor_tensor(out=Gp[:, c % 4, :], in0=sg_h[:, 0:256],
                                                                               in1=gM[:, h * 256:(h + 1) * 256], op=ALU.mult),
                           reads=[sg_t, t_gM], writes=[t_Gp])
                        if c % 4 == 3:
                            cs = c - 3
                            den = numS[:, :, 256]
                            ebs = eb[:, cs:cs + 4, h]
                            op(DVE, lambda: nc.vector.tensor_tensor(out=sm[:, 0, :], in0=den, in1=ebs, op=ALU.mult),
                               reads=[t_numS, t_eb], writes=[t_sm])
                            op(DVE, lambda: nc.vector.scalar_tensor_tensor(out=sm[:, 1, :], in0=sm[:, 0, :], scalar=-1.0,
                                                                           in1=sm[:, 0, :], op0=ALU.mult, op1=ALU.max),
                               reads=[t_sm], writes=[t_sm])
                            op(DVE, lambda: nc.vector.tensor_scalar(out=sm[:, 2, :], in0=sm[:, 1, :], scalar1=1.0,
                                                                    scalar2=None, op0=ALU.max), reads=[t_sm], writes=[t_sm])
                            op(DVE, lambda: nc.vector.reciprocal(out=sm[:, 3, :], in_=sm[:, 2, :]), reads=[t_sm], writes=[t_sm])
                            op(DVE, lambda: nc.vector.tensor_tensor(out=sm[:, 4, :], in0=sm[:, 3, :], in1=ebs, op=ALU.mult),
                               reads=[t_sm, t_eb], writes=[t_sm])
                            for c8 in range(4):
                                op(ACT, lambda c8=c8: nc.scalar.activation(out=junk[:], in_=numS[:, c8, 0:256],
                                                                           func=AF.Square, accum_out=sm[:, 5, c8:c8 + 1]),
                                   reads=[t_numS], writes=[t_junk, t_sm])
                            op(DVE, lambda: nc.vector.tensor_tensor(out=sm[:, 6, :], in0=sm[:, 4, :], in1=sm[:, 4, :],
                                                                    op=ALU.mult), reads=[t_sm], writes=[t_sm])
                            op(DVE, lambda: nc.vector.tensor_tensor(out=sm[:, 7, :], in0=sm[:, 6, :], in1=sm[:, 5, :],
                                                                    op=ALU.mult), reads=[t_sm], writes=[t_sm])
                            op(ACT, lambda: nc.scalar.activation(out=sm[:, 8, :], in_=sm[:, 7, :], func=AF.Sqrt,
                                                                 scale=1.0 / 256, bias=epsT[:]),
                               reads=[t_sm, t_eps], writes=[t_sm])
                            op(DVE, lambda: nc.vector.reciprocal(out=sm[:, 9, :], in_=sm[:, 8, :]), reads=[t_sm], writes=[t_sm])
                            op(DVE, lambda: nc.vector.scalar_tensor_tensor(out=sm[:, 10, :], in0=sm[:, 9, :], scalar=0.25,
                                                                           in1=sm[:, 4, :], op0=ALU.mult, op1=ALU.mult),
                               reads=[t_sm], writes=[t_sm])
                            for c8 in range(4):
                                hn_h, hn_t = hn[c8 % 2]
                                op(DVE, lambda c8=c8, hn_h=hn_h: nc.vector.scalar_tensor_tensor(
                                    out=hn_h[:], in0=numS[:, c8, 0:256], scalar=sm[:, 10, c8:c8 + 1], in1=Gp[:, c8, :],
                                    op0=ALU.mult, op1=ALU.mult), reads=[t_numS, t_sm, t_Gp], writes=[hn_t])
                                for j in range(2):
                                    op(PE, lambda c8=c8, j=j, hn_h=hn_h: nc.tensor.transpose(
                                        out=PT[j][0][:, c8, :], in_=hn_h[:, j * 128:(j + 1) * 128], identity=ident[:]),
                                       reads=[hn_t, t_ident], writes=[PT[j][1]])
                            for j in range(2):
                                op(ACT, lambda j=j, cs=cs: nc.scalar.copy(
                                    out=bufA[:, 2 * h + j, cs * 128:(cs + 4) * 128],
                                    in_=PT[j][0][:, 0:4, :].rearrange("p a b -> p (a b)")),
                                   reads=[PT[j][1]], writes=[t_bufA])
                    ws.release(s3i)
                    ws.release(s4i)
                kb.end_scratch_phase()
            if first:
                tap("hbT", bufA[:], [128, KC, S], BF16, [t_bufA])

            if upto == "M":
                break

            ys = es.enter_context(ExitStack())
            bufY, t_bufY = kb.scratch(ys, "bufY", [128, KC, S], BF16)
            proj_n[0] = 6
            proj_idx[0] = None
            with ExitStack() as ms:
                sgs = [kb.scratch(ms, f"d_sg{i}", [128, 512], F32) for i in range(2)]
                for oc in range(8):
                    si, u, t_u = ws.next()
                    for tc in range(4):
                        py_h, py_t = proj_bank()
                        for kc in range(KC):
                            op(PE, lambda kc=kc, tc=tc, py_h=py_h: nc.tensor.matmul(
                                py_h[:, 0:512], lhsT=u[:, kc, 0:128], rhs=bufA[:, kc, tc * 512:(tc + 1) * 512],
                                start=(kc == 0), stop=(kc == KC - 1)), reads=[t_u, t_bufA], writes=[py_t], inc=(kc == KC - 1))
                        pg_h, pg_t = proj_bank()
                        for kc in range(KC):
                            op(PE, lambda kc=kc, tc=tc, pg_h=pg_h: nc.tensor.matmul(
                                pg_h[:, 0:512], lhsT=u[:, kc, 128:256], rhs=xnT[:, kc, tc * 512:(tc + 1) * 512],
                                start=(kc == 0), stop=(kc == KC - 1)), reads=[t_u, t_xnT], writes=[pg_t], inc=(kc == KC - 1))
                        sg_h, sg_t = sgs[tc % 2]
                        op(ACT, lambda pg_h=pg_h, sg_h=sg_h, oc=oc: nc.scalar.activation(
                            out=sg_h[:], in_=pg_h[:, 0:512], func=AF.Tanh, bias=bgh[:, 8 + oc:9 + oc], scale=0.5),
                           reads=[pg_t, t_bgh], writes=[sg_t])
                        op(DVE, lambda py_h=py_h, sg_h=sg_h, oc=oc, tc=tc: nc.vector.scalar_tensor_tensor(
                            out=bufY[:, oc, tc * 512:(tc + 1) * 512], in0=sg_h[:], scalar=1.0, in1=py_h[:, 0:512],
                            op0=ALU.add, op1=ALU.mult), reads=[py_t, sg_t], writes=[t_bufY])
                    ws.release(si)
                kb.end_scratch_phase()
            if first:
                tap("ybg", bufY[:], [128, KC, S], BF16, [t_bufY])

            if upto == "MD":
                break

            proj_n[0] = 2
            with ExitStack() as ms:
                fq = [kb.scratch(ms, f"f_q{i}", [128, S], BF16) for i in range(2)]
                fk = [kb.scratch(ms, f"f_k{i}", [128, S], BF16) for i in range(2)]
                fv = [kb.scratch(ms, f"f_v{i}", [128, NT, 130], BF16) for i in range(2)]
                fzs = [kb.scratch(ms, f"f_z{i}", [128, NT, 128], BF16) for i in range(2)]
                bms = [kb.scratch(ms, f"f_bm{i}", [128, NT, NT], F32) for i in range(2)]
                ptb = [kb.scratch(ms, f"f_pt{i}", [128, 512], BF16) for i in range(NPTB)]
                obs = [kb.scratch(ms, f"f_ob{i}", [128, 4, 128], BF16) for i in range(4)]
                rin, t_rin = kb.scratch(ms, "f_rin", [128, 16], F32)
                tzs = [kb.scratch(ms, f"f_tz{i}", [128, 128], F32) for i in range(4)]
                for i in range(2):
                    op(DVE, lambda i=i: nc.vector.memset(fv[i][0][:, :, 128:129], 2.0), writes=[fv[i][1]])
                sc = float(128 ** -0.5)
                sb_i = 0
                for h in range(8):
                    s1i, u1, t_u1 = ws.next()
                    s2i, u2, t_u2 = ws.next()
                    qT, t_qT = fq[h % 2]
                    kT, t_kT = fk[h % 2]
                    V, t_V = fv[h % 2]
                    FZ, t_FZ = fzs[h % 2]
                    bm, t_bm = bms[h % 2]
                    for which, (dst, t_dst) in enumerate(((qT, t_qT), (kT, t_kT))):
                        for tc in range(4):
                            pp_h, pp_t = proj_bank()
                            for kc in range(KC):
                                op(PE, lambda kc=kc, tc=tc, which=which, pp_h=pp_h: nc.tensor.matmul(
                                    pp_h[:, 0:512], lhsT=u1[:, kc, which * 128:(which + 1) * 128],
                                    rhs=xnT[:, kc, tc * 512:(tc + 1) * 512], start=(kc == 0), stop=(kc == KC - 1)),
                                   reads=[t_u1, t_xnT], writes=[pp_t], inc=(kc == KC - 1))
                            op(DVE, lambda tc=tc, pp_h=pp_h, dst=dst: nc.vector.tensor_copy(
                                out=dst[:, tc * 512:(tc + 1) * 512], in_=pp_h[:, 0:512]), reads=[pp_t], writes=[t_dst])
                    ws.release(s1i)
                    for tb in range(NT):
                        pp_h, pp_t = proj_bank()
                        for kc in range(KC):
                            op(PE, lambda kc=kc, tb=tb, pp_h=pp_h: nc.tensor.matmul(
                                pp_h[:, 0:256], lhsT=xnT[:, kc, tb * 128:(tb + 1) * 128], rhs=u2[:, kc, :],
                                start=(kc == 0), stop=(kc == KC - 1)), reads=[t_u2, t_xnT], writes=[pp_t], inc=(kc == KC - 1))
                        op(DVE, lambda tb=tb, pp_h=pp_h: nc.vector.tensor_copy(out=V[:, tb, 0:128], in_=pp_h[:, 0:128]),
                           reads=[pp_t], writes=[t_V])
                        tz_h, tz_t = tzs[tb % 4]
                        op(ACT, lambda pp_h=pp_h, tz_h=tz_h: nc.scalar.activation(out=tz_h[:], in_=pp_h[:, 128:256],
                                                                                  func=AF.Tanh, scale=0.5),
                           reads=[pp_t], writes=[tz_t])
                        op(DVE, lambda tb=tb, pp_h=pp_h, tz_h=tz_h: nc.vector.scalar_tensor_tensor(
                            out=FZ[:, tb, :], in0=tz_h[:], scalar=1.0, in1=pp_h[:, 128:256], op0=ALU.add, op1=ALU.mult),
                           reads=[pp_t, tz_t], writes=[t_FZ])
                    ws.release(s2i)
                    for j in range(NT):
                        op(DVE, lambda j=j: nc.vector.tensor_scalar(out=bm[:, j, :], in0=RR[:, :, h],
                                                                    scalar1=cc[:, j, h:h + 1], scalar2=None,
                                                                    op0=ALU.subtract), reads=[t_RR, t_cc], writes=[t_bm])
                    steps = [(I, j) for I in range(4) for j in range(4 * I + 4)]

                    def emit_ST(n):
                        I, j = steps[n]
                        i0 = max(j, 4 * I)
                        ps_h, ps_t = PB[2 + n % 2]
                        N = (4 * I + 4 - i0) * 128
                        op(PE, lambda: nc.tensor.matmul(ps_h[:, 0:N], lhsT=kT[:, j * 128:(j + 1) * 128],
                                                        rhs=qT[:, i0 * 128:(4 * I + 4) * 128], start=True, stop=True),
                           reads=[t_kT, t_qT], writes=[ps_t])

                    def emit_exp(n):
                        I, j = steps[n]
                        i0 = max(j, 4 * I)
                        ps_h, ps_t = PB[2 + n % 2]
                        p_h, p_t = ptb[n % NPTB]
                        for i in range(i0, 4 * I + 4):
                            lo = (i - i0) * 128
                            op(ACT, lambda i=i, lo=lo: nc.scalar.activation(
                                out=p_h[:, lo:lo + 128], in_=ps_h[:, lo:lo + 128], func=AF.Exp, scale=sc,
                                bias=bm[:, j, i:i + 1]), reads=[ps_t, t_bm], writes=[p_t])
                        if j >= 4 * I:
                            op(DVE, lambda: nc.vector.tensor_tensor(out=p_h[:, 0:128], in0=p_h[:, 0:128], in1=maskb[:],
                                                                    op=ALU.mult), reads=[p_t, t_maskb], writes=[p_t])

                    def emit_PV(n):
                        I, j = steps[n]
                        i0 = max(j, 4 * I)
                        p_h, p_t = ptb[n % NPTB]
                        for i in range(i0, 4 * I + 4):
                            il = i - 4 * I
                            lo = (i - i0) * 128
                            po_h, po_t = PB[4 + il // 2]
                            off = (il % 2) * 129
                            op(PE, lambda i=i, lo=lo, po_h=po_h, off=off, il=il: nc.tensor.matmul(
                                po_h[:, off:off + 129], lhsT=p_h[:, lo:lo + 128], rhs=V[:, j, 0:129],
                                start=(j == 0 and il % 2 == 0), stop=(j == i), skip_group_check=True),
                               reads=[p_t, t_V], writes=[po_t])
                            if j == i:
                                ob_h, ob_t = obs[(4 * h + I) % 4]
                                op(DVE, lambda po_h=po_h, off=off, i=i: nc.vector.reciprocal(
                                    out=rin[:, i:i + 1], in_=po_h[:, off + 128:off + 129]), reads=[po_t], writes=[t_rin])
                                op(DVE, lambda po_h=po_h, off=off, i=i, il=il, ob_h=ob_h: nc.vector.scalar_tensor_tensor(
                                    out=ob_h[:, il, :], in0=po_h[:, off:off + 128], scalar=rin[:, i:i + 1], in1=FZ[:, i, :],
                                    op0=ALU.mult, op1=ALU.mult), reads=[po_t, t_rin, t_FZ], writes=[ob_t])
                                pt_h, pt_t = PT[I % 2]
                                pb0 = 0
                                op(PE, lambda il=il, ob_h=ob_h, pt_h=pt_h, pb0=pb0: nc.tensor.transpose(
                                    out=pt_h[:, pb0 + il, :], in_=ob_h[:, il, :], identity=ident[:]),
                                   reads=[ob_t, t_ident], writes=[pt_t])
                                if il == 3:
                                    op(DVE, lambda pt_h=pt_h, I=I: nc.vector.tensor_copy(
                                        out=bufA[:, h, I * 512:(I + 1) * 512],
                                        in_=pt_h[:, pb0:pb0 + 4, :].rearrange("p a b -> p (a b)")),
                                       reads=[pt_t], writes=[t_bufA])

                    emit_ST(0)
                    for n in range(len(steps)):
                        emit_exp(n)
                        if n + 1 < len(steps):
                            emit_ST(n + 1)
                        emit_PV(n)
                kb.end_scratch_phase()
            if first:
                tap("oaT", bufA[:], [128, KC, S], BF16, [t_bufA])

            if upto == "F":
                break

            proj_n[0] = 6
            with ExitStack() as ms:
                sgs = [kb.scratch(ms, f"e_sg{i}", [128, 512], F32) for i in range(2)]
                for oc in range(8):
                    si, u, t_u = ws.next()
                    for tc in range(4):
                        py_h, py_t = proj_bank()
                        for kc in range(KC):
                            op(PE, lambda kc=kc, tc=tc, py_h=py_h: nc.tensor.matmul(
                                py_h[:, 0:512], lhsT=u[:, kc, 0:128], rhs=bufA[:, kc, tc * 512:(tc + 1) * 512],
                                start=(kc == 0), stop=(kc == KC - 1)), reads=[t_u, t_bufA], writes=[py_t], inc=(kc == KC - 1))
                        pg_h, pg_t = proj_bank()
                        for kc in range(KC):
                            op(PE, lambda kc=kc, tc=tc, pg_h=pg_h: nc.tensor.matmul(
                                pg_h[:, 0:512], lhsT=u[:, kc, 128:256], rhs=xnT[:, kc, tc * 512:(tc + 1) * 512],
                                start=(kc == 0), stop=(kc == KC - 1)), reads=[t_u, t_xnT], writes=[pg_t], inc=(kc == KC - 1))
                        sg_h, sg_t = sgs[tc % 2]
                        op(ACT, lambda pg_h=pg_h, sg_h=sg_h, oc=oc: nc.scalar.activation(
                            out=sg_h[:], in_=pg_h[:, 0:512], func=AF.Tanh, bias=bgh[:, oc:oc + 1], scale=0.5),
                           reads=[pg_t, t_bgh], writes=[sg_t])
                        op(DVE, lambda py_h=py_h, sg_h=sg_h: nc.vector.scalar_tensor_tensor(
                            out=sg_h[:], in0=sg_h[:], scalar=1.0, in1=py_h[:, 0:512], op0=ALU.add, op1=ALU.mult),
                           reads=[py_t, sg_t], writes=[sg_t])
                        op(DVE, lambda sg_h=sg_h, oc=oc, tc=tc: nc.vector.tensor_tensor(
                            out=bufY[:, oc, tc * 512:(tc + 1) * 512], in0=sg_h[:], in1=bufY[:, oc, tc * 512:(tc + 1) * 512],
                            op=ALU.add), reads=[sg_t, t_bufY], writes=[t_bufY])
                    ws.release(si)
                kb.end_scratch_phase()
            if first:
                tap("yT", bufY[:], [128, KC, S], BF16, [t_bufY])

            if upto == "FD":
                break

            def _phase_O(seq=seq, bufY=bufY, t_bufY=t_bufY, ys=ys):
                dma(SP, out=gX[:], in_=final_g.partition_broadcast(128), writes=[t_gX])
                us = [ws.next() for _ in range(4)]
                for tb in range(NT):
                    xt_h, xt_t = xts[tb % NXT]
                    dma(SP, out=xt_h[:], in_=x_d[seq, tb * 128:(tb + 1) * 128, :], writes=[xt_t])
                    banks = [proj_bank(), proj_bank()]
                    for hf in range(2):
                        po_h, po_t = banks[hf]
                        for n in range(2):
                            si, u, t_u = us[2 * hf + n]
                            for kc in range(KC):
                                op(PE, lambda kc=kc, n=n, u=u, po_h=po_h: nc.tensor.matmul(
                                    po_h[:, n * 256:(n + 1) * 256], lhsT=bufY[:, kc, tb * 128:(tb + 1) * 128], rhs=u[:, kc, :],
                                    start=(kc == 0), stop=(kc == KC - 1)), reads=[t_u, t_bufY], writes=[po_t],
                                   inc=(kc == KC - 1 and n == 1))
                        op(DVE, lambda hf=hf, po_h=po_h: nc.vector.scalar_tensor_tensor(
                            out=xt_h[:, hf * 512:(hf + 1) * 512], in0=po_h[:, 0:512], scalar=0.5,
                            in1=xt_h[:, hf * 512:(hf + 1) * 512], op0=ALU.mult, op1=ALU.add),
                           reads=[po_t, xt_t], writes=[xt_t])
                    xn_h, xn_t = xns[tb % NXN]
                    rmsnorm_tail(xt_h, xt_t, tb, gF, t_gF, xt_h[:], [xt_t], xn_h[:], [xn_t])
                    dma(SP, out=y_d[seq, tb * 128:(tb + 1) * 128, :], in_=xt_h[:], reads=[xt_t])
                for (si, u, t_u) in us:
                    ws.release(si)
                kb.scr_tiles.append(t_bufY)
                kb.end_scratch_phase()
                ys.close()

            pending_O = _phase_O

        if pending_O is not None:
            pending_O()
            pending_O = None
        if kb.mode == "defer":
            kb.flush()
            build.n_inst = kb.n_inst
            build.nsem = kb.nsem
            build.est_us = kb.est_makespan
        out_costs = [(o.cost, o.aset) for o in kb.ops]
    return nc, dbg_out, out_costs


def host_inputs(inputs):
    f = lambda a: np.ascontiguousarray(np.asarray(a, dtype=np.float32))
    gb = np.concatenate([f(inputs["b_fox_f"])[0], f(inputs["b_ml_i"])[0], f(inputs["b_ml_f"])[0]])
    gbias = np.ascontiguousarray(np.broadcast_to(gb[None, None, :], (128, NT, 16)))
    convw = np.ascontiguousarray(f(inputs["conv_w"])[0].reshape(4, 8, 128).transpose(2, 1, 0))
    convb = np.ascontiguousarray(f(inputs["conv_b"])[0].reshape(8, 128).T)
    bgate = np.ascontiguousarray(f(inputs["b_gate"])[0].reshape(16, 128).T)
    return {
        "w_in": f(inputs["w_in"])[0],
        "w_fox_down": f(inputs["w_fox_down"])[0],
        "w_ml_down": f(inputs["w_ml_down"])[0],
        "w_out": f(inputs["w_out"])[0],
        "norm_g": f(inputs["norm_g"])[0],
        "final_g": f(inputs["final_g"]),
        "ml_norm_g": f(inputs["ml_norm_g"])[0],
        "gbias": gbias,
        "convw": convw,
        "convb": convb,
        "bgate": bgate,
    }


def kernel(**inputs):
    x = np.ascontiguousarray(np.asarray(inputs["x"], dtype=np.float32))
    B = x.shape[0]
    nseq = B // NCORES
    shared = host_inputs(inputs)
    nc, _ = build(nseq)
    in_maps = []
    for c in range(NCORES):
        m = dict(shared)
        m["x"] = x[c * nseq:(c + 1) * nseq]
        in_maps.append(m)
    res = run_bass_kernel_spmd(nc, in_maps, core_ids=list(range(NCORES)))
    out = np.concatenate([np.asarray(r["y"]) for r in res.results], axis=0)
    return out.astype(np.float32, copy=False)
```

```python
import types
import numpy as np
from contextlib import ExitStack
import concourse.bass as bass
import concourse.mybir as mybir
from concourse.bass_utils import run_bass_kernel_spmd

F32 = mybir.dt.float32
BF16 = mybir.dt.bfloat16
AF = mybir.ActivationFunctionType
ALU = mybir.AluOpType

S = 2048
D = 1024
NT = 16
KC = 8
NCORES = 8
EPS = 1e-6
C_FQ, C_FK, C_FV, C_FF, C_FZ = 0, 1024, 2048, 3072, 3080
C_MQ, C_MK, C_MV, C_MI, C_MF, C_MO, C_MZ = 4104, 4616, 5128, 6152, 6156, 6160, 7184
C_GA, C_GB = 8208, 9232
IN_W = 10256
SEM_LIMIT = 30000
SLACK_US = 0.0
PE_MAXG = 2
NPTB = 5
NXT = 4
MDB = 1
NCB = 2
NTH = 1
NQB = 2
NXN = 2
M_PROJ_BANKS = [0, 1, 5]
NSLOT = 6


class SemObj:
    __slots__ = ("h", "owner", "val")

    def __init__(self, h, owner):
        self.h = h
        self.owner = owner
        self.val = 0


class Tile:
    __slots__ = ("w", "r", "name", "psum")

    def __init__(self, name, init=None, psum=False):
        self.name = name
        self.w = list(init) if init else []
        self.r = []
        self.psum = psum


class Eng:
    def __init__(self, kb, name, h, is_pe=False):
        self.kb = kb
        self.name = name
        self.h = h
        self.is_pe = is_pe
        self.waited = {}
        self.cur = None

    def sem(self):
        if self.cur is None or self.cur.val >= SEM_LIMIT:
            self.cur = self.kb.new_sem(self)
        return self.cur


class Op:
    __slots__ = ("id", "eng", "fns", "deps", "cost", "is_dma", "tok", "start", "ninstr", "aset")

    def __init__(self, oid, eng, is_dma):
        self.id = oid
        self.eng = eng
        self.fns = []
        self.deps = []
        self.cost = 0.0
        self.is_dma = is_dma
        self.tok = None
        self.start = 0.0
        self.aset = None


def _snap(fn):
    if fn.__closure__ is None:
        return fn
    cells = []
    for c in fn.__closure__:
        try:
            cells.append(types.CellType(c.cell_contents))
        except ValueError:
            cells.append(c)
    return types.FunctionType(fn.__code__, fn.__globals__, fn.__name__, fn.__defaults__, tuple(cells))


def _free_elems(acc):
    try:
        ap = acc.ap
        n = 1
        for (_, c) in ap[1:]:
            n *= c
        return max(int(n), 1)
    except Exception:
        return 64


class KB:
    def __init__(self, nc, es, costs=None):
        self.nc = nc
        self.es = es
        self.mode = "measure" if costs is None else "defer"
        self.costs_in = costs
        self.nsem = 0
        self.PE = Eng(self, "pe", nc.tensor, is_pe=True)
        self.ACT = Eng(self, "act", nc.scalar)
        self.DVE = Eng(self, "dve", nc.vector)
        self.POOL = Eng(self, "pool", nc.gpsimd)
        self.SP = Eng(self, "sp", nc.sync)
        self.engs = [self.PE, self.ACT, self.DVE, self.POOL, self.SP]
        self.dma_pool = {}
        self.dma_rr = {}
        self.scr_tokens = []
        self.scr_tiles = []
        self.scr_id = 0
        self.n_inst = 0
        self.ops = []
        self.open_pe = None

    def new_sem(self, owner):
        self.nsem += 1
        h = self.es.enter_context(self.nc.semaphore(f"s{self.nsem}"))
        return SemObj(h, owner)

    def _deps(self, o, reads, writes):
        eng = o.eng
        ops = self.ops
        d = o.deps
        for t in reads:
            d.extend(t.w)
            if t.psum:
                d.extend(r for r in t.r if ops[r].eng is not eng)
        for t in writes:
            d.extend(t.w)
            d.extend(t.r)

    def _touch(self, oid, reads, writes):
        for t in reads:
            if not t.r or t.r[-1] != oid:
                t.r.append(oid)
        for t in writes:
            t.w = [oid]
            t.r = []

    def _cost_of(self, eng, ins):
        try:
            i = ins.ins
            n = _free_elems(i.outs[0])
            if eng.is_pe:
                f32 = False
                try:
                    f32 = (i.ins[0].dtype == F32) and not getattr(i, "is_transpose", False)
                except Exception:
                    pass
                return (max(n, 200) * (4 if f32 else 1) * 0.5 + 8.0) / 1000.0
            if eng is self.ACT:
                return 0.22 + n * 0.00083
            if eng is self.DVE:
                return 0.10 + n * 0.00104
            return 0.25 + n * 0.002
        except Exception:
            return 0.3

    def op(self, eng, fn, reads=(), writes=(), inc=True):
        if eng.is_pe and self.open_pe is not None:
            o = self.open_pe
        else:
            assert self.open_pe is None, "non-PE op recorded while a PE group is open"
            o = Op(len(self.ops), eng, False)
            self.ops.append(o)
            if eng.is_pe and not inc:
                self.open_pe = o
        if eng.is_pe:
            o.ninstr = getattr(o, "ninstr", 0) + 1
            if inc or o.ninstr >= PE_MAXG:
                self.open_pe = None
        self._deps(o, reads, writes)
        self._touch(o.id, reads, writes)
        if self.mode == "measure":
            ins = fn()
            o.cost += self._cost_of(eng, ins)
            if eng is self.ACT:
                try:
                    fname = str(ins.ins.func)
                except Exception:
                    fname = ""
                if "Exp" in fname or "Tanh" in fname:
                    o.aset = "A"
                elif "Sqrt" in fname:
                    o.aset = "S"
                elif "Ln" in fname:
                    o.aset = "L"
            return ins
        o.fns.append(_snap(fn))
        return None

    def dma(self, q, out, in_, reads=(), writes=()):
        assert self.open_pe is None
        o = Op(len(self.ops), q, True)
        self.ops.append(o)
        self._deps(o, reads, writes)
        self._touch(o.id, reads, writes)
        if self.mode == "measure":
            q.h.dma_start(out=out, in_=in_)
            try:
                n = 1
                for (_, c) in out.ap:
                    n *= c
            except Exception:
                n = 131072
            o.cost = 2.0 + n * 4 / 150e3
            return None
        o.fns.append(lambda: q.h.dma_start(out=out, in_=in_))
        return None

    def sb(self, name, shape, dt, es=None):
        h = (es or self.es).enter_context(self.nc.sbuf_tensor(name, shape, dt))
        return h, Tile(name)

    def scratch(self, es, name, shape, dt):
        self.scr_id += 1
        h = es.enter_context(self.nc.sbuf_tensor(f"{name}_{self.scr_id}", shape, dt))
        t = Tile(name, self.scr_tokens)
        self.scr_tiles.append(t)
        return h, t

    def end_scratch_phase(self):
        acc = set(self.scr_tokens)
        for t in self.scr_tiles:
            acc.update(t.w)
            acc.update(t.r)
        self.scr_tokens = sorted(acc)
        self.scr_tiles = []

    def schedule(self):
        ops = self.ops
        n = len(ops)
        assert len(self.costs_in) == n, (len(self.costs_in), n)
        costs = [c for (c, _) in self.costs_in]
        asets = [a for (_, a) in self.costs_in]
        act = self.ACT
        cur_set = [None]
        TL = 0.0
        succ = [[] for _ in range(n)]
        for o in ops:
            o.deps = sorted(set(d for d in o.deps if d != o.id))
            for d in o.deps:
                succ[d].append(o.id)
        pend = {e: [] for e in self.engs}
        for o in ops:
            pend[o.eng].append(o.id)
        done = [False] * n
        fin = [0.0] * n
        efree = {e: 0.0 for e in self.engs}
        best = {e: None for e in self.engs}
        dirty = set(self.engs)
        HOP, SAME, W = 0.30, 0.25, 400
        SLACK = SLACK_US
        order = []
        INF = float("inf")
        while len(order) < n:
            for e in dirty:
                lst = pend[e]
                be, bi, bpos = INF, -1, -1
                ef = efree[e]
                lim = min(W, len(lst))
                for pos in range(lim):
                    oid = lst[pos]
                    rt = ef
                    ok = True
                    for d in ops[oid].deps:
                        if not done[d]:
                            ok = False
                            break
                        t = fin[d] + (SAME if ops[d].eng is e else HOP)
                        if t > rt:
                            rt = t
                    if ok and e is act and asets[oid] is not None and cur_set[0] is not None and asets[oid] != cur_set[0]:
                        rt += TL
                    if ok and rt < be - SLACK:
                        be, bi, bpos = rt, oid, pos
                        if rt <= ef:
                            break
                best[e] = (be, bi, bpos) if bi >= 0 else None
            dirty.clear()
            ce, cb = None, None
            for e in self.engs:
                b = best[e]
                if b is not None and (cb is None or b[0] < cb[0] or (b[0] == cb[0] and b[1] < cb[1])):
                    ce, cb = e, b
            assert ce is not None, "scheduler deadlock"
            st, oid, pos = cb
            o = ops[oid]
            o.start = st
            if o.is_dma:
                efree[ce] = st + 0.15
                fin[oid] = st + costs[oid]
            else:
                efree[ce] = st + costs[oid]
                fin[oid] = efree[ce]
                if ce is act and asets[oid] is not None:
                    cur_set[0] = asets[oid]
            done[oid] = True
            del pend[ce][pos]
            order.append(o)
            dirty.add(ce)
            for sidx in succ[oid]:
                dirty.add(ops[sidx].eng)
        self.est_makespan = max(fin) if fin else 0.0
        return order

    def flush(self):
        order = self.schedule()
        ops = self.ops
        for o in order:
            eng = o.eng
            for d in o.deps:
                p = ops[d]
                if eng.is_pe and p.eng is eng and not p.is_dma:
                    continue
                s, v = p.tok
                if eng.waited.get(s, 0) >= v:
                    continue
                eng.h.wait_ge(s.h, v)
                eng.waited[s] = v
                self.n_inst += 1
            if o.is_dma:
                pool = self.dma_pool.setdefault(eng, [])
                if len(pool) < 8:
                    dsem = self.new_sem(None)
                    pool.append(dsem)
                else:
                    i = self.dma_rr.get(eng, 0)
                    dsem = pool[i % len(pool)]
                    self.dma_rr[eng] = i + 1
                if dsem.val and eng.waited.get(dsem, 0) < dsem.val:
                    eng.h.wait_ge(dsem.h, dsem.val)
                    eng.waited[dsem] = dsem.val
                    self.n_inst += 1
                ins = o.fns[0]()
                dsem.val += 16
                ins.then_inc(dsem.h, 16)
                o.tok = (dsem, dsem.val)
            else:
                ins = None
                for fn in o.fns:
                    ins = fn()
                    self.n_inst += 1
                s = eng.sem()
                s.val += 1
                ins.then_inc(s.h, 1)
                o.tok = (s, s.val)
            self.n_inst += 1
        SP = self.SP
        for q, pool in self.dma_pool.items():
            for dsem in pool:
                if dsem.val and SP.waited.get(dsem, 0) < dsem.val:
                    self.nc.sync.wait_ge(dsem.h, dsem.val)
                    SP.waited[dsem] = dsem.val


class WStream:
    def __init__(self, kb, slots, plan):
        self.kb = kb
        self.slots = slots
        self.plan = plan
        self.issued = 0
        self.taken = 0
        self.free = [True] * len(slots)

    def _pump(self):
        while self.issued < len(self.plan):
            si = self.issued % len(self.slots)
            if not self.free[si]:
                break
            h, t = self.slots[si]
            for (dst, src, c0, n) in self.plan[self.issued]:
                self.kb.dma(self.kb.POOL, out=h[:, :, dst:dst + n], in_=src[:, :, c0:c0 + n], writes=[t])
            self.free[si] = False
            self.issued += 1

    def next(self):
        self._pump()
        assert self.taken < self.issued, "weight stream stalled (slot not released)"
        si = self.taken % len(self.slots)
        self.taken += 1
        h, t = self.slots[si]
        return si, h, t

    def release(self, si):
        self.free[si] = True
        self._pump()


def unit_plan(nseq, wv, wmd, wfd, wout):
    plan = []
    for _ in range(nseq):
        for h in range(4):
            plan.append([(0, wv, C_MQ + h * 128, 128), (128, wv, C_MK + h * 128, 128)])
            plan.append([(0, wv, C_MV + h * 256, 256)])
            plan.append([(0, wv, C_MO + h * 256, 256)])
            plan.append([(0, wv, C_MZ + h * 256, 256)])
        for oc in range(8):
            plan.append([(0, wmd, oc * 128, 128), (128, wv, C_GB + oc * 128, 128)])
        for h in range(8):
            plan.append([(0, wv, C_FQ + h * 128, 128), (128, wv, C_FK + h * 128, 128)])
            plan.append([(0, wv, C_FV + h * 128, 128), (128, wv, C_FZ + h * 128, 128)])
        for oc in range(8):
            plan.append([(0, wfd, oc * 128, 128), (128, wv, C_GA + oc * 128, 128)])
        for u in range(4):
            plan.append([(0, wout, u * 256, 256)])
    return plan


def build(nseq=2, dbg=None, upto=None):
    _, _, costs = _build_pass(nseq, dbg, upto, None)
    nc, dbg_out, _ = _build_pass(nseq, dbg, upto, costs)
    return nc, dbg_out


def _build_pass(nseq, dbg, upto, costs):
    dbg = dbg or set()
    nc = bass.Bass("TRN2", target_bir_lowering=False)
    x_d = nc.dram_tensor("x", [nseq, S, D], F32, kind="ExternalInput").ap()
    w_in = nc.dram_tensor("w_in", [D, IN_W], F32, kind="ExternalInput").ap()
    w_fd = nc.dram_tensor("w_fox_down", [D, D], F32, kind="ExternalInput").ap()
    w_md = nc.dram_tensor("w_ml_down", [D, D], F32, kind="ExternalInput").ap()
    w_o = nc.dram_tensor("w_out", [D, D], F32, kind="ExternalInput").ap()
    norm_g = nc.dram_tensor("norm_g", [D], F32, kind="ExternalInput").ap()
    final_g = nc.dram_tensor("final_g", [D], F32, kind="ExternalInput").ap()
    ml_g = nc.dram_tensor("ml_norm_g", [D], F32, kind="ExternalInput").ap()
    gbias_d = nc.dram_tensor("gbias", [128, NT, 16], F32, kind="ExternalInput").ap()
    convw_d = nc.dram_tensor("convw", [128, 8, 4], F32, kind="ExternalInput").ap()
    convb_d = nc.dram_tensor("convb", [128, 8], F32, kind="ExternalInput").ap()
    bgate_d = nc.dram_tensor("bgate", [128, 16], F32, kind="ExternalInput").ap()
    y_d = nc.dram_tensor("y", [nseq, S, D], F32, kind="ExternalOutput").ap()
    dbg_out = {}

    def dbg_tensor(name, shape, dt):
        dbg_out[name] = nc.dram_tensor("dbg_" + name, shape, dt, kind="ExternalOutput").ap()
        return dbg_out[name]

    wv = w_in.rearrange("(kc p) c -> p kc c", p=128)
    wfd = w_fd.rearrange("(kc p) c -> p kc c", p=128)
    wmd = w_md.rearrange("(kc p) c -> p kc c", p=128)
    wout = w_o.rearrange("(kc p) c -> p kc c", p=128)

    with ExitStack() as es:
        kb = KB(nc, es, costs)
        PE, ACT, DVE, POOL, SP = kb.PE, kb.ACT, kb.DVE, kb.POOL, kb.SP
        op, dma = kb.op, kb.dma

        xnT, t_xnT = kb.sb("xnT", [128, KC, S], BF16)
        bufA, t_bufA = kb.sb("bufA", [128, KC, S], BF16)
        slots = [kb.sb(f"wslot{i}", [128, KC, 256], BF16) for i in range(NSLOT)]
        wg, t_wg = kb.sb("wg", [128, KC, 16], BF16)
        gX, t_gX = kb.sb("gX", [128, D], F32)
        gN, t_gN = kb.sb("gN", [128, D], F32)
        gF = gM = gX
        t_gF = t_gM = t_gX
        gbias, t_gbias = kb.sb("gbias_sb", [128, NT, 16], F32)
        cw, t_cw = kb.sb("cw", [128, 8, 4], F32)
        cb, t_cb = kb.sb("cb", [128, 8], F32)
        bg, t_bg = kb.sb("bg", [128, 16], F32)
        cbh, t_cbh = kb.sb("cbh", [128, 8], F32)
        bgh, t_bgh = kb.sb("bgh", [128, 16], F32)
        epsT, t_eps = kb.sb("epsT", [128, 1], F32)
        ident, t_ident = kb.sb("ident", [128, 128], BF16)
        onesb, t_onesb = kb.sb("onesb", [128, 128], BF16)
        maskb, t_maskb = kb.sb("maskb", [128, 128], BF16)
        maskf, t_maskf = kb.sb("maskf", [128, 128], F32)
        onesf, t_onesf = kb.sb("onesf", [128, 128], F32)
        hmid, t_hmid = kb.sb("hmid", [128, 128], F32)
        xts = [kb.sb(f"xt{i}", [128, D], F32) for i in range(NXT)]
        xns = [kb.sb(f"xn{i}", [128, D], BF16) for i in range(NXN)]
        stats = [kb.sb(f"stat{i}", [128, 4], F32) for i in range(NXT)]
        zs, t_zs = kb.sb("zs", [128, NT, 16], F32)
        lf, t_lf = kb.sb("lf", [128, NT, 16], F32)
        tot, t_tot = kb.sb("tot", [128, NT, 16], F32)
        bc, t_bc = kb.sb("bc", [128, NT, 16], F32)
        half, t_half = kb.sb("half", [128, NT, 16], F32)
        pre, t_pre = kb.sb("pre", [128, NT, 16], F32)
        cc, t_cc = kb.sb("cc", [128, NT, 16], F32)
        RR, t_RR = kb.sb("RR", [128, NT, 16], F32)
        ew, t_ew = kb.sb("ew", [128, NT, 4], F32)
        eb, t_eb = kb.sb("eb", [128, NT, 4], F32)
        aa, t_aa = kb.sb("aa", [128, NT, 4], F32)
        gtmp, t_gtmp = kb.sb("gtmp", [128, NT, 16], F32)

        PB = []
        for i in range(6):
            h = es.enter_context(nc.psum_tensor(f"pb{i}", [128, 512], F32))
            PB.append((h, Tile(f"pb{i}", psum=True)))
        PT = []
        for i in range(2):
            h = es.enter_context(nc.psum_tensor(f"pt{i}", [128, 8, 128], BF16))
            PT.append((h, Tile(f"pt{i}", psum=True)))

        dma(SP, out=gbias[:], in_=gbias_d[:, :, :], writes=[t_gbias])
        dma(SP, out=cw[:], in_=convw_d[:, :, :], writes=[t_cw])
        dma(SP, out=cb[:], in_=convb_d[:, :], writes=[t_cb])
        dma(SP, out=bg[:], in_=bgate_d[:, :], writes=[t_bg])
        dma(POOL, out=wg[:, :, 0:8], in_=wv[:, :, C_FF:C_FF + 8], writes=[t_wg])
        dma(POOL, out=wg[:, :, 8:16], in_=wv[:, :, C_MI:C_MI + 8], writes=[t_wg])
        op(DVE, lambda: nc.vector.memset(epsT[:], EPS), writes=[t_eps])
        op(DVE, lambda: nc.vector.tensor_scalar(out=cbh[:], in0=cb[:], scalar1=0.5, scalar2=None, op0=ALU.mult),
           reads=[t_cb], writes=[t_cbh])
        op(DVE, lambda: nc.vector.tensor_scalar(out=bgh[:], in0=bg[:], scalar1=0.5, scalar2=None, op0=ALU.mult),
           reads=[t_bg], writes=[t_bgh])
        op(POOL, lambda: nc.gpsimd.memset(onesb[:], 1.0), writes=[t_onesb])
        op(POOL, lambda: nc.gpsimd.memset(onesf[:], 1.0), writes=[t_onesf])
        op(POOL, lambda: nc.gpsimd.memset(hmid[:], 0.0), writes=[t_hmid])
        op(POOL, lambda: nc.gpsimd.memset(hmid[0:64, :], 1.0), writes=[t_hmid])
        op(POOL, lambda: nc.gpsimd.affine_select(out=ident[:], in_=onesb[:], pattern=[[1, 128]],
                                                 compare_op=ALU.is_equal, fill=0.0, base=0,
                                                 channel_multiplier=-1), reads=[t_onesb], writes=[t_ident])
        op(POOL, lambda: nc.gpsimd.affine_select(out=maskb[:], in_=onesb[:], pattern=[[1, 128]],
                                                 compare_op=ALU.is_ge, fill=0.0, base=0,
                                                 channel_multiplier=-1), reads=[t_onesb], writes=[t_maskb])
        op(POOL, lambda: nc.gpsimd.affine_select(out=maskf[:], in_=onesf[:], pattern=[[1, 128]],
                                                 compare_op=ALU.is_ge, fill=0.0, base=0,
                                                 channel_multiplier=-1), reads=[t_onesf], writes=[t_maskf])

        ws = WStream(kb, slots, unit_plan(nseq, wv, wmd, wfd, wout))
        proj_rr = [0]
        proj_n = [2]

        proj_idx = [None]

        def proj_bank():
            proj_rr[0] += 1
            if proj_idx[0] is not None:
                return PB[proj_idx[0][proj_rr[0] % len(proj_idx[0])]]
            return PB[proj_rr[0] % proj_n[0]]

        def tap(name, src_ap, shape, dt, tiles):
            if name in dbg:
                d = dbg_tensor(name, shape, dt)
                dma(SP, out=d, in_=src_ap, reads=tiles)

        def rmsnorm_tail(xt_h, xt_t, si, g_h, g_t, out_ap, out_tiles, sq_out_ap, sq_out_tiles):
            st_h, st_t = stats[si % NXT]
            op(ACT, lambda: nc.scalar.activation(out=sq_out_ap, in_=xt_h[:], func=AF.Square,
                                                 accum_out=st_h[:, 0:1]),
               reads=[xt_t], writes=sq_out_tiles + [st_t])
            op(ACT, lambda: nc.scalar.activation(out=st_h[:, 1:2], in_=st_h[:, 0:1], func=AF.Sqrt,
                                                 scale=1.0 / D, bias=epsT[:]),
               reads=[st_t, t_eps], writes=[st_t])
            op(DVE, lambda: nc.vector.reciprocal(out=st_h[:, 2:3], in_=st_h[:, 1:2]), reads=[st_t], writes=[st_t])
            op(DVE, lambda: nc.vector.scalar_tensor_tensor(out=out_ap, in0=xt_h[:], scalar=st_h[:, 2:3],
                                                           in1=g_h[:], op0=ALU.mult, op1=ALU.mult),
               reads=[xt_t, st_t, g_t], writes=out_tiles)

        pending_O = None
        for seq in range(nseq):
            first = (seq == 0)
            proj_n[0] = 2
            if first:
                dma(SP, out=gN[:], in_=norm_g.partition_broadcast(128), writes=[t_gN])
            for tb in range(NT):
                xt_h, xt_t = xts[tb % NXT]
                xn_h, xn_t = xns[tb % NXN]
                dma(SP, out=xt_h[:], in_=x_d[seq, tb * 128:(tb + 1) * 128, :], writes=[xt_t])
                rmsnorm_tail(xt_h, xt_t, tb, gN, t_gN, xn_h[:], [xn_t], xn_h[:], [xn_t])
                pt_h, pt_t = PT[tb % 2]
                for kc in range(KC):
                    op(PE, lambda kc=kc: nc.tensor.transpose(out=pt_h[:, kc, :], in_=xn_h[:, kc * 128:(kc + 1) * 128],
                                                             identity=ident[:]),
                       reads=[xn_t, t_ident], writes=[pt_t], inc=(kc == KC - 1))
                op(ACT, lambda: nc.scalar.copy(out=xnT[:, :, tb * 128:(tb + 1) * 128], in_=pt_h[:]),
                   reads=[pt_t], writes=[t_xnT])
            if first:
                tap("xnT", xnT[:], [128, KC, S], BF16, [t_xnT])

            if upto == "P0":
                break

            pz_h, pz_t = PB[2]
            for tb in range(NT):
                for kc in range(KC):
                    op(PE, lambda kc=kc, tb=tb: nc.tensor.matmul(pz_h[:, tb * 16:(tb + 1) * 16],
                                                                 lhsT=xnT[:, kc, tb * 128:(tb + 1) * 128],
                                                                 rhs=wg[:, kc, :], start=(kc == 0), stop=(kc == KC - 1)),
                       reads=[t_xnT, t_wg], writes=[pz_t], inc=(kc == KC - 1 and tb == NT - 1))
            op(DVE, lambda: nc.vector.tensor_tensor(out=zs[:].rearrange("p a b -> p (a b)"), in0=pz_h[:, 0:256],
                                                    in1=gbias[:].rearrange("p a b -> p (a b)"), op=ALU.add),
               reads=[pz_t, t_gbias], writes=[t_zs])
            if upto == "G1":
                tap("lf", zs[:], [128, NT, 16], F32, [t_zs])
                break
            op(ACT, lambda: nc.scalar.activation(out=gtmp[:], in_=zs[:], func=AF.Exp, scale=-1.0),
               reads=[t_zs], writes=[t_gtmp])
            op(ACT, lambda: nc.scalar.activation(out=gtmp[:], in_=gtmp[:], func=AF.Ln, bias=1.0, scale=1.0),
               reads=[t_gtmp], writes=[t_gtmp])
            op(DVE, lambda: nc.vector.tensor_scalar(out=lf[:], in0=gtmp[:], scalar1=-1.0, scalar2=None, op0=ALU.mult),
               reads=[t_gtmp], writes=[t_lf])
            if upto == "G2":
                tap("lf", lf[:], [128, NT, 16], F32, [t_lf])
                break
            lf2 = lf[:].rearrange("p a b -> p (a b)")
            for (lhs_h, lhs_t, dst_h, dst_t, pbi) in ((onesf, t_onesf, tot, t_tot, 3), (maskf, t_maskf, bc, t_bc, 4),
                                                       (hmid, t_hmid, half, t_half, 5)):
                pp_h, pp_t = PB[pbi]
                op(PE, lambda lhs_h=lhs_h, pp_h=pp_h: nc.tensor.matmul(pp_h[:, 0:256], lhsT=lhs_h[:], rhs=lf2,
                                                                       start=True, stop=True),
                   reads=[lhs_t, t_lf], writes=[pp_t])
                op(DVE, lambda dst_h=dst_h, pp_h=pp_h: nc.vector.tensor_copy(out=dst_h[:].rearrange("p a b -> p (a b)"),
                                                                             in_=pp_h[:, 0:256]),
                   reads=[pp_t], writes=[dst_t])
            if upto == "G3":
                tap("lf", lf[:], [128, NT, 16], F32, [t_lf])
                tap("cc", bc[:], [128, NT, 16], F32, [t_bc])
                break
            op(DVE, lambda: nc.vector.memset(pre[:, 0, :], 0.0), writes=[t_pre])
            for tb in range(1, NT):
                op(DVE, lambda tb=tb: nc.vector.tensor_tensor(out=pre[:, tb, :], in0=pre[:, tb - 1, :],
                                                              in1=tot[:, tb - 1, :], op=ALU.add),
                   reads=[t_pre, t_tot], writes=[t_pre])
            op(DVE, lambda: nc.vector.tensor_tensor(out=cc[:], in0=bc[:], in1=pre[:], op=ALU.add),
               reads=[t_bc, t_pre], writes=[t_cc])
            op(DVE, lambda: nc.vector.tensor_tensor(out=RR[:], in0=half[:], in1=pre[:], op=ALU.add),
               reads=[t_half, t_pre], writes=[t_RR])
            if upto == "G4":
                tap("cc", cc[:], [128, NT, 16], F32, [t_cc])
                break
            op(DVE, lambda: nc.vector.tensor_tensor(out=ew[:], in0=zs[:, :, 8:12], in1=bc[:, :, 12:16], op=ALU.subtract),
               reads=[t_zs, t_bc], writes=[t_ew])
            if upto == "G5":
                tap("cc", cc[:], [128, NT, 16], F32, [t_cc, t_ew])
                break
            op(ACT, lambda: nc.scalar.activation(out=ew[:], in_=ew[:], func=AF.Exp), reads=[t_ew], writes=[t_ew])
            if upto == "G6":
                tap("cc", cc[:], [128, NT, 16], F32, [t_cc, t_ew])
                break
            op(ACT, lambda: nc.scalar.activation(out=eb[:], in_=bc[:, :, 12:16], func=AF.Exp), reads=[t_bc], writes=[t_eb])
            op(DVE, lambda: nc.vector.tensor_scalar(out=eb[:], in0=eb[:], scalar1=float(128 ** -0.5), scalar2=None,
                                                    op0=ALU.mult), reads=[t_eb], writes=[t_eb])
            if upto == "G7":
                tap("cc", cc[:], [128, NT, 16], F32, [t_cc, t_ew, t_eb])
                break
            op(ACT, lambda: nc.scalar.activation(out=aa[:], in_=tot[:, :, 12:16], func=AF.Exp), reads=[t_tot], writes=[t_aa])
            if first:
                tap("lf", lf[:], [128, NT, 16], F32, [t_lf])
                tap("cc", cc[:], [128, NT, 16], F32, [t_cc])
                tap("RR", RR[:], [128, NT, 16], F32, [t_RR])
                tap("ew", ew[:], [128, NT, 4], F32, [t_ew])
                tap("eb", eb[:], [128, NT, 4], F32, [t_eb])
                tap("aa", aa[:], [128, NT, 4], F32, [t_aa])

            if upto == "G":
                break

            if pending_O is not None:
                pending_O()
                pending_O = None
            proj_idx[0] = M_PROJ_BANKS
            dma(SP, out=gX[:], in_=ml_g.partition_broadcast(128), writes=[t_gX])
            with ExitStack() as ms:
                mq2 = [kb.scratch(ms, f"m_qT{i}", [128, S], BF16) for i in range(MDB)]
                mk2 = [kb.scratch(ms, f"m_kT{i}", [128, S], BF16) for i in range(MDB)]
                mc2 = [kb.scratch(ms, f"m_cbuf{i}", [128, S + 4], F32) for i in range(NCB)]
                accs = [kb.scratch(ms, f"m_acc{i}", [128, 1024], F32) for i in range(2)]
                ths = [kb.scratch(ms, f"m_th{i}", [128, 1024], F32) for i in range(NTH)]
                kTok, t_kTok = kb.scratch(ms, "m_kTok", [128, NT, 128], BF16)
                vaug, t_vaug = kb.scratch(ms, "m_vaug", [128, NT, 258], BF16)
                Crun = [kb.scratch(ms, f"m_C{i}", [128, 258], F32) for i in range(2)]
                Cbf, t_Cbf = kb.scratch(ms, "m_Cbf", [128, NT, 258], BF16)
                numS2 = [kb.scratch(ms, f"m_numS{i}", [128, 4, 258], F32) for i in range(NQB)]
                Gp2 = [kb.scratch(ms, f"m_Gp{i}", [128, 4, 256], BF16) for i in range(NQB)]
                STm = [kb.scratch(ms, f"m_STm{i}", [128, 128], BF16) for i in range(2)]
                hn = [kb.scratch(ms, f"m_hn{i}", [128, 256], BF16) for i in range(2)]
                s12 = [kb.scratch(ms, f"m_s12{i}", [128, 512], F32) for i in range(2)]
                sm2 = [kb.scratch(ms, f"m_sm{i}", [128, 12, 4], F32) for i in range(NQB)]
                junk, t_junk = kb.scratch(ms, "m_junk", [128, 256], BF16)
                for (cb_h, cb_t) in mc2:
                    op(DVE, lambda cb_h=cb_h: nc.vector.memset(cb_h[:, 0:4], 0.0), writes=[cb_t])
                for h in range(4):
                    qT, t_qT = mq2[h % MDB]
                    kT, t_kT = mk2[h % MDB]
                    s1i, u1, t_u1 = ws.next()
                    s2i, u2, t_u2 = ws.next()
                    s3i, u3, t_u3 = ws.next()
                    s4i, u4, t_u4 = ws.next()
                    for which, (dst, t_dst) in enumerate(((qT, t_qT), (kT, t_kT))):
                        chunk = which * 4 + h
                        cbuf, t_cbuf = mc2[which % len(mc2)]
                        for tc in range(4):
                            pp_h, pp_t = proj_bank()
                            for kc in range(KC):
                                op(PE, lambda kc=kc, tc=tc, which=which, pp_h=pp_h: nc.tensor.matmul(
                                    pp_h[:, 0:512], lhsT=u1[:, kc, which * 128:(which + 1) * 128],
                                    rhs=xnT[:, kc, tc * 512:(tc + 1) * 512], start=(kc == 0), stop=(kc == KC - 1)),
                                   reads=[t_u1, t_xnT], writes=[pp_t], inc=(kc == KC - 1))
                            op(ACT, lambda tc=tc, pp_h=pp_h: nc.scalar.copy(out=cbuf[:, 4 + tc * 512:4 + (tc + 1) * 512],
                                                                           in_=pp_h[:, 0:512]),
                               reads=[pp_t], writes=[t_cbuf])
                        for hf in range(2):
                            o = hf * 1024
                            ac_h, ac_t = accs[hf]
                            op(DVE, lambda o=o, ac_h=ac_h, chunk=chunk: nc.vector.tensor_scalar(
                                out=ac_h[:], in0=cbuf[:, 4 + o:4 + o + 1024], scalar1=cw[:, chunk, 3:4], scalar2=None,
                                op0=ALU.mult), reads=[t_cbuf, t_cw], writes=[ac_t])
                            for j in (2, 1, 0):
                                op(DVE, lambda o=o, ac_h=ac_h, chunk=chunk, j=j: nc.vector.scalar_tensor_tensor(
                                    out=ac_h[:], in0=cbuf[:, 1 + j + o:1 + j + o + 1024], scalar=cw[:, chunk, j:j + 1],
                                    in1=ac_h[:], op0=ALU.mult, op1=ALU.add), reads=[t_cbuf, t_cw, ac_t], writes=[ac_t])
                            th_h, th_t = ths[hf % NTH]
                            op(ACT, lambda ac_h=ac_h, chunk=chunk, th_h=th_h: nc.scalar.activation(
                                out=th_h[:], in_=ac_h[:], func=AF.Tanh, bias=cbh[:, chunk:chunk + 1], scale=0.5),
                               reads=[ac_t, t_cbh], writes=[th_t])
                            op(DVE, lambda th_h=th_h: nc.vector.tensor_scalar(
                                out=th_h[:], in0=th_h[:], scalar1=0.5, scalar2=0.5, op0=ALU.mult, op1=ALU.add),
                               reads=[th_t], writes=[th_t])
                            op(DVE, lambda o=o, ac_h=ac_h, chunk=chunk, dst=dst, th_h=th_h: nc.vector.scalar_tensor_tensor(
                                out=dst[:, o:o + 1024], in0=ac_h[:], scalar=cb[:, chunk:chunk + 1], in1=th_h[:],
                                op0=ALU.add, op1=ALU.mult), reads=[ac_t, t_cb, th_t], writes=[t_dst])
                    ws.release(s1i)
                    for g in range(2):
                        pt_h, pt_t = PT[g]
                        for c8 in range(8):
                            c = g * 8 + c8
                            op(PE, lambda c=c, c8=c8, pt_h=pt_h: nc.tensor.transpose(
                                out=pt_h[:, c8, :], in_=kT[:, c * 128:(c + 1) * 128], identity=ident[:]),
                               reads=[t_kT, t_ident], writes=[pt_t], inc=(c8 == 7))
                        for c8 in range(8):
                            c = g * 8 + c8
                            op(ACT, lambda c=c, c8=c8, pt_h=pt_h: nc.scalar.activation(
                                out=kTok[:, c, :], in_=pt_h[:, c8, :], func=AF.Copy, scale=aa[:, c, h:h + 1]),
                               reads=[pt_t, t_aa], writes=[t_kTok])
                    for tb in range(NT):
                        pp_h, pp_t = proj_bank()
                        for kc in range(KC):
                            op(PE, lambda kc=kc, tb=tb, pp_h=pp_h: nc.tensor.matmul(
                                pp_h[:, 0:256], lhsT=xnT[:, kc, tb * 128:(tb + 1) * 128], rhs=u2[:, kc, :],
                                start=(kc == 0), stop=(kc == KC - 1)),
                               reads=[t_u2, t_xnT], writes=[pp_t], inc=(kc == KC - 1))
                        op(ACT, lambda tb=tb, pp_h=pp_h: nc.scalar.activation(
                            out=vaug[:, tb, 0:256], in_=pp_h[:, 0:256], func=AF.Copy, scale=ew[:, tb, h:h + 1]),
                           reads=[pp_t, t_ew], writes=[t_vaug])
                    ws.release(s2i)
                    op(DVE, lambda: nc.vector.tensor_copy(out=vaug[:, :, 256:257], in_=ew[:, :, h:h + 1]),
                       reads=[t_ew], writes=[t_vaug])
                    dcb = (PB[5], PB[2])
                    for c in range(NT - 1):
                        pd_h, pd_t = dcb[c % 2]
                        op(PE, lambda c=c, pd_h=pd_h: nc.tensor.matmul(pd_h[:, 0:257], lhsT=kTok[:, c, :],
                                                                       rhs=vaug[:, c, 0:257], start=True, stop=True),
                           reads=[t_kTok, t_vaug], writes=[pd_t])
                        cn_h, cn_t = Crun[(c + 1) % 2]
                        cp_h, cp_t = Crun[c % 2]
                        if c == 0:
                            op(DVE, lambda pd_h=pd_h, cn_h=cn_h: nc.vector.tensor_copy(out=cn_h[:, 0:257], in_=pd_h[:, 0:257]),
                               reads=[pd_t], writes=[cn_t])
                        else:
                            op(DVE, lambda c=c, pd_h=pd_h, cn_h=cn_h, cp_h=cp_h: nc.vector.scalar_tensor_tensor(
                                out=cn_h[:, 0:257], in0=cp_h[:, 0:257], scalar=aa[:, c, h:h + 1], in1=pd_h[:, 0:257],
                                op0=ALU.mult, op1=ALU.add), reads=[cp_t, pd_t, t_aa], writes=[cn_t])
                        op(ACT, lambda c=c, cn_h=cn_h: nc.scalar.copy(out=Cbf[:, c + 1, 0:257], in_=cn_h[:, 0:257]),
                           reads=[cn_t], writes=[t_Cbf])
                    for c in range(NT):
                        numS, t_numS = numS2[(c // 4) % NQB]
                        Gp, t_Gp = Gp2[(c // 4) % NQB]
                        sm, t_sm = sm2[(c // 4) % NQB]
                        ps_h, ps_t = PB[2]
                        op(PE, lambda c=c: nc.tensor.matmul(ps_h[:, 0:128], lhsT=kT[:, c * 128:(c + 1) * 128],
                                                            rhs=qT[:, c * 128:(c + 1) * 128], start=True, stop=True),
                           reads=[t_kT, t_qT], writes=[ps_t])
                        st_h, st_t = STm[c % 2]
                        op(DVE, lambda st_h=st_h: nc.vector.tensor_tensor(out=st_h[:], in0=ps_h[:, 0:128], in1=maskf[:],
                                                                          op=ALU.mult),
                           reads=[ps_t, t_maskf], writes=[st_t])
                        pn_h, pn_t = PB[3 + c % 2]
                        op(PE, lambda c=c, pn_h=pn_h, st_h=st_h: nc.tensor.matmul(
                            pn_h[:, 0:257], lhsT=st_h[:], rhs=vaug[:, c, 0:257], start=True, stop=(c == 0)),
                           reads=[st_t, t_vaug], writes=[pn_t], inc=(c == 0))
                        if c > 0:
                            op(PE, lambda c=c, pn_h=pn_h: nc.tensor.matmul(
                                pn_h[:, 0:257], lhsT=qT[:, c * 128:(c + 1) * 128], rhs=Cbf[:, c, 0:257],
                                start=False, stop=True), reads=[t_qT, t_Cbf], writes=[pn_t])
                        op(ACT, lambda c=c, pn_h=pn_h: nc.scalar.copy(out=numS[:, c % 4, 0:257], in_=pn_h[:, 0:257]),
                           reads=[pn_t], writes=[t_numS])
                        pp_h, pp_t = proj_bank()
                        for gi, (ug, t_ug) in enumerate(((u3, t_u3), (u4, t_u4))):
                            for kc in range(KC):
                                op(PE, lambda kc=kc, c=c, gi=gi, ug=ug, pp_h=pp_h: nc.tensor.matmul(
                                    pp_h[:, gi * 256:(gi + 1) * 256], lhsT=xnT[:, kc, c * 128:(c + 1) * 128],
                                    rhs=ug[:, kc, :], start=(kc == 0), stop=(kc == KC - 1)),
                                   reads=[t_ug, t_xnT], writes=[pp_t], inc=(kc == KC - 1 and gi == 1))
                        sg_h, sg_t = s12[c % 2]
                        op(ACT, lambda pp_h=pp_h, sg_h=sg_h: nc.scalar.activation(out=sg_h[:, 0:512], in_=pp_h[:, 0:512],
                                                                                  func=AF.Tanh, scale=0.5),
                           reads=[pp_t], writes=[sg_t])
                        op(DVE, lambda pp_h=pp_h, sg_h=sg_h: nc.vector.scalar_tensor_tensor(
                            out=sg_h[:, 256:512], in0=sg_h[:, 256:512], scalar=1.0, in1=pp_h[:, 256:512],
                            op0=ALU.add, op1=ALU.mult), reads=[sg_t, pp_t], writes=[sg_t])
                        op(DVE, lambda sg_h=sg_h: nc.vector.scalar_tensor_tensor(
                            out=sg_h[:, 0:256], in0=sg_h[:, 0:256], scalar=1.0, in1=sg_h[:, 256:512],
                            op0=ALU.add, op1=ALU.mult), reads=[sg_t], writes=[sg_t])
                        op(DVE, lambda c=c, sg_h=sg_h: nc.vector.tensor_tensor(out=Gp[:, c % 4, :], in0=sg_h[:, 0:256],
                                                                               in1=gM[:, h * 256:(h + 1) * 256], op=ALU.mult),
                           reads=[sg_t, t_gM], writes=[t_Gp])
                        if c % 4 == 3:
                            cs = c - 3
                            den = numS[:, :, 256]
                            ebs = eb[:, cs:cs + 4, h]
                            op(DVE, lambda: nc.vector.tensor_tensor(out=sm[:, 0, :], in0=den, in1=ebs, op=ALU.mult),
                               reads=[t_numS, t_eb], writes=[t_sm])
                            op(DVE, lambda: nc.vector.scalar_tensor_tensor(out=sm[:, 1, :], in0=sm[:, 0, :], scalar=-1.0,
                                                                           in1=sm[:, 0, :], op0=ALU.mult, op1=ALU.max),
                               reads=[t_sm], writes=[t_sm])
                            op(DVE, lambda: nc.vector.tensor_scalar(out=sm[:, 2, :], in0=sm[:, 1, :], scalar1=1.0,
                                                                    scalar2=None, op0=ALU.max), reads=[t_sm], writes=[t_sm])
                            op(DVE, lambda: nc.vector.reciprocal(out=sm[:, 3, :], in_=sm[:, 2, :]), reads=[t_sm], writes=[t_sm])
                            op(DVE, lambda: nc.vector.tensor_tensor(out=sm[:, 4, :], in0=sm[:, 3, :], in1=ebs, op=ALU.mult),
                               reads=[t_sm, t_eb], writes=[t_sm])
                            for c8 in range(4):
                                op(ACT, lambda c8=c8: nc.scalar.activation(out=junk[:], in_=numS[:, c8, 0:256],
                                                                           func=AF.Square, accum_out=sm[:, 5, c8:c8 + 1]),
                                   reads=[t_numS], writes=[t_junk, t_sm])
                            op(DVE, lambda: nc.vector.tensor_tensor(out=sm[:, 6, :], in0=sm[:, 4, :], in1=sm[:, 4, :],
                                                                    op=ALU.mult), reads=[t_sm], writes=[t_sm])
                            op(DVE, lambda: nc.vector.tensor_tensor(out=sm[:, 7, :], in0=sm[:, 6, :], in1=sm[:, 5, :],
                                                                    op=ALU.mult), reads=[t_sm], writes=[t_sm])
                            op(ACT, lambda: nc.scalar.activation(out=sm[:, 8, :], in_=sm[:, 7, :], func=AF.Sqrt,
                                                                 scale=1.0 / 256, bias=epsT[:]),
                               reads=[t_sm, t_eps], writes=[t_sm])
                            op(DVE, lambda: nc.vector.reciprocal(out=sm[:, 9, :], in_=sm[:, 8, :]), reads=[t_sm], writes=[t_sm])
                            op(DVE, lambda: nc.vector.scalar_tensor_tensor(out=sm[:, 10, :], in0=sm[:, 9, :], scalar=0.25,
                                                                           in1=sm[:, 4, :], op0=ALU.mult, op1=ALU.mult),
                               reads=[t_sm], writes=[t_sm])
                            for c8 in range(4):
                                hn_h, hn_t = hn[c8 % 2]
                                op(DVE, lambda c8=c8, hn_h=hn_h: nc.vector.scalar_tensor_tensor(
                                    out=hn_h[:], in0=numS[:, c8, 0:256], scalar=sm[:, 10, c8:c8 + 1], in1=Gp[:, c8, :],
                                    op0=ALU.mult, op1=ALU.mult), reads=[t_numS, t_sm, t_Gp], writes=[hn_t])
                                for j in range(2):
                                    op(PE, lambda c8=c8, j=j, hn_h=hn_h: nc.tensor.transpose(
                                        out=PT[j][0][:, c8, :], in_=hn_h[:, j * 128:(j + 1) * 128], identity=ident[:]),
                                       reads=[hn_t, t_ident], writes=[PT[j][1]])
                            for j in range(2):
                                op(ACT, lambda j=j, cs=cs: nc.scalar.copy(
                                    out=bufA[:, 2 * h + j, cs * 128:(cs + 4) * 128],
                                    in_=PT[j][0][:, 0:4, :].rearrange("p a b -> p (a b)")),
                                   reads=[PT[j][1]], writes=[t_bufA])
                    ws.release(s3i)
                    ws.release(s4i)
                kb.end_scratch_phase()
            if first:
                tap("hbT", bufA[:], [128, KC, S], BF16, [t_bufA])

            if upto == "M":
                break

            ys = es.enter_context(ExitStack())
            bufY, t_bufY = kb.scratch(ys, "bufY", [128, KC, S], BF16)
            proj_n[0] = 6
            proj_idx[0] = None
            with ExitStack() as ms:
                sgs = [kb.scratch(ms, f"d_sg{i}", [128, 512], F32) for i in range(2)]
                for oc in range(8):
                    si, u, t_u = ws.next()
                    for tc in range(4):
                        py_h, py_t = proj_bank()
                        for kc in range(KC):
                            op(PE, lambda kc=kc, tc=tc, py_h=py_h: nc.tensor.matmul(
                                py_h[:, 0:512], lhsT=u[:, kc, 0:128], rhs=bufA[:, kc, tc * 512:(tc + 1) * 512],
                                start=(kc == 0), stop=(kc == KC - 1)), reads=[t_u, t_bufA], writes=[py_t], inc=(kc == KC - 1))
                        pg_h, pg_t = proj_bank()
                        for kc in range(KC):
                            op(PE, lambda kc=kc, tc=tc, pg_h=pg_h: nc.tensor.matmul(
                                pg_h[:, 0:512], lhsT=u[:, kc, 128:256], rhs=xnT[:, kc, tc * 512:(tc + 1) * 512],
                                start=(kc == 0), stop=(kc == KC - 1)), reads=[t_u, t_xnT], writes=[pg_t], inc=(kc == KC - 1))
                        sg_h, sg_t = sgs[tc % 2]
                        op(ACT, lambda pg_h=pg_h, sg_h=sg_h, oc=oc: nc.scalar.activation(
                            out=sg_h[:], in_=pg_h[:, 0:512], func=AF.Tanh, bias=bgh[:, 8 + oc:9 + oc], scale=0.5),
                           reads=[pg_t, t_bgh], writes=[sg_t])
                        op(DVE, lambda py_h=py_h, sg_h=sg_h, oc=oc, tc=tc: nc.vector.scalar_tensor_tensor(
                            out=bufY[:, oc, tc * 512:(tc + 1) * 512], in0=sg_h[:], scalar=1.0, in1=py_h[:, 0:512],
                            op0=ALU.add, op1=ALU.mult), reads=[py_t, sg_t], writes=[t_bufY])
                    ws.release(si)
                kb.end_scratch_phase()
            if first:
                tap("ybg", bufY[:], [128, KC, S], BF16, [t_bufY])

            if upto == "MD":
                break

            proj_n[0] = 2
            with ExitStack() as ms:
                fq = [kb.scratch(ms, f"f_q{i}", [128, S], BF16) for i in range(2)]
                fk = [kb.scratch(ms, f"f_k{i}", [128, S], BF16) for i in range(2)]
                fv = [kb.scratch(ms, f"f_v{i}", [128, NT, 130], BF16) for i in range(2)]
                fzs = [kb.scratch(ms, f"f_z{i}", [128, NT, 128], BF16) for i in range(2)]
                bms = [kb.scratch(ms, f"f_bm{i}", [128, NT, NT], F32) for i in range(2)]
                ptb = [kb.scratch(ms, f"f_pt{i}", [128, 512], BF16) for i in range(NPTB)]
                obs = [kb.scratch(ms, f"f_ob{i}", [128, 4, 128], BF16) for i in range(4)]
                rin, t_rin = kb.scratch(ms, "f_rin", [128, 16], F32)
                tzs = [kb.scratch(ms, f"f_tz{i}", [128, 128], F32) for i in range(4)]
                for i in range(2):
                    op(DVE, lambda i=i: nc.vector.memset(fv[i][0][:, :, 128:129], 2.0), writes=[fv[i][1]])
                sc = float(128 ** -0.5)
                sb_i = 0
                for h in range(8):
                    s1i, u1, t_u1 = ws.next()
                    s2i, u2, t_u2 = ws.next()
                    qT, t_qT = fq[h % 2]
                    kT, t_kT = fk[h % 2]
                    V, t_V = fv[h % 2]
                    FZ, t_FZ = fzs[h % 2]
                    bm, t_bm = bms[h % 2]
                    for which, (dst, t_dst) in enumerate(((qT, t_qT), (kT, t_kT))):
                        for tc in range(4):
                            pp_h, pp_t = proj_bank()
                            for kc in range(KC):
                                op(PE, lambda kc=kc, tc=tc, which=which, pp_h=pp_h: nc.tensor.matmul(
                                    pp_h[:, 0:512], lhsT=u1[:, kc, which * 128:(which + 1) * 128],
                                    rhs=xnT[:, kc, tc * 512:(tc + 1) * 512], start=(kc == 0), stop=(kc == KC - 1)),
                                   reads=[t_u1, t_xnT], writes=[pp_t], inc=(kc == KC - 1))
                            op(DVE, lambda tc=tc, pp_h=pp_h, dst=dst: nc.vector.tensor_copy(
                                out=dst[:, tc * 512:(tc + 1) * 512], in_=pp_h[:, 0:512]), reads=[pp_t], writes=[t_dst])
                    ws.release(s1i)
                    for tb in range(NT):
                        pp_h, pp_t = proj_bank()
                        for kc in range(KC):
                            op(PE, lambda kc=kc, tb=tb, pp_h=pp_h: nc.tensor.matmul(
                                pp_h[:, 0:256], lhsT=xnT[:, kc, tb * 128:(tb + 1) * 128], rhs=u2[:, kc, :],
                                start=(kc == 0), stop=(kc == KC - 1)), reads=[t_u2, t_xnT], writes=[pp_t], inc=(kc == KC - 1))
                        op(DVE, lambda tb=tb, pp_h=pp_h: nc.vector.tensor_copy(out=V[:, tb, 0:128], in_=pp_h[:, 0:128]),
                           reads=[pp_t], writes=[t_V])
                        tz_h, tz_t = tzs[tb % 4]
                        op(ACT, lambda pp_h=pp_h, tz_h=tz_h: nc.scalar.activation(out=tz_h[:], in_=pp_h[:, 128:256],
                                                                                  func=AF.Tanh, scale=0.5),
                           reads=[pp_t], writes=[tz_t])
                        op(DVE, lambda tb=tb, pp_h=pp_h, tz_h=tz_h: nc.vector.scalar_tensor_tensor(
                            out=FZ[:, tb, :], in0=tz_h[:], scalar=1.0, in1=pp_h[:, 128:256], op0=ALU.add, op1=ALU.mult),
                           reads=[pp_t, tz_t], writes=[t_FZ])
                    ws.release(s2i)
                    for j in range(NT):
                        op(DVE, lambda j=j: nc.vector.tensor_scalar(out=bm[:, j, :], in0=RR[:, :, h],
                                                                    scalar1=cc[:, j, h:h + 1], scalar2=None,
                                                                    op0=ALU.subtract), reads=[t_RR, t_cc], writes=[t_bm])
                    steps = [(I, j) for I in range(4) for j in range(4 * I + 4)]

                    def emit_ST(n):
                        I, j = steps[n]
                        i0 = max(j, 4 * I)
                        ps_h, ps_t = PB[2 + n % 2]
                        N = (4 * I + 4 - i0) * 128
                        op(PE, lambda: nc.tensor.matmul(ps_h[:, 0:N], lhsT=kT[:, j * 128:(j + 1) * 128],
                                                        rhs=qT[:, i0 * 128:(4 * I + 4) * 128], start=True, stop=True),
                           reads=[t_kT, t_qT], writes=[ps_t])

                    def emit_exp(n):
                        I, j = steps[n]
                        i0 = max(j, 4 * I)
                        ps_h, ps_t = PB[2 + n % 2]
                        p_h, p_t = ptb[n % NPTB]
                        for i in range(i0, 4 * I + 4):
                            lo = (i - i0) * 128
                            op(ACT, lambda i=i, lo=lo: nc.scalar.activation(
                                out=p_h[:, lo:lo + 128], in_=ps_h[:, lo:lo + 128], func=AF.Exp, scale=sc,
                                bias=bm[:, j, i:i + 1]), reads=[ps_t, t_bm], writes=[p_t])
                        if j >= 4 * I:
                            op(DVE, lambda: nc.vector.tensor_tensor(out=p_h[:, 0:128], in0=p_h[:, 0:128], in1=maskb[:],
                                                                    op=ALU.mult), reads=[p_t, t_maskb], writes=[p_t])

                    def emit_PV(n):
                        I, j = steps[n]
                        i0 = max(j, 4 * I)
                        p_h, p_t = ptb[n % NPTB]
                        for i in range(i0, 4 * I + 4):
                            il = i - 4 * I
                            lo = (i - i0) * 128
                            po_h, po_t = PB[4 + il // 2]
                            off = (il % 2) * 129
                            op(PE, lambda i=i, lo=lo, po_h=po_h, off=off, il=il: nc.tensor.matmul(
                                po_h[:, off:off + 129], lhsT=p_h[:, lo:lo + 128], rhs=V[:, j, 0:129],
                                start=(j == 0 and il % 2 == 0), stop=(j == i), skip_group_check=True),
                               reads=[p_t, t_V], writes=[po_t])
                            if j == i:
                                ob_h, ob_t = obs[(4 * h + I) % 4]
                                op(DVE, lambda po_h=po_h, off=off, i=i: nc.vector.reciprocal(
                                    out=rin[:, i:i + 1], in_=po_h[:, off + 128:off + 129]), reads=[po_t], writes=[t_rin])
                                op(DVE, lambda po_h=po_h, off=off, i=i, il=il, ob_h=ob_h: nc.vector.scalar_tensor_tensor(
                                    out=ob_h[:, il, :], in0=po_h[:, off:off + 128], scalar=rin[:, i:i + 1], in1=FZ[:, i, :],
                                    op0=ALU.mult, op1=ALU.mult), reads=[po_t, t_rin, t_FZ], writes=[ob_t])
                                pt_h, pt_t = PT[I % 2]
                                pb0 = 0
                                op(PE, lambda il=il, ob_h=ob_h, pt_h=pt_h, pb0=pb0: nc.tensor.transpose(
                                    out=pt_h[:, pb0 + il, :], in_=ob_h[:, il, :], identity=ident[:]),
                                   reads=[ob_t, t_ident], writes=[pt_t])
                                if il == 3:
                                    op(DVE, lambda pt_h=pt_h, I=I: nc.vector.tensor_copy(
                                        out=bufA[:, h, I * 512:(I + 1) * 512],
                                        in_=pt_h[:, pb0:pb0 + 4, :].rearrange("p a b -> p (a b)")),
                                       reads=[pt_t], writes=[t_bufA])

                    emit_ST(0)
                    for n in range(len(steps)):
                        emit_exp(n)
                        if n + 1 < len(steps):
                            emit_ST(n + 1)
                        emit_PV(n)
                kb.end_scratch_phase()
            if first:
                tap("oaT", bufA[:], [128, KC, S], BF16, [t_bufA])

            if upto == "F":
                break

            proj_n[0] = 6
            with ExitStack() as ms:
                sgs = [kb.scratch(ms, f"e_sg{i}", [128, 512], F32) for i in range(2)]
                for oc in range(8):
                    si, u, t_u = ws.next()
                    for tc in range(4):
                        py_h, py_t = proj_bank()
                        for kc in range(KC):
                            op(PE, lambda kc=kc, tc=tc, py_h=py_h: nc.tensor.matmul(
                                py_h[:, 0:512], lhsT=u[:, kc, 0:128], rhs=bufA[:, kc, tc * 512:(tc + 1) * 512],
                                start=(kc == 0), stop=(kc == KC - 1)), reads=[t_u, t_bufA], writes=[py_t], inc=(kc == KC - 1))
                        pg_h, pg_t = proj_bank()
                        for kc in range(KC):
                            op(PE, lambda kc=kc, tc=tc, pg_h=pg_h: nc.tensor.matmul(
                                pg_h[:, 0:512], lhsT=u[:, kc, 128:256], rhs=xnT[:, kc, tc * 512:(tc + 1) * 512],
                                start=(kc == 0), stop=(kc == KC - 1)), reads=[t_u, t_xnT], writes=[pg_t], inc=(kc == KC - 1))
                        sg_h, sg_t = sgs[tc % 2]
                        op(ACT, lambda pg_h=pg_h, sg_h=sg_h, oc=oc: nc.scalar.activation(
                            out=sg_h[:], in_=pg_h[:, 0:512], func=AF.Tanh, bias=bgh[:, oc:oc + 1], scale=0.5),
                           reads=[pg_t, t_bgh], writes=[sg_t])
                        op(DVE, lambda py_h=py_h, sg_h=sg_h: nc.vector.scalar_tensor_tensor(
                            out=sg_h[:], in0=sg_h[:], scalar=1.0, in1=py_h[:, 0:512], op0=ALU.add, op1=ALU.mult),
                           reads=[py_t, sg_t], writes=[sg_t])
                        op(DVE, lambda sg_h=sg_h, oc=oc, tc=tc: nc.vector.tensor_tensor(
                            out=bufY[:, oc, tc * 512:(tc + 1) * 512], in0=sg_h[:], in1=bufY[:, oc, tc * 512:(tc + 1) * 512],
                            op=ALU.add), reads=[sg_t, t_bufY], writes=[t_bufY])
                    ws.release(si)
                kb.end_scratch_phase()
            if first:
                tap("yT", bufY[:], [128, KC, S], BF16, [t_bufY])

            if upto == "FD":
                break

            def _phase_O(seq=seq, bufY=bufY, t_bufY=t_bufY, ys=ys):
                dma(SP, out=gX[:], in_=final_g.partition_broadcast(128), writes=[t_gX])
                us = [ws.next() for _ in range(4)]
                for tb in range(NT):
                    xt_h, xt_t = xts[tb % NXT]
                    dma(SP, out=xt_h[:], in_=x_d[seq, tb * 128:(tb + 1) * 128, :], writes=[xt_t])
                    banks = [proj_bank(), proj_bank()]
                    for hf in range(2):
                        po_h, po_t = banks[hf]
                        for n in range(2):
                            si, u, t_u = us[2 * hf + n]
                            for kc in range(KC):
                                op(PE, lambda kc=kc, n=n, u=u, po_h=po_h: nc.tensor.matmul(
                                    po_h[:, n * 256:(n + 1) * 256], lhsT=bufY[:, kc, tb * 128:(tb + 1) * 128], rhs=u[:, kc, :],
                                    start=(kc == 0), stop=(kc == KC - 1)), reads=[t_u, t_bufY], writes=[po_t],
                                   inc=(kc == KC - 1 and n == 1))
                        op(DVE, lambda hf=hf, po_h=po_h: nc.vector.scalar_tensor_tensor(
                            out=xt_h[:, hf * 512:(hf + 1) * 512], in0=po_h[:, 0:512], scalar=0.5,
                            in1=xt_h[:, hf * 512:(hf + 1) * 512], op0=ALU.mult, op1=ALU.add),
                           reads=[po_t, xt_t], writes=[xt_t])
                    xn_h, xn_t = xns[tb % NXN]
                    rmsnorm_tail(xt_h, xt_t, tb, gF, t_gF, xt_h[:], [xt_t], xn_h[:], [xn_t])
                    dma(SP, out=y_d[seq, tb * 128:(tb + 1) * 128, :], in_=xt_h[:], reads=[xt_t])
                for (si, u, t_u) in us:
                    ws.release(si)
                kb.scr_tiles.append(t_bufY)
                kb.end_scratch_phase()
                ys.close()

            pending_O = _phase_O

        if pending_O is not None:
            pending_O()
            pending_O = None
        if kb.mode == "defer":
            kb.flush()
            build.n_inst = kb.n_inst
            build.nsem = kb.nsem
            build.est_us = kb.est_makespan
        out_costs = [(o.cost, o.aset) for o in kb.ops]
    return nc, dbg_out, out_costs


def host_inputs(inputs):
    f = lambda a: np.ascontiguousarray(np.asarray(a, dtype=np.float32))
    gb = np.concatenate([f(inputs["b_fox_f"])[0], f(inputs["b_ml_i"])[0], f(inputs["b_ml_f"])[0]])
    gbias = np.ascontiguousarray(np.broadcast_to(gb[None, None, :], (128, NT, 16)))
    convw = np.ascontiguousarray(f(inputs["conv_w"])[0].reshape(4, 8, 128).transpose(2, 1, 0))
    convb = np.ascontiguousarray(f(inputs["conv_b"])[0].reshape(8, 128).T)
    bgate = np.ascontiguousarray(f(inputs["b_gate"])[0].reshape(16, 128).T)
    return {
        "w_in": f(inputs["w_in"])[0],
        "w_fox_down": f(inputs["w_fox_down"])[0],
        "w_ml_down": f(inputs["w_ml_down"])[0],
        "w_out": f(inputs["w_out"])[0],
        "norm_g": f(inputs["norm_g"])[0],
        "final_g": f(inputs["final_g"]),
        "ml_norm_g": f(inputs["ml_norm_g"])[0],
        "gbias": gbias,
        "convw": convw,
        "convb": convb,
        "bgate": bgate,
    }


def kernel(**inputs):
    x = np.ascontiguousarray(np.asarray(inputs["x"], dtype=np.float32))
    B = x.shape[0]
    nseq = B // NCORES
    shared = host_inputs(inputs)
    nc, _ = build(nseq)
    in_maps = []
    for c in range(NCORES):
        m = dict(shared)
        m["x"] = x[c * nseq:(c + 1) * nseq]
        in_maps.append(m)
    res = run_bass_kernel_spmd(nc, in_maps, core_ids=list(range(NCORES)))
    out = np.concatenate([np.asarray(r["y"]) for r in res.results], axis=0)
    return out.astype(np.float32, copy=False)
```
